# Optimizing a Trainium2 kernel written in Bass

```python
import math
import jax
import jax.numpy as jnp
from jax import lax
import numpy as np

D_MODEL = 2048
BATCH = 1
SEQ = 8192
DEPTH = 4

CHUNK = 64
Q_BLOCK = 128
N_MIXERS = 3
EPS = 1e-6
D_FF = ((8 * D_MODEL) // 3 + 255) // 256 * 256

A_HEADS = D_MODEL // 256
A_HEAD_DIM = 128
A_V_DIM = 2 * A_HEAD_DIM
A_QK = 2 * A_HEADS * A_HEAD_DIM
A_OUT = A_HEADS * A_V_DIM
A_IN = 2 * A_QK + A_OUT

B_HEADS = D_MODEL // 128
B_HEAD_DIM = 128
B_W = B_HEADS * B_HEAD_DIM
B_IN = 4 * B_W + B_HEADS

C_QK_HEADS = D_MODEL // 128
C_V_HEADS = 2 * C_QK_HEADS
C_HEAD_DIM = 128
C_QK = C_QK_HEADS * C_HEAD_DIM
C_VW = C_V_HEADS * C_HEAD_DIM
C_CONV = 4
C_IN = 2 * C_QK + 2 * C_VW + 2 * C_V_HEADS

N_A = (DEPTH + 2) // 3
N_B = (DEPTH + 1) // 3
N_C = DEPTH // 3

kernel_name = 'hybrid_diff_fox_gdn_trunk'


def rms_norm(x, w):
    xf = x.astype(jnp.float32)
    y = xf * lax.rsqrt(jnp.mean(xf * xf, axis=-1, keepdims=True) + EPS)
    return (y * w.astype(jnp.float32)).astype(x.dtype)


def l2_normalize(x):
    xf = x.astype(jnp.float32)
    return xf * lax.rsqrt(jnp.sum(xf * xf, axis=-1, keepdims=True) + EPS)


def to_query_blocks(t):
    b, s = t.shape[:2]
    t = t.reshape((b, s // Q_BLOCK, Q_BLOCK) + t.shape[2:])
    return jnp.moveaxis(t, 1, 0)


def from_query_blocks(t):
    t = jnp.moveaxis(t, 0, 1)
    return t.reshape((t.shape[0], -1) + t.shape[3:])


def diff_lambda_init(layer_idx):
    return 0.8 - 0.6 * math.exp(-0.3 * layer_idx)


def differential_attention(xn, w_in, w_out, lam_q1, lam_k1, lam_q2, lam_k2, sub_norm, lambda_init):
    b, s, _ = xn.shape
    q, k, v = jnp.split(xn @ w_in, [A_QK, 2 * A_QK], axis=-1)
    q = q.reshape(b, s, 2, A_HEADS, A_HEAD_DIM)
    k = k.reshape(b, s, 2, A_HEADS, A_HEAD_DIM)
    v = v.reshape(b, s, A_HEADS, A_V_DIM)
    f32 = jnp.float32
    lam = (jnp.exp(jnp.sum((lam_q1 * lam_k1).astype(f32)))
           - jnp.exp(jnp.sum((lam_q2 * lam_k2).astype(f32))) + lambda_init)
    scale = A_HEAD_DIM ** -0.5
    k_chunk = jnp.arange(s) // CHUNK

    def block(args):
        qb, qpos = args
        sc = jnp.einsum('bqmhd,bkmhd->bmhqk', qb, k, preferred_element_type=f32) * scale
        visible = k_chunk[None, :] <= (qpos // CHUNK)[:, None]
        p = jax.nn.softmax(jnp.where(visible, sc, -jnp.inf), axis=-1)
        a = p[:, 0] - lam * p[:, 1]
        return jnp.einsum('bhqk,bkhe->bqhe', a.astype(v.dtype), v)

    qpos = jnp.arange(s).reshape(-1, Q_BLOCK)
    o = from_query_blocks(lax.map(block, (to_query_blocks(q), qpos)))
    o = rms_norm(o, sub_norm) * (1.0 - lambda_init)
    return o.reshape(b, s, A_OUT) @ w_out


def forgetting_attention(xn, w_in, w_out, forget_bias, q_norm, k_norm):
    b, s, _ = xn.shape
    f32 = jnp.float32
    q, k, v, gate, f_logit = jnp.split(xn @ w_in, [B_W, 2 * B_W, 3 * B_W, 4 * B_W], axis=-1)
    q = rms_norm(q.reshape(b, s, B_HEADS, B_HEAD_DIM), q_norm)
    k = rms_norm(k.reshape(b, s, B_HEADS, B_HEAD_DIM), k_norm)
    v = v.reshape(b, s, B_HEADS, B_HEAD_DIM)
    log_f = jax.nn.log_sigmoid((f_logit + forget_bias).astype(f32))
    cum = jnp.cumsum(log_f, axis=1)
    cum_k = jnp.transpose(cum, (0, 2, 1))
    scale = B_HEAD_DIM ** -0.5
    kpos = jnp.arange(s)

    def block(args):
        qb, cq, qpos = args
        sc = jnp.einsum('bqhd,bkhd->bhqk', qb, k, preferred_element_type=f32) * scale
        sc = sc + jnp.transpose(cq, (0, 2, 1))[..., :, None] - cum_k[..., None, :]
        visible = kpos[None, :] <= qpos[:, None]
        p = jax.nn.softmax(jnp.where(visible, sc, -jnp.inf), axis=-1)
        return jnp.einsum('bhqk,bkhd->bqhd', p.astype(v.dtype), v)

    qpos = jnp.arange(s).reshape(-1, Q_BLOCK)
    o = from_query_blocks(lax.map(block, (to_query_blocks(q), to_query_blocks(cum), qpos)))
    o = o.reshape(b, s, B_W) * jax.nn.sigmoid(gate)
    return o @ w_out


def causal_depthwise_conv(x, w):
    kw = w.shape[0]
    s = x.shape[1]
    xp = jnp.pad(x, ((0, 0), (kw - 1, 0), (0, 0)))
    return sum(xp[:, j:j + s, :] * w[j] for j in range(kw))


def chunked_gated_delta_rule(q, k, v, g, beta):
    b, s, h, dk = q.shape
    dv = v.shape[-1]
    n = s // CHUNK
    out_dtype = v.dtype
    f32 = jnp.float32

    def chunks(t):
        t = jnp.moveaxis(t.astype(f32), 2, 1)
        return t.reshape((b, h, n, CHUNK) + t.shape[3:])

    q, k, v, g, beta = chunks(q), chunks(k), chunks(v), chunks(g), chunks(beta)
    gc = jnp.cumsum(g, axis=-1)
    causal = jnp.tril(jnp.ones((CHUNK, CHUNK), dtype=bool))
    strict = jnp.tril(jnp.ones((CHUNK, CHUNK), dtype=bool), -1)
    decay = jnp.exp(jnp.where(causal, gc[..., :, None] - gc[..., None, :], -jnp.inf))
    kb = k * beta[..., None]
    lower = jnp.where(strict, jnp.einsum('bhnid,bhnjd->bhnij', kb, k) * decay, 0.0)
    tmat = lower + jnp.eye(CHUNK, dtype=f32)
    rhs = jnp.concatenate([v * beta[..., None], kb * jnp.exp(gc)[..., None]], axis=-1)
    sol = lax.linalg.triangular_solve(tmat, rhs, left_side=True, lower=True, unit_diagonal=True)
    u, w = sol[..., :dv], sol[..., dv:]
    intra = jnp.where(causal, jnp.einsum('bhnid,bhnjd->bhnij', q, k) * decay, 0.0)
    g_last = gc[..., -1]
    q_dec = q * jnp.exp(gc)[..., None]
    k_dec = k * jnp.exp(g_last[..., None] - gc)[..., None]
    xs = (jnp.moveaxis(q_dec, 2, 0), jnp.moveaxis(k_dec, 2, 0), jnp.moveaxis(u, 2, 0),
          jnp.moveaxis(w, 2, 0), jnp.moveaxis(intra, 2, 0), jnp.moveaxis(g_last, 2, 0))

    def step(state, inp):
        qd, kd, ui, wi, ai, gl = inp
        v_new = ui - jnp.einsum('bhcd,bhde->bhce', wi, state)
        o = jnp.einsum('bhcd,bhde->bhce', qd, state) + jnp.einsum('bhij,bhje->bhie', ai, v_new)
        state = state * jnp.exp(gl)[..., None, None] + jnp.einsum('bhcd,bhce->bhde', kd, v_new)
        return state, o

    state0 = jnp.zeros((b, h, dk, dv), dtype=f32)
    _, o = lax.scan(step, state0, xs)
    o = jnp.moveaxis(o, 0, 2).reshape(b, h, s, dv)
    return jnp.moveaxis(o, 1, 2).astype(out_dtype)


def gated_deltanet(xn, w_in, w_out, conv_w, a_log, dt_bias, out_norm):
    b, s, _ = xn.shape
    f32 = jnp.float32
    qkv, z, beta_logit, a = jnp.split(
        xn @ w_in, [2 * C_QK + C_VW, 2 * C_QK + 2 * C_VW, 2 * C_QK + 2 * C_VW + C_V_HEADS], axis=-1)
    qkv = jax.nn.silu(causal_depthwise_conv(qkv, conv_w))
    q, k, v = jnp.split(qkv, [C_QK, 2 * C_QK], axis=-1)
    rep = C_V_HEADS // C_QK_HEADS
    q = jnp.repeat(l2_normalize(q.reshape(b, s, C_QK_HEADS, C_HEAD_DIM)), rep, axis=2) * (C_HEAD_DIM ** -0.5)
    k = jnp.repeat(l2_normalize(k.reshape(b, s, C_QK_HEADS, C_HEAD_DIM)), rep, axis=2)
    v = v.reshape(b, s, C_V_HEADS, C_HEAD_DIM)
    beta = jax.nn.sigmoid(beta_logit.astype(f32))
    g = -jnp.exp(a_log.astype(f32)) * jax.nn.softplus((a + dt_bias).astype(f32))
    o = chunked_gated_delta_rule(q, k, v, g, beta)
    o = rms_norm(o, out_norm) * jax.nn.silu(z.reshape(b, s, C_V_HEADS, C_HEAD_DIM))
    return o.reshape(b, s, C_VW) @ w_out


def swiglu(xn, w_gate, w_up, w_down):
    return (jax.nn.silu(xn @ w_gate) * (xn @ w_up)) @ w_down


def setup_inputs(seed: int = 0) -> dict:
    key = jax.random.key(seed)
    ks = jax.random.split(key, 32)
    f32 = jnp.float32

    def nrm(i, shape, scale):
        return jax.random.normal(ks[i], shape, f32) * scale

    def gain(i, shape):
        return 1.0 + 0.05 * jax.random.normal(ks[i], shape, f32)

    dt = jnp.exp(jax.random.uniform(ks[25], (N_C, C_V_HEADS), f32, math.log(1e-3), math.log(1e-1)))
    dt_bias = dt + jnp.log(-jnp.expm1(-dt))
    a_log = jnp.log(jax.random.uniform(ks[26], (N_C, C_V_HEADS), f32, 1.0, 16.0))
    forget_bias = jax.random.uniform(ks[27], (N_B, B_HEADS), f32, 1.0, 4.0)
    return {
        'x': nrm(0, (BATCH, SEQ, D_MODEL), 1.0),
        'mix_norm': gain(1, (DEPTH, D_MODEL)),
        'ffn_norm': gain(2, (DEPTH, D_MODEL)),
        'final_norm': gain(3, (D_MODEL,)),
        'a_w_in': nrm(4, (N_A, D_MODEL, A_IN), D_MODEL ** -0.5),
        'a_w_out': nrm(5, (N_A, A_OUT, D_MODEL), A_OUT ** -0.5),
        'a_lam_q1': nrm(6, (N_A, A_HEAD_DIM), 0.1),
        'a_lam_k1': nrm(7, (N_A, A_HEAD_DIM), 0.1),
        'a_lam_q2': nrm(8, (N_A, A_HEAD_DIM), 0.1),
        'a_lam_k2': nrm(9, (N_A, A_HEAD_DIM), 0.1),
        'a_sub_norm': gain(10, (N_A, A_V_DIM)),
        'b_w_in': nrm(11, (N_B, D_MODEL, B_IN), D_MODEL ** -0.5),
        'b_w_out': nrm(12, (N_B, B_W, D_MODEL), B_W ** -0.5),
        'b_forget_bias': forget_bias,
        'b_q_norm': gain(13, (N_B, B_HEAD_DIM)),
        'b_k_norm': gain(14, (N_B, B_HEAD_DIM)),
        'c_w_in': nrm(15, (N_C, D_MODEL, C_IN), D_MODEL ** -0.5),
        'c_w_out': nrm(16, (N_C, C_VW, D_MODEL), C_VW ** -0.5),
        'c_conv_w': nrm(17, (N_C, C_CONV, 2 * C_QK + C_VW), C_CONV ** -0.5),
        'c_a_log': a_log,
        'c_dt_bias': dt_bias,
        'c_out_norm': gain(18, (N_C, C_HEAD_DIM)),
        'ffn_w_gate': nrm(19, (DEPTH, D_MODEL, D_FF), D_MODEL ** -0.5),
        'ffn_w_up': nrm(20, (DEPTH, D_MODEL, D_FF), D_MODEL ** -0.5),
        'ffn_w_down': nrm(21, (DEPTH, D_FF, D_MODEL), D_FF ** -0.5),
    }


def reference(x, mix_norm, ffn_norm, final_norm, a_w_in, a_w_out, a_lam_q1, a_lam_k1, a_lam_q2, a_lam_k2,
              a_sub_norm, b_w_in, b_w_out, b_forget_bias, b_q_norm, b_k_norm, c_w_in, c_w_out, c_conv_w,
              c_a_log, c_dt_bias, c_out_norm, ffn_w_gate, ffn_w_up, ffn_w_down):
    h = x
    for i in range(DEPTH):
        slot = i // N_MIXERS
        xn = rms_norm(h, mix_norm[i])
        if i % N_MIXERS == 0:
            mixed = differential_attention(xn, a_w_in[slot], a_w_out[slot], a_lam_q1[slot], a_lam_k1[slot],
                                           a_lam_q2[slot], a_lam_k2[slot], a_sub_norm[slot], diff_lambda_init(i))
        elif i % N_MIXERS == 1:
            mixed = forgetting_attention(xn, b_w_in[slot], b_w_out[slot], b_forget_bias[slot],
                                         b_q_norm[slot], b_k_norm[slot])
        else:
            mixed = gated_deltanet(xn, c_w_in[slot], c_w_out[slot], c_conv_w[slot], c_a_log[slot],
                                   c_dt_bias[slot], c_out_norm[slot])
        h = h + mixed.astype(h.dtype)
        h = h + swiglu(rms_norm(h, ffn_norm[i]), ffn_w_gate[i], ffn_w_up[i], ffn_w_down[i]).astype(h.dtype)
    return rms_norm(h, final_norm)
```

```python
import contextlib
import math
import numpy as np
import ml_dtypes
import concourse.bass as bass
import concourse.mybir as mybir
from concourse.bass_utils import run_bass_kernel_spmd

F32 = mybir.dt.float32
BF16 = mybir.dt.bfloat16
AF = mybir.ActivationFunctionType
ALU = mybir.AluOpType
AX = mybir.AxisListType

NCORES = 8
D = 2048
S = 8192
TL = S // NCORES
DFF = 5632
EPS = 1e-6


class _Op:
    __slots__ = ("stream", "fn", "deps", "is_dma", "semkey", "ticket", "signal", "idx")

    def __init__(self, stream, fn, is_dma=False, semkey=None):
        self.stream = stream
        self.fn = fn
        self.deps = []
        self.is_dma = is_dma
        self.semkey = semkey
        self.ticket = None
        self.signal = is_dma
        self.idx = -1


class KB:
    STREAMS = ("pe", "act", "dve", "pool", "sp")

    def __init__(self):
        self.nc = bass.Bass("TRN2", target_bir_lowering=False)
        self.es = contextlib.ExitStack()
        self.ops = []
        self.last_w = {}
        self.readers = {}
        self.last_on = {s: None for s in self.STREAMS}
        self.dma_open = []
        self.out_dmas = []

    def dram(self, name, shape, dt, kind):
        return self.nc.dram_tensor(name, list(shape), dt, kind=kind).ap()

    def _uniq(self, name):
        self._nuniq = getattr(self, "_nuniq", 0) + 1
        return "%s_%d" % (name, self._nuniq)

    def sb(self, name, shape, dt):
        return self.es.enter_context(self.nc.sbuf_tensor(self._uniq(name), list(shape), dt))

    def ps(self, name, shape, dt=F32):
        return self.es.enter_context(self.nc.psum_tensor(self._uniq(name), list(shape), dt))

    def _add(self, op, reads, writes):
        deps = []
        for k in reads:
            w = self.last_w.get(k)
            if w is not None:
                deps.append((w, "raw"))
        for k in writes:
            w = self.last_w.get(k)
            if w is not None:
                deps.append((w, "waw"))
            for r in self.readers.get(k, ()):
                deps.append((r, "war"))
        seen = set()
        for d, kind in deps:
            if d is op or id(d) in seen:
                continue
            if (not op.is_dma) and (not d.is_dma) and d.stream == op.stream:
                if op.stream == "pe" or (kind != "raw" and op.stream != "pool"):
                    continue
            seen.add(id(d))
            op.deps.append(d)
            d.signal = True
        op.idx = len(self.ops)
        self.ops.append(op)
        self.last_on[op.stream] = op
        for k in reads:
            lst = self.readers.setdefault(k, [])
            lst[:] = [r for r in lst if not (r.stream == op.stream and not r.is_dma and not op.is_dma)]
            lst.append(op)
        for k in writes:
            self.last_w[k] = op
            self.readers[k] = []
        return op

    def op(self, stream, fn, reads=(), writes=()):
        return self._add(_Op(stream, fn), reads, writes)

    def dma(self, queue, out, in_, reads=(), writes=(), semkey=None, is_out=False):
        if semkey is None:
            semkey = writes[0] if writes else reads[0]
        o = _Op(queue, lambda e: e.dma_start(out=out, in_=in_), is_dma=True, semkey=("dma", semkey))
        self._add(o, reads, writes)
        self.dma_open.append(o)
        if is_out:
            self.out_dmas.append(o)
        return o

    def barrier(self):
        lasts = [o for o in self.last_on.values() if o is not None] + list(self.dma_open)
        self.dma_open = []
        for s in self.STREAMS:
            b = _Op(s, None)
            for d in lasts:
                if d.stream == s and not d.is_dma and s == "pe":
                    continue
                b.deps.append(d)
                d.signal = True
            b.idx = len(self.ops)
            self.ops.append(b)

    def finish(self):
        b = _Op("sp", None)
        for d in self.out_dmas:
            b.deps.append(d)
        b.idx = len(self.ops)
        self.ops.append(b)

    def _init_emit(self):
        if getattr(self, "_sems", None) is None:
            E = self.es.enter_context
            self._sems = {s: E(self.nc.semaphore("sem_" + s)) for s in self.STREAMS}
            self._cnt = {s: 0 for s in self.STREAMS}
            self._dsem = {}
            self._dcnt = {}
            self._waited = {s: {} for s in self.STREAMS}
            self._emitted = 0

    def flush(self):
        nc = self.nc
        self._init_emit()
        E = self.es.enter_context
        new_ops = self.ops[self._emitted:]
        self._emitted = len(self.ops)
        for o in new_ops:
            if o.is_dma:
                if o.semkey not in self._dsem:
                    self._dsem[o.semkey] = E(nc.semaphore("dsem%d" % len(self._dsem)))
                    self._dcnt[o.semkey] = 0
                self._dcnt[o.semkey] += 16
                o.ticket = (self._dsem[o.semkey], self._dcnt[o.semkey])
            elif o.signal and o.fn is not None:
                self._cnt[o.stream] += 1
                o.ticket = (self._sems[o.stream], self._cnt[o.stream])
        by_stream = {s: [o for o in new_ops if o.stream == s] for s in self.STREAMS}

        def run(stream, eng):
            waited = self._waited[stream]
            for o in by_stream[stream]:
                need = {}
                for d in o.deps:
                    if d.ticket is None:
                        continue
                    sem, val = d.ticket
                    if need.get(id(sem), (None, 0))[1] < val:
                        need[id(sem)] = (sem, val)
                for sid, (sem, val) in need.items():
                    if waited.get(sid, 0) < val:
                        eng.wait_ge(sem, val)
                        waited[sid] = val
                if o.fn is None:
                    continue
                ins = o.fn(eng)
                if o.ticket is not None:
                    ins.then_inc(o.ticket[0], 16 if o.is_dma else 1)

        with nc.Block() as block:
            @block.tensor
            def _(e):
                run("pe", e)

            @block.scalar
            def _(e):
                run("act", e)

            @block.vector
            def _(e):
                run("dve", e)

            @block.gpsimd
            def _(e):
                run("pool", e)

            @block.sync
            def _(e):
                run("sp", e)

    @contextlib.contextmanager
    def phase(self):
        outer = self.es
        inner = contextlib.ExitStack()
        self._init_emit()
        self.es = inner
        try:
            yield
            self.es = outer
            self.barrier()
            self.flush()
        finally:
            self.es = outer
            inner.close()

    def done(self):
        self.finish()
        self.flush()
        self.es.close()
        return self.nc


def _ident(kb, name, dt):
    t = kb.sb(name, [128, 128], dt)
    kb.op("pool", lambda e: e.memset(t[:], 0.0), writes=[name])
    kb.op("pool", lambda e: e.affine_select(out=t[:], in_=t[:], pattern=[[-1, 128]], compare_op=ALU.not_equal,
                                            fill=1.0, base=0, channel_multiplier=1), reads=[name], writes=[name])
    return t


def _rsqrt(kb, out, in_, kin, kout, eps=EPS, rec=None):
    rec = rec or kb.op
    rec("act", lambda e: e.activation(out=out, in_=in_, func=AF.Ln, bias=eps), reads=[kin], writes=[kout])
    rec("act", lambda e: e.activation(out=out, in_=out, func=AF.Exp, scale=-0.5), reads=[kout], writes=[kout])


def _rms_to_xT(kb, h, hkey, nrm_sb, woff, xnT, xkey, identb, ss, rstd, tag):
    junk = kb.sb("junk" + tag, [128, D], BF16)
    xn = [kb.sb("xn%s%d" % (tag, i), [128, D], BF16) for i in range(2)]
    pT = [kb.ps("pT%s%d" % (tag, i), [128, 1024], BF16) for i in range(2)]
    for t in range(8):
        kb.op("act", lambda e, t=t: e.activation(out=junk[:], in_=h[:, t, :], func=AF.Square,
                                                 scale=1.0 / math.sqrt(D), accum_out=ss[:, t:t + 1]),
              reads=[(hkey, t)], writes=["junk" + tag, ("ss" + tag, t)])
        _rsqrt(kb, rstd[:, t:t + 1], ss[:, t:t + 1], ("ss" + tag, t), ("rstd" + tag, t))
        xb = xn[t % 2]
        kb.op("dve", lambda e, t=t, xb=xb: e.tensor_scalar(out=xb[:], in0=h[:, t, :], scalar1=rstd[:, t:t + 1], scalar2=None,
                                                           op0=ALU.mult),
              reads=[(hkey, t), ("rstd" + tag, t)], writes=[("xn" + tag, t % 2)])
        for half in range(2):
            pp = pT[half]
            for j in range(8):
                kt = half * 8 + j
                kb.op("pe", lambda e, pp=pp, j=j, kt=kt, xb=xb: e.transpose(pp[:, j * 128:(j + 1) * 128],
                                                                            xb[:, kt * 128:(kt + 1) * 128], identb[:]),
                      reads=[("xn" + tag, t % 2), "identb"], writes=[("pT" + tag, half)])
            for j in range(8):
                kt = half * 8 + j
                if half == 0:
                    kb.op("act", lambda e, pp=pp, j=j, kt=kt, t=t: e.activation(
                        out=xnT[:, kt, t * 128:(t + 1) * 128], in_=pp[:, j * 128:(j + 1) * 128], func=AF.Copy,
                        scale=nrm_sb[:, woff + kt:woff + kt + 1]),
                        reads=[("pT" + tag, half), "nrm_sb"], writes=[(xkey, kt, t), ("pT" + tag, half)])
                else:
                    kb.op("dve", lambda e, pp=pp, j=j, kt=kt, t=t: e.tensor_scalar(
                        out=xnT[:, kt, t * 128:(t + 1) * 128], in0=pp[:, j * 128:(j + 1) * 128],
                        scalar1=nrm_sb[:, woff + kt:woff + kt + 1], scalar2=None, op0=ALU.mult),
                        reads=[("pT" + tag, half), "nrm_sb"], writes=[(xkey, kt, t), ("pT" + tag, half)])


def build_T(kdim, first, last):
    kb = KB()
    KT = kdim // 128
    if first:
        h_in = kb.dram("h_in", [TL, D], F32, "ExternalInput")
        nrm = kb.dram("nrm", [128, 32], F32, "ExternalInput")
    else:
        h_in = kb.dram("h_in", [TL, D], F32, "ExternalInput")
        oT_in = kb.dram("oT_in", [kdim, TL], BF16, "ExternalInput")
        w_out = kb.dram("w_out", [kdim, D], F32, "ExternalInput")
        nrm = kb.dram("nrm", [128, 32], F32, "ExternalInput")
        w_gate = kb.dram("w_gate", [D, DFF], F32, "ExternalInput")
        w_up = kb.dram("w_up", [D, DFF], F32, "ExternalInput")
        w_down = kb.dram("w_down", [DFF, D], F32, "ExternalInput")
    if last:
        fnw = kb.dram("fnw", [1, D], F32, "ExternalInput")
        y_out = kb.dram("y_out", [TL, D], F32, "ExternalOutput")
    else:
        xnT_out = kb.dram("xnT_out", [D, TL], BF16, "ExternalOutput")
        if not first:
            h_out = kb.dram("h_out", [TL, D], F32, "ExternalOutput")

    h = kb.sb("h", [128, 8, D], F32)
    nrm_sb = kb.sb("nrm_sb", [128, 32], F32)
    ss = kb.sb("ss", [128, 16], F32)
    rstd = kb.sb("rstd", [128, 16], F32)
    identb = _ident(kb, "identb", BF16)
    kb.dma("sp", nrm_sb[:], nrm, writes=["nrm_sb"])
    for t in range(8):
        kb.dma("sp", h[:, t, :], h_in[t * 128:(t + 1) * 128, :], writes=[("h", t)])

    if not first:
        with kb.phase():
            oT = kb.sb("oT", [128, KT, TL], BF16)
            wo = [kb.sb("wo%d" % i, [128, KT, 512], BF16) for i in range(2)]
            pso = [kb.ps("pso%d" % i, [128, 512], F32) for i in range(4)]
            oview = oT_in.rearrange("(k p) t -> p k t", p=128)
            kg = KT // 4
            for g in range(4):
                kb.dma("sp", oT[:, g * kg:(g + 1) * kg, :], oview[:, g * kg:(g + 1) * kg, :], writes=[("oT", g)])
            wview = w_out.rearrange("(k p) n -> p k n", p=128)
            n = 0
            for c in range(4):
                wb = wo[c % 2]
                kb.dma("pool", wb[:], wview[:, :, c * 512:(c + 1) * 512], writes=[("wo", c % 2)])
                for t in range(8):
                    pb = pso[n % 4]
                    for k in range(KT):
                        kb.op("pe", lambda e, pb=pb, wb=wb, k=k, t=t: e.matmul(
                            pb[:], lhsT=oT[:, k, t * 128:(t + 1) * 128], rhs=wb[:, k, :], start=(k == 0), stop=(k == KT - 1)),
                            reads=[("oT", k // kg), ("wo", c % 2)], writes=[("pso", n % 4)])
                    kb.op("dve", lambda e, pb=pb, c=c, t=t: e.tensor_tensor(
                        out=h[:, t, c * 512:(c + 1) * 512], in0=h[:, t, c * 512:(c + 1) * 512], in1=pb[:], op=ALU.add),
                        reads=[("pso", n % 4), ("h", t)], writes=[("h", t), ("pso", n % 4)])
                    n += 1

        with kb.phase():
            xnT = kb.sb("xnT", [128, 16, TL], BF16)
            with kb.phase():
                _rms_to_xT(kb, h, "h", nrm_sb, 0, xnT, "xnT", identb, ss, rstd, "a")
            NGU = 4
            wgu = [kb.sb("wgu%d" % i, [128, 16, 256], BF16) for i in range(NGU)]
            wd = [kb.sb("wd%d" % i, [128, 11, 512], BF16) for i in range(2)]
            actT = kb.sb("actT", [128, 11, TL], BF16)
            sgt = [kb.sb("sgt%d" % i, [128, 512], F32) for i in range(2)]
            psg = [kb.ps("psg%d" % i, [128, 512], F32) for i in range(2)]
            psu = [kb.ps("psu%d" % i, [128, 512], F32) for i in range(2)]
            psd = [kb.ps("psd%d" % i, [128, 512], F32) for i in range(3)]
            gview = w_gate.rearrange("(k p) n -> p k n", p=128)
            uview = w_up.rearrange("(k p) n -> p k n", p=128)
            dview = w_down.rearrange("(k p) n -> p k n", p=128)
            xkeys = [("xnT", kt, t) for kt in range(16) for t in range(8)]
            nf = 0
            nh = 0
            nd = 0
            ndw = 0
            for g in range(4):
                for fi in range(11):
                    f = g * 11 + fi
                    sl = nf % NGU
                    wt = wgu[sl]
                    kb.dma("pool", wt[:, :, 0:128], gview[:, :, f * 128:(f + 1) * 128], writes=[("wgu", sl)])
                    kb.dma("pool", wt[:, :, 128:256], uview[:, :, f * 128:(f + 1) * 128], writes=[("wgu", sl)])
                    nf += 1
                    for half in range(2):
                        pg = psg[nh % 2]
                        pu = psu[nh % 2]
                        st = sgt[nh % 2]
                        hk = [("xnT", kt, t) for kt in range(16) for t in range(half * 4, half * 4 + 4)]
                        for k in range(16):
                            kb.op("pe", lambda e, pg=pg, wt=wt, k=k, half=half: e.matmul(
                                pg[:], lhsT=wt[:, k, 0:128], rhs=xnT[:, k, half * 512:(half + 1) * 512],
                                start=(k == 0), stop=(k == 15)),
                                reads=[("wgu", sl)] + (hk if k == 0 else []), writes=[("psg", nh % 2)])
                        for k in range(16):
                            kb.op("pe", lambda e, pu=pu, wt=wt, k=k, half=half: e.matmul(
                                pu[:], lhsT=wt[:, k, 128:256], rhs=xnT[:, k, half * 512:(half + 1) * 512],
                                start=(k == 0), stop=(k == 15)),
                                reads=[("wgu", sl)], writes=[("psu", nh % 2)])
                        kb.op("act", lambda e, pg=pg, st=st: e.activation(out=st[:], in_=pg[:], func=AF.Silu),
                              reads=[("psg", nh % 2)], writes=[("sgt", nh % 2), ("psg", nh % 2)])
                        kb.op("dve", lambda e, pu=pu, st=st, fi=fi, half=half: e.tensor_tensor(
                            out=actT[:, fi, half * 512:(half + 1) * 512], in0=st[:], in1=pu[:], op=ALU.mult),
                            reads=[("sgt", nh % 2), ("psu", nh % 2)], writes=[("actT", fi, half), ("psu", nh % 2)])
                        nh += 1
                for c in range(4):
                    wdt = wd[ndw % 2]
                    kb.dma("pool", wdt[:], dview[:, g * 11:(g + 1) * 11, c * 512:(c + 1) * 512], writes=[("wd", ndw % 2)])
                    for t in range(8):
                        pd = psd[nd % 3]
                        for fi in range(11):
                            kb.op("pe", lambda e, pd=pd, wdt=wdt, fi=fi, t=t: e.matmul(
                                pd[:], lhsT=actT[:, fi, t * 128:(t + 1) * 128], rhs=wdt[:, fi, :],
                                start=(fi == 0), stop=(fi == 10)),
                                reads=[("wd", ndw % 2), ("actT", fi, t // 4)], writes=[("psd", nd % 3)])
                        kb.op("dve", lambda e, pd=pd, c=c, t=t: e.tensor_tensor(
                            out=h[:, t, c * 512:(c + 1) * 512], in0=h[:, t, c * 512:(c + 1) * 512], in1=pd[:], op=ALU.add),
                            reads=[("psd", nd % 3), ("h", t)], writes=[("h", t), ("psd", nd % 3)])
                        nd += 1
                    ndw += 1

    if last:
        with kb.phase():
            fw = kb.sb("fw", [128, D], F32)
            junk = kb.sb("junkf", [128, D], BF16)
            yt = [kb.sb("yt%d" % i, [128, D], F32) for i in range(2)]
            kb.dma("sp", fw[:], fnw.partition_broadcast(128), writes=["fw"])
            for t in range(8):
                kb.op("act", lambda e, t=t: e.activation(out=junk[:], in_=h[:, t, :], func=AF.Square,
                                                         scale=1.0 / math.sqrt(D), accum_out=ss[:, t:t + 1]),
                      reads=[("h", t)], writes=["junkf", ("ssf", t)])
                _rsqrt(kb, rstd[:, t:t + 1], ss[:, t:t + 1], ("ssf", t), ("rstdf", t))
                y = yt[t % 2]
                kb.op("dve", lambda e, t=t, y=y: e.scalar_tensor_tensor(out=y[:], in0=h[:, t, :], scalar=rstd[:, t:t + 1],
                                                                        in1=fw[:], op0=ALU.mult, op1=ALU.mult),
                      reads=[("h", t), ("rstdf", t), "fw"], writes=[("yt", t % 2)])
                kb.dma("sp", y_out[t * 128:(t + 1) * 128, :], y[:], reads=[("yt", t % 2)], is_out=True)
    else:
        with kb.phase():
            xnT2 = kb.sb("xnT2", [128, 16, TL], BF16)
            _rms_to_xT(kb, h, "h", nrm_sb, 16, xnT2, "xnT2", identb, ss, rstd, "b")
            xkeys = [("xnT2", kt, t) for kt in range(16) for t in range(8)]
            xo = xnT_out.rearrange("(k p) t -> p k t", p=128)
            for g in range(4):
                kb.dma("sp", xo[:, g * 4:(g + 1) * 4, :], xnT2[:, g * 4:(g + 1) * 4, :],
                       reads=[("xnT2", kt, t) for kt in range(g * 4, g * 4 + 4) for t in range(8)],
                       semkey=("xo", g), is_out=True)
            if not first:
                for t in range(8):
                    kb.dma("sp", h_out[t * 128:(t + 1) * 128, :], h[:, t, :], reads=[("h", t)], semkey=("ho", t), is_out=True)
    return kb.done()


def _load_xc(kb, xg, xc, c, slot):
    r, off = c // 2, (c % 2) * 512
    src = xg[r].rearrange("(k p) t -> p k t", p=128)
    kb.dma("sp", xc[:], src[:, :, off:off + 512], writes=[("xc", slot)])


def _proj_fm(kb, ps, pskey, w_sb, col0, xc, xslot, evac):
    for k in range(16):
        kb.op("pe", lambda e, k=k: e.matmul(ps[:], lhsT=w_sb[:, k, col0:col0 + 128], rhs=xc[:, k, :],
                                            start=(k == 0), stop=(k == 15)),
              reads=["w_sb", ("xc", xslot)], writes=[pskey])
    evac(ps)


def _proj_tm(kb, ps, pskey, w_sb, col0, ncol, xc, xslot, s, evac):
    for k in range(16):
        kb.op("pe", lambda e, k=k: e.matmul(ps[:, 0:ncol], lhsT=xc[:, k, s * 128:(s + 1) * 128],
                                            rhs=w_sb[:, k, col0:col0 + ncol], start=(k == 0), stop=(k == 15)),
              reads=["w_sb", ("xc", xslot)], writes=[pskey])
    evac(ps)


def _diag_masks(kb, name):
    ms = []
    for r in range(4):
        m = kb.sb("%s%d" % (name, r), [128, 512], BF16)
        key = (name, r)
        kb.op("pool", lambda e, m=m: e.memset(m[:], 1.0), writes=[key])
        for half in range(2):
            base = -(128 * r + 64 * half)
            kb.op("pool", lambda e, m=m, half=half, base=base: e.affine_select(
                out=m[half * 64:(half + 1) * 64, :], in_=m[half * 64:(half + 1) * 64, :], pattern=[[1, 512]],
                compare_op=ALU.is_ge, fill=0.0, base=base, channel_multiplier=0), reads=[key], writes=[key])
        ms.append(m)
    return ms


def _causal_masks(kb, name):
    ms = []
    for r in range(4):
        m = kb.sb("%s%d" % (name, r), [128, 512], BF16)
        key = (name, r)
        kb.op("pool", lambda e, m=m: e.memset(m[:], 1.0), writes=[key])
        kb.op("pool", lambda e, m=m, r=r: e.affine_select(
            out=m[:], in_=m[:], pattern=[[1, 512]], compare_op=ALU.is_ge, fill=0.0, base=-128 * r,
            channel_multiplier=-1), reads=[key], writes=[key])
        ms.append(m)
    return ms


def _attn_pipe(kb, tag, maps, Vaug, vreads, dv, masks, mkey, scale, finish):
    NPS, NPT = 3, 4
    pS = [kb.ps("pS%s%d" % (tag, i), [128, 512], F32) for i in range(NPS)]
    psO = [kb.ps("pO%s%d" % (tag, i), [128, 512], F32) for i in range(4)]
    PT = [kb.sb("PT%s%d" % (tag, i), [128, 512], BF16) for i in range(NPT)]
    items = [(qc, m, k) for qc in range(16) for m in range(len(maps)) for k in range(qc * 4 + 4)]

    def rec_score(i):
        qc, m, kb_i = items[i]
        mp = maps[m]
        qT, kT = mp["qT"], mp["kT"]
        r = kb_i - qc * 4
        q0 = max(r, 0) * 128
        ps = pS[i % NPS]
        pt = PT[i % NPT]
        psk = ("pS" + tag, i % NPS)
        ptk = ("PT" + tag, i % NPT)
        aux = mp.get("aux")
        kb.op("pe", lambda e: e.matmul(ps[:, q0:512], lhsT=kT[:, kb_i * 128:(kb_i + 1) * 128],
                                       rhs=qT[:, qc * 512 + q0:(qc + 1) * 512], start=True, stop=(aux is None)),
              reads=[mp["qkey"], mp["kkey"]], writes=[psk])
        if aux is not None:
            ka, qa = aux
            kb.op("pe", lambda e: e.matmul(ps[:, q0:512], lhsT=ka[:, kb_i * 128:(kb_i + 1) * 128],
                                           rhs=qa[:, qc * 512 + q0:(qc + 1) * 512], start=False, stop=True),
                  reads=["ka", "qa"], writes=[psk])
        kb.op("act", lambda e: e.activation(out=pt[:, q0:512], in_=ps[:, q0:512], func=AF.Exp, scale=scale),
              reads=[psk], writes=[ptk, psk])
        if r >= 0:
            kb.op("pool", lambda e: e.tensor_tensor(out=pt[:, q0:512], in0=pt[:, q0:512], in1=masks[r][:, q0:512],
                                                    op=ALU.mult), reads=[ptk, (mkey, r)], writes=[ptk])

    def rec_pv(i):
        qc, m, kb_i = items[i]
        r = kb_i - qc * 4
        pt = PT[i % NPT]
        ptk = ("PT" + tag, i % NPT)
        for s in range(4):
            if r >= 0 and s < r:
                continue
            last = qc * 4 + s
            kb.op("pe", lambda e, s=s, last=last: e.matmul(
                psO[s][:, 0:dv + 1], lhsT=pt[:, s * 128:(s + 1) * 128], rhs=Vaug[:, kb_i, 0:dv + 1],
                start=(kb_i == 0), stop=(kb_i == last)), reads=[ptk] + list(vreads), writes=[("pO" + tag, s)])

    n = len(items)
    rec_score(0)
    for i in range(n):
        if i + 1 < n:
            rec_score(i + 1)
        rec_pv(i)
        qc, m, kb_i = items[i]
        if kb_i == qc * 4 + 3:
            finish(qc, m, psO)


def build_A():
    kb = KB()
    xg = kb.dram("xg", [8, D, TL], BF16, "ExternalInput")
    w_loc = kb.dram("w_loc", [D, 768], F32, "ExternalInput")
    lamv = kb.dram("lamv", [4, 128], F32, "ExternalInput")
    subw = kb.dram("subw", [1, 256], F32, "ExternalInput")
    cst = kb.dram("cst", [128, 2], F32, "ExternalInput")
    oT_out = kb.dram("oT_out", [8, 256, TL], BF16, "ExternalOutput")

    qT = [kb.sb("qT%d" % m, [128, S], BF16) for m in range(2)]
    kT = [kb.sb("kT%d" % m, [128, S], BF16) for m in range(2)]
    Vaug = kb.sb("Vaug", [128, 64, 264], BF16)
    identb = _ident(kb, "identb", BF16)
    cst_sb = kb.sb("cst_sb", [128, 2], F32)
    sw = kb.sb("sw", [128, 256], F32)
    lam4 = kb.sb("lam4", [128, 4, 128], F32)
    lsc = kb.sb("lsc", [128, 8], F32)
    kb.dma("sp", cst_sb[:], cst, writes=["cst_sb"])
    kb.dma("sp", sw[:], subw.partition_broadcast(128), writes=["sw"])
    for i in range(4):
        kb.dma("sp", lam4[:, i, :], lamv[i:i + 1, :].partition_broadcast(128), writes=["lam4"], semkey=("lam4", i))
    kb.op("pool", lambda e: e.memset(Vaug[:, :, 256:257], 1.0), writes=["Vones"])
    prod = kb.sb("lprod", [128, 2, 128], F32)
    kb.op("dve", lambda e: e.tensor_tensor(out=prod[:, 0, :], in0=lam4[:, 0, :], in1=lam4[:, 1, :], op=ALU.mult),
          reads=["lam4"], writes=["lprod"])
    kb.op("dve", lambda e: e.tensor_tensor(out=prod[:, 1, :], in0=lam4[:, 2, :], in1=lam4[:, 3, :], op=ALU.mult),
          reads=["lam4"], writes=["lprod"])
    kb.op("dve", lambda e: e.reduce_sum(out=lsc[:, 0:2], in_=prod[:], axis=AX.X), reads=["lprod"], writes=["lsc"])
    kb.op("act", lambda e: e.activation(out=lsc[:, 2:4], in_=lsc[:, 0:2], func=AF.Exp), reads=["lsc"], writes=["lsc"])
    kb.op("dve", lambda e: e.tensor_tensor(out=lsc[:, 4:5], in0=lsc[:, 3:4], in1=lsc[:, 2:3], op=ALU.subtract),
          reads=["lsc"], writes=["lsc"])
    kb.op("dve", lambda e: e.tensor_tensor(out=lsc[:, 4:5], in0=lsc[:, 4:5], in1=cst_sb[:, 0:1], op=ALU.subtract),
          reads=["lsc", "cst_sb"], writes=["lsc"])
    kb.op("dve", lambda e: e.tensor_scalar(out=sw[:], in0=sw[:], scalar1=cst_sb[:, 1:2], scalar2=None, op0=ALU.mult),
          reads=["sw", "cst_sb"], writes=["sw"])

    with kb.phase():
        w_sb = kb.sb("w_sb", [128, 16, 768], BF16)
        kb.dma("pool", w_sb[:], w_loc.rearrange("(k p) n -> p k n", p=128), writes=["w_sb"])
        xcs = [kb.sb("xc%d" % i, [128, 16, 512], BF16) for i in range(2)]
        pp = [kb.ps("pp%d" % i, [128, 512], F32) for i in range(4)]
        n = 0
        dests = [qT[0], qT[1], kT[0], kT[1]]
        dkeys = ["qT0", "qT1", "kT0", "kT1"]
        for c in range(16):
            xc = xcs[c % 2]
            _load_xc(kb, xg, xc, c, c % 2)
            for j in range(4):
                ps = pp[n % 4]
                eng = "act" if n % 2 == 0 else "dve"

                def evac(ps, j=j, c=c, eng=eng, n=n):
                    dst = dests[j][:, c * 512:(c + 1) * 512]
                    if eng == "act":
                        kb.op("act", lambda e: e.copy(out=dst, in_=ps[:]), reads=[("pp", n % 4)],
                              writes=[dkeys[j], ("pp", n % 4)])
                    else:
                        kb.op("dve", lambda e: e.tensor_copy(out=dst, in_=ps[:]), reads=[("pp", n % 4)],
                              writes=[dkeys[j], ("pp", n % 4)])
                _proj_fm(kb, ps, ("pp", n % 4), w_sb, j * 128, xc, c % 2, evac)
                n += 1
            for s in range(4):
                ps = pp[n % 4]
                eng = "act" if n % 2 == 0 else "dve"

                def evac(ps, s=s, c=c, eng=eng, n=n):
                    dst = Vaug[:, c * 4 + s, 0:256]
                    if eng == "act":
                        kb.op("act", lambda e: e.copy(out=dst, in_=ps[:, 0:256]), reads=[("pp", n % 4)],
                              writes=["Vaug", ("pp", n % 4)])
                    else:
                        kb.op("dve", lambda e: e.tensor_copy(out=dst, in_=ps[:, 0:256]), reads=[("pp", n % 4)],
                              writes=["Vaug", ("pp", n % 4)])
                _proj_tm(kb, ps, ("pp", n % 4), w_sb, 512, 256, xc, c % 2, s, evac)
                n += 1

    with kb.phase():
        masks = _diag_masks(kb, "dmask")
        acc = [kb.sb("acc%d" % s, [128, 256], F32) for s in range(4)]
        rc = kb.sb("rc", [128, 8], F32)
        st = kb.sb("st", [128, 8], F32)
        junk = kb.sb("junkA", [128, 256], F32)
        ob = [kb.sb("ob%d" % s, [128, 256], BF16) for s in range(2)]
        oTt = [kb.sb("oTt%d" % i, [128, 2, 512], BF16) for i in range(2)]
        pTr = kb.ps("pTr", [128, 1024], BF16)
        scale = 128 ** -0.5

        def finish(qc, m, psO):
            for s in range(4):
                pk = ("pOA", s)
                kb.op("dve", lambda e, s=s: e.reciprocal(out=rc[:, s:s + 1], in_=psO[s][:, 256:257]),
                      reads=[pk], writes=[("rc", s), pk])
                if m == 0:
                    kb.op("dve", lambda e, s=s: e.tensor_scalar(out=acc[s][:], in0=psO[s][:, 0:256],
                                                                 scalar1=rc[:, s:s + 1], scalar2=None, op0=ALU.mult),
                          reads=[pk, ("rc", s)], writes=[("acc", s), pk])
                else:
                    kb.op("dve", lambda e, s=s: e.tensor_tensor(out=rc[:, s:s + 1], in0=rc[:, s:s + 1], in1=lsc[:, 4:5],
                                                                 op=ALU.mult),
                          reads=[("rc", s), "lsc"], writes=[("rc", s)])
                    kb.op("dve", lambda e, s=s: e.scalar_tensor_tensor(out=acc[s][:], in0=psO[s][:, 0:256],
                                                                        scalar=rc[:, s:s + 1], in1=acc[s][:],
                                                                        op0=ALU.mult, op1=ALU.add),
                          reads=[pk, ("rc", s), ("acc", s)], writes=[("acc", s), pk])
            if m == 0:
                return
            ot = oTt[qc % 2]
            for s in range(4):
                kb.op("act", lambda e, s=s: e.activation(out=junk[:], in_=acc[s][:], func=AF.Square, scale=1.0 / 16.0,
                                                         accum_out=st[:, s:s + 1]),
                      reads=[("acc", s)], writes=["junkA", ("st", s)])
                _rsqrt(kb, st[:, s:s + 1], st[:, s:s + 1], ("st", s), ("st", s))
                o = ob[s % 2]
                kb.op("dve", lambda e, s=s, o=o: e.scalar_tensor_tensor(out=o[:], in0=acc[s][:], scalar=st[:, s:s + 1],
                                                                         in1=sw[:], op0=ALU.mult, op1=ALU.mult),
                      reads=[("acc", s), ("st", s), "sw"], writes=[("ob", s % 2)])
                for f in range(2):
                    kb.op("pe", lambda e, s=s, f=f, o=o: e.transpose(pTr[:, (s * 2 + f) * 128:(s * 2 + f + 1) * 128],
                                                                     o[:, f * 128:(f + 1) * 128], identb[:]),
                          reads=[("ob", s % 2), "identb"], writes=["pTr"])
                kb.op("dve", lambda e, s=s, ot=ot: e.tensor_copy(
                    out=ot[:, :, s * 128:(s + 1) * 128],
                    in_=pTr[:, s * 256:(s + 1) * 256].rearrange("p (f t) -> p f t", f=2)),
                    reads=["pTr"], writes=[("oTt", qc % 2), "pTr"])
            j, off = qc // 2, (qc % 2) * 512
            kb.dma("sp", oT_out[j].rearrange("(f p) t -> p f t", p=128)[:, :, off:off + 512], ot[:],
                   reads=[("oTt", qc % 2)], is_out=True)

        maps = [dict(qT=qT[m], kT=kT[m], qkey="qT%d" % m, kkey="kT%d" % m) for m in range(2)]
        _attn_pipe(kb, "A", maps, Vaug, ["Vaug", "Vones"], 256, masks, "dmask", scale, finish)
    return kb.done()


def build_B():
    kb = KB()
    xg = kb.dram("xg", [8, D, TL], BF16, "ExternalInput")
    w_loc = kb.dram("w_loc", [2, D, 514], F32, "ExternalInput")
    qkw = kb.dram("qkw", [128, 2], F32, "ExternalInput")
    fb = kb.dram("fb", [1, 2], F32, "ExternalInput")
    oT_out = kb.dram("oT_out", [8, 256, TL], BF16, "ExternalOutput")

    identb = _ident(kb, "identb", BF16)
    ones_f = kb.sb("ones_f", [128, 128], F32)
    kb.op("pool", lambda e: e.memset(ones_f[:], 1.0), writes=["ones_f"])
    qkw_sb = kb.sb("qkw_sb", [128, 2], F32)
    kb.dma("sp", qkw_sb[:], qkw, writes=["qkw_sb"])
    kb.op("dve", lambda e: e.tensor_scalar(out=qkw_sb[:, 0:1], in0=qkw_sb[:, 0:1], scalar1=128 ** -0.5, scalar2=None,
                                           op0=ALU.mult), reads=["qkw_sb"], writes=["qkw_sb"])
    nfb = kb.sb("nfb", [1, 2], F32)
    kb.dma("sp", nfb[:], fb, writes=["nfb"])
    kb.op("dve", lambda e: e.tensor_scalar(out=nfb[:], in0=nfb[:], scalar1=-1.0, scalar2=None, op0=ALU.mult),
          reads=["nfb"], writes=["nfb"])
    ones_r = kb.sb("ones_r", [1, 512], F32)
    kb.op("pool", lambda e: e.memset(ones_r[:], 1.0), writes=["ones_r"])

    for hl in range(2):
        with kb.phase():
            qT = kb.sb("qT", [128, S], BF16)
            kT = kb.sb("kT", [128, S], BF16)
            Vaug = kb.sb("Vaug", [128, 64, 136], BF16)
            G = kb.sb("G", [128, 64, 128], BF16)
            ka = kb.sb("ka", [6, S], BF16)
            qa = kb.sb("qa", [6, S], BF16)
            kb.op("pool", lambda e: e.memset(Vaug[:, :, 128:129], 1.0), writes=["Vones"])
            kb.op("pool", lambda e: e.memset(ka[:], 1.0), writes=["ka"])
            kb.op("pool", lambda e: e.memset(qa[:], 1.0), writes=["qa"])
            with kb.phase():
                w_sb = kb.sb("w_sb", [128, 16, 514], BF16)
                kb.dma("pool", w_sb[:], w_loc[hl].rearrange("(k p) n -> p k n", p=128), writes=["w_sb"])
                xc = kb.sb("xc", [128, 16, 512], BF16)
                pp = [kb.ps("pp%d" % i, [128, 512], F32) for i in range(4)]
                pss = kb.ps("pss", [128, 512], F32)
                psf = kb.ps("psf", [128, 512], F32)
                raw = [kb.sb("raw%d" % i, [128, 512], F32) for i in range(2)]
                sq = [kb.sb("sq%d" % i, [128, 512], F32) for i in range(2)]
                rs = [kb.sb("rs%d" % i, [128, 512], F32) for i in range(2)]
                e_t = kb.sb("e_t", [1, 512], F32)
                l_t = kb.sb("l_t", [1, 512], F32)
                cum = [kb.sb("cum%d" % i, [1, 512], F32) for i in range(2)]
                r1 = kb.sb("r1", [1, 512], F32)
                hml = kb.sb("hml", [1, 3, 512], BF16)
                nhml = kb.sb("nhml", [1, 3, 512], BF16)
                n = 0
                nq = 0
                for c in range(16):
                    _load_xc(kb, xg, xc, c, 0)
                    for j in range(2):
                        ps = pp[n % 4]
                        pk = ("pp", n % 4)
                        rw, sqq, rss = raw[nq % 2], sq[nq % 2], rs[nq % 2]
                        i2 = nq % 2

                        def evac(ps, j=j, c=c, pk=pk, rw=rw, sqq=sqq, rss=rss, i2=i2):
                            kb.op("act", lambda e: e.copy(out=rw[:], in_=ps[:]), reads=[pk], writes=[("raw", i2), pk])
                            kb.op("dve", lambda e: e.tensor_tensor(out=sqq[:], in0=rw[:], in1=rw[:], op=ALU.mult),
                                  reads=[("raw", i2)], writes=[("sq", i2)])
                            kb.op("pe", lambda e: e.matmul(pss[:], lhsT=ones_f[:], rhs=sqq[:], start=True, stop=True),
                                  reads=[("sq", i2), "ones_f"], writes=["pss"])
                            kb.op("dve", lambda e: e.tensor_scalar(out=rss[:], in0=pss[:], scalar1=1.0 / 128.0, scalar2=EPS,
                                                                   op0=ALU.mult, op1=ALU.add),
                                  reads=["pss"], writes=[("rs", i2), "pss"])
                            kb.op("act", lambda e: e.activation(out=rss[:], in_=rss[:], func=AF.Sqrt),
                                  reads=[("rs", i2)], writes=[("rs", i2)])
                            kb.op("dve", lambda e: e.reciprocal(out=rss[:], in_=rss[:]), reads=[("rs", i2)], writes=[("rs", i2)])
                            dst = (qT if j == 0 else kT)[:, c * 512:(c + 1) * 512]
                            kb.op("dve", lambda e: e.scalar_tensor_tensor(out=dst, in0=rw[:], scalar=qkw_sb[:, j:j + 1],
                                                                          in1=rss[:], op0=ALU.mult, op1=ALU.mult),
                                  reads=[("raw", i2), ("rs", i2), "qkw_sb"], writes=["qT" if j == 0 else "kT"])
                        _proj_fm(kb, ps, pk, w_sb, j * 128, xc, 0, evac)
                        n += 1
                        nq += 1
                    for s in range(4):
                        ps = pp[n % 4]
                        pk = ("pp", n % 4)

                        def evac(ps, s=s, c=c, pk=pk):
                            kb.op("dve", lambda e: e.tensor_copy(out=Vaug[:, c * 4 + s, 0:128], in_=ps[:, 0:128]),
                                  reads=[pk], writes=["Vaug", pk])
                            kb.op("act", lambda e: e.activation(out=G[:, c * 4 + s, :], in_=ps[:, 128:256], func=AF.Sigmoid),
                                  reads=[pk], writes=["G", pk])
                        _proj_tm(kb, ps, pk, w_sb, 256, 256, xc, 0, s, evac)
                        n += 1
                    for k in range(16):
                        kb.op("pe", lambda e, k=k: e.matmul(psf[0:1, :], lhsT=w_sb[:, k, 512:513], rhs=xc[:, k, :],
                                                            start=(k == 0), stop=(k == 15)),
                              reads=["w_sb", ("xc", 0)], writes=["psf"])
                    kb.op("act", lambda e: e.activation(out=e_t[:], in_=psf[0:1, :], func=AF.Exp, scale=-1.0,
                                                        bias=nfb[0:1, hl:hl + 1]),
                          reads=["psf", "nfb"], writes=["e_t", "psf"])
                    kb.op("act", lambda e: e.activation(out=l_t[:], in_=e_t[:], func=AF.Ln, bias=1.0),
                          reads=["e_t"], writes=["l_t"])
                    cu = cum[c % 2]
                    prev = cum[(c + 1) % 2]
                    init = 0.0 if c == 0 else prev[:, 511:512]
                    kb.op("dve", lambda e, cu=cu, init=init: e.tensor_tensor_scan(
                        out=cu[:], data0=ones_r[:], data1=l_t[:], initial=init, op0=ALU.mult, op1=ALU.subtract),
                        reads=["l_t", "ones_r", ("cum", (c + 1) % 2)], writes=[("cum", c % 2)])
                    ck = ("cum", c % 2)
                    kb.op("dve", lambda e, cu=cu: e.tensor_copy(out=hml[:, 0, :], in_=cu[:]), reads=[ck], writes=["hml"])
                    kb.op("dve", lambda e, cu=cu: e.tensor_tensor(out=r1[:], in0=cu[:], in1=hml[:, 0, :], op=ALU.subtract),
                          reads=[ck, "hml"], writes=["r1"])
                    kb.op("dve", lambda e: e.tensor_copy(out=hml[:, 1, :], in_=r1[:]), reads=["r1"], writes=["hml"])
                    kb.op("dve", lambda e: e.tensor_tensor(out=r1[:], in0=r1[:], in1=hml[:, 1, :], op=ALU.subtract),
                          reads=["r1", "hml"], writes=["r1"])
                    kb.op("dve", lambda e: e.tensor_copy(out=hml[:, 2, :], in_=r1[:]), reads=["r1"], writes=["hml"])
                    kb.op("dve", lambda e: e.tensor_scalar(out=nhml[:], in0=hml[:], scalar1=-1.0, scalar2=None, op0=ALU.mult),
                          reads=["hml"], writes=["nhml"])
                    for a in range(3):
                        kb.dma("sp", qa[a:a + 1, c * 512:(c + 1) * 512], hml[0:1, a, :], reads=["hml"], writes=["qa"],
                               semkey=("auxq", a))
                        kb.dma("sp", ka[3 + a:4 + a, c * 512:(c + 1) * 512], nhml[0:1, a, :], reads=["nhml"], writes=["ka"],
                               semkey=("auxk", a))
            with kb.phase():
                masks = _causal_masks(kb, "cmask")
                rc = kb.sb("rc", [128, 4], F32)
                ob = [kb.sb("ob%d" % i, [128, 128], BF16) for i in range(2)]
                oTt = [kb.sb("oTt%d" % i, [128, 512], BF16) for i in range(2)]
                pTr = kb.ps("pTr", [128, 1024], BF16)
                tag = "B"

                def finish(qc, m, psO):
                    ot = oTt[qc % 2]
                    for s in range(4):
                        pk = ("pO" + tag, s)
                        kb.op("dve", lambda e, s=s: e.reciprocal(out=rc[:, s:s + 1], in_=psO[s][:, 128:129]),
                              reads=[pk], writes=[("rc", s), pk])
                        o = ob[s % 2]
                        kb.op("dve", lambda e, s=s, o=o: e.scalar_tensor_tensor(
                            out=o[:], in0=psO[s][:, 0:128], scalar=rc[:, s:s + 1], in1=G[:, qc * 4 + s, :],
                            op0=ALU.mult, op1=ALU.mult), reads=[pk, ("rc", s), "G"], writes=[("ob", s % 2), pk])
                        kb.op("pe", lambda e, s=s, o=o: e.transpose(pTr[:, s * 128:(s + 1) * 128], o[:], identb[:]),
                              reads=[("ob", s % 2), "identb"], writes=["pTr"])
                    kb.op("act", lambda e, ot=ot: e.copy(out=ot[:], in_=pTr[:, 0:512]), reads=["pTr"],
                          writes=[("oTt", qc % 2), "pTr"])
                    j, off = qc // 2, (qc % 2) * 512
                    kb.dma("sp", oT_out[j, hl * 128:(hl + 1) * 128, off:off + 512], ot[:], reads=[("oTt", qc % 2)],
                           is_out=True)
                _attn_pipe(kb, tag, [dict(qT=qT, kT=kT, qkey="qT", kkey="kT", aux=(ka, qa))], Vaug, ["Vaug", "Vones"],
                           128, masks, "cmask", 1.0, finish)
    return kb.done()


def _affine_mat(kb, name, dt, pattern, base, cm, op):
    t = kb.sb(name, [128, 128], dt)
    kb.op("pool", lambda e: e.memset(t[:], 1.0), writes=[name])
    kb.op("pool", lambda e: e.affine_select(out=t[:], in_=t[:], pattern=pattern, compare_op=op, fill=0.0, base=base,
                                            channel_multiplier=cm), reads=[name], writes=[name])
    return t


def build_C():
    kb = KB()
    xg = kb.dram("xg", [8, D, TL], BF16, "ExternalInput")
    w_loc = kb.dram("w_loc", [4, D, 514], F32, "ExternalInput")
    cw = kb.dram("cw", [4, 128, 12], F32, "ExternalInput")
    hp = kb.dram("hp", [4, 2], F32, "ExternalInput")
    onw = kb.dram("onw", [1, 128], F32, "ExternalInput")
    oT_out = kb.dram("oT_out", [8, 512, TL], BF16, "ExternalOutput")

    identb = _ident(kb, "identb", BF16)
    identf = _ident(kb, "identf", F32)
    ones_f = kb.sb("ones_f", [128, 128], F32)
    kb.op("pool", lambda e: e.memset(ones_f[:], 1.0), writes=["ones_f"])
    triu = _affine_mat(kb, "triu", F32, [[1, 128]], 0, -1, ALU.is_ge)
    sel = _affine_mat(kb, "sel", F32, [[0, 128]], -127, 1, ALU.is_equal)
    lowm = _affine_mat(kb, "lowm", F32, [[-1, 128]], 0, 1, ALU.is_ge)
    strm = _affine_mat(kb, "strm", F32, [[-1, 128]], -1, 1, ALU.is_ge)
    wn = kb.sb("wn", [128, 128], F32)
    kb.dma("sp", wn[:], onw.partition_broadcast(128), writes=["wn"])
    bdm = kb.sb("bdm", [128, 128], F32)
    offm = kb.sb("offm", [128, 128], F32)
    kb.op("pool", lambda e: e.memset(bdm[:], 0.0), writes=["bdm"])
    kb.op("pool", lambda e: e.memset(offm[:], 1.0), writes=["offm"])
    for b in range(4):
        kb.op("pool", lambda e, b=b: e.memset(bdm[32 * b:32 * b + 32, 32 * b:32 * b + 32], 1.0), writes=["bdm"])
        kb.op("pool", lambda e, b=b: e.memset(offm[32 * b:32 * b + 32, 32 * b:32 * b + 32], 0.0), writes=["offm"])

    for hl in range(4):
        with kb.phase():
            qT = kb.sb("qT", [128, S], BF16)
            kT = kb.sb("kT", [128, S], BF16)
            ktm = kb.sb("ktm", [128, 64, 128], BF16)
            vb = kb.sb("vb", [128, 64, 128], BF16)
            Zs = kb.sb("Zs", [128, 64, 128], BF16)
            beta = kb.sb("beta", [128, 64], F32)
            g = kb.sb("g", [128, 64], F32)
            cw_sb = kb.sb("cw_sb", [128, 12], F32)
            hp_sb = kb.sb("hp_sb", [128, 2], F32)
            kb.dma("sp", cw_sb[:], cw[hl], writes=["cw_sb"])
            kb.dma("sp", hp_sb[:], hp[hl:hl + 1, :].partition_broadcast(128), writes=["hp_sb"])
            kb.op("act", lambda e: e.activation(out=hp_sb[:, 0:1], in_=hp_sb[:, 0:1], func=AF.Exp), reads=["hp_sb"],
                  writes=["hp_sb"])
            kb.op("dve", lambda e: e.tensor_scalar(out=hp_sb[:, 0:1], in0=hp_sb[:, 0:1], scalar1=-1.0, scalar2=None,
                                                   op0=ALU.mult), reads=["hp_sb"], writes=["hp_sb"])
            with kb.phase():
                w_sb = kb.sb("w_sb", [128, 16, 514], BF16)
                kb.dma("pool", w_sb[:], w_loc[hl].rearrange("(k p) n -> p k n", p=128), writes=["w_sb"])
                xc = kb.sb("xc", [128, 16, 512], BF16)
                pp = [kb.ps("pp%d" % i, [128, 512], F32) for i in range(4)]
                pss = kb.ps("pss", [128, 512], F32)
                ptr = kb.ps("ptr", [128, 1024], BF16)
                rawc = [kb.sb("rawc%d" % j, [128, 515], F32) for j in range(3)]
                acc = [kb.sb("cacc%d" % i, [128, 512], F32) for i in range(2)]
                sil = [kb.sb("sil%d" % i, [128, 512], F32) for i in range(2)]
                sq = kb.sb("sq", [128, 512], F32)
                rs = kb.sb("rs", [128, 512], F32)
                vbf = kb.sb("vbf", [128, 512], BF16)
                et = kb.sb("et", [128, 4], F32)
                for j in range(3):
                    kb.op("pool", lambda e, j=j: e.memset(rawc[j][:], 0.0), writes=[("rawc", j)])
                n = 0
                nq = 0
                for c in range(16):
                    _load_xc(kb, xg, xc, c, 0)
                    for s in range(4):
                        ps = pp[n % 4]
                        pk = ("pp", n % 4)
                        blk = c * 4 + s

                        def evac(ps, pk=pk, blk=blk, s=s):
                            kb.op("act", lambda e: e.activation(out=Zs[:, blk, :], in_=ps[:, 0:128], func=AF.Silu),
                                  reads=[pk], writes=["Zs", pk])
                            kb.op("act", lambda e: e.activation(out=beta[:, blk:blk + 1], in_=ps[:, 128:129], func=AF.Sigmoid),
                                  reads=[pk], writes=["beta", pk])
                            kb.op("act", lambda e: e.activation(out=et[:, s:s + 1], in_=ps[:, 129:130], func=AF.Exp,
                                                                bias=hp_sb[:, 1:2]),
                                  reads=[pk, "hp_sb"], writes=[("et", s), pk])
                            kb.op("act", lambda e: e.activation(out=et[:, s:s + 1], in_=et[:, s:s + 1], func=AF.Ln, bias=1.0),
                                  reads=[("et", s)], writes=[("et", s)])
                            kb.op("dve", lambda e: e.tensor_scalar(out=g[:, blk:blk + 1], in0=et[:, s:s + 1],
                                                                   scalar1=hp_sb[:, 0:1], scalar2=None, op0=ALU.mult),
                                  reads=[("et", s), "hp_sb"], writes=["g"])
                        _proj_tm(kb, ps, pk, w_sb, 384, 130, xc, 0, s, evac)
                        n += 1
                    for j in range(3):
                        ps = pp[n % 4]
                        pk = ("pp", n % 4)
                        rw = rawc[j]
                        ac = acc[nq % 2]
                        sl = sil[nq % 2]
                        i2 = nq % 2

                        def evac(ps, pk=pk, rw=rw, ac=ac, sl=sl, i2=i2, j=j, c=c):
                            rk = ("rawc", j)
                            if c > 0:
                                kb.op("dve", lambda e: e.tensor_copy(out=rw[:, 0:3], in_=rw[:, 512:515]), reads=[rk], writes=[rk])
                            kb.op("act", lambda e: e.copy(out=rw[:, 3:515], in_=ps[:]), reads=[pk], writes=[rk, pk])
                            kb.op("dve", lambda e: e.tensor_scalar(out=ac[:], in0=rw[:, 0:512], scalar1=cw_sb[:, j * 4:j * 4 + 1],
                                                                   scalar2=None, op0=ALU.mult),
                                  reads=[rk, "cw_sb"], writes=[("cacc", i2)])
                            for tap in range(1, 4):
                                kb.op("dve", lambda e, tap=tap: e.scalar_tensor_tensor(
                                    out=ac[:], in0=rw[:, tap:tap + 512], scalar=cw_sb[:, j * 4 + tap:j * 4 + tap + 1],
                                    in1=ac[:], op0=ALU.mult, op1=ALU.add), reads=[rk, "cw_sb", ("cacc", i2)],
                                    writes=[("cacc", i2)])
                            kb.op("act", lambda e: e.activation(out=sl[:], in_=ac[:], func=AF.Silu), reads=[("cacc", i2)],
                                  writes=[("sil", i2)])
                            if j < 2:
                                kb.op("dve", lambda e: e.tensor_tensor(out=sq[:], in0=sl[:], in1=sl[:], op=ALU.mult),
                                      reads=[("sil", i2)], writes=["sq"])
                                kb.op("pe", lambda e: e.matmul(pss[:], lhsT=ones_f[:], rhs=sq[:], start=True, stop=True),
                                      reads=["sq", "ones_f"], writes=["pss"])
                                kb.op("dve", lambda e: e.tensor_scalar(out=rs[:], in0=pss[:], scalar1=EPS, scalar2=None,
                                                                       op0=ALU.add), reads=["pss"], writes=["rs", "pss"])
                                kb.op("act", lambda e: e.activation(out=rs[:], in_=rs[:], func=AF.Sqrt), reads=["rs"],
                                      writes=["rs"])
                                kb.op("dve", lambda e: e.reciprocal(out=rs[:], in_=rs[:]), reads=["rs"], writes=["rs"])
                                dst = (qT if j == 0 else kT)[:, c * 512:(c + 1) * 512]
                                sc = 128 ** -0.5 if j == 0 else 1.0
                                kb.op("dve", lambda e: e.scalar_tensor_tensor(out=dst, in0=sl[:], scalar=sc, in1=rs[:],
                                                                              op0=ALU.mult, op1=ALU.mult),
                                      reads=[("sil", i2), "rs"], writes=["qT" if j == 0 else "kT"])
                                if j == 1:
                                    for s in range(4):
                                        kb.op("pe", lambda e, s=s: e.transpose(
                                            ptr[:, s * 128:(s + 1) * 128], kT[:, c * 512 + s * 128:c * 512 + (s + 1) * 128],
                                            identb[:]), reads=["kT", "identb"], writes=["ptr"])
                                    kb.op("act", lambda e: e.copy(
                                        out=ktm[:, c * 4:(c + 1) * 4, :],
                                        in_=ptr[:, 0:512].rearrange("p (s d) -> p s d", s=4)), reads=["ptr"], writes=["ktm", "ptr"])
                            else:
                                kb.op("dve", lambda e: e.tensor_copy(out=vbf[:], in_=sl[:]), reads=[("sil", i2)], writes=["vbf"])
                                for s in range(4):
                                    kb.op("pe", lambda e, s=s: e.transpose(ptr[:, 512 + s * 128:512 + (s + 1) * 128],
                                                                           vbf[:, s * 128:(s + 1) * 128], identb[:]),
                                          reads=["vbf", "identb"], writes=["ptr"])
                                for s in range(4):
                                    blk = c * 4 + s
                                    kb.op("dve", lambda e, s=s, blk=blk: e.tensor_scalar(
                                        out=vb[:, blk, :], in0=ptr[:, 512 + s * 128:512 + (s + 1) * 128],
                                        scalar1=beta[:, blk:blk + 1], scalar2=None, op0=ALU.mult),
                                        reads=["ptr", "beta"], writes=["vb", "ptr"])
                        _proj_fm(kb, ps, pk, w_sb, j * 128, xc, 0, evac)
                        n += 1
                        nq += 1
            gc = kb.sb("gc", [128, 64], F32)
            glb = kb.sb("glb", [128, 64], F32)
            egc = kb.sb("egc", [128, 64], F32)
            ekd = kb.sb("ekd", [128, 64], F32)
            egl = kb.sb("egl", [128, 64], F32)
            begc = kb.sb("begc", [128, 64], F32)
            nbeta = kb.sb("nbeta", [128, 64], F32)
            with kb.phase():
                pg = kb.ps("pg", [128, 512], F32)
                kb.op("pe", lambda e: e.matmul(pg[:, 0:64], lhsT=triu[:], rhs=g[:], start=True, stop=True),
                      reads=["triu", "g"], writes=["pg"])
                kb.op("dve", lambda e: e.tensor_copy(out=gc[:], in_=pg[:, 0:64]), reads=["pg"], writes=["gc", "pg"])
                kb.op("pe", lambda e: e.matmul(pg[:, 64:128], lhsT=sel[:], rhs=gc[:], start=True, stop=True),
                      reads=["sel", "gc"], writes=["pg"])
                kb.op("dve", lambda e: e.tensor_copy(out=glb[:], in_=pg[:, 64:128]), reads=["pg"], writes=["glb", "pg"])
                kb.op("act", lambda e: e.activation(out=egc[:], in_=gc[:], func=AF.Exp), reads=["gc"], writes=["egc"])
                kb.op("act", lambda e: e.activation(out=egl[:], in_=glb[:], func=AF.Exp), reads=["glb"], writes=["egl"])
                kb.op("dve", lambda e: e.tensor_tensor(out=ekd[:], in0=glb[:], in1=gc[:], op=ALU.subtract),
                      reads=["glb", "gc"], writes=["ekd"])
                kb.op("act", lambda e: e.activation(out=ekd[:], in_=ekd[:], func=AF.Exp), reads=["ekd"], writes=["ekd"])
                kb.op("dve", lambda e: e.tensor_tensor(out=begc[:], in0=beta[:], in1=egc[:], op=ALU.mult),
                      reads=["beta", "egc"], writes=["begc"])
                kb.op("dve", lambda e: e.tensor_scalar(out=nbeta[:], in0=beta[:], scalar1=-1.0, scalar2=None, op0=ALU.mult),
                      reads=["beta"], writes=["nbeta"])
            with kb.phase():
                bA = kb.ps("bA", [128, 512], F32)
                bB = kb.ps("bB", [128, 512], F32)
                bC = kb.ps("bC", [128, 512], F32)
                bD = kb.ps("bD", [128, 512], F32)
                bE = kb.ps("bE", [128, 512], F32)
                bF = kb.ps("bF", [128, 512], F32)
                bG = kb.ps("bG", [128, 1024], BF16)
                bH = kb.ps("bH", [128, 512], F32)
                St = kb.sb("St", [128, 128], F32)
                Sb = kb.sb("Sb", [128, 128], BF16)
                kb.op("pool", lambda e: e.memset(St[:], 0.0), writes=["St"])
                kb.op("pool", lambda e: e.memset(Sb[:], 0.0), writes=["Sb"])
                dg = kb.sb("dg", [128, 128], F32)
                Dm = kb.sb("Dm", [128, 128], F32)
                Ds = kb.sb("Ds", [128, 128], F32)
                Nf = kb.sb("Nf", [128, 128], F32)
                Noff = kb.sb("Noff", [128, 128], F32)
                M = [kb.sb("M%d" % i, [128, 128], F32) for i in range(2)]
                MT = [kb.sb("MT%d" % i, [128, 128], F32) for i in range(2)]
                X = [kb.sb("X%d" % i, [128, 128], F32) for i in range(2)]
                Bi = kb.sb("Bi", [128, 128], F32)
                Pm = kb.sb("Pm", [128, 128], F32)
                PTm = kb.sb("PTm", [128, 128], F32)
                P2T = kb.sb("P2T", [128, 128], F32)
                Ym = kb.sb("Ym", [128, 128], F32)
                Wm = kb.sb("Wm", [128, 128], F32)
                Xf = kb.sb("Xf", [128, 128], BF16)
                intra = kb.sb("intra", [128, 128], BF16)
                intraT = [kb.sb("intraT%d" % i, [128, 128], BF16) for i in range(2)]
                kbg = kb.sb("kbg", [128, 128], BF16)
                kdec = [kb.sb("kdec%d" % i, [128, 128], BF16) for i in range(2)]
                u = [kb.sb("u%d" % i, [128, 128], F32) for i in range(2)]
                wT = [kb.sb("wT%d" % i, [128, 128], BF16) for i in range(2)]
                vn = kb.sb("vn", [128, 128], BF16)
                t1 = kb.sb("t1", [128, 128], F32)
                o = kb.sb("o", [128, 128], F32)
                junk = kb.sb("junkC", [128, 128], F32)
                st = kb.sb("stC", [128, 4], F32)
                on = kb.sb("on", [128, 128], F32)
                ob = kb.sb("obC", [128, 128], BF16)
                oTt = [kb.sb("oTtC%d" % i, [128, 512], BF16) for i in range(2)]
                def gen_prep(n):
                    lst = []

                    def P(*a, **k):
                        lst.append(lambda: kb.op(*a, **k))
                    tok = slice(n * 128, (n + 1) * 128)
                    col = slice(n, n + 1)
                    P("pe", lambda e, tok=tok: e.matmul(bA[:, 0:128], lhsT=kT[:, tok], rhs=kT[:, tok], start=True, stop=True),
                          reads=["kT"], writes=["bA"])
                    P("dve", lambda e, col=col: e.tensor_scalar(out=dg[:], in0=identf[:], scalar1=gc[:, col], scalar2=None,
                                                                    op0=ALU.mult), reads=["identf", "gc"], writes=["dg"])
                    P("pe", lambda e: e.matmul(bB[:, 0:128], lhsT=ones_f[:], rhs=dg[:], start=True, stop=True),
                          reads=["dg", "ones_f"], writes=["bB"])
                    P("pe", lambda e, tok=tok: e.matmul(bC[:, 0:128], lhsT=qT[:, tok], rhs=kT[:, tok], start=True, stop=True),
                          reads=["qT", "kT"], writes=["bC"])
                    P("dve", lambda e, col=col: e.tensor_scalar(out=Dm[:], in0=bB[:, 0:128], scalar1=-1.0, scalar2=gc[:, col],
                                                                    op0=ALU.mult, op1=ALU.add),
                          reads=["bB", "gc"], writes=["Dm", "bB"])
                    P("dve", lambda e: e.tensor_scalar(out=Dm[:], in0=Dm[:], scalar1=0.0, scalar2=None, op0=ALU.min),
                          reads=["Dm"], writes=["Dm"])
                    P("act", lambda e: e.activation(out=Dm[:], in_=Dm[:], func=AF.Exp), reads=["Dm"], writes=["Dm"])
                    P("pool", lambda e: e.tensor_tensor(out=Ds[:], in0=Dm[:], in1=strm[:], op=ALU.mult),
                          reads=["Dm", "strm"], writes=["Ds"])
                    P("pool", lambda e: e.tensor_tensor(out=Dm[:], in0=Dm[:], in1=lowm[:], op=ALU.mult),
                          reads=["Dm", "lowm", "Ds"], writes=["Dm"])
                    P("dve", lambda e, col=col: e.scalar_tensor_tensor(out=Nf[:], in0=bA[:, 0:128], scalar=nbeta[:, col],
                                                                           in1=Ds[:], op0=ALU.mult, op1=ALU.mult),
                          reads=["bA", "nbeta", "Ds"], writes=["Nf", "bA"])
                    P("dve", lambda e: e.tensor_tensor(out=intra[:], in0=bC[:, 0:128], in1=Dm[:], op=ALU.mult),
                          reads=["bC", "Dm"], writes=["intra", "bC"])
                    P("pe", lambda e: e.transpose(bG[:, 128:256], intra[:], identb[:]), reads=["intra", "identb"], writes=["bG"])
                    P("act", lambda e: e.copy(out=intraT[n % 2][:], in_=bG[:, 128:256]), reads=["bG"], writes=[("intraT", n % 2), "bG"])
                    P("pool", lambda e: e.tensor_tensor(out=M[0][:], in0=Nf[:], in1=bdm[:], op=ALU.mult),
                          reads=["Nf", "bdm"], writes=[("M", 0)])
                    P("pool", lambda e: e.tensor_tensor(out=Noff[:], in0=Nf[:], in1=offm[:], op=ALU.mult),
                          reads=["Nf", "offm"], writes=["Noff"])
                    P("pe", lambda e: e.transpose(bD[:, 0:128], M[0][:], identf[:]), reads=[("M", 0), "identf"], writes=["bD"])
                    P("act", lambda e: e.copy(out=MT[0][:], in_=bD[:, 0:128]), reads=["bD"], writes=[("MT", 0), "bD"])
                    P("dve", lambda e: e.tensor_tensor(out=X[0][:], in0=MT[0][:], in1=identf[:], op=ALU.add),
                          reads=[("MT", 0), "identf"], writes=[("X", 0)])
                    xi = 0
                    mi = 0
                    for lev in range(1, 5):
                        mo = 1 - mi
                        P("pe", lambda e, mi=mi: e.matmul(bD[:, 0:128], lhsT=MT[mi][:], rhs=M[mi][:], start=True, stop=True),
                              reads=[("M", mi), ("MT", mi)], writes=["bD"])
                        if lev < 4:
                            P("pe", lambda e, mi=mi: e.matmul(bE[:, 0:128], lhsT=M[mi][:], rhs=MT[mi][:], start=True, stop=True),
                                  reads=[("M", mi), ("MT", mi)], writes=["bE"])
                        P("act", lambda e, mo=mo: e.copy(out=M[mo][:], in_=bD[:, 0:128]), reads=["bD"],
                              writes=[("M", mo), "bD"])
                        if lev < 4:
                            P("dve", lambda e, mo=mo: e.tensor_copy(out=MT[mo][:], in_=bE[:, 0:128]), reads=["bE"],
                                  writes=[("MT", mo), "bE"])
                        P("pe", lambda e, mo=mo, xi=xi: e.matmul(bF[:, 0:128], lhsT=M[mo][:], rhs=X[xi][:], start=True, stop=True),
                              reads=[("M", mo), ("X", xi)], writes=["bF"])
                        P("dve", lambda e, xi=xi: e.tensor_tensor(out=X[1 - xi][:], in0=bF[:, 0:128], in1=X[xi][:], op=ALU.add),
                              reads=["bF", ("X", xi)], writes=[("X", 1 - xi), "bF"])
                        xi = 1 - xi
                        mi = mo
                    BiT = X[xi]
                    bk = ("X", xi)
                    P("pe", lambda e, BiT=BiT: e.transpose(bD[:, 0:128], BiT[:], identf[:]), reads=[bk, "identf"], writes=["bD"])
                    P("act", lambda e: e.copy(out=Bi[:], in_=bD[:, 0:128]), reads=["bD"], writes=["Bi", "bD"])
                    P("pe", lambda e, BiT=BiT: e.matmul(bE[:, 0:128], lhsT=Noff[:], rhs=BiT[:], start=True, stop=True),
                          reads=["Noff", bk], writes=["bE"])
                    P("pe", lambda e, BiT=BiT: e.matmul(bF[:, 0:128], lhsT=BiT[:], rhs=Noff[:], start=True, stop=True),
                          reads=["Noff", bk], writes=["bF"])
                    P("dve", lambda e: e.tensor_copy(out=Pm[:], in_=bE[:, 0:128]), reads=["bE"], writes=["Pm", "bE"])
                    P("act", lambda e: e.copy(out=PTm[:], in_=bF[:, 0:128]), reads=["bF"], writes=["PTm", "bF"])
                    P("pe", lambda e: e.matmul(bD[:, 0:128], lhsT=Pm[:], rhs=PTm[:], start=True, stop=True),
                          reads=["Pm", "PTm"], writes=["bD"])
                    P("act", lambda e: e.copy(out=P2T[:], in_=bD[:, 0:128]), reads=["bD"], writes=["P2T", "bD"])
                    P("dve", lambda e: e.tensor_tensor(out=Ym[:], in0=Pm[:], in1=identf[:], op=ALU.add),
                          reads=["Pm", "identf"], writes=["Ym"])
                    P("pe", lambda e: e.matmul(bE[:, 0:128], lhsT=P2T[:], rhs=Ym[:], start=True, stop=True),
                          reads=["P2T", "Ym"], writes=["bE"])
                    P("dve", lambda e: e.tensor_tensor(out=Wm[:], in0=bE[:, 0:128], in1=Ym[:], op=ALU.add),
                          reads=["bE", "Ym"], writes=["Wm", "bE"])
                    P("pe", lambda e: e.matmul(bF[:, 0:128], lhsT=Bi[:], rhs=Wm[:], start=True, stop=True),
                          reads=["Bi", "Wm"], writes=["bF"])
                    P("act", lambda e: e.copy(out=Xf[:], in_=bF[:, 0:128]), reads=["bF"], writes=["Xf", "bF"])
                    xk = "Xf"
                    P("act", lambda e, n=n, col=col: e.activation(out=kbg[:], in_=ktm[:, n, :], func=AF.Copy, scale=begc[:, col]),
                          reads=["ktm", "begc"], writes=["kbg"])
                    P("dve", lambda e, n=n, col=col: e.tensor_scalar(out=kdec[n % 2][:], in0=ktm[:, n, :], scalar1=ekd[:, col],
                                                                          scalar2=None, op0=ALU.mult),
                          reads=["ktm", "ekd"], writes=[("kdec", n % 2)])
                    P("pe", lambda e, n=n, Xf=Xf: e.matmul(bA[:, 0:128], lhsT=Xf[:], rhs=vb[:, n, :], start=True, stop=True),
                          reads=[xk, "vb"], writes=["bA"])
                    P("pe", lambda e, Xf=Xf: e.matmul(bB[:, 0:128], lhsT=kbg[:], rhs=Xf[:], start=True, stop=True),
                          reads=[xk, "kbg"], writes=["bB"])
                    P("act", lambda e: e.copy(out=u[n % 2][:], in_=bA[:, 0:128]), reads=["bA"], writes=[("u", n % 2), "bA"])
                    P("dve", lambda e: e.tensor_copy(out=wT[n % 2][:], in_=bB[:, 0:128]), reads=["bB"], writes=[("wT", n % 2), "bB"])
                    return lst

                def gen_scan(n):
                    lst = []

                    def Sx(*a, **k):
                        lst.append(lambda: kb.op(*a, **k))

                    def Sd(*a, **k):
                        lst.append(lambda: kb.dma(*a, **k))
                    tok = slice(n * 128, (n + 1) * 128)
                    col = slice(n, n + 1)
                    Sx("pe", lambda e: e.matmul(bH[:, 256:384], lhsT=wT[n % 2][:], rhs=Sb[:], start=True, stop=True),
                          reads=[("wT", n % 2), "Sb"], writes=["bH"])
                    Sx("dve", lambda e: e.tensor_tensor(out=vn[:], in0=u[n % 2][:], in1=bH[:, 256:384], op=ALU.subtract),
                          reads=[("u", n % 2), "bH"], writes=["vn", "bH"])
                    Sx("pe", lambda e, tok=tok: e.matmul(bH[:, 0:128], lhsT=qT[:, tok], rhs=Sb[:], start=True, stop=True),
                          reads=["qT", "Sb"], writes=["bH"])
                    Sx("pe", lambda e: e.matmul(bH[:, 384:512], lhsT=intraT[n % 2][:], rhs=vn[:], start=True, stop=True),
                          reads=[("intraT", n % 2), "vn"], writes=["bH"])
                    Sx("act", lambda e: e.copy(out=t1[:], in_=bH[:, 384:512]), reads=["bH"], writes=["t1", "bH"])
                    Sx("dve", lambda e, col=col: e.scalar_tensor_tensor(out=o[:], in0=bH[:, 0:128], scalar=egc[:, col],
                                                                           in1=t1[:], op0=ALU.mult, op1=ALU.add),
                          reads=["bH", "egc", "t1"], writes=["o", "bH"])
                    Sx("pe", lambda e: e.matmul(bH[:, 128:256], lhsT=kdec[n % 2][:], rhs=vn[:], start=True, stop=True),
                          reads=[("kdec", n % 2), "vn"], writes=["bH"])
                    Sx("dve", lambda e, col=col: e.scalar_tensor_tensor(out=St[:], in0=St[:], scalar=egl[:, col],
                                                                           in1=bH[:, 128:256], op0=ALU.mult, op1=ALU.add),
                          reads=["St", "egl", "bH"], writes=["St", "bH"])
                    Sx("act", lambda e: e.copy(out=Sb[:], in_=St[:]), reads=["St"], writes=["Sb"])
                    s4 = n % 4
                    Sx("act", lambda e, s4=s4: e.activation(out=junk[:], in_=o[:], func=AF.Square, scale=128 ** -0.5,
                                                               accum_out=st[:, s4:s4 + 1]),
                          reads=["o"], writes=["junkC", ("stC", s4)])
                    _rsqrt(kb, st[:, s4:s4 + 1], st[:, s4:s4 + 1], ("stC", s4), ("stC", s4), rec=Sx)
                    Sx("dve", lambda e, s4=s4: e.scalar_tensor_tensor(out=on[:], in0=o[:], scalar=st[:, s4:s4 + 1], in1=wn[:],
                                                                         op0=ALU.mult, op1=ALU.mult),
                          reads=["o", ("stC", s4), "wn"], writes=["on"])
                    Sx("pool", lambda e, n=n: e.tensor_tensor(out=ob[:], in0=on[:], in1=Zs[:, n, :], op=ALU.mult),
                          reads=["on", "Zs"], writes=["obC"])
                    Sx("pe", lambda e, s4=s4: e.transpose(bG[:, 512 + s4 * 128:512 + (s4 + 1) * 128], ob[:], identb[:]),
                          reads=["obC", "identb"], writes=["bG"])
                    if s4 == 3:
                        qc = n // 4
                        ot = oTt[qc % 2]
                        Sx("act", lambda e, ot=ot: e.copy(out=ot[:], in_=bG[:, 512:1024]), reads=["bG"],
                              writes=[("oTtC", qc % 2), "bG"])
                        j, off = qc // 2, (qc % 2) * 512
                        Sd("sp", oT_out[j, hl * 128:(hl + 1) * 128, off:off + 512], ot[:], reads=[("oTtC", qc % 2)],
                               is_out=True)
                    return lst

                for th in gen_prep(0):
                    th()
                for n in range(64):
                    a = gen_prep(n + 1) if n + 1 < 64 else []
                    b = gen_scan(n)
                    ia = ib = 0
                    while ia < len(a) or ib < len(b):
                        for _ in range(2):
                            if ia < len(a):
                                a[ia]()
                                ia += 1
                        if ib < len(b):
                            b[ib]()
                            ib += 1
    return kb.done()


_PROGS = {}


def _prog(key, builder):
    if key not in _PROGS:
        _PROGS[key] = builder()
    return _PROGS[key]


def _run(nc, in_maps):
    res = run_bass_kernel_spmd(nc, in_maps, core_ids=list(range(NCORES)))
    return res.results


def _pk(w):
    return np.ascontiguousarray(np.asarray(w, np.float32).reshape(16, 128).T)


def _lambda_init(layer):
    return 0.8 - 0.6 * math.exp(-0.3 * layer)


def _mixer_A(xg, layer, slot, inp):
    w_in = inp["a_w_in"][slot]
    lamv = np.stack([inp["a_lam_q1"][slot], inp["a_lam_k1"][slot], inp["a_lam_q2"][slot], inp["a_lam_k2"][slot]])
    li = _lambda_init(layer)
    cst = np.tile(np.array([[li, 1.0 - li]], np.float32), (128, 1))
    subw = np.ascontiguousarray(inp["a_sub_norm"][slot].reshape(1, 256))
    maps = []
    for hd in range(NCORES):
        cols = np.concatenate([np.arange(m * 1024 + hd * 128, m * 1024 + hd * 128 + 128) for m in range(2)] +
                              [2048 + np.arange(m * 1024 + hd * 128, m * 1024 + hd * 128 + 128) for m in range(2)] +
                              [4096 + np.arange(hd * 256, hd * 256 + 256)])
        maps.append({"xg": xg, "w_loc": np.ascontiguousarray(w_in[:, cols]), "lamv": lamv, "subw": subw, "cst": cst})
    return _a2a(_run(_prog("A", build_A), maps), "oT_out")


def _a2a(res, key):
    return [np.ascontiguousarray(np.concatenate([res[r][key][j] for r in range(NCORES)], axis=0))
            for j in range(NCORES)]


def _mixer_B(xg, slot, inp):
    w_in = inp["b_w_in"][slot]
    qkw = np.ascontiguousarray(np.stack([inp["b_q_norm"][slot], inp["b_k_norm"][slot]], axis=1).astype(np.float32))
    maps = []
    for c in range(NCORES):
        wl = []
        for hl in range(2):
            h = 2 * c + hl
            cols = np.concatenate([np.arange(h * 128, h * 128 + 128) + o for o in (0, 2048, 4096, 6144)] +
                                  [np.array([8192 + h, 8192 + h])])
            wl.append(w_in[:, cols])
        fb = np.ascontiguousarray(inp["b_forget_bias"][slot][2 * c:2 * c + 2].reshape(1, 2).astype(np.float32))
        maps.append({"xg": xg, "w_loc": np.ascontiguousarray(np.stack(wl)), "qkw": qkw, "fb": fb})
    return _a2a(_run(_prog("B", build_B), maps), "oT_out")


def _mixer_C(xg, slot, inp):
    w_in = inp["c_w_in"][slot]
    conv = inp["c_conv_w"][slot]
    onw = np.ascontiguousarray(inp["c_out_norm"][slot].reshape(1, 128).astype(np.float32))
    maps = []
    for c in range(NCORES):
        wl, cws, hps = [], [], []
        for hl in range(4):
            hv = 4 * c + hl
            hq = hv // 2
            qc = np.arange(hq * 128, hq * 128 + 128)
            vc = np.arange(hv * 128, hv * 128 + 128)
            cols = np.concatenate([qc, 2048 + qc, 4096 + vc, 8192 + vc, np.array([12288 + hv, 12320 + hv])])
            wl.append(w_in[:, cols])
            cws.append(np.concatenate([conv[:, ch].T for ch in (qc, 2048 + qc, 4096 + vc)], axis=1))
            hps.append([inp["c_a_log"][slot][hv], inp["c_dt_bias"][slot][hv]])
        maps.append({"xg": xg, "w_loc": np.ascontiguousarray(np.stack(wl)),
                     "cw": np.ascontiguousarray(np.stack(cws)).astype(np.float32),
                     "hp": np.array(hps, np.float32), "onw": onw})
    return _a2a(_run(_prog("C", build_C), maps), "oT_out")


def kernel(**inp):
    inp = {k: np.asarray(v) for k, v in inp.items()}
    x = inp["x"][0]
    hs = [np.ascontiguousarray(x[c * TL:(c + 1) * TL]) for c in range(NCORES)]
    nrm0 = np.concatenate([_pk(inp["mix_norm"][0]), _pk(inp["mix_norm"][0])], axis=1)
    res = _run(_prog(("T", 0, True, False), lambda: build_T(2048, True, False)),
               [{"h_in": hs[c], "nrm": nrm0} for c in range(NCORES)])
    xg = np.ascontiguousarray(np.stack([res[c]["xnT_out"] for c in range(NCORES)]))
    depth = inp["mix_norm"].shape[0]
    out = None
    for i in range(depth):
        slot = i // 3
        if i % 3 == 0:
            oTs = _mixer_A(xg, i, slot, inp)
            w_out = inp["a_w_out"][slot]
        elif i % 3 == 1:
            oTs = _mixer_B(xg, slot, inp)
            w_out = inp["b_w_out"][slot]
        else:
            oTs = _mixer_C(xg, slot, inp)
            w_out = inp["c_w_out"][slot]
        last = i == depth - 1
        kdim = w_out.shape[0]
        nxt = inp["final_norm"] if last else inp["mix_norm"][i + 1]
        nrm = np.concatenate([_pk(inp["ffn_norm"][i]), _pk(nxt)], axis=1)
        maps = []
        for c in range(NCORES):
            m = {"h_in": hs[c], "oT_in": oTs[c], "w_out": w_out, "nrm": nrm, "w_gate": inp["ffn_w_gate"][i],
                 "w_up": inp["ffn_w_up"][i], "w_down": inp["ffn_w_down"][i]}
            if last:
                m["fnw"] = np.ascontiguousarray(inp["final_norm"].reshape(1, D))
            maps.append(m)
        res = _run(_prog(("T", kdim, False, last), lambda: build_T(kdim, False, last)), maps)
        if last:
            out = np.concatenate([res[c]["y_out"] for c in range(NCORES)], axis=0)[None]
        else:
            hs = [res[c]["h_out"] for c in range(NCORES)]
            xg = np.ascontiguousarray(np.stack([res[c]["xnT_out"] for c in range(NCORES)]))
    return out.astype(np.float32)
```

```python
import contextlib
import math
import numpy as np
import ml_dtypes
import concourse.bass as bass
import concourse.mybir as mybir
from concourse.bass_utils import run_bass_kernel_spmd

F32 = mybir.dt.float32
BF16 = mybir.dt.bfloat16
AF = mybir.ActivationFunctionType
ALU = mybir.AluOpType
AX = mybir.AxisListType

NCORES = 8
D = 2048
S = 8192
TL = S // NCORES
DFF = 5632
EPS = 1e-6


class _Op:
    __slots__ = ("stream", "fn", "deps", "is_dma", "semkey", "ticket", "signal", "idx")

    def __init__(self, stream, fn, is_dma=False, semkey=None):
        self.stream = stream
        self.fn = fn
        self.deps = []
        self.is_dma = is_dma
        self.semkey = semkey
        self.ticket = None
        self.signal = is_dma
        self.idx = -1


class KB:
    STREAMS = ("pe", "act", "dve", "pool", "sp")

    def __init__(self):
        self.nc = bass.Bass("TRN2", target_bir_lowering=False)
        self.es = contextlib.ExitStack()
        self.ops = []
        self.last_w = {}
        self.readers = {}
        self.last_on = {s: None for s in self.STREAMS}
        self.dma_open = []
        self.out_dmas = []

    def dram(self, name, shape, dt, kind):
        return self.nc.dram_tensor(name, list(shape), dt, kind=kind).ap()

    def _uniq(self, name):
        self._nuniq = getattr(self, "_nuniq", 0) + 1
        return "%s_%d" % (name, self._nuniq)

    def sb(self, name, shape, dt):
        return self.es.enter_context(self.nc.sbuf_tensor(self._uniq(name), list(shape), dt))

    def ps(self, name, shape, dt=F32):
        return self.es.enter_context(self.nc.psum_tensor(self._uniq(name), list(shape), dt))

    def _add(self, op, reads, writes):
        deps = []
        for k in reads:
            w = self.last_w.get(k)
            if w is not None:
                deps.append((w, "raw"))
        for k in writes:
            w = self.last_w.get(k)
            if w is not None:
                deps.append((w, "waw"))
            for r in self.readers.get(k, ()):
                deps.append((r, "war"))
        seen = set()
        for d, kind in deps:
            if d is op or id(d) in seen:
                continue
            if (not op.is_dma) and (not d.is_dma) and d.stream == op.stream:
                if op.stream == "pe" or (kind != "raw" and op.stream != "pool"):
                    continue
            seen.add(id(d))
            op.deps.append(d)
            d.signal = True
        op.idx = len(self.ops)
        self.ops.append(op)
        self.last_on[op.stream] = op
        for k in reads:
            lst = self.readers.setdefault(k, [])
            lst[:] = [r for r in lst if not (r.stream == op.stream and not r.is_dma and not op.is_dma)]
            lst.append(op)
        for k in writes:
            self.last_w[k] = op
            self.readers[k] = []
        return op

    def op(self, stream, fn, reads=(), writes=()):
        return self._add(_Op(stream, fn), reads, writes)

    def dma(self, queue, out, in_, reads=(), writes=(), semkey=None, is_out=False):
        if semkey is None:
            semkey = writes[0] if writes else reads[0]
        o = _Op(queue, lambda e: e.dma_start(out=out, in_=in_), is_dma=True, semkey=("dma", semkey))
        self._add(o, reads, writes)
        self.dma_open.append(o)
        if is_out:
            self.out_dmas.append(o)
        return o

    def barrier(self):
        lasts = [o for o in self.last_on.values() if o is not None] + list(self.dma_open)
        self.dma_open = []
        for s in self.STREAMS:
            b = _Op(s, None)
            for d in lasts:
                if d.stream == s and not d.is_dma and s == "pe":
                    continue
                b.deps.append(d)
                d.signal = True
            b.idx = len(self.ops)
            self.ops.append(b)

    def finish(self):
        b = _Op("sp", None)
        for d in self.out_dmas:
            b.deps.append(d)
        b.idx = len(self.ops)
        self.ops.append(b)

    def _init_emit(self):
        if getattr(self, "_sems", None) is None:
            E = self.es.enter_context
            self._sems = {s: E(self.nc.semaphore("sem_" + s)) for s in self.STREAMS}
            self._cnt = {s: 0 for s in self.STREAMS}
            self._dsem = {}
            self._dcnt = {}
            self._waited = {s: {} for s in self.STREAMS}
            self._emitted = 0

    def flush(self):
        nc = self.nc
        self._init_emit()
        E = self.es.enter_context
        new_ops = self.ops[self._emitted:]
        self._emitted = len(self.ops)
        for o in new_ops:
            if o.is_dma:
                if o.semkey not in self._dsem:
                    self._dsem[o.semkey] = E(nc.semaphore("dsem%d" % len(self._dsem)))
                    self._dcnt[o.semkey] = 0
                self._dcnt[o.semkey] += 16
                o.ticket = (self._dsem[o.semkey], self._dcnt[o.semkey])
            elif o.signal and o.fn is not None:
                self._cnt[o.stream] += 1
                o.ticket = (self._sems[o.stream], self._cnt[o.stream])
        by_stream = {s: [o for o in new_ops if o.stream == s] for s in self.STREAMS}

        def run(stream, eng):
            waited = self._waited[stream]
            for o in by_stream[stream]:
                need = {}
                for d in o.deps:
                    if d.ticket is None:
                        continue
                    sem, val = d.ticket
                    if need.get(id(sem), (None, 0))[1] < val:
                        need[id(sem)] = (sem, val)
                for sid, (sem, val) in need.items():
                    if waited.get(sid, 0) < val:
                        eng.wait_ge(sem, val)
                        waited[sid] = val
                if o.fn is None:
                    continue
                ins = o.fn(eng)
                if o.ticket is not None:
                    ins.then_inc(o.ticket[0], 16 if o.is_dma else 1)

        with nc.Block() as block:
            @block.tensor
            def _(e):
                run("pe", e)

            @block.scalar
            def _(e):
                run("act", e)

            @block.vector
            def _(e):
                run("dve", e)

            @block.gpsimd
            def _(e):
                run("pool", e)

            @block.sync
            def _(e):
                run("sp", e)

    @contextlib.contextmanager
    def phase(self):
        outer = self.es
        inner = contextlib.ExitStack()
        self._init_emit()
        self.es = inner
        try:
            yield
            self.es = outer
            self.barrier()
            self.flush()
        finally:
            self.es = outer
            inner.close()

    def done(self):
        self.finish()
        self.flush()
        self.es.close()
        return self.nc


def _ident(kb, name, dt):
    t = kb.sb(name, [128, 128], dt)
    kb.op("pool", lambda e: e.memset(t[:], 0.0), writes=[name])
    kb.op("pool", lambda e: e.affine_select(out=t[:], in_=t[:], pattern=[[-1, 128]], compare_op=ALU.not_equal,
                                            fill=1.0, base=0, channel_multiplier=1), reads=[name], writes=[name])
    return t


def _rsqrt(kb, out, in_, kin, kout, eps=EPS, rec=None):
    rec = rec or kb.op
    rec("act", lambda e: e.activation(out=out, in_=in_, func=AF.Ln, bias=eps), reads=[kin], writes=[kout])
    rec("act", lambda e: e.activation(out=out, in_=out, func=AF.Exp, scale=-0.5), reads=[kout], writes=[kout])


def _rms_to_xT(kb, h, hkey, nrm_sb, woff, xnT, xkey, identb, ss, rstd, tag):
    junk = kb.sb("junk" + tag, [128, D], BF16)
    xn = [kb.sb("xn%s%d" % (tag, i), [128, D], BF16) for i in range(2)]
    pT = [kb.ps("pT%s%d" % (tag, i), [128, 1024], BF16) for i in range(2)]
    for t in range(8):
        kb.op("act", lambda e, t=t: e.activation(out=junk[:], in_=h[:, t, :], func=AF.Square,
                                                 scale=1.0 / math.sqrt(D), accum_out=ss[:, t:t + 1]),
              reads=[(hkey, t)], writes=["junk" + tag, ("ss" + tag, t)])
        _rsqrt(kb, rstd[:, t:t + 1], ss[:, t:t + 1], ("ss" + tag, t), ("rstd" + tag, t))
        xb = xn[t % 2]
        kb.op("dve", lambda e, t=t, xb=xb: e.tensor_scalar(out=xb[:], in0=h[:, t, :], scalar1=rstd[:, t:t + 1], scalar2=None,
                                                           op0=ALU.mult),
              reads=[(hkey, t), ("rstd" + tag, t)], writes=[("xn" + tag, t % 2)])
        for half in range(2):
            pp = pT[half]
            for j in range(8):
                kt = half * 8 + j
                kb.op("pe", lambda e, pp=pp, j=j, kt=kt, xb=xb: e.transpose(pp[:, j * 128:(j + 1) * 128],
                                                                            xb[:, kt * 128:(kt + 1) * 128], identb[:]),
                      reads=[("xn" + tag, t % 2), "identb"], writes=[("pT" + tag, half)])
            for j in range(8):
                kt = half * 8 + j
                if half == 0:
                    kb.op("act", lambda e, pp=pp, j=j, kt=kt, t=t: e.activation(
                        out=xnT[:, kt, t * 128:(t + 1) * 128], in_=pp[:, j * 128:(j + 1) * 128], func=AF.Copy,
                        scale=nrm_sb[:, woff + kt:woff + kt + 1]),
                        reads=[("pT" + tag, half), "nrm_sb"], writes=[(xkey, kt, t), ("pT" + tag, half)])
                else:
                    kb.op("dve", lambda e, pp=pp, j=j, kt=kt, t=t: e.tensor_scalar(
                        out=xnT[:, kt, t * 128:(t + 1) * 128], in0=pp[:, j * 128:(j + 1) * 128],
                        scalar1=nrm_sb[:, woff + kt:woff + kt + 1], scalar2=None, op0=ALU.mult),
                        reads=[("pT" + tag, half), "nrm_sb"], writes=[(xkey, kt, t), ("pT" + tag, half)])


def build_T(kdim, first, last):
    kb = KB()
    KT = kdim // 128
    if first:
        h_in = kb.dram("h_in", [TL, D], F32, "ExternalInput")
        nrm = kb.dram("nrm", [128, 32], F32, "ExternalInput")
    else:
        h_in = kb.dram("h_in", [TL, D], F32, "ExternalInput")
        oT_in = kb.dram("oT_in", [kdim, TL], BF16, "ExternalInput")
        w_out = kb.dram("w_out", [kdim, D], F32, "ExternalInput")
        nrm = kb.dram("nrm", [128, 32], F32, "ExternalInput")
        w_gate = kb.dram("w_gate", [D, DFF], F32, "ExternalInput")
        w_up = kb.dram("w_up", [D, DFF], F32, "ExternalInput")
        w_down = kb.dram("w_down", [DFF, D], F32, "ExternalInput")
    if last:
        fnw = kb.dram("fnw", [1, D], F32, "ExternalInput")
        y_out = kb.dram("y_out", [TL, D], F32, "ExternalOutput")
    else:
        xnT_out = kb.dram("xnT_out", [D, TL], BF16, "ExternalOutput")
        if not first:
            h_out = kb.dram("h_out", [TL, D], F32, "ExternalOutput")

    h = kb.sb("h", [128, 8, D], F32)
    nrm_sb = kb.sb("nrm_sb", [128, 32], F32)
    ss = kb.sb("ss", [128, 16], F32)
    rstd = kb.sb("rstd", [128, 16], F32)
    identb = _ident(kb, "identb", BF16)
    kb.dma("sp", nrm_sb[:], nrm, writes=["nrm_sb"])
    for t in range(8):
        kb.dma("sp", h[:, t, :], h_in[t * 128:(t + 1) * 128, :], writes=[("h", t)])

    if not first:
        with kb.phase():
            oT = kb.sb("oT", [128, KT, TL], BF16)
            wo = [kb.sb("wo%d" % i, [128, KT, 512], BF16) for i in range(2)]
            pso = [kb.ps("pso%d" % i, [128, 512], F32) for i in range(4)]
            oview = oT_in.rearrange("(k p) t -> p k t", p=128)
            kg = KT // 4
            for g in range(4):
                kb.dma("sp", oT[:, g * kg:(g + 1) * kg, :], oview[:, g * kg:(g + 1) * kg, :], writes=[("oT", g)])
            wview = w_out.rearrange("(k p) n -> p k n", p=128)
            n = 0
            for c in range(4):
                wb = wo[c % 2]
                kb.dma("pool", wb[:], wview[:, :, c * 512:(c + 1) * 512], writes=[("wo", c % 2)])
                for t in range(8):
                    pb = pso[n % 4]
                    for k in range(KT):
                        kb.op("pe", lambda e, pb=pb, wb=wb, k=k, t=t: e.matmul(
                            pb[:], lhsT=oT[:, k, t * 128:(t + 1) * 128], rhs=wb[:, k, :], start=(k == 0), stop=(k == KT - 1)),
                            reads=[("oT", k // kg), ("wo", c % 2)], writes=[("pso", n % 4)])
                    kb.op("dve", lambda e, pb=pb, c=c, t=t: e.tensor_tensor(
                        out=h[:, t, c * 512:(c + 1) * 512], in0=h[:, t, c * 512:(c + 1) * 512], in1=pb[:], op=ALU.add),
                        reads=[("pso", n % 4), ("h", t)], writes=[("h", t), ("pso", n % 4)])
                    n += 1

        with kb.phase():
            xnT = kb.sb("xnT", [128, 16, TL], BF16)
            with kb.phase():
                _rms_to_xT(kb, h, "h", nrm_sb, 0, xnT, "xnT", identb, ss, rstd, "a")
            NGU = 4
            wgu = [kb.sb("wgu%d" % i, [128, 16, 256], BF16) for i in range(NGU)]
            wd = [kb.sb("wd%d" % i, [128, 11, 512], BF16) for i in range(2)]
            actT = kb.sb("actT", [128, 11, TL], BF16)
            sgt = [kb.sb("sgt%d" % i, [128, 512], F32) for i in range(2)]
            psg = [kb.ps("psg%d" % i, [128, 512], F32) for i in range(2)]
            psu = [kb.ps("psu%d" % i, [128, 512], F32) for i in range(2)]
            psd = [kb.ps("psd%d" % i, [128, 512], F32) for i in range(3)]
            gview = w_gate.rearrange("(k p) n -> p k n", p=128)
            uview = w_up.rearrange("(k p) n -> p k n", p=128)
            dview = w_down.rearrange("(k p) n -> p k n", p=128)
            xkeys = [("xnT", kt, t) for kt in range(16) for t in range(8)]
            nf = 0
            nh = 0
            nd = 0
            ndw = 0
            for g in range(4):
                for fi in range(11):
                    f = g * 11 + fi
                    sl = nf % NGU
                    wt = wgu[sl]
                    kb.dma("pool", wt[:, :, 0:128], gview[:, :, f * 128:(f + 1) * 128], writes=[("wgu", sl)])
                    kb.dma("pool", wt[:, :, 128:256], uview[:, :, f * 128:(f + 1) * 128], writes=[("wgu", sl)])
                    nf += 1
                    for half in range(2):
                        pg = psg[nh % 2]
                        pu = psu[nh % 2]
                        st = sgt[nh % 2]
                        hk = [("xnT", kt, t) for kt in range(16) for t in range(half * 4, half * 4 + 4)]
                        for k in range(16):
                            kb.op("pe", lambda e, pg=pg, wt=wt, k=k, half=half: e.matmul(
                                pg[:], lhsT=wt[:, k, 0:128], rhs=xnT[:, k, half * 512:(half + 1) * 512],
                                start=(k == 0), stop=(k == 15)),
                                reads=[("wgu", sl)] + (hk if k == 0 else []), writes=[("psg", nh % 2)])
                        for k in range(16):
                            kb.op("pe", lambda e, pu=pu, wt=wt, k=k, half=half: e.matmul(
                                pu[:], lhsT=wt[:, k, 128:256], rhs=xnT[:, k, half * 512:(half + 1) * 512],
                                start=(k == 0), stop=(k == 15)),
                                reads=[("wgu", sl)], writes=[("psu", nh % 2)])
                        kb.op("act", lambda e, pg=pg, st=st: e.activation(out=st[:], in_=pg[:], func=AF.Silu),
                              reads=[("psg", nh % 2)], writes=[("sgt", nh % 2), ("psg", nh % 2)])
                        kb.op("dve", lambda e, pu=pu, st=st, fi=fi, half=half: e.tensor_tensor(
                            out=actT[:, fi, half * 512:(half + 1) * 512], in0=st[:], in1=pu[:], op=ALU.mult),
                            reads=[("sgt", nh % 2), ("psu", nh % 2)], writes=[("actT", fi, half), ("psu", nh % 2)])
                        nh += 1
                for c in range(4):
                    wdt = wd[ndw % 2]
                    kb.dma("pool", wdt[:], dview[:, g * 11:(g + 1) * 11, c * 512:(c + 1) * 512], writes=[("wd", ndw % 2)])
                    for t in range(8):
                        pd = psd[nd % 3]
                        for fi in range(11):
                            kb.op("pe", lambda e, pd=pd, wdt=wdt, fi=fi, t=t: e.matmul(
                                pd[:], lhsT=actT[:, fi, t * 128:(t + 1) * 128], rhs=wdt[:, fi, :],
                                start=(fi == 0), stop=(fi == 10)),
                                reads=[("wd", ndw % 2), ("actT", fi, t // 4)], writes=[("psd", nd % 3)])
                        kb.op("dve", lambda e, pd=pd, c=c, t=t: e.tensor_tensor(
                            out=h[:, t, c * 512:(c + 1) * 512], in0=h[:, t, c * 512:(c + 1) * 512], in1=pd[:], op=ALU.add),
                            reads=[("psd", nd % 3), ("h", t)], writes=[("h", t), ("psd", nd % 3)])
                        nd += 1
                    ndw += 1

    if last:
        with kb.phase():
            fw = kb.sb("fw", [128, D], F32)
            junk = kb.sb("junkf", [128, D], BF16)
            yt = [kb.sb("yt%d" % i, [128, D], F32) for i in range(2)]
            kb.dma("sp", fw[:], fnw.partition_broadcast(128), writes=["fw"])
            for t in range(8):
                kb.op("act", lambda e, t=t: e.activation(out=junk[:], in_=h[:, t, :], func=AF.Square,
                                                         scale=1.0 / math.sqrt(D), accum_out=ss[:, t:t + 1]),
                      reads=[("h", t)], writes=["junkf", ("ssf", t)])
                _rsqrt(kb, rstd[:, t:t + 1], ss[:, t:t + 1], ("ssf", t), ("rstdf", t))
                y = yt[t % 2]
                kb.op("dve", lambda e, t=t, y=y: e.scalar_tensor_tensor(out=y[:], in0=h[:, t, :], scalar=rstd[:, t:t + 1],
                                                                        in1=fw[:], op0=ALU.mult, op1=ALU.mult),
                      reads=[("h", t), ("rstdf", t), "fw"], writes=[("yt", t % 2)])
                kb.dma("sp", y_out[t * 128:(t + 1) * 128, :], y[:], reads=[("yt", t % 2)], is_out=True)
    else:
        with kb.phase():
            xnT2 = kb.sb("xnT2", [128, 16, TL], BF16)
            _rms_to_xT(kb, h, "h", nrm_sb, 16, xnT2, "xnT2", identb, ss, rstd, "b")
            xkeys = [("xnT2", kt, t) for kt in range(16) for t in range(8)]
            xo = xnT_out.rearrange("(k p) t -> p k t", p=128)
            for g in range(4):
                kb.dma("sp", xo[:, g * 4:(g + 1) * 4, :], xnT2[:, g * 4:(g + 1) * 4, :],
                       reads=[("xnT2", kt, t) for kt in range(g * 4, g * 4 + 4) for t in range(8)],
                       semkey=("xo", g), is_out=True)
            if not first:
                for t in range(8):
                    kb.dma("sp", h_out[t * 128:(t + 1) * 128, :], h[:, t, :], reads=[("h", t)], semkey=("ho", t), is_out=True)
    return kb.done()


def _load_xc(kb, xg, xc, c, slot):
    r, off = c // 2, (c % 2) * 512
    src = xg[r].rearrange("(k p) t -> p k t", p=128)
    kb.dma("sp", xc[:], src[:, :, off:off + 512], writes=[("xc", slot)])


def _proj_fm(kb, ps, pskey, w_sb, col0, xc, xslot, evac):
    for k in range(16):
        kb.op("pe", lambda e, k=k: e.matmul(ps[:], lhsT=w_sb[:, k, col0:col0 + 128], rhs=xc[:, k, :],
                                            start=(k == 0), stop=(k == 15)),
              reads=["w_sb", ("xc", xslot)], writes=[pskey])
    evac(ps)


def _proj_tm(kb, ps, pskey, w_sb, col0, ncol, xc, xslot, s, evac):
    for k in range(16):
        kb.op("pe", lambda e, k=k: e.matmul(ps[:, 0:ncol], lhsT=xc[:, k, s * 128:(s + 1) * 128],
                                            rhs=w_sb[:, k, col0:col0 + ncol], start=(k == 0), stop=(k == 15)),
              reads=["w_sb", ("xc", xslot)], writes=[pskey])
    evac(ps)


def _diag_masks(kb, name):
    ms = []
    for r in range(4):
        m = kb.sb("%s%d" % (name, r), [128, 512], BF16)
        key = (name, r)
        kb.op("pool", lambda e, m=m: e.memset(m[:], 1.0), writes=[key])
        for half in range(2):
            base = -(128 * r + 64 * half)
            kb.op("pool", lambda e, m=m, half=half, base=base: e.affine_select(
                out=m[half * 64:(half + 1) * 64, :], in_=m[half * 64:(half + 1) * 64, :], pattern=[[1, 512]],
                compare_op=ALU.is_ge, fill=0.0, base=base, channel_multiplier=0), reads=[key], writes=[key])
        ms.append(m)
    return ms


def _causal_masks(kb, name):
    ms = []
    for r in range(4):
        m = kb.sb("%s%d" % (name, r), [128, 512], BF16)
        key = (name, r)
        kb.op("pool", lambda e, m=m: e.memset(m[:], 1.0), writes=[key])
        kb.op("pool", lambda e, m=m, r=r: e.affine_select(
            out=m[:], in_=m[:], pattern=[[1, 512]], compare_op=ALU.is_ge, fill=0.0, base=-128 * r,
            channel_multiplier=-1), reads=[key], writes=[key])
        ms.append(m)
    return ms


def _attn_pipe(kb, tag, maps, Vaug, vreads, dv, masks, mkey, scale, finish):
    NPS, NPT = 3, 4
    pS = [kb.ps("pS%s%d" % (tag, i), [128, 512], F32) for i in range(NPS)]
    psO = [kb.ps("pO%s%d" % (tag, i), [128, 512], F32) for i in range(4)]
    PT = [kb.sb("PT%s%d" % (tag, i), [128, 512], BF16) for i in range(NPT)]
    items = [(qc, m, k) for qc in range(16) for m in range(len(maps)) for k in range(qc * 4 + 4)]

    def rec_score(i):
        qc, m, kb_i = items[i]
        mp = maps[m]
        qT, kT = mp["qT"], mp["kT"]
        r = kb_i - qc * 4
        q0 = max(r, 0) * 128
        ps = pS[i % NPS]
        pt = PT[i % NPT]
        psk = ("pS" + tag, i % NPS)
        ptk = ("PT" + tag, i % NPT)
        aux = mp.get("aux")
        kb.op("pe", lambda e: e.matmul(ps[:, q0:512], lhsT=kT[:, kb_i * 128:(kb_i + 1) * 128],
                                       rhs=qT[:, qc * 512 + q0:(qc + 1) * 512], start=True, stop=(aux is None)),
              reads=[mp["qkey"], mp["kkey"]], writes=[psk])
        if aux is not None:
            ka, qa = aux
            kb.op("pe", lambda e: e.matmul(ps[:, q0:512], lhsT=ka[:, kb_i * 128:(kb_i + 1) * 128],
                                           rhs=qa[:, qc * 512 + q0:(qc + 1) * 512], start=False, stop=True),
                  reads=["ka", "qa"], writes=[psk])
        kb.op("act", lambda e: e.activation(out=pt[:, q0:512], in_=ps[:, q0:512], func=AF.Exp, scale=scale),
              reads=[psk], writes=[ptk, psk])
        if r >= 0:
            kb.op("pool", lambda e: e.tensor_tensor(out=pt[:, q0:512], in0=pt[:, q0:512], in1=masks[r][:, q0:512],
                                                    op=ALU.mult), reads=[ptk, (mkey, r)], writes=[ptk])

    def rec_pv(i):
        qc, m, kb_i = items[i]
        r = kb_i - qc * 4
        pt = PT[i % NPT]
        ptk = ("PT" + tag, i % NPT)
        for s in range(4):
            if r >= 0 and s < r:
                continue
            last = qc * 4 + s
            kb.op("pe", lambda e, s=s, last=last: e.matmul(
                psO[s][:, 0:dv + 1], lhsT=pt[:, s * 128:(s + 1) * 128], rhs=Vaug[:, kb_i, 0:dv + 1],
                start=(kb_i == 0), stop=(kb_i == last)), reads=[ptk] + list(vreads), writes=[("pO" + tag, s)])

    n = len(items)
    rec_score(0)
    for i in range(n):
        if i + 1 < n:
            rec_score(i + 1)
        rec_pv(i)
        qc, m, kb_i = items[i]
        if kb_i == qc * 4 + 3:
            finish(qc, m, psO)


def build_A():
    kb = KB()
    xg = kb.dram("xg", [8, D, TL], BF16, "ExternalInput")
    w_loc = kb.dram("w_loc", [D, 768], F32, "ExternalInput")
    lamv = kb.dram("lamv", [4, 128], F32, "ExternalInput")
    subw = kb.dram("subw", [1, 256], F32, "ExternalInput")
    cst = kb.dram("cst", [128, 2], F32, "ExternalInput")
    oT_out = kb.dram("oT_out", [8, 256, TL], BF16, "ExternalOutput")

    qT = [kb.sb("qT%d" % m, [128, S], BF16) for m in range(2)]
    kT = [kb.sb("kT%d" % m, [128, S], BF16) for m in range(2)]
    Vaug = kb.sb("Vaug", [128, 64, 264], BF16)
    identb = _ident(kb, "identb", BF16)
    cst_sb = kb.sb("cst_sb", [128, 2], F32)
    sw = kb.sb("sw", [128, 256], F32)
    lam4 = kb.sb("lam4", [128, 4, 128], F32)
    lsc = kb.sb("lsc", [128, 8], F32)
    kb.dma("sp", cst_sb[:], cst, writes=["cst_sb"])
    kb.dma("sp", sw[:], subw.partition_broadcast(128), writes=["sw"])
    for i in range(4):
        kb.dma("sp", lam4[:, i, :], lamv[i:i + 1, :].partition_broadcast(128), writes=["lam4"], semkey=("lam4", i))
    kb.op("pool", lambda e: e.memset(Vaug[:, :, 256:257], 1.0), writes=["Vones"])
    prod = kb.sb("lprod", [128, 2, 128], F32)
    kb.op("dve", lambda e: e.tensor_tensor(out=prod[:, 0, :], in0=lam4[:, 0, :], in1=lam4[:, 1, :], op=ALU.mult),
          reads=["lam4"], writes=["lprod"])
    kb.op("dve", lambda e: e.tensor_tensor(out=prod[:, 1, :], in0=lam4[:, 2, :], in1=lam4[:, 3, :], op=ALU.mult),
          reads=["lam4"], writes=["lprod"])
    kb.op("dve", lambda e: e.reduce_sum(out=lsc[:, 0:2], in_=prod[:], axis=AX.X), reads=["lprod"], writes=["lsc"])
    kb.op("act", lambda e: e.activation(out=lsc[:, 2:4], in_=lsc[:, 0:2], func=AF.Exp), reads=["lsc"], writes=["lsc"])
    kb.op("dve", lambda e: e.tensor_tensor(out=lsc[:, 4:5], in0=lsc[:, 3:4], in1=lsc[:, 2:3], op=ALU.subtract),
          reads=["lsc"], writes=["lsc"])
    kb.op("dve", lambda e: e.tensor_tensor(out=lsc[:, 4:5], in0=lsc[:, 4:5], in1=cst_sb[:, 0:1], op=ALU.subtract),
          reads=["lsc", "cst_sb"], writes=["lsc"])
    kb.op("dve", lambda e: e.tensor_scalar(out=sw[:], in0=sw[:], scalar1=cst_sb[:, 1:2], scalar2=None, op0=ALU.mult),
          reads=["sw", "cst_sb"], writes=["sw"])

    with kb.phase():
        w_sb = kb.sb("w_sb", [128, 16, 768], BF16)
        kb.dma("pool", w_sb[:], w_loc.rearrange("(k p) n -> p k n", p=128), writes=["w_sb"])
        xcs = [kb.sb("xc%d" % i, [128, 16, 512], BF16) for i in range(2)]
        pp = [kb.ps("pp%d" % i, [128, 512], F32) for i in range(4)]
        n = 0
        dests = [qT[0], qT[1], kT[0], kT[1]]
        dkeys = ["qT0", "qT1", "kT0", "kT1"]
        for c in range(16):
            xc = xcs[c % 2]
            _load_xc(kb, xg, xc, c, c % 2)
            for j in range(4):
                ps = pp[n % 4]
                eng = "act" if n % 2 == 0 else "dve"

                def evac(ps, j=j, c=c, eng=eng, n=n):
                    dst = dests[j][:, c * 512:(c + 1) * 512]
                    if eng == "act":
                        kb.op("act", lambda e: e.copy(out=dst, in_=ps[:]), reads=[("pp", n % 4)],
                              writes=[dkeys[j], ("pp", n % 4)])
                    else:
                        kb.op("dve", lambda e: e.tensor_copy(out=dst, in_=ps[:]), reads=[("pp", n % 4)],
                              writes=[dkeys[j], ("pp", n % 4)])
                _proj_fm(kb, ps, ("pp", n % 4), w_sb, j * 128, xc, c % 2, evac)
                n += 1
            for s in range(4):
                ps = pp[n % 4]
                eng = "act" if n % 2 == 0 else "dve"

                def evac(ps, s=s, c=c, eng=eng, n=n):
                    dst = Vaug[:, c * 4 + s, 0:256]
                    if eng == "act":
                        kb.op("act", lambda e: e.copy(out=dst, in_=ps[:, 0:256]), reads=[("pp", n % 4)],
                              writes=["Vaug", ("pp", n % 4)])
                    else:
                        kb.op("dve", lambda e: e.tensor_copy(out=dst, in_=ps[:, 0:256]), reads=[("pp", n % 4)],
                              writes=["Vaug", ("pp", n % 4)])
                _proj_tm(kb, ps, ("pp", n % 4), w_sb, 512, 256, xc, c % 2, s, evac)
                n += 1

    with kb.phase():
        masks = _diag_masks(kb, "dmask")
        acc = [kb.sb("acc%d" % s, [128, 256], F32) for s in range(4)]
        rc = kb.sb("rc", [128, 8], F32)
        st = kb.sb("st", [128, 8], F32)
        junk = kb.sb("junkA", [128, 256], F32)
        ob = [kb.sb("ob%d" % s, [128, 256], BF16) for s in range(2)]
        oTt = [kb.sb("oTt%d" % i, [128, 2, 512], BF16) for i in range(2)]
        pTr = kb.ps("pTr", [128, 1024], BF16)
        scale = 128 ** -0.5

        def finish(qc, m, psO):
            for s in range(4):
                pk = ("pOA", s)
                kb.op("dve", lambda e, s=s: e.reciprocal(out=rc[:, s:s + 1], in_=psO[s][:, 256:257]),
                      reads=[pk], writes=[("rc", s), pk])
                if m == 0:
                    kb.op("dve", lambda e, s=s: e.tensor_scalar(out=acc[s][:], in0=psO[s][:, 0:256],
                                                                 scalar1=rc[:, s:s + 1], scalar2=None, op0=ALU.mult),
                          reads=[pk, ("rc", s)], writes=[("acc", s), pk])
                else:
                    kb.op("dve", lambda e, s=s: e.tensor_tensor(out=rc[:, s:s + 1], in0=rc[:, s:s + 1], in1=lsc[:, 4:5],
                                                                 op=ALU.mult),
                          reads=[("rc", s), "lsc"], writes=[("rc", s)])
                    kb.op("dve", lambda e, s=s: e.scalar_tensor_tensor(out=acc[s][:], in0=psO[s][:, 0:256],
                                                                        scalar=rc[:, s:s + 1], in1=acc[s][:],
                                                                        op0=ALU.mult, op1=ALU.add),
                          reads=[pk, ("rc", s), ("acc", s)], writes=[("acc", s), pk])
            if m == 0:
                return
            ot = oTt[qc % 2]
            for s in range(4):
                kb.op("act", lambda e, s=s: e.activation(out=junk[:], in_=acc[s][:], func=AF.Square, scale=1.0 / 16.0,
                                                         accum_out=st[:, s:s + 1]),
                      reads=[("acc", s)], writes=["junkA", ("st", s)])
                _rsqrt(kb, st[:, s:s + 1], st[:, s:s + 1], ("st", s), ("st", s))
                o = ob[s % 2]
                kb.op("dve", lambda e, s=s, o=o: e.scalar_tensor_tensor(out=o[:], in0=acc[s][:], scalar=st[:, s:s + 1],
                                                                         in1=sw[:], op0=ALU.mult, op1=ALU.mult),
                      reads=[("acc", s), ("st", s), "sw"], writes=[("ob", s % 2)])
                for f in range(2):
                    kb.op("pe", lambda e, s=s, f=f, o=o: e.transpose(pTr[:, (s * 2 + f) * 128:(s * 2 + f + 1) * 128],
                                                                     o[:, f * 128:(f + 1) * 128], identb[:]),
                          reads=[("ob", s % 2), "identb"], writes=["pTr"])
                kb.op("dve", lambda e, s=s, ot=ot: e.tensor_copy(
                    out=ot[:, :, s * 128:(s + 1) * 128],
                    in_=pTr[:, s * 256:(s + 1) * 256].rearrange("p (f t) -> p f t", f=2)),
                    reads=["pTr"], writes=[("oTt", qc % 2), "pTr"])
            j, off = qc // 2, (qc % 2) * 512
            kb.dma("sp", oT_out[j].rearrange("(f p) t -> p f t", p=128)[:, :, off:off + 512], ot[:],
                   reads=[("oTt", qc % 2)], is_out=True)

        maps = [dict(qT=qT[m], kT=kT[m], qkey="qT%d" % m, kkey="kT%d" % m) for m in range(2)]
        _attn_pipe(kb, "A", maps, Vaug, ["Vaug", "Vones"], 256, masks, "dmask", scale, finish)
    return kb.done()


def build_B():
    kb = KB()
    xg = kb.dram("xg", [8, D, TL], BF16, "ExternalInput")
    w_loc = kb.dram("w_loc", [2, D, 514], F32, "ExternalInput")
    qkw = kb.dram("qkw", [128, 2], F32, "ExternalInput")
    fb = kb.dram("fb", [1, 2], F32, "ExternalInput")
    oT_out = kb.dram("oT_out", [8, 256, TL], BF16, "ExternalOutput")

    identb = _ident(kb, "identb", BF16)
    ones_f = kb.sb("ones_f", [128, 128], F32)
    kb.op("pool", lambda e: e.memset(ones_f[:], 1.0), writes=["ones_f"])
    qkw_sb = kb.sb("qkw_sb", [128, 2], F32)
    kb.dma("sp", qkw_sb[:], qkw, writes=["qkw_sb"])
    kb.op("dve", lambda e: e.tensor_scalar(out=qkw_sb[:, 0:1], in0=qkw_sb[:, 0:1], scalar1=128 ** -0.5, scalar2=None,
                                           op0=ALU.mult), reads=["qkw_sb"], writes=["qkw_sb"])
    nfb = kb.sb("nfb", [1, 2], F32)
    kb.dma("sp", nfb[:], fb, writes=["nfb"])
    kb.op("dve", lambda e: e.tensor_scalar(out=nfb[:], in0=nfb[:], scalar1=-1.0, scalar2=None, op0=ALU.mult),
          reads=["nfb"], writes=["nfb"])
    ones_r = kb.sb("ones_r", [1, 512], F32)
    kb.op("pool", lambda e: e.memset(ones_r[:], 1.0), writes=["ones_r"])

    for hl in range(2):
        with kb.phase():
            qT = kb.sb("qT", [128, S], BF16)
            kT = kb.sb("kT", [128, S], BF16)
            Vaug = kb.sb("Vaug", [128, 64, 136], BF16)
            G = kb.sb("G", [128, 64, 128], BF16)
            ka = kb.sb("ka", [6, S], BF16)
            qa = kb.sb("qa", [6, S], BF16)
            kb.op("pool", lambda e: e.memset(Vaug[:, :, 128:129], 1.0), writes=["Vones"])
            kb.op("pool", lambda e: e.memset(ka[:], 1.0), writes=["ka"])
            kb.op("pool", lambda e: e.memset(qa[:], 1.0), writes=["qa"])
            with kb.phase():
                w_sb = kb.sb("w_sb", [128, 16, 514], BF16)
                kb.dma("pool", w_sb[:], w_loc[hl].rearrange("(k p) n -> p k n", p=128), writes=["w_sb"])
                xc = kb.sb("xc", [128, 16, 512], BF16)
                pp = [kb.ps("pp%d" % i, [128, 512], F32) for i in range(4)]
                pss = kb.ps("pss", [128, 512], F32)
                psf = kb.ps("psf", [128, 512], F32)
                raw = [kb.sb("raw%d" % i, [128, 512], F32) for i in range(2)]
                sq = [kb.sb("sq%d" % i, [128, 512], F32) for i in range(2)]
                rs = [kb.sb("rs%d" % i, [128, 512], F32) for i in range(2)]
                e_t = kb.sb("e_t", [1, 512], F32)
                l_t = kb.sb("l_t", [1, 512], F32)
                cum = [kb.sb("cum%d" % i, [1, 512], F32) for i in range(2)]
                r1 = kb.sb("r1", [1, 512], F32)
                hml = kb.sb("hml", [1, 3, 512], BF16)
                nhml = kb.sb("nhml", [1, 3, 512], BF16)
                n = 0
                nq = 0
                for c in range(16):
                    _load_xc(kb, xg, xc, c, 0)
                    for j in range(2):
                        ps = pp[n % 4]
                        pk = ("pp", n % 4)
                        rw, sqq, rss = raw[nq % 2], sq[nq % 2], rs[nq % 2]
                        i2 = nq % 2

                        def evac(ps, j=j, c=c, pk=pk, rw=rw, sqq=sqq, rss=rss, i2=i2):
                            kb.op("act", lambda e: e.copy(out=rw[:], in_=ps[:]), reads=[pk], writes=[("raw", i2), pk])
                            kb.op("dve", lambda e: e.tensor_tensor(out=sqq[:], in0=rw[:], in1=rw[:], op=ALU.mult),
                                  reads=[("raw", i2)], writes=[("sq", i2)])
                            kb.op("pe", lambda e: e.matmul(pss[:], lhsT=ones_f[:], rhs=sqq[:], start=True, stop=True),
                                  reads=[("sq", i2), "ones_f"], writes=["pss"])
                            kb.op("dve", lambda e: e.tensor_scalar(out=rss[:], in0=pss[:], scalar1=1.0 / 128.0, scalar2=EPS,
                                                                   op0=ALU.mult, op1=ALU.add),
                                  reads=["pss"], writes=[("rs", i2), "pss"])
                            kb.op("act", lambda e: e.activation(out=rss[:], in_=rss[:], func=AF.Sqrt),
                                  reads=[("rs", i2)], writes=[("rs", i2)])
                            kb.op("dve", lambda e: e.reciprocal(out=rss[:], in_=rss[:]), reads=[("rs", i2)], writes=[("rs", i2)])
                            dst = (qT if j == 0 else kT)[:, c * 512:(c + 1) * 512]
                            kb.op("dve", lambda e: e.scalar_tensor_tensor(out=dst, in0=rw[:], scalar=qkw_sb[:, j:j + 1],
                                                                          in1=rss[:], op0=ALU.mult, op1=ALU.mult),
                                  reads=[("raw", i2), ("rs", i2), "qkw_sb"], writes=["qT" if j == 0 else "kT"])
                        _proj_fm(kb, ps, pk, w_sb, j * 128, xc, 0, evac)
                        n += 1
                        nq += 1
                    for s in range(4):
                        ps = pp[n % 4]
                        pk = ("pp", n % 4)

                        def evac(ps, s=s, c=c, pk=pk):
                            kb.op("dve", lambda e: e.tensor_copy(out=Vaug[:, c * 4 + s, 0:128], in_=ps[:, 0:128]),
                                  reads=[pk], writes=["Vaug", pk])
                            kb.op("act", lambda e: e.activation(out=G[:, c * 4 + s, :], in_=ps[:, 128:256], func=AF.Sigmoid),
                                  reads=[pk], writes=["G", pk])
                        _proj_tm(kb, ps, pk, w_sb, 256, 256, xc, 0, s, evac)
                        n += 1
                    for k in range(16):
                        kb.op("pe", lambda e, k=k: e.matmul(psf[0:1, :], lhsT=w_sb[:, k, 512:513], rhs=xc[:, k, :],
                                                            start=(k == 0), stop=(k == 15)),
                              reads=["w_sb", ("xc", 0)], writes=["psf"])
                    kb.op("act", lambda e: e.activation(out=e_t[:], in_=psf[0:1, :], func=AF.Exp, scale=-1.0,
                                                        bias=nfb[0:1, hl:hl + 1]),
                          reads=["psf", "nfb"], writes=["e_t", "psf"])
                    kb.op("act", lambda e: e.activation(out=l_t[:], in_=e_t[:], func=AF.Ln, bias=1.0),
                          reads=["e_t"], writes=["l_t"])
                    cu = cum[c % 2]
                    prev = cum[(c + 1) % 2]
                    init = 0.0 if c == 0 else prev[:, 511:512]
                    kb.op("dve", lambda e, cu=cu, init=init: e.tensor_tensor_scan(
                        out=cu[:], data0=ones_r[:], data1=l_t[:], initial=init, op0=ALU.mult, op1=ALU.subtract),
                        reads=["l_t", "ones_r", ("cum", (c + 1) % 2)], writes=[("cum", c % 2)])
                    ck = ("cum", c % 2)
                    kb.op("dve", lambda e, cu=cu: e.tensor_copy(out=hml[:, 0, :], in_=cu[:]), reads=[ck], writes=["hml"])
                    kb.op("dve", lambda e, cu=cu: e.tensor_tensor(out=r1[:], in0=cu[:], in1=hml[:, 0, :], op=ALU.subtract),
                          reads=[ck, "hml"], writes=["r1"])
                    kb.op("dve", lambda e: e.tensor_copy(out=hml[:, 1, :], in_=r1[:]), reads=["r1"], writes=["hml"])
                    kb.op("dve", lambda e: e.tensor_tensor(out=r1[:], in0=r1[:], in1=hml[:, 1, :], op=ALU.subtract),
                          reads=["r1", "hml"], writes=["r1"])
                    kb.op("dve", lambda e: e.tensor_copy(out=hml[:, 2, :], in_=r1[:]), reads=["r1"], writes=["hml"])
                    kb.op("dve", lambda e: e.tensor_scalar(out=nhml[:], in0=hml[:], scalar1=-1.0, scalar2=None, op0=ALU.mult),
                          reads=["hml"], writes=["nhml"])
                    for a in range(3):
                        kb.dma("sp", qa[a:a + 1, c * 512:(c + 1) * 512], hml[0:1, a, :], reads=["hml"], writes=["qa"],
                               semkey=("auxq", a))
                        kb.dma("sp", ka[3 + a:4 + a, c * 512:(c + 1) * 512], nhml[0:1, a, :], reads=["nhml"], writes=["ka"],
                               semkey=("auxk", a))
            with kb.phase():
                masks = _causal_masks(kb, "cmask")
                rc = kb.sb("rc", [128, 4], F32)
                ob = [kb.sb("ob%d" % i, [128, 128], BF16) for i in range(2)]
                oTt = [kb.sb("oTt%d" % i, [128, 512], BF16) for i in range(2)]
                pTr = kb.ps("pTr", [128, 1024], BF16)
                tag = "B"

                def finish(qc, m, psO):
                    ot = oTt[qc % 2]
                    for s in range(4):
                        pk = ("pO" + tag, s)
                        kb.op("dve", lambda e, s=s: e.reciprocal(out=rc[:, s:s + 1], in_=psO[s][:, 128:129]),
                              reads=[pk], writes=[("rc", s), pk])
                        o = ob[s % 2]
                        kb.op("dve", lambda e, s=s, o=o: e.scalar_tensor_tensor(
                            out=o[:], in0=psO[s][:, 0:128], scalar=rc[:, s:s + 1], in1=G[:, qc * 4 + s, :],
                            op0=ALU.mult, op1=ALU.mult), reads=[pk, ("rc", s), "G"], writes=[("ob", s % 2), pk])
                        kb.op("pe", lambda e, s=s, o=o: e.transpose(pTr[:, s * 128:(s + 1) * 128], o[:], identb[:]),
                              reads=[("ob", s % 2), "identb"], writes=["pTr"])
                    kb.op("act", lambda e, ot=ot: e.copy(out=ot[:], in_=pTr[:, 0:512]), reads=["pTr"],
                          writes=[("oTt", qc % 2), "pTr"])
                    j, off = qc // 2, (qc % 2) * 512
                    kb.dma("sp", oT_out[j, hl * 128:(hl + 1) * 128, off:off + 512], ot[:], reads=[("oTt", qc % 2)],
                           is_out=True)
                _attn_pipe(kb, tag, [dict(qT=qT, kT=kT, qkey="qT", kkey="kT", aux=(ka, qa))], Vaug, ["Vaug", "Vones"],
                           128, masks, "cmask", 1.0, finish)
    return kb.done()


def _affine_mat(kb, name, dt, pattern, base, cm, op):
    t = kb.sb(name, [128, 128], dt)
    kb.op("pool", lambda e: e.memset(t[:], 1.0), writes=[name])
    kb.op("pool", lambda e: e.affine_select(out=t[:], in_=t[:], pattern=pattern, compare_op=op, fill=0.0, base=base,
                                            channel_multiplier=cm), reads=[name], writes=[name])
    return t


def _b1(ap):
    return ap.unsqueeze(1).broadcast_to([128, 2, 128])


def _b2(ap):
    return ap.unsqueeze(2).broadcast_to([128, 2, 128])


def _h2(ap):
    return ap.rearrange("p (h d) -> p h d", h=2)


def build_C2():
    kb = KB()
    xg = kb.dram("xg", [8, D, TL], BF16, "ExternalInput")
    w_loc = kb.dram("w_loc", [2, D, 772], F32, "ExternalInput")
    cw = kb.dram("cw", [2, 128, 16], F32, "ExternalInput")
    hp = kb.dram("hp", [2, 4], F32, "ExternalInput")
    onw = kb.dram("onw", [1, 128], F32, "ExternalInput")
    oT_out = kb.dram("oT_out", [8, 512, TL], BF16, "ExternalOutput")

    identb = _ident(kb, "identb", BF16)
    identf = _ident(kb, "identf", F32)
    ones_f = kb.sb("ones_f", [128, 128], F32)
    kb.op("pool", lambda e: e.memset(ones_f[:], 1.0), writes=["ones_f"])
    triu = _affine_mat(kb, "triu", F32, [[1, 128]], 0, -1, ALU.is_ge)
    sel = _affine_mat(kb, "sel", F32, [[0, 128]], -127, 1, ALU.is_equal)
    lowm = _affine_mat(kb, "lowm", F32, [[-1, 128]], 0, 1, ALU.is_ge)
    strm = _affine_mat(kb, "strm", F32, [[-1, 128]], -1, 1, ALU.is_ge)
    wn = kb.sb("wn", [128, 128], F32)
    kb.dma("sp", wn[:], onw.partition_broadcast(128), writes=["wn"])
    bdm = kb.sb("bdm", [128, 128], F32)
    offm = kb.sb("offm", [128, 128], F32)
    kb.op("pool", lambda e: e.memset(bdm[:], 0.0), writes=["bdm"])
    kb.op("pool", lambda e: e.memset(offm[:], 1.0), writes=["offm"])
    for b in range(4):
        kb.op("pool", lambda e, b=b: e.memset(bdm[32 * b:32 * b + 32, 32 * b:32 * b + 32], 1.0), writes=["bdm"])
        kb.op("pool", lambda e, b=b: e.memset(offm[32 * b:32 * b + 32, 32 * b:32 * b + 32], 0.0), writes=["offm"])

    for pr in range(2):
        with kb.phase():
            qT = kb.sb("qT", [128, S], BF16)
            kT = kb.sb("kT", [128, S], BF16)
            ktm = kb.sb("ktm", [128, 64, 128], BF16)
            vb = kb.sb("vb", [128, 64, 2, 128], BF16)
            Zs = kb.sb("Zs", [128, 64, 2, 128], BF16)
            beta = kb.sb("beta", [128, 64, 2], F32)
            g = kb.sb("g", [128, 64, 2], F32)
            cw_sb = kb.sb("cw_sb", [128, 16], F32)
            hp_sb = kb.sb("hp_sb", [128, 4], F32)
            kb.dma("sp", cw_sb[:], cw[pr], writes=["cw_sb"])
            kb.dma("sp", hp_sb[:], hp[pr:pr + 1, :].partition_broadcast(128), writes=["hp_sb"])
            kb.op("act", lambda e: e.activation(out=hp_sb[:, 0:2], in_=hp_sb[:, 0:2], func=AF.Exp), reads=["hp_sb"],
                  writes=["hp_sb"])
            kb.op("dve", lambda e: e.tensor_scalar(out=hp_sb[:, 0:2], in0=hp_sb[:, 0:2], scalar1=-1.0, scalar2=None,
                                                   op0=ALU.mult), reads=["hp_sb"], writes=["hp_sb"])
            with kb.phase():
                w_sb = kb.sb("w_sb", [128, 16, 772], BF16)
                kb.dma("pool", w_sb[:], w_loc[pr].rearrange("(k p) n -> p k n", p=128), writes=["w_sb"])
                xcs = [kb.sb("xc%d" % i, [128, 16, 512], BF16) for i in range(2)]
                pp = [kb.ps("pp%d" % i, [128, 512], F32) for i in range(4)]
                pss = kb.ps("pss", [128, 512], F32)
                ptr = kb.ps("ptr", [128, 1024], BF16)
                rawc = [kb.sb("rawc%d" % j, [128, 515], F32) for j in range(4)]
                acc = [kb.sb("cacc%d" % i, [128, 512], F32) for i in range(2)]
                sil = [kb.sb("sil%d" % i, [128, 512], F32) for i in range(2)]
                sq = kb.sb("sq", [128, 512], F32)
                rs = kb.sb("rs", [128, 512], F32)
                vbf = kb.sb("vbf", [128, 512], BF16)
                et = kb.sb("et", [128, 4, 2], F32)
                for j in range(4):
                    kb.op("pool", lambda e, j=j: e.memset(rawc[j][:], 0.0), writes=[("rawc", j)])
                n = 0
                nq = 0
                for c in range(16):
                    xc = xcs[c % 2]
                    xs = c % 2
                    _load_xc(kb, xg, xc, c, xs)
                    for s in range(4):
                        ps = pp[n % 4]
                        pk = ("pp", n % 4)
                        blk = c * 4 + s

                        def evac(ps, pk=pk, blk=blk, s=s):
                            kb.op("act", lambda e: e.activation(out=Zs[:, blk, :, :], in_=_h2(ps[:, 0:256]), func=AF.Silu),
                                  reads=[pk], writes=["Zs", pk])
                            kb.op("act", lambda e: e.activation(out=beta[:, blk, :], in_=ps[:, 256:258], func=AF.Sigmoid),
                                  reads=[pk], writes=["beta", pk])
                            kb.op("dve", lambda e: e.tensor_tensor(out=et[:, s, :], in0=ps[:, 258:260], in1=hp_sb[:, 2:4],
                                                                   op=ALU.add), reads=[pk, "hp_sb"], writes=[("et", s), pk])
                            kb.op("act", lambda e: e.activation(out=et[:, s, :], in_=et[:, s, :], func=AF.Exp),
                                  reads=[("et", s)], writes=[("et", s)])
                            kb.op("act", lambda e: e.activation(out=et[:, s, :], in_=et[:, s, :], func=AF.Ln, bias=1.0),
                                  reads=[("et", s)], writes=[("et", s)])
                            kb.op("dve", lambda e: e.tensor_tensor(out=g[:, blk, :], in0=et[:, s, :], in1=hp_sb[:, 0:2],
                                                                   op=ALU.mult), reads=[("et", s), "hp_sb"], writes=["g"])
                        _proj_tm(kb, ps, pk, w_sb, 512, 260, xc, xs, s, evac)
                        n += 1
                    for j in range(4):
                        ps = pp[n % 4]
                        pk = ("pp", n % 4)
                        rw = rawc[j]
                        ac = acc[nq % 2]
                        sl = sil[nq % 2]
                        i2 = nq % 2

                        def evac(ps, pk=pk, rw=rw, ac=ac, sl=sl, i2=i2, j=j, c=c):
                            rk = ("rawc", j)
                            if c > 0:
                                kb.op("dve", lambda e: e.tensor_copy(out=rw[:, 0:3], in_=rw[:, 512:515]), reads=[rk], writes=[rk])
                            kb.op("act", lambda e: e.copy(out=rw[:, 3:515], in_=ps[:]), reads=[pk], writes=[rk, pk])
                            kb.op("dve", lambda e: e.tensor_scalar(out=ac[:], in0=rw[:, 0:512], scalar1=cw_sb[:, j * 4:j * 4 + 1],
                                                                   scalar2=None, op0=ALU.mult),
                                  reads=[rk, "cw_sb"], writes=[("cacc", i2)])
                            for tap in range(1, 4):
                                kb.op("dve", lambda e, tap=tap: e.scalar_tensor_tensor(
                                    out=ac[:], in0=rw[:, tap:tap + 512], scalar=cw_sb[:, j * 4 + tap:j * 4 + tap + 1],
                                    in1=ac[:], op0=ALU.mult, op1=ALU.add), reads=[rk, "cw_sb", ("cacc", i2)],
                                    writes=[("cacc", i2)])
                            kb.op("act", lambda e: e.activation(out=sl[:], in_=ac[:], func=AF.Silu), reads=[("cacc", i2)],
                                  writes=[("sil", i2)])
                            if j < 2:
                                kb.op("dve", lambda e: e.tensor_tensor(out=sq[:], in0=sl[:], in1=sl[:], op=ALU.mult),
                                      reads=[("sil", i2)], writes=["sq"])
                                kb.op("pe", lambda e: e.matmul(pss[:], lhsT=ones_f[:], rhs=sq[:], start=True, stop=True),
                                      reads=["sq", "ones_f"], writes=["pss"])
                                kb.op("act", lambda e: e.activation(out=rs[:], in_=pss[:], func=AF.Ln, bias=EPS),
                                      reads=["pss"], writes=["rs", "pss"])
                                kb.op("act", lambda e: e.activation(out=rs[:], in_=rs[:], func=AF.Exp, scale=-0.5),
                                      reads=["rs"], writes=["rs"])
                                dst = (qT if j == 0 else kT)[:, c * 512:(c + 1) * 512]
                                sc = 128 ** -0.5 if j == 0 else 1.0
                                kb.op("dve", lambda e: e.scalar_tensor_tensor(out=dst, in0=sl[:], scalar=sc, in1=rs[:],
                                                                              op0=ALU.mult, op1=ALU.mult),
                                      reads=[("sil", i2), "rs"], writes=["qT" if j == 0 else "kT"])
                                if j == 1:
                                    for s in range(4):
                                        kb.op("pe", lambda e, s=s: e.transpose(
                                            ptr[:, s * 128:(s + 1) * 128], kT[:, c * 512 + s * 128:c * 512 + (s + 1) * 128],
                                            identb[:]), reads=["kT", "identb"], writes=["ptr"])
                                    kb.op("act", lambda e: e.copy(
                                        out=ktm[:, c * 4:(c + 1) * 4, :],
                                        in_=ptr[:, 0:512].rearrange("p (s d) -> p s d", s=4)), reads=["ptr"], writes=["ktm", "ptr"])
                            else:
                                hh = j - 2
                                kb.op("dve", lambda e: e.tensor_copy(out=vbf[:], in_=sl[:]), reads=[("sil", i2)], writes=["vbf"])
                                for s in range(4):
                                    kb.op("pe", lambda e, s=s: e.transpose(ptr[:, 512 + s * 128:512 + (s + 1) * 128],
                                                                           vbf[:, s * 128:(s + 1) * 128], identb[:]),
                                          reads=["vbf", "identb"], writes=["ptr"])
                                for s in range(4):
                                    blk = c * 4 + s
                                    kb.op("dve", lambda e, s=s, blk=blk: e.tensor_scalar(
                                        out=vb[:, blk, hh, :], in0=ptr[:, 512 + s * 128:512 + (s + 1) * 128],
                                        scalar1=beta[:, blk, hh:hh + 1], scalar2=None, op0=ALU.mult),
                                        reads=["ptr", "beta"], writes=["vb", "ptr"])
                        _proj_fm(kb, ps, pk, w_sb, j * 128, xc, xs, evac)
                        n += 1
                        nq += 1
            gc = kb.sb("gc", [128, 64, 2], F32)
            glb = kb.sb("glb", [128, 64, 2], F32)
            egc = kb.sb("egc", [128, 64, 2], F32)
            ekd = kb.sb("ekd", [128, 64, 2], F32)
            egl = kb.sb("egl", [128, 64, 2], F32)
            begc = kb.sb("begc", [128, 64, 2], F32)
            nbeta = kb.sb("nbeta", [128, 64, 2], F32)
            fl = lambda t: t[:].rearrange("p a b -> p (a b)")
            with kb.phase():
                pg = kb.ps("pg", [128, 512], F32)
                kb.op("pe", lambda e: e.matmul(pg[:, 0:128], lhsT=triu[:], rhs=fl(g), start=True, stop=True),
                      reads=["triu", "g"], writes=["pg"])
                kb.op("dve", lambda e: e.tensor_copy(out=fl(gc), in_=pg[:, 0:128]), reads=["pg"], writes=["gc", "pg"])
                kb.op("pe", lambda e: e.matmul(pg[:, 128:256], lhsT=sel[:], rhs=fl(gc), start=True, stop=True),
                      reads=["sel", "gc"], writes=["pg"])
                kb.op("dve", lambda e: e.tensor_copy(out=fl(glb), in_=pg[:, 128:256]), reads=["pg"], writes=["glb", "pg"])
                kb.op("act", lambda e: e.activation(out=fl(egc), in_=fl(gc), func=AF.Exp), reads=["gc"], writes=["egc"])
                kb.op("act", lambda e: e.activation(out=fl(egl), in_=fl(glb), func=AF.Exp), reads=["glb"], writes=["egl"])
                kb.op("dve", lambda e: e.tensor_tensor(out=fl(ekd), in0=fl(glb), in1=fl(gc), op=ALU.subtract),
                      reads=["glb", "gc"], writes=["ekd"])
                kb.op("act", lambda e: e.activation(out=fl(ekd), in_=fl(ekd), func=AF.Exp), reads=["ekd"], writes=["ekd"])
                kb.op("dve", lambda e: e.tensor_tensor(out=fl(begc), in0=fl(beta), in1=fl(egc), op=ALU.mult),
                      reads=["beta", "egc"], writes=["begc"])
                kb.op("dve", lambda e: e.tensor_scalar(out=fl(nbeta), in0=fl(beta), scalar1=-1.0, scalar2=None, op0=ALU.mult),
                      reads=["beta"], writes=["nbeta"])
            with kb.phase():
                bA = kb.ps("bA", [128, 512], F32)
                bB = kb.ps("bB", [128, 512], F32)
                bD = kb.ps("bD", [128, 512], F32)
                bE = kb.ps("bE", [128, 512], F32)
                bF = kb.ps("bF", [128, 512], F32)
                bG = kb.ps("bG", [128, 1024], BF16)
                bH = kb.ps("bH", [128, 512], F32)
                bI = kb.ps("bI", [128, 512], F32)
                T2 = lambda nm, dt: kb.sb(nm, [128, 2, 128], dt)
                St = T2("St", F32)
                Sb = T2("Sb", BF16)
                kb.op("pool", lambda e: e.memset(St[:], 0.0), writes=["St"])
                kb.op("pool", lambda e: e.memset(Sb[:], 0.0), writes=["Sb"])
                dg = T2("dg", F32)
                Dm = T2("Dm", F32)
                Ds = T2("Ds", F32)
                Nf = T2("Nf", F32)
                Noff = T2("Noff", F32)
                M = [T2("M%d" % i, F32) for i in range(2)]
                MT = [T2("MT%d" % i, F32) for i in range(2)]
                X = [T2("X%d" % i, F32) for i in range(2)]
                Bi = T2("Bi", F32)
                Pm = T2("Pm", F32)
                PTm = T2("PTm", F32)
                P2T = T2("P2T", F32)
                Ym = T2("Ym", F32)
                Wm = T2("Wm", F32)
                Xf = T2("Xf", BF16)
                intra = T2("intra", BF16)
                intraT = [T2("intraT%d" % i, BF16) for i in range(2)]
                kbg = T2("kbg", BF16)
                kdec = [T2("kdec%d" % i, BF16) for i in range(2)]
                u = [T2("u%d" % i, F32) for i in range(2)]
                wT = [T2("wT%d" % i, BF16) for i in range(2)]
                vn = T2("vn", BF16)
                tq = T2("tq", F32)
                o = T2("o", F32)
                junk = T2("junkC", F32)
                st = kb.sb("stC", [128, 2], F32)
                on = T2("on", F32)
                ob = T2("obC", BF16)
                oTt = [kb.sb("oTtC%d" % i, [128, 2, 512], BF16) for i in range(2)]
                f2 = lambda t: t[:].rearrange("p h d -> p (h d)")

                def mm2(rec, bank, bkey, c0, lhs, rhs, reads):
                    for hh in range(2):
                        rec("pe", lambda e, hh=hh: e.matmul(bank[:, c0 + hh * 128:c0 + (hh + 1) * 128], lhsT=lhs(hh), rhs=rhs(hh),
                                                            start=True, stop=True), reads=reads, writes=[bkey])

                def gen_prep(n):
                    lst = []

                    def P(*a, **k):
                        lst.append(lambda: kb.op(*a, **k))
                    tok = slice(n * 128, (n + 1) * 128)
                    nb = n % 2
                    P("pe", lambda e: e.matmul(bA[:, 0:128], lhsT=kT[:, tok], rhs=kT[:, tok], start=True, stop=True),
                      reads=["kT"], writes=["bA"])
                    P("pe", lambda e: e.matmul(bA[:, 128:256], lhsT=qT[:, tok], rhs=kT[:, tok], start=True, stop=True),
                      reads=["qT", "kT"], writes=["bA"])
                    P("dve", lambda e: e.tensor_tensor(out=dg[:], in0=_b1(identf[:]), in1=_b2(gc[:, n, :]), op=ALU.mult),
                      reads=["identf", "gc"], writes=["dg"])
                    P("pe", lambda e: e.matmul(bB[:, 0:256], lhsT=ones_f[:], rhs=f2(dg), start=True, stop=True),
                      reads=["dg", "ones_f"], writes=["bB"])
                    P("dve", lambda e: e.tensor_tensor(out=Dm[:], in0=_b2(gc[:, n, :]), in1=_h2(bB[:, 0:256]), op=ALU.subtract),
                      reads=["bB", "gc"], writes=["Dm", "bB"])
                    P("dve", lambda e: e.tensor_scalar(out=f2(Dm), in0=f2(Dm), scalar1=0.0, scalar2=None, op0=ALU.min),
                      reads=["Dm"], writes=["Dm"])
                    P("act", lambda e: e.activation(out=f2(Dm), in_=f2(Dm), func=AF.Exp), reads=["Dm"], writes=["Dm"])
                    P("pool", lambda e: e.tensor_tensor(out=Ds[:], in0=Dm[:], in1=_b1(strm[:]), op=ALU.mult),
                      reads=["Dm", "strm"], writes=["Ds"])
                    P("pool", lambda e: e.tensor_tensor(out=Dm[:], in0=Dm[:], in1=_b1(lowm[:]), op=ALU.mult),
                      reads=["Dm", "lowm", "Ds"], writes=["Dm"])
                    P("pool", lambda e: e.tensor_tensor(out=Ds[:], in0=Ds[:], in1=_b2(nbeta[:, n, :]), op=ALU.mult),
                      reads=["Ds", "nbeta"], writes=["Ds"])
                    P("dve", lambda e: e.tensor_tensor(out=Nf[:], in0=Ds[:], in1=_b1(bA[:, 0:128]), op=ALU.mult),
                      reads=["bA", "Ds"], writes=["Nf", "bA"])
                    P("dve", lambda e: e.tensor_tensor(out=intra[:], in0=Dm[:], in1=_b1(bA[:, 128:256]), op=ALU.mult),
                      reads=["bA", "Dm"], writes=["intra", "bA"])
                    for hh in range(2):
                        P("pe", lambda e, hh=hh: e.transpose(bG[:, hh * 128:(hh + 1) * 128], intra[:, hh, :], identb[:]),
                          reads=["intra", "identb"], writes=["bG"])
                    P("act", lambda e: e.copy(out=f2(intraT[nb]), in_=bG[:, 0:256]), reads=["bG"], writes=[("intraT", nb), "bG"])
                    P("pool", lambda e: e.tensor_tensor(out=M[0][:], in0=Nf[:], in1=_b1(bdm[:]), op=ALU.mult),
                      reads=["Nf", "bdm"], writes=[("M", 0)])
                    P("pool", lambda e: e.tensor_tensor(out=Noff[:], in0=Nf[:], in1=_b1(offm[:]), op=ALU.mult),
                      reads=["Nf", "offm"], writes=["Noff"])
                    for hh in range(2):
                        P("pe", lambda e, hh=hh: e.transpose(bD[:, hh * 128:(hh + 1) * 128], M[0][:, hh, :], identf[:]),
                          reads=[("M", 0), "identf"], writes=["bD"])
                    P("act", lambda e: e.copy(out=f2(MT[0]), in_=bD[:, 0:256]), reads=["bD"], writes=[("MT", 0), "bD"])
                    P("dve", lambda e: e.tensor_tensor(out=X[0][:], in0=MT[0][:], in1=_b1(identf[:]), op=ALU.add),
                      reads=[("MT", 0), "identf"], writes=[("X", 0)])
                    xi = 0
                    mi = 0
                    for lev in range(1, 5):
                        mo = 1 - mi
                        mm2(P, bD, "bD", 0, lambda hh, mi=mi: MT[mi][:, hh, :], lambda hh, mi=mi: M[mi][:, hh, :],
                            [("M", mi), ("MT", mi)])
                        if lev < 4:
                            mm2(P, bE, "bE", 0, lambda hh, mi=mi: M[mi][:, hh, :], lambda hh, mi=mi: MT[mi][:, hh, :],
                                [("M", mi), ("MT", mi)])
                        P("act", lambda e, mo=mo: e.copy(out=f2(M[mo]), in_=bD[:, 0:256]), reads=["bD"], writes=[("M", mo), "bD"])
                        if lev < 4:
                            P("dve", lambda e, mo=mo: e.tensor_copy(out=f2(MT[mo]), in_=bE[:, 0:256]), reads=["bE"],
                              writes=[("MT", mo), "bE"])
                        mm2(P, bF, "bF", 0, lambda hh, mo=mo: M[mo][:, hh, :], lambda hh, xi=xi: X[xi][:, hh, :],
                            [("M", mo), ("X", xi)])
                        P("dve", lambda e, xi=xi: e.tensor_tensor(out=f2(X[1 - xi]), in0=bF[:, 0:256], in1=f2(X[xi]), op=ALU.add),
                          reads=["bF", ("X", xi)], writes=[("X", 1 - xi), "bF"])
                        xi = 1 - xi
                        mi = mo
                    BiT = X[xi]
                    bk = ("X", xi)
                    for hh in range(2):
                        P("pe", lambda e, hh=hh: e.transpose(bD[:, hh * 128:(hh + 1) * 128], BiT[:, hh, :], identf[:]),
                          reads=[bk, "identf"], writes=["bD"])
                    P("act", lambda e: e.copy(out=f2(Bi), in_=bD[:, 0:256]), reads=["bD"], writes=["Bi", "bD"])
                    mm2(P, bE, "bE", 0, lambda hh: Noff[:, hh, :], lambda hh: BiT[:, hh, :], ["Noff", bk])
                    mm2(P, bF, "bF", 0, lambda hh: BiT[:, hh, :], lambda hh: Noff[:, hh, :], ["Noff", bk])
                    P("dve", lambda e: e.tensor_copy(out=f2(Pm), in_=bE[:, 0:256]), reads=["bE"], writes=["Pm", "bE"])
                    P("act", lambda e: e.copy(out=f2(PTm), in_=bF[:, 0:256]), reads=["bF"], writes=["PTm", "bF"])
                    mm2(P, bD, "bD", 0, lambda hh: Pm[:, hh, :], lambda hh: PTm[:, hh, :], ["Pm", "PTm"])
                    P("act", lambda e: e.copy(out=f2(P2T), in_=bD[:, 0:256]), reads=["bD"], writes=["P2T", "bD"])
                    P("dve", lambda e: e.tensor_tensor(out=Ym[:], in0=Pm[:], in1=_b1(identf[:]), op=ALU.add),
                      reads=["Pm", "identf"], writes=["Ym"])
                    mm2(P, bE, "bE", 0, lambda hh: P2T[:, hh, :], lambda hh: Ym[:, hh, :], ["P2T", "Ym"])
                    P("dve", lambda e: e.tensor_tensor(out=f2(Wm), in0=bE[:, 0:256], in1=f2(Ym), op=ALU.add),
                      reads=["bE", "Ym"], writes=["Wm", "bE"])
                    mm2(P, bF, "bF", 0, lambda hh: Bi[:, hh, :], lambda hh: Wm[:, hh, :], ["Bi", "Wm"])
                    P("act", lambda e: e.copy(out=f2(Xf), in_=bF[:, 0:256]), reads=["bF"], writes=["Xf", "bF"])
                    P("dve", lambda e: e.tensor_tensor(out=kbg[:], in0=_b1(ktm[:, n, :]), in1=_b2(begc[:, n, :]), op=ALU.mult),
                      reads=["ktm", "begc"], writes=["kbg"])
                    P("pool", lambda e: e.tensor_tensor(out=kdec[nb][:], in0=_b1(ktm[:, n, :]), in1=_b2(ekd[:, n, :]), op=ALU.mult),
                      reads=["ktm", "ekd"], writes=[("kdec", nb)])
                    mm2(P, bA, "bA", 256, lambda hh: Xf[:, hh, :], lambda hh: vb[:, n, hh, :], ["Xf", "vb"])
                    mm2(P, bB, "bB", 256, lambda hh: kbg[:, hh, :], lambda hh: Xf[:, hh, :], ["Xf", "kbg"])
                    P("act", lambda e: e.copy(out=f2(u[nb]), in_=bA[:, 256:512]), reads=["bA"], writes=[("u", nb), "bA"])
                    P("dve", lambda e: e.tensor_copy(out=f2(wT[nb]), in_=bB[:, 256:512]), reads=["bB"], writes=[("wT", nb), "bB"])
                    return lst

                def gen_scan(n):
                    lst = []

                    def Sx(*a, **k):
                        lst.append(lambda: kb.op(*a, **k))
                    tok = slice(n * 128, (n + 1) * 128)
                    nb = n % 2
                    mm2(Sx, bH, "bH", 0, lambda hh: wT[nb][:, hh, :], lambda hh: Sb[:, hh, :], [("wT", nb), "Sb"])
                    Sx("dve", lambda e: e.tensor_tensor(out=f2(vn), in0=f2(u[nb]), in1=bH[:, 0:256], op=ALU.subtract),
                       reads=[("u", nb), "bH"], writes=["vn", "bH"])
                    mm2(Sx, bH, "bH", 256, lambda hh: qT[:, tok], lambda hh: Sb[:, hh, :], ["qT", "Sb"])
                    mm2(Sx, bI, "bI", 0, lambda hh: intraT[nb][:, hh, :], lambda hh: vn[:, hh, :], [("intraT", nb), "vn"])
                    Sx("dve", lambda e: e.tensor_tensor(out=tq[:], in0=_b2(egc[:, n, :]), in1=_h2(bH[:, 256:512]), op=ALU.mult),
                       reads=["bH", "egc"], writes=["tq", "bH"])
                    Sx("dve", lambda e: e.tensor_tensor(out=f2(o), in0=f2(tq), in1=bI[:, 0:256], op=ALU.add),
                       reads=["tq", "bI"], writes=["o", "bI"])
                    mm2(Sx, bI, "bI", 256, lambda hh: kdec[nb][:, hh, :], lambda hh: vn[:, hh, :], [("kdec", nb), "vn"])
                    Sx("pool", lambda e: e.tensor_tensor(out=St[:], in0=St[:], in1=_b2(egl[:, n, :]), op=ALU.mult),
                       reads=["St", "egl"], writes=["St"])
                    Sx("dve", lambda e: e.tensor_tensor(out=f2(St), in0=f2(St), in1=bI[:, 256:512], op=ALU.add),
                       reads=["St", "bI"], writes=["St", "bI"])
                    Sx("act", lambda e: e.copy(out=f2(Sb), in_=f2(St)), reads=["St"], writes=["Sb"])
                    for hh in range(2):
                        Sx("act", lambda e, hh=hh: e.activation(out=junk[:, hh, :], in_=o[:, hh, :], func=AF.Square,
                                                                scale=128 ** -0.5, accum_out=st[:, hh:hh + 1]),
                           reads=["o"], writes=[("junkC", hh), ("stC", hh)])
                    Sx("act", lambda e: e.activation(out=st[:], in_=st[:], func=AF.Ln, bias=EPS),
                       reads=[("stC", 0), ("stC", 1)], writes=["stC"])
                    Sx("act", lambda e: e.activation(out=st[:], in_=st[:], func=AF.Exp, scale=-0.5), reads=["stC"], writes=["stC"])
                    Sx("dve", lambda e: e.tensor_tensor(out=on[:], in0=o[:], in1=_b2(st[:]), op=ALU.mult),
                       reads=["o", "stC"], writes=["on", ("stC", 0), ("stC", 1)])
                    Sx("dve", lambda e: e.tensor_tensor(out=on[:], in0=on[:], in1=_b1(wn[:]), op=ALU.mult),
                       reads=["on", "wn"], writes=["on"])
                    Sx("pool", lambda e: e.tensor_tensor(out=ob[:], in0=on[:], in1=Zs[:, n, :, :], op=ALU.mult),
                       reads=["on", "Zs"], writes=["obC"])
                    s2 = n % 2
                    for hh in range(2):
                        Sx("pe", lambda e, hh=hh: e.transpose(bG[:, 512 + (s2 * 2 + hh) * 128:512 + (s2 * 2 + hh + 1) * 128],
                                                               ob[:, hh, :], identb[:]), reads=["obC", "identb"], writes=["bG"])
                    if s2 == 1:
                        qc = n // 4
                        ot = oTt[qc % 2]
                        half = (n % 4) // 2
                        Sx("act", lambda e: e.copy(
                            out=ot[:, :, half * 256:(half + 1) * 256].rearrange("p h (b d) -> p b h d", b=2),
                            in_=bG[:, 512:1024].rearrange("p (b h d) -> p b h d", b=2, h=2)),
                            reads=["bG"], writes=[("oTtC", qc % 2), "bG"])
                        if n % 4 == 3:
                            j, off = qc // 2, (qc % 2) * 512
                            lst.append(lambda: kb.dma(
                                "sp", oT_out[j, pr * 256:(pr + 1) * 256, off:off + 512].rearrange("(h p) t -> p h t", p=128),
                                ot[:], reads=[("oTtC", qc % 2)], is_out=True))
                    return lst

                for th in gen_prep(0):
                    th()
                for n in range(64):
                    a = gen_prep(n + 1) if n + 1 < 64 else []
                    b = gen_scan(n)
                    ia = ib = 0
                    while ia < len(a) or ib < len(b):
                        for _ in range(2):
                            if ia < len(a):
                                a[ia]()
                                ia += 1
                        if ib < len(b):
                            b[ib]()
                            ib += 1
    return kb.done()


_PROGS = {}


def _prog(key, builder):
    if key not in _PROGS:
        _PROGS[key] = builder()
    return _PROGS[key]


def _run(nc, in_maps):
    res = run_bass_kernel_spmd(nc, in_maps, core_ids=list(range(NCORES)))
    return res.results


def _pk(w):
    return np.ascontiguousarray(np.asarray(w, np.float32).reshape(16, 128).T)


def _lambda_init(layer):
    return 0.8 - 0.6 * math.exp(-0.3 * layer)


def _mixer_A(xg, layer, slot, inp):
    w_in = inp["a_w_in"][slot]
    lamv = np.stack([inp["a_lam_q1"][slot], inp["a_lam_k1"][slot], inp["a_lam_q2"][slot], inp["a_lam_k2"][slot]])
    li = _lambda_init(layer)
    cst = np.tile(np.array([[li, 1.0 - li]], np.float32), (128, 1))
    subw = np.ascontiguousarray(inp["a_sub_norm"][slot].reshape(1, 256))
    maps = []
    for hd in range(NCORES):
        cols = np.concatenate([np.arange(m * 1024 + hd * 128, m * 1024 + hd * 128 + 128) for m in range(2)] +
                              [2048 + np.arange(m * 1024 + hd * 128, m * 1024 + hd * 128 + 128) for m in range(2)] +
                              [4096 + np.arange(hd * 256, hd * 256 + 256)])
        maps.append({"xg": xg, "w_loc": np.ascontiguousarray(w_in[:, cols]), "lamv": lamv, "subw": subw, "cst": cst})
    return _a2a(_run(_prog("A", build_A), maps), "oT_out")


def _a2a(res, key):
    return [np.ascontiguousarray(np.concatenate([res[r][key][j] for r in range(NCORES)], axis=0))
            for j in range(NCORES)]


def _mixer_B(xg, slot, inp):
    w_in = inp["b_w_in"][slot]
    qkw = np.ascontiguousarray(np.stack([inp["b_q_norm"][slot], inp["b_k_norm"][slot]], axis=1).astype(np.float32))
    maps = []
    for c in range(NCORES):
        wl = []
        for hl in range(2):
            h = 2 * c + hl
            cols = np.concatenate([np.arange(h * 128, h * 128 + 128) + o for o in (0, 2048, 4096, 6144)] +
                                  [np.array([8192 + h, 8192 + h])])
            wl.append(w_in[:, cols])
        fb = np.ascontiguousarray(inp["b_forget_bias"][slot][2 * c:2 * c + 2].reshape(1, 2).astype(np.float32))
        maps.append({"xg": xg, "w_loc": np.ascontiguousarray(np.stack(wl)), "qkw": qkw, "fb": fb})
    return _a2a(_run(_prog("B", build_B), maps), "oT_out")


def _mixer_C(xg, slot, inp):
    w_in = inp["c_w_in"][slot]
    conv = inp["c_conv_w"][slot]
    onw = np.ascontiguousarray(inp["c_out_norm"][slot].reshape(1, 128).astype(np.float32))
    a_log = inp["c_a_log"][slot]
    dtb = inp["c_dt_bias"][slot]
    maps = []
    for c in range(NCORES):
        wl, cws, hps = [], [], []
        for pr in range(2):
            hq = 2 * c + pr
            hv0 = 4 * c + 2 * pr
            hv1 = hv0 + 1
            qc = np.arange(hq * 128, hq * 128 + 128)
            v0 = np.arange(hv0 * 128, hv0 * 128 + 128)
            v1 = np.arange(hv1 * 128, hv1 * 128 + 128)
            cols = np.concatenate([qc, 2048 + qc, 4096 + v0, 4096 + v1, 8192 + v0, 8192 + v1,
                                   np.array([12288 + hv0, 12288 + hv1, 12320 + hv0, 12320 + hv1])])
            wl.append(w_in[:, cols])
            cws.append(np.concatenate([conv[:, ch].T for ch in (qc, 2048 + qc, 4096 + v0, 4096 + v1)], axis=1))
            hps.append([a_log[hv0], a_log[hv1], dtb[hv0], dtb[hv1]])
        maps.append({"xg": xg, "w_loc": np.ascontiguousarray(np.stack(wl)),
                     "cw": np.ascontiguousarray(np.stack(cws)).astype(np.float32),
                     "hp": np.array(hps, np.float32), "onw": onw})
    return _a2a(_run(_prog("C", build_C2), maps), "oT_out")


def kernel(**inp):
    inp = {k: np.asarray(v) for k, v in inp.items()}
    x = inp["x"][0]
    hs = [np.ascontiguousarray(x[c * TL:(c + 1) * TL]) for c in range(NCORES)]
    nrm0 = np.concatenate([_pk(inp["mix_norm"][0]), _pk(inp["mix_norm"][0])], axis=1)
    res = _run(_prog(("T", 0, True, False), lambda: build_T(2048, True, False)),
               [{"h_in": hs[c], "nrm": nrm0} for c in range(NCORES)])
    xg = np.ascontiguousarray(np.stack([res[c]["xnT_out"] for c in range(NCORES)]))
    depth = inp["mix_norm"].shape[0]
    out = None
    for i in range(depth):
        slot = i // 3
        if i % 3 == 0:
            oTs = _mixer_A(xg, i, slot, inp)
            w_out = inp["a_w_out"][slot]
        elif i % 3 == 1:
            oTs = _mixer_B(xg, slot, inp)
            w_out = inp["b_w_out"][slot]
        else:
            oTs = _mixer_C(xg, slot, inp)
            w_out = inp["c_w_out"][slot]
        last = i == depth - 1
        kdim = w_out.shape[0]
        nxt = inp["final_norm"] if last else inp["mix_norm"][i + 1]
        nrm = np.concatenate([_pk(inp["ffn_norm"][i]), _pk(nxt)], axis=1)
        maps = []
        for c in range(NCORES):
            m = {"h_in": hs[c], "oT_in": oTs[c], "w_out": w_out, "nrm": nrm, "w_gate": inp["ffn_w_gate"][i],
                 "w_up": inp["ffn_w_up"][i], "w_down": inp["ffn_w_down"][i]}
            if last:
                m["fnw"] = np.ascontiguousarray(inp["final_norm"].reshape(1, D))
            maps.append(m)
        res = _run(_prog(("T", kdim, False, last), lambda: build_T(kdim, False, last)), maps)
        if last:
            out = np.concatenate([res[c]["y_out"] for c in range(NCORES)], axis=0)[None]
        else:
            hs = [res[c]["h_out"] for c in range(NCORES)]
            xg = np.ascontiguousarray(np.stack([res[c]["xnT_out"] for c in range(NCORES)]))
    return out.astype(np.float32)
```

```python
import contextlib
import math
import numpy as np
import ml_dtypes
import concourse.bass as bass
import concourse.mybir as mybir
from concourse.bass_utils import run_bass_kernel_spmd

F32 = mybir.dt.float32
BF16 = mybir.dt.bfloat16
AF = mybir.ActivationFunctionType
ALU = mybir.AluOpType
AX = mybir.AxisListType

NCORES = 8
D = 2048
S = 8192
TL = S // NCORES
DFF = 5632
EPS = 1e-6


class _Op:
    __slots__ = ("stream", "fn", "deps", "is_dma", "semkey", "ticket", "signal", "idx")

    def __init__(self, stream, fn, is_dma=False, semkey=None):
        self.stream = stream
        self.fn = fn
        self.deps = []
        self.is_dma = is_dma
        self.semkey = semkey
        self.ticket = None
        self.signal = is_dma
        self.idx = -1


class KB:
    STREAMS = ("pe", "act", "dve", "pool", "sp")

    def __init__(self):
        self.nc = bass.Bass("TRN2", target_bir_lowering=False)
        self.es = contextlib.ExitStack()
        self.ops = []
        self.last_w = {}
        self.readers = {}
        self.last_on = {s: None for s in self.STREAMS}
        self.dma_open = []
        self.out_dmas = []

    def dram(self, name, shape, dt, kind):
        return self.nc.dram_tensor(name, list(shape), dt, kind=kind).ap()

    def _uniq(self, name):
        self._nuniq = getattr(self, "_nuniq", 0) + 1
        return "%s_%d" % (name, self._nuniq)

    def sb(self, name, shape, dt):
        return self.es.enter_context(self.nc.sbuf_tensor(self._uniq(name), list(shape), dt))

    def ps(self, name, shape, dt=F32):
        return self.es.enter_context(self.nc.psum_tensor(self._uniq(name), list(shape), dt))

    def _add(self, op, reads, writes):
        deps = []
        for k in reads:
            w = self.last_w.get(k)
            if w is not None:
                deps.append((w, "raw"))
        for k in writes:
            w = self.last_w.get(k)
            if w is not None:
                deps.append((w, "waw"))
            for r in self.readers.get(k, ()):
                deps.append((r, "war"))
        seen = set()
        for d, kind in deps:
            if d is op or id(d) in seen:
                continue
            if (not op.is_dma) and (not d.is_dma) and d.stream == op.stream:
                if op.stream == "pe" or (kind != "raw" and op.stream != "pool"):
                    continue
            seen.add(id(d))
            op.deps.append(d)
            d.signal = True
        op.idx = len(self.ops)
        self.ops.append(op)
        self.last_on[op.stream] = op
        for k in reads:
            lst = self.readers.setdefault(k, [])
            lst[:] = [r for r in lst if not (r.stream == op.stream and not r.is_dma and not op.is_dma)]
            lst.append(op)
        for k in writes:
            self.last_w[k] = op
            self.readers[k] = []
        return op

    def op(self, stream, fn, reads=(), writes=()):
        return self._add(_Op(stream, fn), reads, writes)

    def dma(self, queue, out, in_, reads=(), writes=(), semkey=None, is_out=False):
        if semkey is None:
            semkey = writes[0] if writes else reads[0]
        o = _Op(queue, lambda e: e.dma_start(out=out, in_=in_), is_dma=True, semkey=("dma", semkey))
        self._add(o, reads, writes)
        self.dma_open.append(o)
        if is_out:
            self.out_dmas.append(o)
        return o

    def barrier(self):
        lasts = [o for o in self.last_on.values() if o is not None] + list(self.dma_open)
        self.dma_open = []
        for s in self.STREAMS:
            b = _Op(s, None)
            for d in lasts:
                if d.stream == s and not d.is_dma and s == "pe":
                    continue
                b.deps.append(d)
                d.signal = True
            b.idx = len(self.ops)
            self.ops.append(b)

    def finish(self):
        b = _Op("sp", None)
        for d in self.out_dmas:
            b.deps.append(d)
        b.idx = len(self.ops)
        self.ops.append(b)

    def _init_emit(self):
        if getattr(self, "_sems", None) is None:
            E = self.es.enter_context
            self._sems = {s: E(self.nc.semaphore("sem_" + s)) for s in self.STREAMS}
            self._cnt = {s: 0 for s in self.STREAMS}
            self._dsem = {}
            self._dcnt = {}
            self._waited = {s: {} for s in self.STREAMS}
            self._emitted = 0

    def flush(self):
        nc = self.nc
        self._init_emit()
        E = self.es.enter_context
        new_ops = self.ops[self._emitted:]
        self._emitted = len(self.ops)
        for o in new_ops:
            if o.is_dma:
                if o.semkey not in self._dsem:
                    self._dsem[o.semkey] = E(nc.semaphore("dsem%d" % len(self._dsem)))
                    self._dcnt[o.semkey] = 0
                self._dcnt[o.semkey] += 16
                o.ticket = (self._dsem[o.semkey], self._dcnt[o.semkey])
            elif o.signal and o.fn is not None:
                self._cnt[o.stream] += 1
                o.ticket = (self._sems[o.stream], self._cnt[o.stream])
        by_stream = {s: [o for o in new_ops if o.stream == s] for s in self.STREAMS}

        def run(stream, eng):
            waited = self._waited[stream]
            for o in by_stream[stream]:
                need = {}
                for d in o.deps:
                    if d.ticket is None:
                        continue
                    sem, val = d.ticket
                    if need.get(id(sem), (None, 0))[1] < val:
                        need[id(sem)] = (sem, val)
                for sid, (sem, val) in need.items():
                    if waited.get(sid, 0) < val:
                        eng.wait_ge(sem, val)
                        waited[sid] = val
                if o.fn is None:
                    continue
                ins = o.fn(eng)
                if o.ticket is not None:
                    ins.then_inc(o.ticket[0], 16 if o.is_dma else 1)

        with nc.Block() as block:
            @block.tensor
            def _(e):
                run("pe", e)

            @block.scalar
            def _(e):
                run("act", e)

            @block.vector
            def _(e):
                run("dve", e)

            @block.gpsimd
            def _(e):
                run("pool", e)

            @block.sync
            def _(e):
                run("sp", e)

    @contextlib.contextmanager
    def phase(self):
        outer = self.es
        inner = contextlib.ExitStack()
        self._init_emit()
        self.es = inner
        try:
            yield
            self.es = outer
            self.barrier()
            self.flush()
        finally:
            self.es = outer
            inner.close()

    def done(self):
        self.finish()
        self.flush()
        self.es.close()
        return self.nc


def _ident(kb, name, dt):
    t = kb.sb(name, [128, 128], dt)
    kb.op("pool", lambda e: e.memset(t[:], 0.0), writes=[name])
    kb.op("pool", lambda e: e.affine_select(out=t[:], in_=t[:], pattern=[[-1, 128]], compare_op=ALU.not_equal,
                                            fill=1.0, base=0, channel_multiplier=1), reads=[name], writes=[name])
    return t


def _rsqrt(kb, out, in_, kin, kout, eps=EPS, rec=None):
    rec = rec or kb.op
    rec("act", lambda e: e.activation(out=out, in_=in_, func=AF.Ln, bias=eps), reads=[kin], writes=[kout])
    rec("act", lambda e: e.activation(out=out, in_=out, func=AF.Exp, scale=-0.5), reads=[kout], writes=[kout])


def _rms_to_xT(kb, h, hkey, nrm_sb, woff, xnT, xkey, identb, ss, rstd, tag):
    junk = kb.sb("junk" + tag, [128, D], BF16)
    xn = [kb.sb("xn%s%d" % (tag, i), [128, D], BF16) for i in range(2)]
    pT = [kb.ps("pT%s%d" % (tag, i), [128, 1024], BF16) for i in range(2)]
    for t in range(8):
        kb.op("act", lambda e, t=t: e.activation(out=junk[:], in_=h[:, t, :], func=AF.Square,
                                                 scale=1.0 / math.sqrt(D), accum_out=ss[:, t:t + 1]),
              reads=[(hkey, t)], writes=["junk" + tag, ("ss" + tag, t)])
        _rsqrt(kb, rstd[:, t:t + 1], ss[:, t:t + 1], ("ss" + tag, t), ("rstd" + tag, t))
        xb = xn[t % 2]
        kb.op("dve", lambda e, t=t, xb=xb: e.tensor_scalar(out=xb[:], in0=h[:, t, :], scalar1=rstd[:, t:t + 1], scalar2=None,
                                                           op0=ALU.mult),
              reads=[(hkey, t), ("rstd" + tag, t)], writes=[("xn" + tag, t % 2)])
        for half in range(2):
            pp = pT[half]
            for j in range(8):
                kt = half * 8 + j
                kb.op("pe", lambda e, pp=pp, j=j, kt=kt, xb=xb: e.transpose(pp[:, j * 128:(j + 1) * 128],
                                                                            xb[:, kt * 128:(kt + 1) * 128], identb[:]),
                      reads=[("xn" + tag, t % 2), "identb"], writes=[("pT" + tag, half)])
            for j in range(8):
                kt = half * 8 + j
                if half == 0:
                    kb.op("act", lambda e, pp=pp, j=j, kt=kt, t=t: e.activation(
                        out=xnT[:, kt, t * 128:(t + 1) * 128], in_=pp[:, j * 128:(j + 1) * 128], func=AF.Copy,
                        scale=nrm_sb[:, woff + kt:woff + kt + 1]),
                        reads=[("pT" + tag, half), "nrm_sb"], writes=[(xkey, kt, t), ("pT" + tag, half)])
                else:
                    kb.op("dve", lambda e, pp=pp, j=j, kt=kt, t=t: e.tensor_scalar(
                        out=xnT[:, kt, t * 128:(t + 1) * 128], in0=pp[:, j * 128:(j + 1) * 128],
                        scalar1=nrm_sb[:, woff + kt:woff + kt + 1], scalar2=None, op0=ALU.mult),
                        reads=[("pT" + tag, half), "nrm_sb"], writes=[(xkey, kt, t), ("pT" + tag, half)])


def build_T(kdim, first, last):
    kb = KB()
    KT = kdim // 128
    if first:
        h_in = kb.dram("h_in", [TL, D], F32, "ExternalInput")
        nrm = kb.dram("nrm", [128, 32], F32, "ExternalInput")
    else:
        h_in = kb.dram("h_in", [TL, D], F32, "ExternalInput")
        oT_in = kb.dram("oT_in", [kdim, TL], BF16, "ExternalInput")
        w_out = kb.dram("w_out", [kdim, D], F32, "ExternalInput")
        nrm = kb.dram("nrm", [128, 32], F32, "ExternalInput")
        w_gate = kb.dram("w_gate", [D, DFF], F32, "ExternalInput")
        w_up = kb.dram("w_up", [D, DFF], F32, "ExternalInput")
        w_down = kb.dram("w_down", [DFF, D], F32, "ExternalInput")
    if last:
        fnw = kb.dram("fnw", [1, D], F32, "ExternalInput")
        y_out = kb.dram("y_out", [TL, D], F32, "ExternalOutput")
    else:
        xnT_out = kb.dram("xnT_out", [D, TL], BF16, "ExternalOutput")
        if not first:
            h_out = kb.dram("h_out", [TL, D], F32, "ExternalOutput")

    h = kb.sb("h", [128, 8, D], F32)
    nrm_sb = kb.sb("nrm_sb", [128, 32], F32)
    ss = kb.sb("ss", [128, 16], F32)
    rstd = kb.sb("rstd", [128, 16], F32)
    identb = _ident(kb, "identb", BF16)
    kb.dma("sp", nrm_sb[:], nrm, writes=["nrm_sb"])
    for t in range(8):
        kb.dma("sp", h[:, t, :], h_in[t * 128:(t + 1) * 128, :], writes=[("h", t)])

    if not first:
        with kb.phase():
            oT = kb.sb("oT", [128, KT, TL], BF16)
            wo = [kb.sb("wo%d" % i, [128, KT, 512], BF16) for i in range(2)]
            pso = [kb.ps("pso%d" % i, [128, 512], F32) for i in range(4)]
            oview = oT_in.rearrange("(k p) t -> p k t", p=128)
            kg = KT // 4
            for g in range(4):
                kb.dma("sp", oT[:, g * kg:(g + 1) * kg, :], oview[:, g * kg:(g + 1) * kg, :], writes=[("oT", g)])
            wview = w_out.rearrange("(k p) n -> p k n", p=128)
            n = 0
            for c in range(4):
                wb = wo[c % 2]
                kb.dma("pool", wb[:], wview[:, :, c * 512:(c + 1) * 512], writes=[("wo", c % 2)])
                for t in range(8):
                    pb = pso[n % 4]
                    for k in range(KT):
                        kb.op("pe", lambda e, pb=pb, wb=wb, k=k, t=t: e.matmul(
                            pb[:], lhsT=oT[:, k, t * 128:(t + 1) * 128], rhs=wb[:, k, :], start=(k == 0), stop=(k == KT - 1)),
                            reads=[("oT", k // kg), ("wo", c % 2)], writes=[("pso", n % 4)])
                    kb.op("dve", lambda e, pb=pb, c=c, t=t: e.tensor_tensor(
                        out=h[:, t, c * 512:(c + 1) * 512], in0=h[:, t, c * 512:(c + 1) * 512], in1=pb[:], op=ALU.add),
                        reads=[("pso", n % 4), ("h", t)], writes=[("h", t), ("pso", n % 4)])
                    n += 1

        with kb.phase():
            xnT = kb.sb("xnT", [128, 16, TL], BF16)
            with kb.phase():
                _rms_to_xT(kb, h, "h", nrm_sb, 0, xnT, "xnT", identb, ss, rstd, "a")
            NGU = 4
            wgu = [kb.sb("wgu%d" % i, [128, 16, 256], BF16) for i in range(NGU)]
            wd = [kb.sb("wd%d" % i, [128, 11, 512], BF16) for i in range(2)]
            actT = kb.sb("actT", [128, 11, TL], BF16)
            sgt = [kb.sb("sgt%d" % i, [128, 512], F32) for i in range(2)]
            psg = [kb.ps("psg%d" % i, [128, 512], F32) for i in range(2)]
            psu = [kb.ps("psu%d" % i, [128, 512], F32) for i in range(2)]
            psd = [kb.ps("psd%d" % i, [128, 512], F32) for i in range(3)]
            gview = w_gate.rearrange("(k p) n -> p k n", p=128)
            uview = w_up.rearrange("(k p) n -> p k n", p=128)
            dview = w_down.rearrange("(k p) n -> p k n", p=128)
            xkeys = [("xnT", kt, t) for kt in range(16) for t in range(8)]
            nf = 0
            nh = 0
            nd = 0
            ndw = 0
            for g in range(4):
                for fi in range(11):
                    f = g * 11 + fi
                    sl = nf % NGU
                    wt = wgu[sl]
                    kb.dma("pool", wt[:, :, 0:128], gview[:, :, f * 128:(f + 1) * 128], writes=[("wgu", sl)])
                    kb.dma("pool", wt[:, :, 128:256], uview[:, :, f * 128:(f + 1) * 128], writes=[("wgu", sl)])
                    nf += 1
                    for half in range(2):
                        pg = psg[nh % 2]
                        pu = psu[nh % 2]
                        st = sgt[nh % 2]
                        hk = [("xnT", kt, t) for kt in range(16) for t in range(half * 4, half * 4 + 4)]
                        for k in range(16):
                            kb.op("pe", lambda e, pg=pg, wt=wt, k=k, half=half: e.matmul(
                                pg[:], lhsT=wt[:, k, 0:128], rhs=xnT[:, k, half * 512:(half + 1) * 512],
                                start=(k == 0), stop=(k == 15)),
                                reads=[("wgu", sl)] + (hk if k == 0 else []), writes=[("psg", nh % 2)])
                        for k in range(16):
                            kb.op("pe", lambda e, pu=pu, wt=wt, k=k, half=half: e.matmul(
                                pu[:], lhsT=wt[:, k, 128:256], rhs=xnT[:, k, half * 512:(half + 1) * 512],
                                start=(k == 0), stop=(k == 15)),
                                reads=[("wgu", sl)], writes=[("psu", nh % 2)])
                        kb.op("act", lambda e, pg=pg, st=st: e.activation(out=st[:], in_=pg[:], func=AF.Silu),
                              reads=[("psg", nh % 2)], writes=[("sgt", nh % 2), ("psg", nh % 2)])
                        kb.op("dve", lambda e, pu=pu, st=st, fi=fi, half=half: e.tensor_tensor(
                            out=actT[:, fi, half * 512:(half + 1) * 512], in0=st[:], in1=pu[:], op=ALU.mult),
                            reads=[("sgt", nh % 2), ("psu", nh % 2)], writes=[("actT", fi, half), ("psu", nh % 2)])
                        nh += 1
                for c in range(4):
                    wdt = wd[ndw % 2]
                    kb.dma("pool", wdt[:], dview[:, g * 11:(g + 1) * 11, c * 512:(c + 1) * 512], writes=[("wd", ndw % 2)])
                    for t in range(8):
                        pd = psd[nd % 3]
                        for fi in range(11):
                            kb.op("pe", lambda e, pd=pd, wdt=wdt, fi=fi, t=t: e.matmul(
                                pd[:], lhsT=actT[:, fi, t * 128:(t + 1) * 128], rhs=wdt[:, fi, :],
                                start=(fi == 0), stop=(fi == 10)),
                                reads=[("wd", ndw % 2), ("actT", fi, t // 4)], writes=[("psd", nd % 3)])
                        kb.op("dve", lambda e, pd=pd, c=c, t=t: e.tensor_tensor(
                            out=h[:, t, c * 512:(c + 1) * 512], in0=h[:, t, c * 512:(c + 1) * 512], in1=pd[:], op=ALU.add),
                            reads=[("psd", nd % 3), ("h", t)], writes=[("h", t), ("psd", nd % 3)])
                        nd += 1
                    ndw += 1

    if last:
        with kb.phase():
            fw = kb.sb("fw", [128, D], F32)
            junk = kb.sb("junkf", [128, D], BF16)
            yt = [kb.sb("yt%d" % i, [128, D], F32) for i in range(2)]
            kb.dma("sp", fw[:], fnw.partition_broadcast(128), writes=["fw"])
            for t in range(8):
                kb.op("act", lambda e, t=t: e.activation(out=junk[:], in_=h[:, t, :], func=AF.Square,
                                                         scale=1.0 / math.sqrt(D), accum_out=ss[:, t:t + 1]),
                      reads=[("h", t)], writes=["junkf", ("ssf", t)])
                _rsqrt(kb, rstd[:, t:t + 1], ss[:, t:t + 1], ("ssf", t), ("rstdf", t))
                y = yt[t % 2]
                kb.op("dve", lambda e, t=t, y=y: e.scalar_tensor_tensor(out=y[:], in0=h[:, t, :], scalar=rstd[:, t:t + 1],
                                                                        in1=fw[:], op0=ALU.mult, op1=ALU.mult),
                      reads=[("h", t), ("rstdf", t), "fw"], writes=[("yt", t % 2)])
                kb.dma("sp", y_out[t * 128:(t + 1) * 128, :], y[:], reads=[("yt", t % 2)], is_out=True)
    else:
        with kb.phase():
            xnT2 = kb.sb("xnT2", [128, 16, TL], BF16)
            _rms_to_xT(kb, h, "h", nrm_sb, 16, xnT2, "xnT2", identb, ss, rstd, "b")
            xkeys = [("xnT2", kt, t) for kt in range(16) for t in range(8)]
            xo = xnT_out.rearrange("(k p) t -> p k t", p=128)
            for g in range(4):
                kb.dma("sp", xo[:, g * 4:(g + 1) * 4, :], xnT2[:, g * 4:(g + 1) * 4, :],
                       reads=[("xnT2", kt, t) for kt in range(g * 4, g * 4 + 4) for t in range(8)],
                       semkey=("xo", g), is_out=True)
            if not first:
                for t in range(8):
                    kb.dma("sp", h_out[t * 128:(t + 1) * 128, :], h[:, t, :], reads=[("h", t)], semkey=("ho", t), is_out=True)
    return kb.done()


class _Rec:
    def __init__(self, kb):
        self.kb = kb
        self.lst = []

    def op(self, *a, **k):
        self.lst.append(lambda: self.kb.op(*a, **k))

    def dma(self, *a, **k):
        self.lst.append(lambda: self.kb.dma(*a, **k))

    def add(self, fn):
        self.lst.append(fn)


def _interleave(lists):
    idx = [0] * len(lists)
    left = sum(len(l) for l in lists)
    while left:
        for i, l in enumerate(lists):
            if idx[i] < len(l):
                l[idx[i]]()
                idx[i] += 1
                left -= 1


def _noop(ps):
    return None


def _load_xc(kb, xg, xc, c, slot):
    r, off = c // 2, (c % 2) * 512
    src = xg[r].rearrange("(k p) t -> p k t", p=128)
    kb.dma("sp", xc[:], src[:, :, off:off + 512], writes=[("xc", slot)])


def _proj_fm(kb, ps, pskey, w_sb, col0, xc, xslot, evac):
    for k in range(16):
        kb.op("pe", lambda e, k=k: e.matmul(ps[:], lhsT=w_sb[:, k, col0:col0 + 128], rhs=xc[:, k, :],
                                            start=(k == 0), stop=(k == 15)),
              reads=["w_sb", ("xc", xslot)], writes=[pskey])
    evac(ps)


def _proj_tm(kb, ps, pskey, w_sb, col0, ncol, xc, xslot, s, evac):
    for k in range(16):
        kb.op("pe", lambda e, k=k: e.matmul(ps[:, 0:ncol], lhsT=xc[:, k, s * 128:(s + 1) * 128],
                                            rhs=w_sb[:, k, col0:col0 + ncol], start=(k == 0), stop=(k == 15)),
              reads=["w_sb", ("xc", xslot)], writes=[pskey])
    evac(ps)


def _diag_masks(kb, name):
    ms = []
    for r in range(4):
        m = kb.sb("%s%d" % (name, r), [128, 512], BF16)
        key = (name, r)
        kb.op("pool", lambda e, m=m: e.memset(m[:], 1.0), writes=[key])
        for half in range(2):
            base = -(128 * r + 64 * half)
            kb.op("pool", lambda e, m=m, half=half, base=base: e.affine_select(
                out=m[half * 64:(half + 1) * 64, :], in_=m[half * 64:(half + 1) * 64, :], pattern=[[1, 512]],
                compare_op=ALU.is_ge, fill=0.0, base=base, channel_multiplier=0), reads=[key], writes=[key])
        ms.append(m)
    return ms


def _causal_masks(kb, name):
    ms = []
    for r in range(4):
        m = kb.sb("%s%d" % (name, r), [128, 512], BF16)
        key = (name, r)
        kb.op("pool", lambda e, m=m: e.memset(m[:], 1.0), writes=[key])
        kb.op("pool", lambda e, m=m, r=r: e.affine_select(
            out=m[:], in_=m[:], pattern=[[1, 512]], compare_op=ALU.is_ge, fill=0.0, base=-128 * r,
            channel_multiplier=-1), reads=[key], writes=[key])
        ms.append(m)
    return ms


def _attn_pipe(kb, tag, maps, Vaug, vreads, dv, masks, mkey, scale, finish):
    NPS, NPT = 3, 4
    pS = [kb.ps("pS%s%d" % (tag, i), [128, 512], F32) for i in range(NPS)]
    psO = [kb.ps("pO%s%d" % (tag, i), [128, 512], F32) for i in range(4)]
    PT = [kb.sb("PT%s%d" % (tag, i), [128, 512], BF16) for i in range(NPT)]
    items = [(qc, m, k) for qc in range(16) for m in range(len(maps)) for k in range(qc * 4 + 4)]

    def rec_score(i):
        qc, m, kb_i = items[i]
        mp = maps[m]
        qT, kT = mp["qT"], mp["kT"]
        r = kb_i - qc * 4
        q0 = max(r, 0) * 128
        ps = pS[i % NPS]
        pt = PT[i % NPT]
        psk = ("pS" + tag, i % NPS)
        ptk = ("PT" + tag, i % NPT)
        aux = mp.get("aux")
        kb.op("pe", lambda e: e.matmul(ps[:, q0:512], lhsT=kT[:, kb_i * 128:(kb_i + 1) * 128],
                                       rhs=qT[:, qc * 512 + q0:(qc + 1) * 512], start=True, stop=(aux is None)),
              reads=[mp["qkey"], mp["kkey"]], writes=[psk])
        if aux is not None:
            ka, qa = aux
            kb.op("pe", lambda e: e.matmul(ps[:, q0:512], lhsT=ka[:, kb_i * 128:(kb_i + 1) * 128],
                                           rhs=qa[:, qc * 512 + q0:(qc + 1) * 512], start=False, stop=True),
                  reads=[("ka", r) for r in range(6)] + [("qa", r) for r in range(6)], writes=[psk])
        kb.op("act", lambda e: e.activation(out=pt[:, q0:512], in_=ps[:, q0:512], func=AF.Exp, scale=scale),
              reads=[psk], writes=[ptk, psk])
        if r >= 0:
            kb.op("pool", lambda e: e.tensor_tensor(out=pt[:, q0:512], in0=pt[:, q0:512], in1=masks[r][:, q0:512],
                                                    op=ALU.mult), reads=[ptk, (mkey, r)], writes=[ptk])

    def rec_pv(i):
        qc, m, kb_i = items[i]
        r = kb_i - qc * 4
        pt = PT[i % NPT]
        ptk = ("PT" + tag, i % NPT)
        for s in range(4):
            if r >= 0 and s < r:
                continue
            last = qc * 4 + s
            kb.op("pe", lambda e, s=s, last=last: e.matmul(
                psO[s][:, 0:dv + 1], lhsT=pt[:, s * 128:(s + 1) * 128], rhs=Vaug[:, kb_i, 0:dv + 1],
                start=(kb_i == 0), stop=(kb_i == last)), reads=[ptk] + list(vreads), writes=[("pO" + tag, s)])

    n = len(items)
    rec_score(0)
    for i in range(n):
        if i + 1 < n:
            rec_score(i + 1)
        rec_pv(i)
        qc, m, kb_i = items[i]
        if kb_i == qc * 4 + 3:
            finish(qc, m, psO)


def build_A():
    kb = KB()
    xg = kb.dram("xg", [8, D, TL], BF16, "ExternalInput")
    w_loc = kb.dram("w_loc", [D, 768], F32, "ExternalInput")
    lamv = kb.dram("lamv", [4, 128], F32, "ExternalInput")
    subw = kb.dram("subw", [1, 256], F32, "ExternalInput")
    cst = kb.dram("cst", [128, 2], F32, "ExternalInput")
    oT_out = kb.dram("oT_out", [8, 256, TL], BF16, "ExternalOutput")

    qT = [kb.sb("qT%d" % m, [128, S], BF16) for m in range(2)]
    kT = [kb.sb("kT%d" % m, [128, S], BF16) for m in range(2)]
    Vaug = kb.sb("Vaug", [128, 64, 264], BF16)
    identb = _ident(kb, "identb", BF16)
    cst_sb = kb.sb("cst_sb", [128, 2], F32)
    sw = kb.sb("sw", [128, 256], F32)
    lam4 = kb.sb("lam4", [128, 4, 128], F32)
    lsc = kb.sb("lsc", [128, 8], F32)
    kb.dma("sp", cst_sb[:], cst, writes=["cst_sb"])
    kb.dma("sp", sw[:], subw.partition_broadcast(128), writes=["sw"])
    for i in range(4):
        kb.dma("sp", lam4[:, i, :], lamv[i:i + 1, :].partition_broadcast(128), writes=["lam4"], semkey=("lam4", i))
    kb.op("pool", lambda e: e.memset(Vaug[:, :, 256:257], 1.0), writes=["Vones"])
    prod = kb.sb("lprod", [128, 2, 128], F32)
    kb.op("dve", lambda e: e.tensor_tensor(out=prod[:, 0, :], in0=lam4[:, 0, :], in1=lam4[:, 1, :], op=ALU.mult),
          reads=["lam4"], writes=["lprod"])
    kb.op("dve", lambda e: e.tensor_tensor(out=prod[:, 1, :], in0=lam4[:, 2, :], in1=lam4[:, 3, :], op=ALU.mult),
          reads=["lam4"], writes=["lprod"])
    kb.op("dve", lambda e: e.reduce_sum(out=lsc[:, 0:2], in_=prod[:], axis=AX.X), reads=["lprod"], writes=["lsc"])
    kb.op("act", lambda e: e.activation(out=lsc[:, 2:4], in_=lsc[:, 0:2], func=AF.Exp), reads=["lsc"], writes=["lsc"])
    kb.op("dve", lambda e: e.tensor_tensor(out=lsc[:, 4:5], in0=lsc[:, 3:4], in1=lsc[:, 2:3], op=ALU.subtract),
          reads=["lsc"], writes=["lsc"])
    kb.op("dve", lambda e: e.tensor_tensor(out=lsc[:, 4:5], in0=lsc[:, 4:5], in1=cst_sb[:, 0:1], op=ALU.subtract),
          reads=["lsc", "cst_sb"], writes=["lsc"])
    kb.op("dve", lambda e: e.tensor_scalar(out=sw[:], in0=sw[:], scalar1=cst_sb[:, 1:2], scalar2=None, op0=ALU.mult),
          reads=["sw", "cst_sb"], writes=["sw"])

    with kb.phase():
        w_sb = kb.sb("w_sb", [128, 16, 768], BF16)
        kb.dma("pool", w_sb[:], w_loc.rearrange("(k p) n -> p k n", p=128), writes=["w_sb"])
        xcs = [kb.sb("xc%d" % i, [128, 16, 512], BF16) for i in range(2)]
        pp = [kb.ps("pp%d" % i, [128, 512], F32) for i in range(4)]
        n = 0
        dests = [qT[0], qT[1], kT[0], kT[1]]
        dkeys = ["qT0", "qT1", "kT0", "kT1"]
        for c in range(16):
            xc = xcs[c % 2]
            _load_xc(kb, xg, xc, c, c % 2)
            for j in range(4):
                ps = pp[n % 4]
                eng = "act" if n % 2 == 0 else "dve"

                def evac(ps, j=j, c=c, eng=eng, n=n):
                    dst = dests[j][:, c * 512:(c + 1) * 512]
                    if eng == "act":
                        kb.op("act", lambda e: e.copy(out=dst, in_=ps[:]), reads=[("pp", n % 4)],
                              writes=[dkeys[j], ("pp", n % 4)])
                    else:
                        kb.op("dve", lambda e: e.tensor_copy(out=dst, in_=ps[:]), reads=[("pp", n % 4)],
                              writes=[dkeys[j], ("pp", n % 4)])
                _proj_fm(kb, ps, ("pp", n % 4), w_sb, j * 128, xc, c % 2, evac)
                n += 1
            for s in range(4):
                ps = pp[n % 4]
                eng = "act" if n % 2 == 0 else "dve"

                def evac(ps, s=s, c=c, eng=eng, n=n):
                    dst = Vaug[:, c * 4 + s, 0:256]
                    if eng == "act":
                        kb.op("act", lambda e: e.copy(out=dst, in_=ps[:, 0:256]), reads=[("pp", n % 4)],
                              writes=["Vaug", ("pp", n % 4)])
                    else:
                        kb.op("dve", lambda e: e.tensor_copy(out=dst, in_=ps[:, 0:256]), reads=[("pp", n % 4)],
                              writes=["Vaug", ("pp", n % 4)])
                _proj_tm(kb, ps, ("pp", n % 4), w_sb, 512, 256, xc, c % 2, s, evac)
                n += 1

    with kb.phase():
        masks = _diag_masks(kb, "dmask")
        acc = [kb.sb("acc%d" % s, [128, 256], F32) for s in range(4)]
        rc = kb.sb("rc", [128, 8], F32)
        st = kb.sb("st", [128, 8], F32)
        junk = kb.sb("junkA", [128, 256], F32)
        ob = [kb.sb("ob%d" % s, [128, 256], BF16) for s in range(2)]
        oTt = [kb.sb("oTt%d" % i, [128, 2, 512], BF16) for i in range(2)]
        pTr = kb.ps("pTr", [128, 1024], BF16)
        scale = 128 ** -0.5

        def finish(qc, m, psO):
            for s in range(4):
                pk = ("pOA", s)
                kb.op("dve", lambda e, s=s: e.reciprocal(out=rc[:, s:s + 1], in_=psO[s][:, 256:257]),
                      reads=[pk], writes=[("rc", s), pk])
                if m == 0:
                    kb.op("dve", lambda e, s=s: e.tensor_scalar(out=acc[s][:], in0=psO[s][:, 0:256],
                                                                 scalar1=rc[:, s:s + 1], scalar2=None, op0=ALU.mult),
                          reads=[pk, ("rc", s)], writes=[("acc", s), pk])
                else:
                    kb.op("dve", lambda e, s=s: e.tensor_tensor(out=rc[:, s:s + 1], in0=rc[:, s:s + 1], in1=lsc[:, 4:5],
                                                                 op=ALU.mult),
                          reads=[("rc", s), "lsc"], writes=[("rc", s)])
                    kb.op("dve", lambda e, s=s: e.scalar_tensor_tensor(out=acc[s][:], in0=psO[s][:, 0:256],
                                                                        scalar=rc[:, s:s + 1], in1=acc[s][:],
                                                                        op0=ALU.mult, op1=ALU.add),
                          reads=[pk, ("rc", s), ("acc", s)], writes=[("acc", s), pk])
            if m == 0:
                return
            ot = oTt[qc % 2]
            for s in range(4):
                kb.op("act", lambda e, s=s: e.activation(out=junk[:], in_=acc[s][:], func=AF.Square, scale=1.0 / 16.0,
                                                         accum_out=st[:, s:s + 1]),
                      reads=[("acc", s)], writes=["junkA", ("st", s)])
                _rsqrt(kb, st[:, s:s + 1], st[:, s:s + 1], ("st", s), ("st", s))
                o = ob[s % 2]
                kb.op("dve", lambda e, s=s, o=o: e.scalar_tensor_tensor(out=o[:], in0=acc[s][:], scalar=st[:, s:s + 1],
                                                                         in1=sw[:], op0=ALU.mult, op1=ALU.mult),
                      reads=[("acc", s), ("st", s), "sw"], writes=[("ob", s % 2)])
                for f in range(2):
                    kb.op("pe", lambda e, s=s, f=f, o=o: e.transpose(pTr[:, (s * 2 + f) * 128:(s * 2 + f + 1) * 128],
                                                                     o[:, f * 128:(f + 1) * 128], identb[:]),
                          reads=[("ob", s % 2), "identb"], writes=["pTr"])
                kb.op("dve", lambda e, s=s, ot=ot: e.tensor_copy(
                    out=ot[:, :, s * 128:(s + 1) * 128],
                    in_=pTr[:, s * 256:(s + 1) * 256].rearrange("p (f t) -> p f t", f=2)),
                    reads=["pTr"], writes=[("oTt", qc % 2), "pTr"])
            j, off = qc // 2, (qc % 2) * 512
            kb.dma("sp", oT_out[j].rearrange("(f p) t -> p f t", p=128)[:, :, off:off + 512], ot[:],
                   reads=[("oTt", qc % 2)], is_out=True)

        maps = [dict(qT=qT[m], kT=kT[m], qkey="qT%d" % m, kkey="kT%d" % m) for m in range(2)]
        _attn_pipe(kb, "A", maps, Vaug, ["Vaug", "Vones"], 256, masks, "dmask", scale, finish)
    return kb.done()


def build_B():
    kb = KB()
    xg = kb.dram("xg", [8, D, TL], BF16, "ExternalInput")
    w_loc = kb.dram("w_loc", [2, D, 514], F32, "ExternalInput")
    qkw = kb.dram("qkw", [128, 2], F32, "ExternalInput")
    fb = kb.dram("fb", [1, 2], F32, "ExternalInput")
    oT_out = kb.dram("oT_out", [8, 256, TL], BF16, "ExternalOutput")

    identb = _ident(kb, "identb", BF16)
    ones_f = kb.sb("ones_f", [128, 128], F32)
    kb.op("pool", lambda e: e.memset(ones_f[:], 1.0), writes=["ones_f"])
    qkw_sb = kb.sb("qkw_sb", [128, 2], F32)
    kb.dma("sp", qkw_sb[:], qkw, writes=["qkw_sb"])
    kb.op("dve", lambda e: e.tensor_scalar(out=qkw_sb[:, 0:1], in0=qkw_sb[:, 0:1], scalar1=128 ** -0.5, scalar2=None,
                                           op0=ALU.mult), reads=["qkw_sb"], writes=["qkw_sb"])
    nfb = kb.sb("nfb", [1, 2], F32)
    kb.dma("sp", nfb[:], fb, writes=["nfb"])
    kb.op("dve", lambda e: e.tensor_scalar(out=nfb[:], in0=nfb[:], scalar1=-1.0, scalar2=None, op0=ALU.mult),
          reads=["nfb"], writes=["nfb"])
    ones_r = kb.sb("ones_r", [1, 512], F32)
    kb.op("pool", lambda e: e.memset(ones_r[:], 1.0), writes=["ones_r"])

    for hl in range(2):
        with kb.phase():
            qT = kb.sb("qT", [128, S], BF16)
            kT = kb.sb("kT", [128, S], BF16)
            Vaug = kb.sb("Vaug", [128, 64, 136], BF16)
            G = kb.sb("G", [128, 64, 128], BF16)
            ka = kb.sb("ka", [6, S], BF16)
            qa = kb.sb("qa", [6, S], BF16)
            kb.op("pool", lambda e: e.memset(Vaug[:, :, 128:129], 1.0), writes=["Vones"])
            kb.op("pool", lambda e: e.memset(ka[:], 1.0), writes=[("ka", r) for r in range(6)])
            kb.op("pool", lambda e: e.memset(qa[:], 1.0), writes=[("qa", r) for r in range(6)])
            with kb.phase():
                w_sb = kb.sb("w_sb", [128, 16, 514], BF16)
                kb.dma("pool", w_sb[:], w_loc[hl].rearrange("(k p) n -> p k n", p=128), writes=["w_sb"])
                xcs = [kb.sb("xc%d" % i, [128, 16, 512], BF16) for i in range(2)]
                pp = [kb.ps("pp%d" % i, [128, 512], F32) for i in range(4)]
                pss = [kb.ps("pss%d" % i, [128, 512], F32) for i in range(2)]
                psf = kb.ps("psf", [128, 512], F32)
                raw = [kb.sb("raw%d" % i, [128, 512], F32) for i in range(2)]
                sq = [kb.sb("sq%d" % i, [128, 512], F32) for i in range(2)]
                rs = [kb.sb("rs%d" % i, [128, 512], F32) for i in range(2)]
                e_t = kb.sb("e_t", [1, 512], F32)
                l_t = kb.sb("l_t", [1, 512], F32)
                cum = [kb.sb("cum%d" % i, [1, 512], F32) for i in range(2)]
                r1 = kb.sb("r1", [1, 512], F32)
                hml = kb.sb("hml", [1, 3, 512], BF16)
                nhml = kb.sb("nhml", [1, 3, 512], BF16)

                def do_chunk(c):
                    xc = xcs[c % 2]
                    xs = c % 2
                    _load_xc(kb, xg, xc, c, xs)

                    def qk_task(j):
                        R = _Rec(kb)
                        ps, pk = pp[j], ("pp", j)
                        rw, sqq, rss, psj = raw[j], sq[j], rs[j], pss[j]
                        R.add(lambda: _proj_fm(kb, ps, pk, w_sb, j * 128, xc, xs, _noop))
                        R.op("act", lambda e: e.copy(out=rw[:], in_=ps[:]), reads=[pk], writes=[("raw", j), pk])
                        R.op("dve", lambda e: e.tensor_tensor(out=sqq[:], in0=rw[:], in1=rw[:], op=ALU.mult),
                             reads=[("raw", j)], writes=[("sq", j)])
                        R.op("pe", lambda e: e.matmul(psj[:], lhsT=ones_f[:], rhs=sqq[:], start=True, stop=True),
                             reads=[("sq", j), "ones_f"], writes=[("pss", j)])
                        R.op("act", lambda e: e.activation(out=rss[:], in_=psj[:], func=AF.Ln, scale=1.0 / 128.0, bias=EPS),
                             reads=[("pss", j)], writes=[("rs", j), ("pss", j)])
                        R.op("act", lambda e: e.activation(out=rss[:], in_=rss[:], func=AF.Exp, scale=-0.5),
                             reads=[("rs", j)], writes=[("rs", j)])
                        dst = (qT if j == 0 else kT)[:, c * 512:(c + 1) * 512]
                        R.op("dve", lambda e: e.scalar_tensor_tensor(out=dst, in0=rw[:], scalar=qkw_sb[:, j:j + 1],
                                                                     in1=rss[:], op0=ALU.mult, op1=ALU.mult),
                             reads=[("raw", j), ("rs", j), "qkw_sb"], writes=["qT" if j == 0 else "kT"])
                        return R.lst

                    def vg_task(s, bank):
                        R = _Rec(kb)
                        ps, pk = pp[bank], ("pp", bank)
                        R.add(lambda: _proj_tm(kb, ps, pk, w_sb, 256, 256, xc, xs, s, _noop))
                        R.op("dve", lambda e: e.tensor_copy(out=Vaug[:, c * 4 + s, 0:128], in_=ps[:, 0:128]),
                             reads=[pk], writes=["Vaug", pk])
                        R.op("act", lambda e: e.activation(out=G[:, c * 4 + s, :], in_=ps[:, 128:256], func=AF.Sigmoid),
                             reads=[pk], writes=["G", pk])
                        return R.lst

                    def f_task():
                        R = _Rec(kb)

                        def mm():
                            for k in range(16):
                                kb.op("pe", lambda e, k=k: e.matmul(psf[0:1, :], lhsT=w_sb[:, k, 512:513], rhs=xc[:, k, :],
                                                                    start=(k == 0), stop=(k == 15)),
                                      reads=["w_sb", ("xc", xs)], writes=["psf"])
                        R.add(mm)
                        R.op("act", lambda e: e.activation(out=e_t[:], in_=psf[0:1, :], func=AF.Exp, scale=-1.0,
                                                           bias=nfb[0:1, hl:hl + 1]),
                             reads=["psf", "nfb"], writes=["e_t", "psf"])
                        R.op("act", lambda e: e.activation(out=l_t[:], in_=e_t[:], func=AF.Ln, bias=1.0),
                             reads=["e_t"], writes=["l_t"])
                        cu = cum[c % 2]
                        prev = cum[(c + 1) % 2]
                        init = 0.0 if c == 0 else prev[:, 511:512]
                        R.op("dve", lambda e: e.tensor_tensor_scan(out=cu[:], data0=ones_r[:], data1=l_t[:], initial=init,
                                                                   op0=ALU.mult, op1=ALU.subtract),
                             reads=["l_t", "ones_r", ("cum", (c + 1) % 2)], writes=[("cum", c % 2)])
                        ck = ("cum", c % 2)
                        R.op("dve", lambda e: e.tensor_copy(out=hml[:, 0, :], in_=cu[:]), reads=[ck], writes=["hml"])
                        R.op("dve", lambda e: e.tensor_tensor(out=r1[:], in0=cu[:], in1=hml[:, 0, :], op=ALU.subtract),
                             reads=[ck, "hml"], writes=["r1"])
                        R.op("dve", lambda e: e.tensor_copy(out=hml[:, 1, :], in_=r1[:]), reads=["r1"], writes=["hml"])
                        R.op("dve", lambda e: e.tensor_tensor(out=r1[:], in0=r1[:], in1=hml[:, 1, :], op=ALU.subtract),
                             reads=["r1", "hml"], writes=["r1"])
                        R.op("dve", lambda e: e.tensor_copy(out=hml[:, 2, :], in_=r1[:]), reads=["r1"], writes=["hml"])
                        R.op("dve", lambda e: e.tensor_scalar(out=nhml[:], in0=hml[:], scalar1=-1.0, scalar2=None, op0=ALU.mult),
                             reads=["hml"], writes=["nhml"])
                        for a in range(3):
                            R.dma("sp", qa[a:a + 1, c * 512:(c + 1) * 512], hml[0:1, a, :], reads=["hml"], writes=[("qa", a)],
                                  semkey=("auxq", a))
                            R.dma("sp", ka[3 + a:4 + a, c * 512:(c + 1) * 512], nhml[0:1, a, :], reads=["nhml"],
                                  writes=[("ka", 3 + a)], semkey=("auxk", a))
                        return R.lst

                    _interleave([qk_task(0), qk_task(1), vg_task(0, 2), vg_task(1, 3)])
                    _interleave([vg_task(2, 0), vg_task(3, 1), f_task()])
                for c in range(16):
                    do_chunk(c)
            with kb.phase():
                masks = _causal_masks(kb, "cmask")
                rc = kb.sb("rc", [128, 4], F32)
                ob = [kb.sb("ob%d" % i, [128, 128], BF16) for i in range(2)]
                oTt = [kb.sb("oTt%d" % i, [128, 512], BF16) for i in range(2)]
                pTr = kb.ps("pTr", [128, 1024], BF16)
                tag = "B"

                def finish(qc, m, psO):
                    ot = oTt[qc % 2]
                    for s in range(4):
                        pk = ("pO" + tag, s)
                        kb.op("dve", lambda e, s=s: e.reciprocal(out=rc[:, s:s + 1], in_=psO[s][:, 128:129]),
                              reads=[pk], writes=[("rc", s), pk])
                        o = ob[s % 2]
                        kb.op("dve", lambda e, s=s, o=o: e.scalar_tensor_tensor(
                            out=o[:], in0=psO[s][:, 0:128], scalar=rc[:, s:s + 1], in1=G[:, qc * 4 + s, :],
                            op0=ALU.mult, op1=ALU.mult), reads=[pk, ("rc", s), "G"], writes=[("ob", s % 2), pk])
                        kb.op("pe", lambda e, s=s, o=o: e.transpose(pTr[:, s * 128:(s + 1) * 128], o[:], identb[:]),
                              reads=[("ob", s % 2), "identb"], writes=["pTr"])
                    kb.op("act", lambda e, ot=ot: e.copy(out=ot[:], in_=pTr[:, 0:512]), reads=["pTr"],
                          writes=[("oTt", qc % 2), "pTr"])
                    j, off = qc // 2, (qc % 2) * 512
                    kb.dma("sp", oT_out[j, hl * 128:(hl + 1) * 128, off:off + 512], ot[:], reads=[("oTt", qc % 2)],
                           is_out=True)
                _attn_pipe(kb, tag, [dict(qT=qT, kT=kT, qkey="qT", kkey="kT", aux=(ka, qa))], Vaug, ["Vaug", "Vones"],
                           128, masks, "cmask", 1.0, finish)
    return kb.done()


def _affine_mat(kb, name, dt, pattern, base, cm, op):
    t = kb.sb(name, [128, 128], dt)
    kb.op("pool", lambda e: e.memset(t[:], 1.0), writes=[name])
    kb.op("pool", lambda e: e.affine_select(out=t[:], in_=t[:], pattern=pattern, compare_op=op, fill=0.0, base=base,
                                            channel_multiplier=cm), reads=[name], writes=[name])
    return t


def _b1(ap):
    return ap.unsqueeze(1).broadcast_to([128, 2, 128])


def _b2(ap):
    return ap.unsqueeze(2).broadcast_to([128, 2, 128])


def _h2(ap):
    return ap.rearrange("p (h d) -> p h d", h=2)


def build_C2():
    kb = KB()
    xg = kb.dram("xg", [8, D, TL], BF16, "ExternalInput")
    w_loc = kb.dram("w_loc", [2, D, 772], F32, "ExternalInput")
    cw = kb.dram("cw", [2, 128, 16], F32, "ExternalInput")
    hp = kb.dram("hp", [2, 4], F32, "ExternalInput")
    onw = kb.dram("onw", [1, 128], F32, "ExternalInput")
    oT_out = kb.dram("oT_out", [8, 512, TL], BF16, "ExternalOutput")

    identb = _ident(kb, "identb", BF16)
    identf = _ident(kb, "identf", F32)
    ones_f = kb.sb("ones_f", [128, 128], F32)
    kb.op("pool", lambda e: e.memset(ones_f[:], 1.0), writes=["ones_f"])
    triu = _affine_mat(kb, "triu", F32, [[1, 128]], 0, -1, ALU.is_ge)
    sel = _affine_mat(kb, "sel", F32, [[0, 128]], -127, 1, ALU.is_equal)
    lowm = _affine_mat(kb, "lowm", F32, [[-1, 128]], 0, 1, ALU.is_ge)
    strm = _affine_mat(kb, "strm", F32, [[-1, 128]], -1, 1, ALU.is_ge)
    wn = kb.sb("wn", [128, 128], F32)
    kb.dma("sp", wn[:], onw.partition_broadcast(128), writes=["wn"])
    bdm = kb.sb("bdm", [128, 128], F32)
    offm = kb.sb("offm", [128, 128], F32)
    kb.op("pool", lambda e: e.memset(bdm[:], 0.0), writes=["bdm"])
    kb.op("pool", lambda e: e.memset(offm[:], 1.0), writes=["offm"])
    for b in range(4):
        kb.op("pool", lambda e, b=b: e.memset(bdm[32 * b:32 * b + 32, 32 * b:32 * b + 32], 1.0), writes=["bdm"])
        kb.op("pool", lambda e, b=b: e.memset(offm[32 * b:32 * b + 32, 32 * b:32 * b + 32], 0.0), writes=["offm"])

    for pr in range(2):
        with kb.phase():
            qT = kb.sb("qT", [128, S], BF16)
            kT = kb.sb("kT", [128, S], BF16)
            ktm = kb.sb("ktm", [128, 64, 128], BF16)
            vb = kb.sb("vb", [128, 64, 2, 128], BF16)
            Zs = kb.sb("Zs", [128, 64, 2, 128], BF16)
            beta = kb.sb("beta", [128, 64, 2], F32)
            g = kb.sb("g", [128, 64, 2], F32)
            cw_sb = kb.sb("cw_sb", [128, 16], F32)
            hp_sb = kb.sb("hp_sb", [128, 4], F32)
            kb.dma("sp", cw_sb[:], cw[pr], writes=["cw_sb"])
            kb.dma("sp", hp_sb[:], hp[pr:pr + 1, :].partition_broadcast(128), writes=["hp_sb"])
            kb.op("act", lambda e: e.activation(out=hp_sb[:, 0:2], in_=hp_sb[:, 0:2], func=AF.Exp), reads=["hp_sb"],
                  writes=["hp_sb"])
            kb.op("dve", lambda e: e.tensor_scalar(out=hp_sb[:, 0:2], in0=hp_sb[:, 0:2], scalar1=-1.0, scalar2=None,
                                                   op0=ALU.mult), reads=["hp_sb"], writes=["hp_sb"])
            with kb.phase():
                w_sb = kb.sb("w_sb", [128, 16, 772], BF16)
                kb.dma("pool", w_sb[:], w_loc[pr].rearrange("(k p) n -> p k n", p=128), writes=["w_sb"])
                xcs = [kb.sb("xc%d" % i, [128, 16, 512], BF16) for i in range(2)]
                pp = [kb.ps("pp%d" % i, [128, 512], F32) for i in range(4)]
                pss = [kb.ps("pss%d" % i, [128, 512], F32) for i in range(2)]
                ptr = kb.ps("ptr", [128, 1024], BF16)
                ptr2 = kb.ps("ptr2", [128, 1024], BF16)
                rawc = [kb.sb("rawc%d" % j, [128, 515], F32) for j in range(4)]
                acc = [kb.sb("cacc%d" % i, [128, 512], F32) for i in range(4)]
                sil = acc
                sq = [kb.sb("sq%d" % i, [128, 512], F32) for i in range(2)]
                rs = [kb.sb("rs%d" % i, [128, 512], F32) for i in range(2)]
                vbf = [kb.sb("vbf%d" % i, [128, 512], BF16) for i in range(2)]
                et = kb.sb("et", [128, 4, 2], F32)
                for j in range(4):
                    kb.op("pool", lambda e, j=j: e.memset(rawc[j][:], 0.0), writes=[("rawc", j)])
                def do_chunk(c):
                    xc = xcs[c % 2]
                    xs = c % 2
                    _load_xc(kb, xg, xc, c, xs)
                    tasks = []
                    for s in range(4):
                        R = _Rec(kb)
                        ps = pp[s]
                        pk = ("pp", s)
                        blk = c * 4 + s
                        R.add(lambda ps=ps, pk=pk, s=s: _proj_tm(kb, ps, pk, w_sb, 512, 260, xc, xs, s, _noop))
                        R.op("act", lambda e, ps=ps, blk=blk: e.activation(out=Zs[:, blk, :, :], in_=_h2(ps[:, 0:256]), func=AF.Silu),
                             reads=[pk], writes=["Zs", pk])
                        R.op("act", lambda e, ps=ps, blk=blk: e.activation(out=beta[:, blk, :], in_=ps[:, 256:258], func=AF.Sigmoid),
                             reads=[pk], writes=[("beta", blk), pk])
                        R.op("dve", lambda e, ps=ps, s=s: e.tensor_tensor(out=et[:, s, :], in0=ps[:, 258:260], in1=hp_sb[:, 2:4],
                                                                          op=ALU.add), reads=[pk, "hp_sb"], writes=[("et", s), pk])
                        R.op("act", lambda e, s=s: e.activation(out=et[:, s, :], in_=et[:, s, :], func=AF.Exp),
                             reads=[("et", s)], writes=[("et", s)])
                        R.op("act", lambda e, s=s: e.activation(out=et[:, s, :], in_=et[:, s, :], func=AF.Ln, bias=1.0),
                             reads=[("et", s)], writes=[("et", s)])
                        R.op("dve", lambda e, s=s, blk=blk: e.tensor_tensor(out=g[:, blk, :], in0=et[:, s, :], in1=hp_sb[:, 0:2],
                                                                            op=ALU.mult), reads=[("et", s), "hp_sb"], writes=["g"])
                        tasks.append(R.lst)
                    _interleave(tasks)
                    tasks = []
                    for j in range(4):
                        R = _Rec(kb)
                        ps = pp[j]
                        pk = ("pp", j)
                        rw, ac, sl = rawc[j], acc[j], sil[j]
                        rk = ("rawc", j)
                        R.add(lambda ps=ps, pk=pk, j=j: _proj_fm(kb, ps, pk, w_sb, j * 128, xc, xs, _noop))
                        if c > 0:
                            R.op("dve", lambda e, rw=rw: e.tensor_copy(out=rw[:, 0:3], in_=rw[:, 512:515]), reads=[rk], writes=[rk])
                        R.op("act", lambda e, rw=rw, ps=ps: e.copy(out=rw[:, 3:515], in_=ps[:]), reads=[pk], writes=[rk, pk])
                        R.op("dve", lambda e, rw=rw, ac=ac, j=j: e.tensor_scalar(out=ac[:], in0=rw[:, 0:512],
                                                                                  scalar1=cw_sb[:, j * 4:j * 4 + 1],
                                                                                  scalar2=None, op0=ALU.mult),
                             reads=[rk, "cw_sb"], writes=[("cacc", j)])
                        for tap in range(1, 4):
                            R.op("dve", lambda e, tap=tap, rw=rw, ac=ac, j=j: e.scalar_tensor_tensor(
                                out=ac[:], in0=rw[:, tap:tap + 512], scalar=cw_sb[:, j * 4 + tap:j * 4 + tap + 1],
                                in1=ac[:], op0=ALU.mult, op1=ALU.add), reads=[rk, "cw_sb", ("cacc", j)], writes=[("cacc", j)])
                        R.op("act", lambda e, ac=ac, sl=sl: e.activation(out=sl[:], in_=ac[:], func=AF.Silu), reads=[("cacc", j)],
                             writes=[("cacc", j)])
                        if j < 2:
                            sqq, rss, psj = sq[j], rs[j], pss[j]
                            R.op("dve", lambda e, sl=sl, sqq=sqq: e.tensor_tensor(out=sqq[:], in0=sl[:], in1=sl[:], op=ALU.mult),
                                 reads=[("cacc", j)], writes=[("sq", j)])
                            R.op("pe", lambda e, sqq=sqq, psj=psj: e.matmul(psj[:], lhsT=ones_f[:], rhs=sqq[:], start=True, stop=True),
                                 reads=[("sq", j), "ones_f"], writes=[("pss", j)])
                            R.op("act", lambda e, rss=rss, psj=psj: e.activation(out=rss[:], in_=psj[:], func=AF.Ln, bias=EPS),
                                 reads=[("pss", j)], writes=[("rs", j), ("pss", j)])
                            R.op("act", lambda e, rss=rss: e.activation(out=rss[:], in_=rss[:], func=AF.Exp, scale=-0.5),
                                 reads=[("rs", j)], writes=[("rs", j)])
                            dst = (qT if j == 0 else kT)[:, c * 512:(c + 1) * 512]
                            sc = 128 ** -0.5 if j == 0 else 1.0
                            dk = ("qTc", c) if j == 0 else ("kTc", c)
                            R.op("dve", lambda e, dst=dst, sl=sl, sc=sc, rss=rss: e.scalar_tensor_tensor(
                                out=dst, in0=sl[:], scalar=sc, in1=rss[:], op0=ALU.mult, op1=ALU.mult),
                                reads=[("cacc", j), ("rs", j)], writes=[dk, "qT" if j == 0 else "kT"])
                            if j == 1:
                                for s in range(4):
                                    R.op("pe", lambda e, s=s: e.transpose(
                                        ptr[:, s * 128:(s + 1) * 128], kT[:, c * 512 + s * 128:c * 512 + (s + 1) * 128],
                                        identb[:]), reads=[dk, "identb"], writes=["ptr"])
                                R.op("act", lambda e: e.copy(out=ktm[:, c * 4:(c + 1) * 4, :],
                                                             in_=ptr[:, 0:512].rearrange("p (s d) -> p s d", s=4)),
                                     reads=["ptr"], writes=["ktm", "ptr"])
                        else:
                            hh = j - 2
                            vf = vbf[hh]
                            pt_, ptk, pc0 = (ptr, "ptr", 512) if hh == 0 else (ptr2, "ptr2", 0)
                            R.op("dve", lambda e, vf=vf, sl=sl: e.tensor_copy(out=vf[:], in_=sl[:]), reads=[("cacc", j)],
                                 writes=[("vbf", hh)])
                            for s in range(4):
                                R.op("pe", lambda e, s=s, vf=vf, pt_=pt_, pc0=pc0: e.transpose(
                                    pt_[:, pc0 + s * 128:pc0 + (s + 1) * 128], vf[:, s * 128:(s + 1) * 128], identb[:]),
                                    reads=[("vbf", hh), "identb"], writes=[ptk])
                            for s in range(4):
                                blk = c * 4 + s
                                R.op("dve", lambda e, s=s, blk=blk, hh=hh, pt_=pt_, pc0=pc0: e.tensor_scalar(
                                    out=vb[:, blk, hh, :], in0=pt_[:, pc0 + s * 128:pc0 + (s + 1) * 128],
                                    scalar1=beta[:, blk, hh:hh + 1], scalar2=None, op0=ALU.mult),
                                    reads=[ptk, ("beta", blk)], writes=["vb", ptk])
                        tasks.append(R.lst)
                    _interleave(tasks)
                for c in range(16):
                    do_chunk(c)
            gc = kb.sb("gc", [128, 64, 2], F32)
            glb = kb.sb("glb", [128, 64, 2], F32)
            egc = kb.sb("egc", [128, 64, 2], F32)
            ekd = kb.sb("ekd", [128, 64, 2], F32)
            egl = kb.sb("egl", [128, 64, 2], F32)
            begc = kb.sb("begc", [128, 64, 2], F32)
            nbeta = kb.sb("nbeta", [128, 64, 2], F32)
            fl = lambda t: t[:].rearrange("p a b -> p (a b)")
            with kb.phase():
                pg = kb.ps("pg", [128, 512], F32)
                kb.op("pe", lambda e: e.matmul(pg[:, 0:128], lhsT=triu[:], rhs=fl(g), start=True, stop=True),
                      reads=["triu", "g"], writes=["pg"])
                kb.op("dve", lambda e: e.tensor_copy(out=fl(gc), in_=pg[:, 0:128]), reads=["pg"], writes=["gc", "pg"])
                kb.op("pe", lambda e: e.matmul(pg[:, 128:256], lhsT=sel[:], rhs=fl(gc), start=True, stop=True),
                      reads=["sel", "gc"], writes=["pg"])
                kb.op("dve", lambda e: e.tensor_copy(out=fl(glb), in_=pg[:, 128:256]), reads=["pg"], writes=["glb", "pg"])
                kb.op("act", lambda e: e.activation(out=fl(egc), in_=fl(gc), func=AF.Exp), reads=["gc"], writes=["egc"])
                kb.op("act", lambda e: e.activation(out=fl(egl), in_=fl(glb), func=AF.Exp), reads=["glb"], writes=["egl"])
                kb.op("dve", lambda e: e.tensor_tensor(out=fl(ekd), in0=fl(glb), in1=fl(gc), op=ALU.subtract),
                      reads=["glb", "gc"], writes=["ekd"])
                kb.op("act", lambda e: e.activation(out=fl(ekd), in_=fl(ekd), func=AF.Exp), reads=["ekd"], writes=["ekd"])
                bkeys = [("beta", b) for b in range(64)]
                kb.op("dve", lambda e: e.tensor_tensor(out=fl(begc), in0=fl(beta), in1=fl(egc), op=ALU.mult),
                      reads=bkeys + ["egc"], writes=["begc"])
                kb.op("dve", lambda e: e.tensor_scalar(out=fl(nbeta), in0=fl(beta), scalar1=-1.0, scalar2=None, op0=ALU.mult),
                      reads=bkeys, writes=["nbeta"])
            with kb.phase():
                bA = kb.ps("bA", [128, 512], F32)
                bB = kb.ps("bB", [128, 512], F32)
                bD = kb.ps("bD", [128, 512], F32)
                bE = kb.ps("bE", [128, 512], F32)
                bF = kb.ps("bF", [128, 512], F32)
                bG = kb.ps("bG", [128, 1024], BF16)
                bH = kb.ps("bH", [128, 512], F32)
                bI = kb.ps("bI", [128, 512], F32)
                T2 = lambda nm, dt: kb.sb(nm, [128, 2, 128], dt)
                St = T2("St", F32)
                Sb = T2("Sb", BF16)
                kb.op("pool", lambda e: e.memset(St[:], 0.0), writes=["St"])
                kb.op("pool", lambda e: e.memset(Sb[:], 0.0), writes=["Sb"])
                dg = T2("dg", F32)
                Dm = T2("Dm", F32)
                Ds = T2("Ds", F32)
                Nf = T2("Nf", F32)
                Noff = T2("Noff", F32)
                M = [T2("M%d" % i, F32) for i in range(2)]
                MT = [T2("MT%d" % i, F32) for i in range(2)]
                X = [T2("X%d" % i, F32) for i in range(2)]
                Bi = T2("Bi", F32)
                Pm = T2("Pm", F32)
                PTm = T2("PTm", F32)
                P2T = T2("P2T", F32)
                Ym = T2("Ym", F32)
                Wm = T2("Wm", F32)
                Xf = T2("Xf", BF16)
                intra = T2("intra", BF16)
                intraT = [T2("intraT%d" % i, BF16) for i in range(2)]
                kbg = T2("kbg", BF16)
                kdec = [T2("kdec%d" % i, BF16) for i in range(2)]
                u = [T2("u%d" % i, F32) for i in range(2)]
                wT = [T2("wT%d" % i, BF16) for i in range(2)]
                vn = T2("vn", BF16)
                tq = T2("tq", F32)
                o = T2("o", F32)
                junk = T2("junkC", F32)
                st = kb.sb("stC", [128, 2], F32)
                on = T2("on", F32)
                ob = T2("obC", BF16)
                oTt = [kb.sb("oTtC%d" % i, [128, 2, 512], BF16) for i in range(2)]
                f2 = lambda t: t[:].rearrange("p h d -> p (h d)")

                def mm2(rec, bank, bkey, c0, lhs, rhs, reads):
                    for hh in range(2):
                        rec("pe", lambda e, hh=hh: e.matmul(bank[:, c0 + hh * 128:c0 + (hh + 1) * 128], lhsT=lhs(hh), rhs=rhs(hh),
                                                            start=True, stop=True), reads=reads, writes=[bkey])

                def gen_prep(n):
                    lst = []

                    def P(*a, **k):
                        lst.append(lambda: kb.op(*a, **k))
                    tok = slice(n * 128, (n + 1) * 128)
                    nb = n % 2
                    P("pe", lambda e: e.matmul(bA[:, 0:128], lhsT=kT[:, tok], rhs=kT[:, tok], start=True, stop=True),
                      reads=["kT"], writes=["bA"])
                    P("pe", lambda e: e.matmul(bA[:, 128:256], lhsT=qT[:, tok], rhs=kT[:, tok], start=True, stop=True),
                      reads=["qT", "kT"], writes=["bA"])
                    P("dve", lambda e: e.tensor_tensor(out=dg[:], in0=_b1(identf[:]), in1=_b2(gc[:, n, :]), op=ALU.mult),
                      reads=["identf", "gc"], writes=["dg"])
                    P("pe", lambda e: e.matmul(bB[:, 0:256], lhsT=ones_f[:], rhs=f2(dg), start=True, stop=True),
                      reads=["dg", "ones_f"], writes=["bB"])
                    P("dve", lambda e: e.tensor_tensor(out=Dm[:], in0=_b2(gc[:, n, :]), in1=_h2(bB[:, 0:256]), op=ALU.subtract),
                      reads=["bB", "gc"], writes=["Dm", "bB"])
                    P("dve", lambda e: e.tensor_scalar(out=f2(Dm), in0=f2(Dm), scalar1=0.0, scalar2=None, op0=ALU.min),
                      reads=["Dm"], writes=["Dm"])
                    P("act", lambda e: e.activation(out=f2(Dm), in_=f2(Dm), func=AF.Exp), reads=["Dm"], writes=["Dm"])
                    P("pool", lambda e: e.tensor_tensor(out=Ds[:], in0=Dm[:], in1=_b1(strm[:]), op=ALU.mult),
                      reads=["Dm", "strm"], writes=["Ds"])
                    P("pool", lambda e: e.tensor_tensor(out=Dm[:], in0=Dm[:], in1=_b1(lowm[:]), op=ALU.mult),
                      reads=["Dm", "lowm", "Ds"], writes=["Dm"])
                    P("pool", lambda e: e.tensor_tensor(out=Ds[:], in0=Ds[:], in1=_b2(nbeta[:, n, :]), op=ALU.mult),
                      reads=["Ds", "nbeta"], writes=["Ds"])
                    P("dve", lambda e: e.tensor_tensor(out=Nf[:], in0=Ds[:], in1=_b1(bA[:, 0:128]), op=ALU.mult),
                      reads=["bA", "Ds"], writes=["Nf", "bA"])
                    P("dve", lambda e: e.tensor_tensor(out=intra[:], in0=Dm[:], in1=_b1(bA[:, 128:256]), op=ALU.mult),
                      reads=["bA", "Dm"], writes=["intra", "bA"])
                    for hh in range(2):
                        P("pe", lambda e, hh=hh: e.transpose(bG[:, hh * 128:(hh + 1) * 128], intra[:, hh, :], identb[:]),
                          reads=["intra", "identb"], writes=["bG"])
                    P("act", lambda e: e.copy(out=f2(intraT[nb]), in_=bG[:, 0:256]), reads=["bG"], writes=[("intraT", nb), "bG"])
                    P("pool", lambda e: e.tensor_tensor(out=M[0][:], in0=Nf[:], in1=_b1(bdm[:]), op=ALU.mult),
                      reads=["Nf", "bdm"], writes=[("M", 0)])
                    P("pool", lambda e: e.tensor_tensor(out=Noff[:], in0=Nf[:], in1=_b1(offm[:]), op=ALU.mult),
                      reads=["Nf", "offm"], writes=["Noff"])
                    for hh in range(2):
                        P("pe", lambda e, hh=hh: e.transpose(bD[:, hh * 128:(hh + 1) * 128], M[0][:, hh, :], identf[:]),
                          reads=[("M", 0), "identf"], writes=["bD"])
                    P("act", lambda e: e.copy(out=f2(MT[0]), in_=bD[:, 0:256]), reads=["bD"], writes=[("MT", 0), "bD"])
                    P("dve", lambda e: e.tensor_tensor(out=X[0][:], in0=MT[0][:], in1=_b1(identf[:]), op=ALU.add),
                      reads=[("MT", 0), "identf"], writes=[("X", 0)])
                    xi = 0
                    mi = 0
                    for lev in range(1, 5):
                        mo = 1 - mi
                        mm2(P, bD, "bD", 0, lambda hh, mi=mi: MT[mi][:, hh, :], lambda hh, mi=mi: M[mi][:, hh, :],
                            [("M", mi), ("MT", mi)])
                        if lev < 4:
                            mm2(P, bE, "bE", 0, lambda hh, mi=mi: M[mi][:, hh, :], lambda hh, mi=mi: MT[mi][:, hh, :],
                                [("M", mi), ("MT", mi)])
                        P("act", lambda e, mo=mo: e.copy(out=f2(M[mo]), in_=bD[:, 0:256]), reads=["bD"], writes=[("M", mo), "bD"])
                        if lev < 4:
                            P("dve", lambda e, mo=mo: e.tensor_copy(out=f2(MT[mo]), in_=bE[:, 0:256]), reads=["bE"],
                              writes=[("MT", mo), "bE"])
                        mm2(P, bF, "bF", 0, lambda hh, mo=mo: M[mo][:, hh, :], lambda hh, xi=xi: X[xi][:, hh, :],
                            [("M", mo), ("X", xi)])
                        P("dve", lambda e, xi=xi: e.tensor_tensor(out=f2(X[1 - xi]), in0=bF[:, 0:256], in1=f2(X[xi]), op=ALU.add),
                          reads=["bF", ("X", xi)], writes=[("X", 1 - xi), "bF"])
                        xi = 1 - xi
                        mi = mo
                    BiT = X[xi]
                    bk = ("X", xi)
                    for hh in range(2):
                        P("pe", lambda e, hh=hh: e.transpose(bD[:, hh * 128:(hh + 1) * 128], BiT[:, hh, :], identf[:]),
                          reads=[bk, "identf"], writes=["bD"])
                    P("act", lambda e: e.copy(out=f2(Bi), in_=bD[:, 0:256]), reads=["bD"], writes=["Bi", "bD"])
                    mm2(P, bE, "bE", 0, lambda hh: Noff[:, hh, :], lambda hh: BiT[:, hh, :], ["Noff", bk])
                    mm2(P, bF, "bF", 0, lambda hh: BiT[:, hh, :], lambda hh: Noff[:, hh, :], ["Noff", bk])
                    P("dve", lambda e: e.tensor_copy(out=f2(Pm), in_=bE[:, 0:256]), reads=["bE"], writes=["Pm", "bE"])
                    P("act", lambda e: e.copy(out=f2(PTm), in_=bF[:, 0:256]), reads=["bF"], writes=["PTm", "bF"])
                    mm2(P, bD, "bD", 0, lambda hh: Pm[:, hh, :], lambda hh: PTm[:, hh, :], ["Pm", "PTm"])
                    P("act", lambda e: e.copy(out=f2(P2T), in_=bD[:, 0:256]), reads=["bD"], writes=["P2T", "bD"])
                    P("dve", lambda e: e.tensor_tensor(out=Ym[:], in0=Pm[:], in1=_b1(identf[:]), op=ALU.add),
                      reads=["Pm", "identf"], writes=["Ym"])
                    mm2(P, bE, "bE", 0, lambda hh: P2T[:, hh, :], lambda hh: Ym[:, hh, :], ["P2T", "Ym"])
                    P("dve", lambda e: e.tensor_tensor(out=f2(Wm), in0=bE[:, 0:256], in1=f2(Ym), op=ALU.add),
                      reads=["bE", "Ym"], writes=["Wm", "bE"])
                    mm2(P, bF, "bF", 0, lambda hh: Bi[:, hh, :], lambda hh: Wm[:, hh, :], ["Bi", "Wm"])
                    P("act", lambda e: e.copy(out=f2(Xf), in_=bF[:, 0:256]), reads=["bF"], writes=["Xf", "bF"])
                    P("dve", lambda e: e.tensor_tensor(out=kbg[:], in0=_b1(ktm[:, n, :]), in1=_b2(begc[:, n, :]), op=ALU.mult),
                      reads=["ktm", "begc"], writes=["kbg"])
                    P("pool", lambda e: e.tensor_tensor(out=kdec[nb][:], in0=_b1(ktm[:, n, :]), in1=_b2(ekd[:, n, :]), op=ALU.mult),
                      reads=["ktm", "ekd"], writes=[("kdec", nb)])
                    mm2(P, bA, "bA", 256, lambda hh: Xf[:, hh, :], lambda hh: vb[:, n, hh, :], ["Xf", "vb"])
                    mm2(P, bB, "bB", 256, lambda hh: kbg[:, hh, :], lambda hh: Xf[:, hh, :], ["Xf", "kbg"])
                    P("act", lambda e: e.copy(out=f2(u[nb]), in_=bA[:, 256:512]), reads=["bA"], writes=[("u", nb), "bA"])
                    P("dve", lambda e: e.tensor_copy(out=f2(wT[nb]), in_=bB[:, 256:512]), reads=["bB"], writes=[("wT", nb), "bB"])
                    return lst

                def gen_scan(n):
                    lst = []

                    def Sx(*a, **k):
                        lst.append(lambda: kb.op(*a, **k))
                    tok = slice(n * 128, (n + 1) * 128)
                    nb = n % 2
                    mm2(Sx, bH, "bH", 0, lambda hh: wT[nb][:, hh, :], lambda hh: Sb[:, hh, :], [("wT", nb), "Sb"])
                    Sx("dve", lambda e: e.tensor_tensor(out=f2(vn), in0=f2(u[nb]), in1=bH[:, 0:256], op=ALU.subtract),
                       reads=[("u", nb), "bH"], writes=["vn", "bH"])
                    mm2(Sx, bH, "bH", 256, lambda hh: qT[:, tok], lambda hh: Sb[:, hh, :], ["qT", "Sb"])
                    mm2(Sx, bI, "bI", 0, lambda hh: intraT[nb][:, hh, :], lambda hh: vn[:, hh, :], [("intraT", nb), "vn"])
                    Sx("dve", lambda e: e.tensor_tensor(out=tq[:], in0=_b2(egc[:, n, :]), in1=_h2(bH[:, 256:512]), op=ALU.mult),
                       reads=["bH", "egc"], writes=["tq", "bH"])
                    Sx("dve", lambda e: e.tensor_tensor(out=f2(o), in0=f2(tq), in1=bI[:, 0:256], op=ALU.add),
                       reads=["tq", "bI"], writes=["o", "bI"])
                    mm2(Sx, bI, "bI", 256, lambda hh: kdec[nb][:, hh, :], lambda hh: vn[:, hh, :], [("kdec", nb), "vn"])
                    Sx("pool", lambda e: e.tensor_tensor(out=St[:], in0=St[:], in1=_b2(egl[:, n, :]), op=ALU.mult),
                       reads=["St", "egl"], writes=["St"])
                    Sx("dve", lambda e: e.tensor_tensor(out=f2(St), in0=f2(St), in1=bI[:, 256:512], op=ALU.add),
                       reads=["St", "bI"], writes=["St", "bI"])
                    Sx("act", lambda e: e.copy(out=f2(Sb), in_=f2(St)), reads=["St"], writes=["Sb"])
                    for hh in range(2):
                        Sx("act", lambda e, hh=hh: e.activation(out=junk[:, hh, :], in_=o[:, hh, :], func=AF.Square,
                                                                scale=128 ** -0.5, accum_out=st[:, hh:hh + 1]),
                           reads=["o"], writes=[("junkC", hh), ("stC", hh)])
                    Sx("act", lambda e: e.activation(out=st[:], in_=st[:], func=AF.Ln, bias=EPS),
                       reads=[("stC", 0), ("stC", 1)], writes=["stC"])
                    Sx("act", lambda e: e.activation(out=st[:], in_=st[:], func=AF.Exp, scale=-0.5), reads=["stC"], writes=["stC"])
                    Sx("dve", lambda e: e.tensor_tensor(out=on[:], in0=o[:], in1=_b2(st[:]), op=ALU.mult),
                       reads=["o", "stC"], writes=["on", ("stC", 0), ("stC", 1)])
                    Sx("dve", lambda e: e.tensor_tensor(out=on[:], in0=on[:], in1=_b1(wn[:]), op=ALU.mult),
                       reads=["on", "wn"], writes=["on"])
                    Sx("pool", lambda e: e.tensor_tensor(out=ob[:], in0=on[:], in1=Zs[:, n, :, :], op=ALU.mult),
                       reads=["on", "Zs"], writes=["obC"])
                    s2 = n % 2
                    for hh in range(2):
                        Sx("pe", lambda e, hh=hh: e.transpose(bG[:, 512 + (s2 * 2 + hh) * 128:512 + (s2 * 2 + hh + 1) * 128],
                                                               ob[:, hh, :], identb[:]), reads=["obC", "identb"], writes=["bG"])
                    if s2 == 1:
                        qc = n // 4
                        ot = oTt[qc % 2]
                        half = (n % 4) // 2
                        Sx("act", lambda e: e.copy(
                            out=ot[:, :, half * 256:(half + 1) * 256].rearrange("p h (b d) -> p b h d", b=2),
                            in_=bG[:, 512:1024].rearrange("p (b h d) -> p b h d", b=2, h=2)),
                            reads=["bG"], writes=[("oTtC", qc % 2), "bG"])
                        if n % 4 == 3:
                            j, off = qc // 2, (qc % 2) * 512
                            lst.append(lambda: kb.dma(
                                "sp", oT_out[j, pr * 256:(pr + 1) * 256, off:off + 512].rearrange("(h p) t -> p h t", p=128),
                                ot[:], reads=[("oTtC", qc % 2)], is_out=True))
                    return lst

                for th in gen_prep(0):
                    th()
                for n in range(64):
                    a = gen_prep(n + 1) if n + 1 < 64 else []
                    b = gen_scan(n)
                    ia = ib = 0
                    while ia < len(a) or ib < len(b):
                        for _ in range(2):
                            if ia < len(a):
                                a[ia]()
                                ia += 1
                        if ib < len(b):
                            b[ib]()
                            ib += 1
    return kb.done()


_PROGS = {}


def _prog(key, builder):
    if key not in _PROGS:
        _PROGS[key] = builder()
    return _PROGS[key]


def _run(nc, in_maps):
    res = run_bass_kernel_spmd(nc, in_maps, core_ids=list(range(NCORES)))
    return res.results


def _pk(w):
    return np.ascontiguousarray(np.asarray(w, np.float32).reshape(16, 128).T)


def _lambda_init(layer):
    return 0.8 - 0.6 * math.exp(-0.3 * layer)


def _mixer_A(xg, layer, slot, inp):
    w_in = inp["a_w_in"][slot]
    lamv = np.stack([inp["a_lam_q1"][slot], inp["a_lam_k1"][slot], inp["a_lam_q2"][slot], inp["a_lam_k2"][slot]])
    li = _lambda_init(layer)
    cst = np.tile(np.array([[li, 1.0 - li]], np.float32), (128, 1))
    subw = np.ascontiguousarray(inp["a_sub_norm"][slot].reshape(1, 256))
    maps = []
    for hd in range(NCORES):
        cols = np.concatenate([np.arange(m * 1024 + hd * 128, m * 1024 + hd * 128 + 128) for m in range(2)] +
                              [2048 + np.arange(m * 1024 + hd * 128, m * 1024 + hd * 128 + 128) for m in range(2)] +
                              [4096 + np.arange(hd * 256, hd * 256 + 256)])
        maps.append({"xg": xg, "w_loc": np.ascontiguousarray(w_in[:, cols]), "lamv": lamv, "subw": subw, "cst": cst})
    return _a2a(_run(_prog("A", build_A), maps), "oT_out")


def _a2a(res, key):
    return [np.ascontiguousarray(np.concatenate([res[r][key][j] for r in range(NCORES)], axis=0))
            for j in range(NCORES)]


def _mixer_B(xg, slot, inp):
    w_in = inp["b_w_in"][slot]
    qkw = np.ascontiguousarray(np.stack([inp["b_q_norm"][slot], inp["b_k_norm"][slot]], axis=1).astype(np.float32))
    maps = []
    for c in range(NCORES):
        wl = []
        for hl in range(2):
            h = 2 * c + hl
            cols = np.concatenate([np.arange(h * 128, h * 128 + 128) + o for o in (0, 2048, 4096, 6144)] +
                                  [np.array([8192 + h, 8192 + h])])
            wl.append(w_in[:, cols])
        fb = np.ascontiguousarray(inp["b_forget_bias"][slot][2 * c:2 * c + 2].reshape(1, 2).astype(np.float32))
        maps.append({"xg": xg, "w_loc": np.ascontiguousarray(np.stack(wl)), "qkw": qkw, "fb": fb})
    return _a2a(_run(_prog("B", build_B), maps), "oT_out")


def _mixer_C(xg, slot, inp):
    w_in = inp["c_w_in"][slot]
    conv = inp["c_conv_w"][slot]
    onw = np.ascontiguousarray(inp["c_out_norm"][slot].reshape(1, 128).astype(np.float32))
    a_log = inp["c_a_log"][slot]
    dtb = inp["c_dt_bias"][slot]
    maps = []
    for c in range(NCORES):
        wl, cws, hps = [], [], []
        for pr in range(2):
            hq = 2 * c + pr
            hv0 = 4 * c + 2 * pr
            hv1 = hv0 + 1
            qc = np.arange(hq * 128, hq * 128 + 128)
            v0 = np.arange(hv0 * 128, hv0 * 128 + 128)
            v1 = np.arange(hv1 * 128, hv1 * 128 + 128)
            cols = np.concatenate([qc, 2048 + qc, 4096 + v0, 4096 + v1, 8192 + v0, 8192 + v1,
                                   np.array([12288 + hv0, 12288 + hv1, 12320 + hv0, 12320 + hv1])])
            wl.append(w_in[:, cols])
            cws.append(np.concatenate([conv[:, ch].T for ch in (qc, 2048 + qc, 4096 + v0, 4096 + v1)], axis=1))
            hps.append([a_log[hv0], a_log[hv1], dtb[hv0], dtb[hv1]])
        maps.append({"xg": xg, "w_loc": np.ascontiguousarray(np.stack(wl)),
                     "cw": np.ascontiguousarray(np.stack(cws)).astype(np.float32),
                     "hp": np.array(hps, np.float32), "onw": onw})
    return _a2a(_run(_prog("C", build_C2), maps), "oT_out")


def kernel(**inp):
    inp = {k: np.asarray(v) for k, v in inp.items()}
    x = inp["x"][0]
    hs = [np.ascontiguousarray(x[c * TL:(c + 1) * TL]) for c in range(NCORES)]
    nrm0 = np.concatenate([_pk(inp["mix_norm"][0]), _pk(inp["mix_norm"][0])], axis=1)
    res = _run(_prog(("T", 0, True, False), lambda: build_T(2048, True, False)),
               [{"h_in": hs[c], "nrm": nrm0} for c in range(NCORES)])
    xg = np.ascontiguousarray(np.stack([res[c]["xnT_out"] for c in range(NCORES)]))
    depth = inp["mix_norm"].shape[0]
    out = None
    for i in range(depth):
        slot = i // 3
        if i % 3 == 0:
            oTs = _mixer_A(xg, i, slot, inp)
            w_out = inp["a_w_out"][slot]
        elif i % 3 == 1:
            oTs = _mixer_B(xg, slot, inp)
            w_out = inp["b_w_out"][slot]
        else:
            oTs = _mixer_C(xg, slot, inp)
            w_out = inp["c_w_out"][slot]
        last = i == depth - 1
        kdim = w_out.shape[0]
        nxt = inp["final_norm"] if last else inp["mix_norm"][i + 1]
        nrm = np.concatenate([_pk(inp["ffn_norm"][i]), _pk(nxt)], axis=1)
        maps = []
        for c in range(NCORES):
            m = {"h_in": hs[c], "oT_in": oTs[c], "w_out": w_out, "nrm": nrm, "w_gate": inp["ffn_w_gate"][i],
                 "w_up": inp["ffn_w_up"][i], "w_down": inp["ffn_w_down"][i]}
            if last:
                m["fnw"] = np.ascontiguousarray(inp["final_norm"].reshape(1, D))
            maps.append(m)
        res = _run(_prog(("T", kdim, False, last), lambda: build_T(kdim, False, last)), maps)
        if last:
            out = np.concatenate([res[c]["y_out"] for c in range(NCORES)], axis=0)[None]
        else:
            hs = [res[c]["h_out"] for c in range(NCORES)]
            xg = np.ascontiguousarray(np.stack([res[c]["xnT_out"] for c in range(NCORES)]))
    return out.astype(np.float32)
```

```python
import contextlib
import math
import numpy as np
import ml_dtypes
import concourse.bass as bass
import concourse.mybir as mybir
from concourse.bass_utils import run_bass_kernel_spmd

F32 = mybir.dt.float32
BF16 = mybir.dt.bfloat16
AF = mybir.ActivationFunctionType
ALU = mybir.AluOpType
AX = mybir.AxisListType

NCORES = 8
D = 2048
S = 8192
TL = S // NCORES
DFF = 5632
EPS = 1e-6


class _Op:
    __slots__ = ("stream", "fn", "deps", "is_dma", "semkey", "ticket", "signal", "idx")

    def __init__(self, stream, fn, is_dma=False, semkey=None):
        self.stream = stream
        self.fn = fn
        self.deps = []
        self.is_dma = is_dma
        self.semkey = semkey
        self.ticket = None
        self.signal = is_dma
        self.idx = -1


class KB:
    STREAMS = ("pe", "act", "dve", "pool", "sp")

    def __init__(self):
        self.nc = bass.Bass("TRN2", target_bir_lowering=False)
        self.es = contextlib.ExitStack()
        self.ops = []
        self.last_w = {}
        self.readers = {}
        self.last_on = {s: None for s in self.STREAMS}
        self.dma_open = []
        self.out_dmas = []

    def dram(self, name, shape, dt, kind):
        return self.nc.dram_tensor(name, list(shape), dt, kind=kind).ap()

    def _uniq(self, name):
        self._nuniq = getattr(self, "_nuniq", 0) + 1
        return "%s_%d" % (name, self._nuniq)

    def sb(self, name, shape, dt):
        return self.es.enter_context(self.nc.sbuf_tensor(self._uniq(name), list(shape), dt))

    def ps(self, name, shape, dt=F32):
        return self.es.enter_context(self.nc.psum_tensor(self._uniq(name), list(shape), dt))

    def _add(self, op, reads, writes):
        deps = []
        for k in reads:
            w = self.last_w.get(k)
            if w is not None:
                deps.append((w, "raw"))
        for k in writes:
            w = self.last_w.get(k)
            if w is not None:
                deps.append((w, "waw"))
            for r in self.readers.get(k, ()):
                deps.append((r, "war"))
        seen = set()
        for d, kind in deps:
            if d is op or id(d) in seen:
                continue
            if (not op.is_dma) and (not d.is_dma) and d.stream == op.stream:
                if op.stream == "pe" or (kind != "raw" and op.stream != "pool"):
                    continue
            seen.add(id(d))
            op.deps.append(d)
            d.signal = True
        op.idx = len(self.ops)
        self.ops.append(op)
        self.last_on[op.stream] = op
        for k in reads:
            lst = self.readers.setdefault(k, [])
            lst[:] = [r for r in lst if not (r.stream == op.stream and not r.is_dma and not op.is_dma)]
            lst.append(op)
        for k in writes:
            self.last_w[k] = op
            self.readers[k] = []
        return op

    def op(self, stream, fn, reads=(), writes=()):
        return self._add(_Op(stream, fn), reads, writes)

    def dma(self, queue, out, in_, reads=(), writes=(), semkey=None, is_out=False):
        if semkey is None:
            semkey = writes[0] if writes else reads[0]
        o = _Op(queue, lambda e: e.dma_start(out=out, in_=in_), is_dma=True, semkey=("dma", semkey))
        self._add(o, reads, writes)
        self.dma_open.append(o)
        if is_out:
            self.out_dmas.append(o)
        return o

    def barrier(self):
        lasts = [o for o in self.last_on.values() if o is not None] + list(self.dma_open)
        self.dma_open = []
        for s in self.STREAMS:
            b = _Op(s, None)
            for d in lasts:
                if d.stream == s and not d.is_dma and s == "pe":
                    continue
                b.deps.append(d)
                d.signal = True
            b.idx = len(self.ops)
            self.ops.append(b)

    def finish(self):
        b = _Op("sp", None)
        for d in self.out_dmas:
            b.deps.append(d)
        b.idx = len(self.ops)
        self.ops.append(b)

    def _init_emit(self):
        if getattr(self, "_sems", None) is None:
            E = self.es.enter_context
            self._sems = {s: E(self.nc.semaphore("sem_" + s)) for s in self.STREAMS}
            self._cnt = {s: 0 for s in self.STREAMS}
            self._dsem = {}
            self._dcnt = {}
            self._waited = {s: {} for s in self.STREAMS}
            self._emitted = 0

    def flush(self):
        nc = self.nc
        self._init_emit()
        E = self.es.enter_context
        new_ops = self.ops[self._emitted:]
        self._emitted = len(self.ops)
        for o in new_ops:
            if o.is_dma:
                if o.semkey not in self._dsem:
                    self._dsem[o.semkey] = E(nc.semaphore("dsem%d" % len(self._dsem)))
                    self._dcnt[o.semkey] = 0
                self._dcnt[o.semkey] += 16
                o.ticket = (self._dsem[o.semkey], self._dcnt[o.semkey])
            elif o.signal and o.fn is not None:
                self._cnt[o.stream] += 1
                o.ticket = (self._sems[o.stream], self._cnt[o.stream])
        by_stream = {s: [o for o in new_ops if o.stream == s] for s in self.STREAMS}

        def run(stream, eng):
            waited = self._waited[stream]
            for o in by_stream[stream]:
                need = {}
                for d in o.deps:
                    if d.ticket is None:
                        continue
                    sem, val = d.ticket
                    if need.get(id(sem), (None, 0))[1] < val:
                        need[id(sem)] = (sem, val)
                for sid, (sem, val) in need.items():
                    if waited.get(sid, 0) < val:
                        eng.wait_ge(sem, val)
                        waited[sid] = val
                if o.fn is None:
                    continue
                ins = o.fn(eng)
                if o.ticket is not None:
                    ins.then_inc(o.ticket[0], 16 if o.is_dma else 1)

        with nc.Block() as block:
            @block.tensor
            def _(e):
                run("pe", e)

            @block.scalar
            def _(e):
                run("act", e)

            @block.vector
            def _(e):
                run("dve", e)

            @block.gpsimd
            def _(e):
                run("pool", e)

            @block.sync
            def _(e):
                run("sp", e)

    @contextlib.contextmanager
    def phase(self):
        outer = self.es
        inner = contextlib.ExitStack()
        self._init_emit()
        self.es = inner
        try:
            yield
            self.es = outer
            self.barrier()
            self.flush()
        finally:
            self.es = outer
            inner.close()

    def done(self):
        self.finish()
        self.flush()
        self.es.close()
        return self.nc


def _ident(kb, name, dt):
    t = kb.sb(name, [128, 128], dt)
    kb.op("pool", lambda e: e.memset(t[:], 0.0), writes=[name])
    kb.op("pool", lambda e: e.affine_select(out=t[:], in_=t[:], pattern=[[-1, 128]], compare_op=ALU.not_equal,
                                            fill=1.0, base=0, channel_multiplier=1), reads=[name], writes=[name])
    return t


def _rsqrt(kb, out, in_, kin, kout, eps=EPS, rec=None):
    rec = rec or kb.op
    rec("act", lambda e: e.activation(out=out, in_=in_, func=AF.Ln, bias=eps), reads=[kin], writes=[kout])
    rec("act", lambda e: e.activation(out=out, in_=out, func=AF.Exp, scale=-0.5), reads=[kout], writes=[kout])


def _rms_to_xT(kb, h, hkey, nrm_sb, woff, xnT, xkey, identb, ss, rstd, tag):
    junk = kb.sb("junk" + tag, [128, D], BF16)
    xn = [kb.sb("xn%s%d" % (tag, i), [128, D], BF16) for i in range(2)]
    pT = [kb.ps("pT%s%d" % (tag, i), [128, 1024], BF16) for i in range(2)]
    for t in range(8):
        kb.op("act", lambda e, t=t: e.activation(out=junk[:], in_=h[:, t, :], func=AF.Square,
                                                 scale=1.0 / math.sqrt(D), accum_out=ss[:, t:t + 1]),
              reads=[(hkey, t)], writes=["junk" + tag, ("ss" + tag, t)])
        _rsqrt(kb, rstd[:, t:t + 1], ss[:, t:t + 1], ("ss" + tag, t), ("rstd" + tag, t))
        xb = xn[t % 2]
        kb.op("dve", lambda e, t=t, xb=xb: e.tensor_scalar(out=xb[:], in0=h[:, t, :], scalar1=rstd[:, t:t + 1], scalar2=None,
                                                           op0=ALU.mult),
              reads=[(hkey, t), ("rstd" + tag, t)], writes=[("xn" + tag, t % 2)])
        for half in range(2):
            pp = pT[half]
            for j in range(8):
                kt = half * 8 + j
                kb.op("pe", lambda e, pp=pp, j=j, kt=kt, xb=xb: e.transpose(pp[:, j * 128:(j + 1) * 128],
                                                                            xb[:, kt * 128:(kt + 1) * 128], identb[:]),
                      reads=[("xn" + tag, t % 2), "identb"], writes=[("pT" + tag, half)])
            kb.op("dve", lambda e, pp=pp, half=half, t=t: e.tensor_tensor(
                out=xnT[:, half * 8:(half + 1) * 8, t * 128:(t + 1) * 128],
                in0=pp[:, 0:1024].rearrange("p (k d) -> p k d", k=8),
                in1=nrm_sb[:, woff + half * 8:woff + half * 8 + 8].unsqueeze(2).broadcast_to([128, 8, 128]), op=ALU.mult),
                reads=[("pT" + tag, half), "nrm_sb"],
                writes=[(xkey, half * 8 + j, t) for j in range(8)] + [("pT" + tag, half)])


def build_T(kdim, first, last):
    kb = KB()
    KT = kdim // 128
    if first:
        h_in = kb.dram("h_in", [TL, D], F32, "ExternalInput")
        nrm = kb.dram("nrm", [128, 32], F32, "ExternalInput")
    else:
        h_in = kb.dram("h_in", [TL, D], F32, "ExternalInput")
        oT_in = kb.dram("oT_in", [kdim, TL], BF16, "ExternalInput")
        w_out = kb.dram("w_out", [kdim, D], F32, "ExternalInput")
        nrm = kb.dram("nrm", [128, 32], F32, "ExternalInput")
        w_gate = kb.dram("w_gate", [D, DFF], F32, "ExternalInput")
        w_up = kb.dram("w_up", [D, DFF], F32, "ExternalInput")
        w_down = kb.dram("w_down", [DFF, D], F32, "ExternalInput")
    if last:
        fnw = kb.dram("fnw", [1, D], F32, "ExternalInput")
        y_out = kb.dram("y_out", [TL, D], F32, "ExternalOutput")
    else:
        xnT_out = kb.dram("xnT_out", [D, TL], BF16, "ExternalOutput")
        if not first:
            h_out = kb.dram("h_out", [TL, D], F32, "ExternalOutput")

    h = kb.sb("h", [128, 8, D], F32)
    nrm_sb = kb.sb("nrm_sb", [128, 32], F32)
    ss = kb.sb("ss", [128, 16], F32)
    rstd = kb.sb("rstd", [128, 16], F32)
    identb = _ident(kb, "identb", BF16)
    kb.dma("sp", nrm_sb[:], nrm, writes=["nrm_sb"])

    def load_h():
        for t in range(8):
            kb.dma("sp", h[:, t, :], h_in[t * 128:(t + 1) * 128, :], writes=[("h", t)])
    if first:
        load_h()

    if not first:
        with kb.phase():
            oT = kb.sb("oT", [128, KT, TL], BF16)
            wo = [kb.sb("wo%d" % i, [128, KT, 512], BF16) for i in range(2)]
            pso = [kb.ps("pso%d" % i, [128, 512], F32) for i in range(4)]
            oview = oT_in.rearrange("(k p) t -> p k t", p=128)
            kg = KT // 4
            for g in range(4):
                kb.dma("sp", oT[:, g * kg:(g + 1) * kg, :], oview[:, g * kg:(g + 1) * kg, :], writes=[("oT", g)])
            load_h()
            wview = w_out.rearrange("(k p) n -> p k n", p=128)
            n = 0
            for c in range(4):
                wb = wo[c % 2]
                kb.dma("pool", wb[:], wview[:, :, c * 512:(c + 1) * 512], writes=[("wo", c % 2)])
                for t in range(8):
                    pb = pso[n % 4]
                    for k in range(KT):
                        kb.op("pe", lambda e, pb=pb, wb=wb, k=k, t=t: e.matmul(
                            pb[:], lhsT=oT[:, k, t * 128:(t + 1) * 128], rhs=wb[:, k, :], start=(k == 0), stop=(k == KT - 1)),
                            reads=[("oT", k // kg), ("wo", c % 2)], writes=[("pso", n % 4)])
                    kb.op("dve", lambda e, pb=pb, c=c, t=t: e.tensor_tensor(
                        out=h[:, t, c * 512:(c + 1) * 512], in0=h[:, t, c * 512:(c + 1) * 512], in1=pb[:], op=ALU.add),
                        reads=[("pso", n % 4), ("h", t)], writes=[("h", t), ("pso", n % 4)])
                    n += 1

        with kb.phase():
            xnT = kb.sb("xnT", [128, 16, TL], BF16)
            with kb.phase():
                _rms_to_xT(kb, h, "h", nrm_sb, 0, xnT, "xnT", identb, ss, rstd, "a")
            NGU = 4
            wgu = [kb.sb("wgu%d" % i, [128, 16, 256], BF16) for i in range(NGU)]
            wd = [kb.sb("wd%d" % i, [128, 11, 512], BF16) for i in range(2)]
            actT = kb.sb("actT", [128, 11, TL], BF16)
            sgt = [kb.sb("sgt%d" % i, [128, 512], F32) for i in range(2)]
            psg = [kb.ps("psg%d" % i, [128, 512], F32) for i in range(2)]
            psu = [kb.ps("psu%d" % i, [128, 512], F32) for i in range(2)]
            psd = [kb.ps("psd%d" % i, [128, 512], F32) for i in range(3)]
            gview = w_gate.rearrange("(k p) n -> p k n", p=128)
            uview = w_up.rearrange("(k p) n -> p k n", p=128)
            dview = w_down.rearrange("(k p) n -> p k n", p=128)
            xkeys = [("xnT", kt, t) for kt in range(16) for t in range(8)]
            nf = 0
            nh = 0
            nd = 0
            ndw = 0
            for g in range(4):
                for fi in range(11):
                    f = g * 11 + fi
                    sl = nf % NGU
                    wt = wgu[sl]
                    kb.dma("pool", wt[:, :, 0:128], gview[:, :, f * 128:(f + 1) * 128], writes=[("wgu", sl)])
                    kb.dma("pool", wt[:, :, 128:256], uview[:, :, f * 128:(f + 1) * 128], writes=[("wgu", sl)])
                    nf += 1
                    for half in range(2):
                        pg = psg[nh % 2]
                        pu = psu[nh % 2]
                        st = sgt[nh % 2]
                        hk = [("xnT", kt, t) for kt in range(16) for t in range(half * 4, half * 4 + 4)]
                        for k in range(16):
                            kb.op("pe", lambda e, pg=pg, wt=wt, k=k, half=half: e.matmul(
                                pg[:], lhsT=wt[:, k, 0:128], rhs=xnT[:, k, half * 512:(half + 1) * 512],
                                start=(k == 0), stop=(k == 15)),
                                reads=[("wgu", sl)] + (hk if k == 0 else []), writes=[("psg", nh % 2)])
                        for k in range(16):
                            kb.op("pe", lambda e, pu=pu, wt=wt, k=k, half=half: e.matmul(
                                pu[:], lhsT=wt[:, k, 128:256], rhs=xnT[:, k, half * 512:(half + 1) * 512],
                                start=(k == 0), stop=(k == 15)),
                                reads=[("wgu", sl)], writes=[("psu", nh % 2)])
                        kb.op("act", lambda e, pg=pg, st=st: e.activation(out=st[:], in_=pg[:], func=AF.Silu),
                              reads=[("psg", nh % 2)], writes=[("sgt", nh % 2), ("psg", nh % 2)])
                        kb.op("dve", lambda e, pu=pu, st=st, fi=fi, half=half: e.tensor_tensor(
                            out=actT[:, fi, half * 512:(half + 1) * 512], in0=st[:], in1=pu[:], op=ALU.mult),
                            reads=[("sgt", nh % 2), ("psu", nh % 2)], writes=[("actT", fi, half), ("psu", nh % 2)])
                        nh += 1
                for c in range(4):
                    wdt = wd[ndw % 2]
                    kb.dma("pool", wdt[:], dview[:, g * 11:(g + 1) * 11, c * 512:(c + 1) * 512], writes=[("wd", ndw % 2)])
                    for t in range(8):
                        pd = psd[nd % 3]
                        for fi in range(11):
                            kb.op("pe", lambda e, pd=pd, wdt=wdt, fi=fi, t=t: e.matmul(
                                pd[:], lhsT=actT[:, fi, t * 128:(t + 1) * 128], rhs=wdt[:, fi, :],
                                start=(fi == 0), stop=(fi == 10)),
                                reads=[("wd", ndw % 2), ("actT", fi, t // 4)], writes=[("psd", nd % 3)])
                        kb.op("dve", lambda e, pd=pd, c=c, t=t: e.tensor_tensor(
                            out=h[:, t, c * 512:(c + 1) * 512], in0=h[:, t, c * 512:(c + 1) * 512], in1=pd[:], op=ALU.add),
                            reads=[("psd", nd % 3), ("h", t)], writes=[("h", t), ("psd", nd % 3)])
                        nd += 1
                    ndw += 1

    if last:
        with kb.phase():
            fw = kb.sb("fw", [128, D], F32)
            junk = kb.sb("junkf", [128, D], BF16)
            yt = [kb.sb("yt%d" % i, [128, D], F32) for i in range(2)]
            kb.dma("sp", fw[:], fnw.partition_broadcast(128), writes=["fw"])
            for t in range(8):
                kb.op("act", lambda e, t=t: e.activation(out=junk[:], in_=h[:, t, :], func=AF.Square,
                                                         scale=1.0 / math.sqrt(D), accum_out=ss[:, t:t + 1]),
                      reads=[("h", t)], writes=["junkf", ("ssf", t)])
                _rsqrt(kb, rstd[:, t:t + 1], ss[:, t:t + 1], ("ssf", t), ("rstdf", t))
                y = yt[t % 2]
                kb.op("dve", lambda e, t=t, y=y: e.scalar_tensor_tensor(out=y[:], in0=h[:, t, :], scalar=rstd[:, t:t + 1],
                                                                        in1=fw[:], op0=ALU.mult, op1=ALU.mult),
                      reads=[("h", t), ("rstdf", t), "fw"], writes=[("yt", t % 2)])
                kb.dma("sp", y_out[t * 128:(t + 1) * 128, :], y[:], reads=[("yt", t % 2)], is_out=True)
    else:
        with kb.phase():
            xnT2 = kb.sb("xnT2", [128, 16, TL], BF16)
            _rms_to_xT(kb, h, "h", nrm_sb, 16, xnT2, "xnT2", identb, ss, rstd, "b")
            xkeys = [("xnT2", kt, t) for kt in range(16) for t in range(8)]
            xo = xnT_out.rearrange("(k p) t -> p k t", p=128)
            for g in range(4):
                kb.dma("sp", xo[:, g * 4:(g + 1) * 4, :], xnT2[:, g * 4:(g + 1) * 4, :],
                       reads=[("xnT2", kt, t) for kt in range(g * 4, g * 4 + 4) for t in range(8)],
                       semkey=("xo", g), is_out=True)
            if not first:
                for t in range(8):
                    kb.dma("sp", h_out[t * 128:(t + 1) * 128, :], h[:, t, :], reads=[("h", t)], semkey=("ho", t), is_out=True)
    return kb.done()


class _Rec:
    def __init__(self, kb):
        self.kb = kb
        self.lst = []

    def op(self, *a, **k):
        self.lst.append(lambda: self.kb.op(*a, **k))

    def dma(self, *a, **k):
        self.lst.append(lambda: self.kb.dma(*a, **k))

    def add(self, fn):
        self.lst.append(fn)


def _interleave(lists):
    idx = [0] * len(lists)
    left = sum(len(l) for l in lists)
    while left:
        for i, l in enumerate(lists):
            if idx[i] < len(l):
                l[idx[i]]()
                idx[i] += 1
                left -= 1


def _noop(ps):
    return None


def _load_xc(kb, xg, xc, c, slot):
    r, off = c // 2, (c % 2) * 512
    src = xg[r].rearrange("(k p) t -> p k t", p=128)
    kb.dma("sp", xc[:], src[:, :, off:off + 512], writes=[("xc", slot)])


def _proj_fm(kb, ps, pskey, w_sb, col0, xc, xslot, evac):
    for k in range(16):
        kb.op("pe", lambda e, k=k: e.matmul(ps[:], lhsT=w_sb[:, k, col0:col0 + 128], rhs=xc[:, k, :],
                                            start=(k == 0), stop=(k == 15)),
              reads=["w_sb", ("xc", xslot)], writes=[pskey])
    evac(ps)


def _proj_tm(kb, ps, pskey, w_sb, col0, ncol, xc, xslot, s, evac):
    for k in range(16):
        kb.op("pe", lambda e, k=k: e.matmul(ps[:, 0:ncol], lhsT=xc[:, k, s * 128:(s + 1) * 128],
                                            rhs=w_sb[:, k, col0:col0 + ncol], start=(k == 0), stop=(k == 15)),
              reads=["w_sb", ("xc", xslot)], writes=[pskey])
    evac(ps)


def _diag_masks(kb, name):
    ms = []
    for r in range(4):
        m = kb.sb("%s%d" % (name, r), [128, 512], BF16)
        key = (name, r)
        kb.op("pool", lambda e, m=m: e.memset(m[:], 1.0), writes=[key])
        for half in range(2):
            base = -(128 * r + 64 * half)
            kb.op("pool", lambda e, m=m, half=half, base=base: e.affine_select(
                out=m[half * 64:(half + 1) * 64, :], in_=m[half * 64:(half + 1) * 64, :], pattern=[[1, 512]],
                compare_op=ALU.is_ge, fill=0.0, base=base, channel_multiplier=0), reads=[key], writes=[key])
        ms.append(m)
    return ms


def _causal_masks(kb, name):
    ms = []
    for r in range(4):
        m = kb.sb("%s%d" % (name, r), [128, 512], BF16)
        key = (name, r)
        kb.op("pool", lambda e, m=m: e.memset(m[:], 1.0), writes=[key])
        kb.op("pool", lambda e, m=m, r=r: e.affine_select(
            out=m[:], in_=m[:], pattern=[[1, 512]], compare_op=ALU.is_ge, fill=0.0, base=-128 * r,
            channel_multiplier=-1), reads=[key], writes=[key])
        ms.append(m)
    return ms


def _attn_pipe(kb, tag, maps, Vaug, vreads, dv, masks, mkey, scale, finish):
    NPS, NPT = 3, 4
    pS = [kb.ps("pS%s%d" % (tag, i), [128, 512], F32) for i in range(NPS)]
    psO = [kb.ps("pO%s%d" % (tag, i), [128, 512], F32) for i in range(4)]
    PT = [kb.sb("PT%s%d" % (tag, i), [128, 512], BF16) for i in range(NPT)]
    items = [(qc, m, k) for qc in range(16) for m in range(len(maps)) for k in range(qc * 4 + 4)]

    def rec_score(i):
        qc, m, kb_i = items[i]
        mp = maps[m]
        qT, kT = mp["qT"], mp["kT"]
        r = kb_i - qc * 4
        q0 = max(r, 0) * 128
        ps = pS[i % NPS]
        pt = PT[i % NPT]
        psk = ("pS" + tag, i % NPS)
        ptk = ("PT" + tag, i % NPT)
        aux = mp.get("aux")
        kb.op("pe", lambda e: e.matmul(ps[:, q0:512], lhsT=kT[:, kb_i * 128:(kb_i + 1) * 128],
                                       rhs=qT[:, qc * 512 + q0:(qc + 1) * 512], start=True, stop=(aux is None)),
              reads=[mp["qkey"], mp["kkey"]], writes=[psk])
        if aux is not None:
            ka, qa = aux
            kb.op("pe", lambda e: e.matmul(ps[:, q0:512], lhsT=ka[:, kb_i * 128:(kb_i + 1) * 128],
                                           rhs=qa[:, qc * 512 + q0:(qc + 1) * 512], start=False, stop=True),
                  reads=[("ka", r) for r in range(6)] + [("qa", r) for r in range(6)], writes=[psk])
        kb.op("act", lambda e: e.activation(out=pt[:, q0:512], in_=ps[:, q0:512], func=AF.Exp, scale=scale),
              reads=[psk], writes=[ptk, psk])
        if r >= 0:
            kb.op("pool", lambda e: e.tensor_tensor(out=pt[:, q0:512], in0=pt[:, q0:512], in1=masks[r][:, q0:512],
                                                    op=ALU.mult), reads=[ptk, (mkey, r)], writes=[ptk])

    def rec_pv(i):
        qc, m, kb_i = items[i]
        r = kb_i - qc * 4
        pt = PT[i % NPT]
        ptk = ("PT" + tag, i % NPT)
        for s in range(4):
            if r >= 0 and s < r:
                continue
            last = qc * 4 + s
            kb.op("pe", lambda e, s=s, last=last: e.matmul(
                psO[s][:, 0:dv + 1], lhsT=pt[:, s * 128:(s + 1) * 128], rhs=Vaug[:, kb_i, 0:dv + 1],
                start=(kb_i == 0), stop=(kb_i == last)), reads=[ptk] + list(vreads), writes=[("pO" + tag, s)])

    n = len(items)
    rec_score(0)
    for i in range(n):
        if i + 1 < n:
            rec_score(i + 1)
        rec_pv(i)
        qc, m, kb_i = items[i]
        if kb_i == qc * 4 + 3:
            finish(qc, m, psO)


def build_A():
    kb = KB()
    xg = kb.dram("xg", [8, D, TL], BF16, "ExternalInput")
    w_loc = kb.dram("w_loc", [D, 768], F32, "ExternalInput")
    lamv = kb.dram("lamv", [4, 128], F32, "ExternalInput")
    subw = kb.dram("subw", [1, 256], F32, "ExternalInput")
    cst = kb.dram("cst", [128, 2], F32, "ExternalInput")
    oT_out = kb.dram("oT_out", [8, 256, TL], BF16, "ExternalOutput")

    qT = [kb.sb("qT%d" % m, [128, S], BF16) for m in range(2)]
    kT = [kb.sb("kT%d" % m, [128, S], BF16) for m in range(2)]
    Vaug = kb.sb("Vaug", [128, 64, 264], BF16)
    identb = _ident(kb, "identb", BF16)
    cst_sb = kb.sb("cst_sb", [128, 2], F32)
    sw = kb.sb("sw", [128, 256], F32)
    lam4 = kb.sb("lam4", [128, 4, 128], F32)
    lsc = kb.sb("lsc", [128, 8], F32)
    kb.dma("sp", cst_sb[:], cst, writes=["cst_sb"])
    kb.dma("sp", sw[:], subw.partition_broadcast(128), writes=["sw"])
    for i in range(4):
        kb.dma("sp", lam4[:, i, :], lamv[i:i + 1, :].partition_broadcast(128), writes=["lam4"], semkey=("lam4", i))
    kb.op("pool", lambda e: e.memset(Vaug[:, :, 256:257], 1.0), writes=["Vones"])
    prod = kb.sb("lprod", [128, 2, 128], F32)
    kb.op("dve", lambda e: e.tensor_tensor(out=prod[:, 0, :], in0=lam4[:, 0, :], in1=lam4[:, 1, :], op=ALU.mult),
          reads=["lam4"], writes=["lprod"])
    kb.op("dve", lambda e: e.tensor_tensor(out=prod[:, 1, :], in0=lam4[:, 2, :], in1=lam4[:, 3, :], op=ALU.mult),
          reads=["lam4"], writes=["lprod"])
    kb.op("dve", lambda e: e.reduce_sum(out=lsc[:, 0:2], in_=prod[:], axis=AX.X), reads=["lprod"], writes=["lsc"])
    kb.op("act", lambda e: e.activation(out=lsc[:, 2:4], in_=lsc[:, 0:2], func=AF.Exp), reads=["lsc"], writes=["lsc"])
    kb.op("dve", lambda e: e.tensor_tensor(out=lsc[:, 4:5], in0=lsc[:, 3:4], in1=lsc[:, 2:3], op=ALU.subtract),
          reads=["lsc"], writes=["lsc"])
    kb.op("dve", lambda e: e.tensor_tensor(out=lsc[:, 4:5], in0=lsc[:, 4:5], in1=cst_sb[:, 0:1], op=ALU.subtract),
          reads=["lsc", "cst_sb"], writes=["lsc"])
    kb.op("dve", lambda e: e.tensor_scalar(out=sw[:], in0=sw[:], scalar1=cst_sb[:, 1:2], scalar2=None, op0=ALU.mult),
          reads=["sw", "cst_sb"], writes=["sw"])

    with kb.phase():
        w_sb = kb.sb("w_sb", [128, 16, 768], BF16)
        kb.dma("pool", w_sb[:], w_loc.rearrange("(k p) n -> p k n", p=128), writes=["w_sb"])
        xcs = [kb.sb("xc%d" % i, [128, 16, 512], BF16) for i in range(2)]
        pp = [kb.ps("pp%d" % i, [128, 512], F32) for i in range(4)]
        n = 0
        dests = [qT[0], qT[1], kT[0], kT[1]]
        dkeys = ["qT0", "qT1", "kT0", "kT1"]
        for c in range(16):
            xc = xcs[c % 2]
            _load_xc(kb, xg, xc, c, c % 2)
            for j in range(4):
                ps = pp[n % 4]
                eng = "act" if n % 2 == 0 else "dve"

                def evac(ps, j=j, c=c, eng=eng, n=n):
                    dst = dests[j][:, c * 512:(c + 1) * 512]
                    if eng == "act":
                        kb.op("act", lambda e: e.copy(out=dst, in_=ps[:]), reads=[("pp", n % 4)],
                              writes=[dkeys[j], ("pp", n % 4)])
                    else:
                        kb.op("dve", lambda e: e.tensor_copy(out=dst, in_=ps[:]), reads=[("pp", n % 4)],
                              writes=[dkeys[j], ("pp", n % 4)])
                _proj_fm(kb, ps, ("pp", n % 4), w_sb, j * 128, xc, c % 2, evac)
                n += 1
            for s in range(4):
                ps = pp[n % 4]
                eng = "act" if n % 2 == 0 else "dve"

                def evac(ps, s=s, c=c, eng=eng, n=n):
                    dst = Vaug[:, c * 4 + s, 0:256]
                    if eng == "act":
                        kb.op("act", lambda e: e.copy(out=dst, in_=ps[:, 0:256]), reads=[("pp", n % 4)],
                              writes=["Vaug", ("pp", n % 4)])
                    else:
                        kb.op("dve", lambda e: e.tensor_copy(out=dst, in_=ps[:, 0:256]), reads=[("pp", n % 4)],
                              writes=["Vaug", ("pp", n % 4)])
                _proj_tm(kb, ps, ("pp", n % 4), w_sb, 512, 256, xc, c % 2, s, evac)
                n += 1

    with kb.phase():
        masks = _diag_masks(kb, "dmask")
        acc = [kb.sb("acc%d" % s, [128, 256], F32) for s in range(4)]
        rc = kb.sb("rc", [128, 8], F32)
        st = kb.sb("st", [128, 8], F32)
        junk = kb.sb("junkA", [128, 256], F32)
        ob = [kb.sb("ob%d" % s, [128, 256], BF16) for s in range(2)]
        oTt = [kb.sb("oTt%d" % i, [128, 2, 512], BF16) for i in range(2)]
        pTr = kb.ps("pTr", [128, 1024], BF16)
        scale = 128 ** -0.5

        def finish(qc, m, psO):
            for s in range(4):
                pk = ("pOA", s)
                kb.op("dve", lambda e, s=s: e.reciprocal(out=rc[:, s:s + 1], in_=psO[s][:, 256:257]),
                      reads=[pk], writes=[("rc", s), pk])
                if m == 0:
                    kb.op("dve", lambda e, s=s: e.tensor_scalar(out=acc[s][:], in0=psO[s][:, 0:256],
                                                                 scalar1=rc[:, s:s + 1], scalar2=None, op0=ALU.mult),
                          reads=[pk, ("rc", s)], writes=[("acc", s), pk])
                else:
                    kb.op("dve", lambda e, s=s: e.tensor_tensor(out=rc[:, s:s + 1], in0=rc[:, s:s + 1], in1=lsc[:, 4:5],
                                                                 op=ALU.mult),
                          reads=[("rc", s), "lsc"], writes=[("rc", s)])
                    kb.op("dve", lambda e, s=s: e.scalar_tensor_tensor(out=acc[s][:], in0=psO[s][:, 0:256],
                                                                        scalar=rc[:, s:s + 1], in1=acc[s][:],
                                                                        op0=ALU.mult, op1=ALU.add),
                          reads=[pk, ("rc", s), ("acc", s)], writes=[("acc", s), pk])
            if m == 0:
                return
            ot = oTt[qc % 2]
            for s in range(4):
                kb.op("act", lambda e, s=s: e.activation(out=junk[:], in_=acc[s][:], func=AF.Square, scale=1.0 / 16.0,
                                                         accum_out=st[:, s:s + 1]),
                      reads=[("acc", s)], writes=["junkA", ("st", s)])
                _rsqrt(kb, st[:, s:s + 1], st[:, s:s + 1], ("st", s), ("st", s))
                o = ob[s % 2]
                kb.op("dve", lambda e, s=s, o=o: e.scalar_tensor_tensor(out=o[:], in0=acc[s][:], scalar=st[:, s:s + 1],
                                                                         in1=sw[:], op0=ALU.mult, op1=ALU.mult),
                      reads=[("acc", s), ("st", s), "sw"], writes=[("ob", s % 2)])
                for f in range(2):
                    kb.op("pe", lambda e, s=s, f=f, o=o: e.transpose(pTr[:, (s * 2 + f) * 128:(s * 2 + f + 1) * 128],
                                                                     o[:, f * 128:(f + 1) * 128], identb[:]),
                          reads=[("ob", s % 2), "identb"], writes=["pTr"])
                kb.op("dve", lambda e, s=s, ot=ot: e.tensor_copy(
                    out=ot[:, :, s * 128:(s + 1) * 128],
                    in_=pTr[:, s * 256:(s + 1) * 256].rearrange("p (f t) -> p f t", f=2)),
                    reads=["pTr"], writes=[("oTt", qc % 2), "pTr"])
            j, off = qc // 2, (qc % 2) * 512
            kb.dma("sp", oT_out[j].rearrange("(f p) t -> p f t", p=128)[:, :, off:off + 512], ot[:],
                   reads=[("oTt", qc % 2)], is_out=True)

        maps = [dict(qT=qT[m], kT=kT[m], qkey="qT%d" % m, kkey="kT%d" % m) for m in range(2)]
        _attn_pipe(kb, "A", maps, Vaug, ["Vaug", "Vones"], 256, masks, "dmask", scale, finish)
    return kb.done()


def build_B():
    kb = KB()
    xg = kb.dram("xg", [8, D, TL], BF16, "ExternalInput")
    w_loc = kb.dram("w_loc", [2, D, 514], F32, "ExternalInput")
    qkw = kb.dram("qkw", [128, 2], F32, "ExternalInput")
    fb = kb.dram("fb", [1, 2], F32, "ExternalInput")
    oT_out = kb.dram("oT_out", [8, 256, TL], BF16, "ExternalOutput")

    identb = _ident(kb, "identb", BF16)
    ones_f = kb.sb("ones_f", [128, 128], F32)
    kb.op("pool", lambda e: e.memset(ones_f[:], 1.0), writes=["ones_f"])
    qkw_sb = kb.sb("qkw_sb", [128, 2], F32)
    kb.dma("sp", qkw_sb[:], qkw, writes=["qkw_sb"])
    kb.op("dve", lambda e: e.tensor_scalar(out=qkw_sb[:, 0:1], in0=qkw_sb[:, 0:1], scalar1=128 ** -0.5, scalar2=None,
                                           op0=ALU.mult), reads=["qkw_sb"], writes=["qkw_sb"])
    nfb = kb.sb("nfb", [1, 2], F32)
    kb.dma("sp", nfb[:], fb, writes=["nfb"])
    kb.op("dve", lambda e: e.tensor_scalar(out=nfb[:], in0=nfb[:], scalar1=-1.0, scalar2=None, op0=ALU.mult),
          reads=["nfb"], writes=["nfb"])
    ones_r = kb.sb("ones_r", [1, 512], F32)
    kb.op("pool", lambda e: e.memset(ones_r[:], 1.0), writes=["ones_r"])

    for hl in range(2):
        with kb.phase():
            qT = kb.sb("qT", [128, S], BF16)
            kT = kb.sb("kT", [128, S], BF16)
            Vaug = kb.sb("Vaug", [128, 64, 136], BF16)
            G = kb.sb("G", [128, 64, 128], BF16)
            ka = kb.sb("ka", [6, S], BF16)
            qa = kb.sb("qa", [6, S], BF16)
            kb.op("pool", lambda e: e.memset(Vaug[:, :, 128:129], 1.0), writes=["Vones"])
            kb.op("pool", lambda e: e.memset(ka[:], 1.0), writes=[("ka", r) for r in range(6)])
            kb.op("pool", lambda e: e.memset(qa[:], 1.0), writes=[("qa", r) for r in range(6)])
            with kb.phase():
                w_sb = kb.sb("w_sb", [128, 16, 514], BF16)
                kb.dma("pool", w_sb[:], w_loc[hl].rearrange("(k p) n -> p k n", p=128), writes=["w_sb"])
                xcs = [kb.sb("xc%d" % i, [128, 16, 512], BF16) for i in range(2)]
                pp = [kb.ps("pp%d" % i, [128, 512], F32) for i in range(4)]
                pss = [kb.ps("pss%d" % i, [128, 512], F32) for i in range(2)]
                psf = kb.ps("psf", [128, 512], F32)
                raw = [kb.sb("raw%d" % i, [128, 512], F32) for i in range(2)]
                sq = [kb.sb("sq%d" % i, [128, 512], F32) for i in range(2)]
                rs = [kb.sb("rs%d" % i, [128, 512], F32) for i in range(2)]
                e_t = kb.sb("e_t", [1, 512], F32)
                l_t = kb.sb("l_t", [1, 512], F32)
                cum = [kb.sb("cum%d" % i, [1, 512], F32) for i in range(2)]
                r1 = kb.sb("r1", [1, 512], F32)
                hml = kb.sb("hml", [1, 3, 512], BF16)
                nhml = kb.sb("nhml", [1, 3, 512], BF16)

                def do_chunk(c):
                    xc = xcs[c % 2]
                    xs = c % 2
                    _load_xc(kb, xg, xc, c, xs)

                    def qk_task(j):
                        R = _Rec(kb)
                        ps, pk = pp[j], ("pp", j)
                        rw, sqq, rss, psj = raw[j], sq[j], rs[j], pss[j]
                        R.add(lambda: _proj_fm(kb, ps, pk, w_sb, j * 128, xc, xs, _noop))
                        R.op("act", lambda e: e.copy(out=rw[:], in_=ps[:]), reads=[pk], writes=[("raw", j), pk])
                        R.op("dve", lambda e: e.tensor_tensor(out=sqq[:], in0=rw[:], in1=rw[:], op=ALU.mult),
                             reads=[("raw", j)], writes=[("sq", j)])
                        R.op("pe", lambda e: e.matmul(psj[:], lhsT=ones_f[:], rhs=sqq[:], start=True, stop=True),
                             reads=[("sq", j), "ones_f"], writes=[("pss", j)])
                        R.op("act", lambda e: e.activation(out=rss[:], in_=psj[:], func=AF.Ln, scale=1.0 / 128.0, bias=EPS),
                             reads=[("pss", j)], writes=[("rs", j), ("pss", j)])
                        R.op("act", lambda e: e.activation(out=rss[:], in_=rss[:], func=AF.Exp, scale=-0.5),
                             reads=[("rs", j)], writes=[("rs", j)])
                        dst = (qT if j == 0 else kT)[:, c * 512:(c + 1) * 512]
                        R.op("dve", lambda e: e.scalar_tensor_tensor(out=dst, in0=rw[:], scalar=qkw_sb[:, j:j + 1],
                                                                     in1=rss[:], op0=ALU.mult, op1=ALU.mult),
                             reads=[("raw", j), ("rs", j), "qkw_sb"], writes=["qT" if j == 0 else "kT"])
                        return R.lst

                    def vg_task(s, bank):
                        R = _Rec(kb)
                        ps, pk = pp[bank], ("pp", bank)
                        R.add(lambda: _proj_tm(kb, ps, pk, w_sb, 256, 256, xc, xs, s, _noop))
                        R.op("dve", lambda e: e.tensor_copy(out=Vaug[:, c * 4 + s, 0:128], in_=ps[:, 0:128]),
                             reads=[pk], writes=["Vaug", pk])
                        R.op("act", lambda e: e.activation(out=G[:, c * 4 + s, :], in_=ps[:, 128:256], func=AF.Sigmoid),
                             reads=[pk], writes=["G", pk])
                        return R.lst

                    def f_task():
                        R = _Rec(kb)

                        def mm():
                            for k in range(16):
                                kb.op("pe", lambda e, k=k: e.matmul(psf[0:1, :], lhsT=w_sb[:, k, 512:513], rhs=xc[:, k, :],
                                                                    start=(k == 0), stop=(k == 15)),
                                      reads=["w_sb", ("xc", xs)], writes=["psf"])
                        R.add(mm)
                        R.op("act", lambda e: e.activation(out=e_t[:], in_=psf[0:1, :], func=AF.Exp, scale=-1.0,
                                                           bias=nfb[0:1, hl:hl + 1]),
                             reads=["psf", "nfb"], writes=["e_t", "psf"])
                        R.op("act", lambda e: e.activation(out=l_t[:], in_=e_t[:], func=AF.Ln, bias=1.0),
                             reads=["e_t"], writes=["l_t"])
                        cu = cum[c % 2]
                        prev = cum[(c + 1) % 2]
                        init = 0.0 if c == 0 else prev[:, 511:512]
                        R.op("dve", lambda e: e.tensor_tensor_scan(out=cu[:], data0=ones_r[:], data1=l_t[:], initial=init,
                                                                   op0=ALU.mult, op1=ALU.subtract),
                             reads=["l_t", "ones_r", ("cum", (c + 1) % 2)], writes=[("cum", c % 2)])
                        ck = ("cum", c % 2)
                        R.op("dve", lambda e: e.tensor_copy(out=hml[:, 0, :], in_=cu[:]), reads=[ck], writes=["hml"])
                        R.op("dve", lambda e: e.tensor_tensor(out=r1[:], in0=cu[:], in1=hml[:, 0, :], op=ALU.subtract),
                             reads=[ck, "hml"], writes=["r1"])
                        R.op("dve", lambda e: e.tensor_copy(out=hml[:, 1, :], in_=r1[:]), reads=["r1"], writes=["hml"])
                        R.op("dve", lambda e: e.tensor_tensor(out=r1[:], in0=r1[:], in1=hml[:, 1, :], op=ALU.subtract),
                             reads=["r1", "hml"], writes=["r1"])
                        R.op("dve", lambda e: e.tensor_copy(out=hml[:, 2, :], in_=r1[:]), reads=["r1"], writes=["hml"])
                        R.op("dve", lambda e: e.tensor_scalar(out=nhml[:], in0=hml[:], scalar1=-1.0, scalar2=None, op0=ALU.mult),
                             reads=["hml"], writes=["nhml"])
                        for a in range(3):
                            R.dma("sp", qa[a:a + 1, c * 512:(c + 1) * 512], hml[0:1, a, :], reads=["hml"], writes=[("qa", a)],
                                  semkey=("auxq", a))
                            R.dma("sp", ka[3 + a:4 + a, c * 512:(c + 1) * 512], nhml[0:1, a, :], reads=["nhml"],
                                  writes=[("ka", 3 + a)], semkey=("auxk", a))
                        return R.lst

                    _interleave([qk_task(0), qk_task(1), vg_task(0, 2), vg_task(1, 3)])
                    _interleave([vg_task(2, 0), vg_task(3, 1), f_task()])
                for c in range(16):
                    do_chunk(c)
            with kb.phase():
                masks = _causal_masks(kb, "cmask")
                rc = kb.sb("rc", [128, 4], F32)
                ob = [kb.sb("ob%d" % i, [128, 128], BF16) for i in range(2)]
                oTt = [kb.sb("oTt%d" % i, [128, 512], BF16) for i in range(2)]
                pTr = kb.ps("pTr", [128, 1024], BF16)
                tag = "B"

                def finish(qc, m, psO):
                    ot = oTt[qc % 2]
                    for s in range(4):
                        pk = ("pO" + tag, s)
                        kb.op("dve", lambda e, s=s: e.reciprocal(out=rc[:, s:s + 1], in_=psO[s][:, 128:129]),
                              reads=[pk], writes=[("rc", s), pk])
                        o = ob[s % 2]
                        kb.op("dve", lambda e, s=s, o=o: e.scalar_tensor_tensor(
                            out=o[:], in0=psO[s][:, 0:128], scalar=rc[:, s:s + 1], in1=G[:, qc * 4 + s, :],
                            op0=ALU.mult, op1=ALU.mult), reads=[pk, ("rc", s), "G"], writes=[("ob", s % 2), pk])
                        kb.op("pe", lambda e, s=s, o=o: e.transpose(pTr[:, s * 128:(s + 1) * 128], o[:], identb[:]),
                              reads=[("ob", s % 2), "identb"], writes=["pTr"])
                    kb.op("act", lambda e, ot=ot: e.copy(out=ot[:], in_=pTr[:, 0:512]), reads=["pTr"],
                          writes=[("oTt", qc % 2), "pTr"])
                    j, off = qc // 2, (qc % 2) * 512
                    kb.dma("sp", oT_out[j, hl * 128:(hl + 1) * 128, off:off + 512], ot[:], reads=[("oTt", qc % 2)],
                           is_out=True)
                _attn_pipe(kb, tag, [dict(qT=qT, kT=kT, qkey="qT", kkey="kT", aux=(ka, qa))], Vaug, ["Vaug", "Vones"],
                           128, masks, "cmask", 1.0, finish)
    return kb.done()


def _affine_mat(kb, name, dt, pattern, base, cm, op):
    t = kb.sb(name, [128, 128], dt)
    kb.op("pool", lambda e: e.memset(t[:], 1.0), writes=[name])
    kb.op("pool", lambda e: e.affine_select(out=t[:], in_=t[:], pattern=pattern, compare_op=op, fill=0.0, base=base,
                                            channel_multiplier=cm), reads=[name], writes=[name])
    return t


def _b1(ap):
    return ap.unsqueeze(1).broadcast_to([128, 2, 128])


def _b2(ap):
    return ap.unsqueeze(2).broadcast_to([128, 2, 128])


def _h2(ap):
    return ap.rearrange("p (h d) -> p h d", h=2)


def build_C2():
    kb = KB()
    xg = kb.dram("xg", [8, D, TL], BF16, "ExternalInput")
    w_loc = kb.dram("w_loc", [2, D, 772], F32, "ExternalInput")
    cw = kb.dram("cw", [2, 128, 16], F32, "ExternalInput")
    hp = kb.dram("hp", [2, 4], F32, "ExternalInput")
    onw = kb.dram("onw", [1, 128], F32, "ExternalInput")
    oT_out = kb.dram("oT_out", [8, 512, TL], BF16, "ExternalOutput")

    identb = _ident(kb, "identb", BF16)
    identf = _ident(kb, "identf", F32)
    ones_f = kb.sb("ones_f", [128, 128], F32)
    kb.op("pool", lambda e: e.memset(ones_f[:], 1.0), writes=["ones_f"])
    triu = _affine_mat(kb, "triu", F32, [[1, 128]], 0, -1, ALU.is_ge)
    sel = _affine_mat(kb, "sel", F32, [[0, 128]], -127, 1, ALU.is_equal)
    lowm = _affine_mat(kb, "lowm", F32, [[-1, 128]], 0, 1, ALU.is_ge)
    strm = _affine_mat(kb, "strm", F32, [[-1, 128]], -1, 1, ALU.is_ge)
    wn = kb.sb("wn", [128, 128], F32)
    kb.dma("sp", wn[:], onw.partition_broadcast(128), writes=["wn"])
    bdm = kb.sb("bdm", [128, 128], F32)
    offm = kb.sb("offm", [128, 128], F32)
    kb.op("pool", lambda e: e.memset(bdm[:], 0.0), writes=["bdm"])
    kb.op("pool", lambda e: e.memset(offm[:], 1.0), writes=["offm"])
    for b in range(4):
        kb.op("pool", lambda e, b=b: e.memset(bdm[32 * b:32 * b + 32, 32 * b:32 * b + 32], 1.0), writes=["bdm"])
        kb.op("pool", lambda e, b=b: e.memset(offm[32 * b:32 * b + 32, 32 * b:32 * b + 32], 0.0), writes=["offm"])

    for pr in range(2):
        with kb.phase():
            qT = kb.sb("qT", [128, S], BF16)
            kT = kb.sb("kT", [128, S], BF16)
            ktm = kb.sb("ktm", [128, 64, 128], BF16)
            vb = kb.sb("vb", [128, 64, 2, 128], BF16)
            Zs = kb.sb("Zs", [128, 64, 2, 128], BF16)
            beta = kb.sb("beta", [128, 64, 2], F32)
            g = kb.sb("g", [128, 64, 2], F32)
            cw_sb = kb.sb("cw_sb", [128, 16], F32)
            hp_sb = kb.sb("hp_sb", [128, 4], F32)
            kb.dma("sp", cw_sb[:], cw[pr], writes=["cw_sb"])
            kb.dma("sp", hp_sb[:], hp[pr:pr + 1, :].partition_broadcast(128), writes=["hp_sb"])
            kb.op("act", lambda e: e.activation(out=hp_sb[:, 0:2], in_=hp_sb[:, 0:2], func=AF.Exp), reads=["hp_sb"],
                  writes=["hp_sb"])
            kb.op("dve", lambda e: e.tensor_scalar(out=hp_sb[:, 0:2], in0=hp_sb[:, 0:2], scalar1=-1.0, scalar2=None,
                                                   op0=ALU.mult), reads=["hp_sb"], writes=["hp_sb"])
            with kb.phase():
                w_sb = kb.sb("w_sb", [128, 16, 772], BF16)
                kb.dma("pool", w_sb[:], w_loc[pr].rearrange("(k p) n -> p k n", p=128), writes=["w_sb"])
                xcs = [kb.sb("xc%d" % i, [128, 16, 512], BF16) for i in range(2)]
                pp = [kb.ps("pp%d" % i, [128, 512], F32) for i in range(4)]
                pss = [kb.ps("pss%d" % i, [128, 512], F32) for i in range(2)]
                ptr = kb.ps("ptr", [128, 1024], BF16)
                ptr2 = kb.ps("ptr2", [128, 1024], BF16)
                rawc = [kb.sb("rawc%d" % j, [128, 515], F32) for j in range(4)]
                acc = [kb.sb("cacc%d" % i, [128, 512], F32) for i in range(4)]
                sil = acc
                sq = [kb.sb("sq%d" % i, [128, 512], F32) for i in range(2)]
                rs = [kb.sb("rs%d" % i, [128, 512], F32) for i in range(2)]
                vbf = [kb.sb("vbf%d" % i, [128, 512], BF16) for i in range(2)]
                et = kb.sb("et", [128, 4, 2], F32)
                for j in range(4):
                    kb.op("pool", lambda e, j=j: e.memset(rawc[j][:], 0.0), writes=[("rawc", j)])
                def do_chunk(c):
                    xc = xcs[c % 2]
                    xs = c % 2
                    _load_xc(kb, xg, xc, c, xs)
                    tasks = []
                    for s in range(4):
                        R = _Rec(kb)
                        ps = pp[s]
                        pk = ("pp", s)
                        blk = c * 4 + s
                        R.add(lambda ps=ps, pk=pk, s=s: _proj_tm(kb, ps, pk, w_sb, 512, 260, xc, xs, s, _noop))
                        R.op("act", lambda e, ps=ps, blk=blk: e.activation(out=Zs[:, blk, :, :], in_=_h2(ps[:, 0:256]), func=AF.Silu),
                             reads=[pk], writes=["Zs", pk])
                        R.op("act", lambda e, ps=ps, blk=blk: e.activation(out=beta[:, blk, :], in_=ps[:, 256:258], func=AF.Sigmoid),
                             reads=[pk], writes=[("beta", blk), pk])
                        R.op("dve", lambda e, ps=ps, s=s: e.tensor_tensor(out=et[:, s, :], in0=ps[:, 258:260], in1=hp_sb[:, 2:4],
                                                                          op=ALU.add), reads=[pk, "hp_sb"], writes=[("et", s), pk])
                        R.op("act", lambda e, s=s: e.activation(out=et[:, s, :], in_=et[:, s, :], func=AF.Exp),
                             reads=[("et", s)], writes=[("et", s)])
                        R.op("act", lambda e, s=s: e.activation(out=et[:, s, :], in_=et[:, s, :], func=AF.Ln, bias=1.0),
                             reads=[("et", s)], writes=[("et", s)])
                        R.op("dve", lambda e, s=s, blk=blk: e.tensor_tensor(out=g[:, blk, :], in0=et[:, s, :], in1=hp_sb[:, 0:2],
                                                                            op=ALU.mult), reads=[("et", s), "hp_sb"], writes=["g"])
                        tasks.append(R.lst)
                    _interleave(tasks)
                    tasks = []
                    for j in range(4):
                        R = _Rec(kb)
                        ps = pp[j]
                        pk = ("pp", j)
                        rw, ac, sl = rawc[j], acc[j], sil[j]
                        rk = ("rawc", j)
                        R.add(lambda ps=ps, pk=pk, j=j: _proj_fm(kb, ps, pk, w_sb, j * 128, xc, xs, _noop))
                        if c > 0:
                            R.op("dve", lambda e, rw=rw: e.tensor_copy(out=rw[:, 0:3], in_=rw[:, 512:515]), reads=[rk], writes=[rk])
                        R.op("act", lambda e, rw=rw, ps=ps: e.copy(out=rw[:, 3:515], in_=ps[:]), reads=[pk], writes=[rk, pk])
                        R.op("dve", lambda e, rw=rw, ac=ac, j=j: e.tensor_scalar(out=ac[:], in0=rw[:, 0:512],
                                                                                  scalar1=cw_sb[:, j * 4:j * 4 + 1],
                                                                                  scalar2=None, op0=ALU.mult),
                             reads=[rk, "cw_sb"], writes=[("cacc", j)])
                        for tap in range(1, 4):
                            R.op("dve", lambda e, tap=tap, rw=rw, ac=ac, j=j: e.scalar_tensor_tensor(
                                out=ac[:], in0=rw[:, tap:tap + 512], scalar=cw_sb[:, j * 4 + tap:j * 4 + tap + 1],
                                in1=ac[:], op0=ALU.mult, op1=ALU.add), reads=[rk, "cw_sb", ("cacc", j)], writes=[("cacc", j)])
                        R.op("act", lambda e, ac=ac, sl=sl: e.activation(out=sl[:], in_=ac[:], func=AF.Silu), reads=[("cacc", j)],
                             writes=[("cacc", j)])
                        if j < 2:
                            sqq, rss, psj = sq[j], rs[j], pss[j]
                            R.op("dve", lambda e, sl=sl, sqq=sqq: e.tensor_tensor(out=sqq[:], in0=sl[:], in1=sl[:], op=ALU.mult),
                                 reads=[("cacc", j)], writes=[("sq", j)])
                            R.op("pe", lambda e, sqq=sqq, psj=psj: e.matmul(psj[:], lhsT=ones_f[:], rhs=sqq[:], start=True, stop=True),
                                 reads=[("sq", j), "ones_f"], writes=[("pss", j)])
                            R.op("act", lambda e, rss=rss, psj=psj: e.activation(out=rss[:], in_=psj[:], func=AF.Ln, bias=EPS),
                                 reads=[("pss", j)], writes=[("rs", j), ("pss", j)])
                            R.op("act", lambda e, rss=rss: e.activation(out=rss[:], in_=rss[:], func=AF.Exp, scale=-0.5),
                                 reads=[("rs", j)], writes=[("rs", j)])
                            dst = (qT if j == 0 else kT)[:, c * 512:(c + 1) * 512]
                            sc = 128 ** -0.5 if j == 0 else 1.0
                            dk = ("qTc", c) if j == 0 else ("kTc", c)
                            R.op("dve", lambda e, dst=dst, sl=sl, sc=sc, rss=rss: e.scalar_tensor_tensor(
                                out=dst, in0=sl[:], scalar=sc, in1=rss[:], op0=ALU.mult, op1=ALU.mult),
                                reads=[("cacc", j), ("rs", j)], writes=[dk, "qT" if j == 0 else "kT"])
                            if j == 1:
                                for s in range(4):
                                    R.op("pe", lambda e, s=s: e.transpose(
                                        ptr[:, s * 128:(s + 1) * 128], kT[:, c * 512 + s * 128:c * 512 + (s + 1) * 128],
                                        identb[:]), reads=[dk, "identb"], writes=["ptr"])
                                R.op("act", lambda e: e.copy(out=ktm[:, c * 4:(c + 1) * 4, :],
                                                             in_=ptr[:, 0:512].rearrange("p (s d) -> p s d", s=4)),
                                     reads=["ptr"], writes=["ktm", "ptr"])
                        else:
                            hh = j - 2
                            vf = vbf[hh]
                            pt_, ptk, pc0 = (ptr, "ptr", 512) if hh == 0 else (ptr2, "ptr2", 0)
                            R.op("dve", lambda e, vf=vf, sl=sl: e.tensor_copy(out=vf[:], in_=sl[:]), reads=[("cacc", j)],
                                 writes=[("vbf", hh)])
                            for s in range(4):
                                R.op("pe", lambda e, s=s, vf=vf, pt_=pt_, pc0=pc0: e.transpose(
                                    pt_[:, pc0 + s * 128:pc0 + (s + 1) * 128], vf[:, s * 128:(s + 1) * 128], identb[:]),
                                    reads=[("vbf", hh), "identb"], writes=[ptk])
                            for s in range(4):
                                blk = c * 4 + s
                                R.op("dve", lambda e, s=s, blk=blk, hh=hh, pt_=pt_, pc0=pc0: e.tensor_scalar(
                                    out=vb[:, blk, hh, :], in0=pt_[:, pc0 + s * 128:pc0 + (s + 1) * 128],
                                    scalar1=beta[:, blk, hh:hh + 1], scalar2=None, op0=ALU.mult),
                                    reads=[ptk, ("beta", blk)], writes=["vb", ptk])
                        tasks.append(R.lst)
                    _interleave(tasks)
                for c in range(16):
                    do_chunk(c)
            gc = kb.sb("gc", [128, 64, 2], F32)
            glb = kb.sb("glb", [128, 64, 2], F32)
            egc = kb.sb("egc", [128, 64, 2], F32)
            ekd = kb.sb("ekd", [128, 64, 2], F32)
            egl = kb.sb("egl", [128, 64, 2], F32)
            begc = kb.sb("begc", [128, 64, 2], F32)
            nbeta = kb.sb("nbeta", [128, 64, 2], F32)
            fl = lambda t: t[:].rearrange("p a b -> p (a b)")
            with kb.phase():
                pg = kb.ps("pg", [128, 512], F32)
                kb.op("pe", lambda e: e.matmul(pg[:, 0:128], lhsT=triu[:], rhs=fl(g), start=True, stop=True),
                      reads=["triu", "g"], writes=["pg"])
                kb.op("dve", lambda e: e.tensor_copy(out=fl(gc), in_=pg[:, 0:128]), reads=["pg"], writes=["gc", "pg"])
                kb.op("pe", lambda e: e.matmul(pg[:, 128:256], lhsT=sel[:], rhs=fl(gc), start=True, stop=True),
                      reads=["sel", "gc"], writes=["pg"])
                kb.op("dve", lambda e: e.tensor_copy(out=fl(glb), in_=pg[:, 128:256]), reads=["pg"], writes=["glb", "pg"])
                kb.op("act", lambda e: e.activation(out=fl(egc), in_=fl(gc), func=AF.Exp), reads=["gc"], writes=["egc"])
                kb.op("act", lambda e: e.activation(out=fl(egl), in_=fl(glb), func=AF.Exp), reads=["glb"], writes=["egl"])
                kb.op("dve", lambda e: e.tensor_tensor(out=fl(ekd), in0=fl(glb), in1=fl(gc), op=ALU.subtract),
                      reads=["glb", "gc"], writes=["ekd"])
                kb.op("act", lambda e: e.activation(out=fl(ekd), in_=fl(ekd), func=AF.Exp), reads=["ekd"], writes=["ekd"])
                bkeys = [("beta", b) for b in range(64)]
                kb.op("dve", lambda e: e.tensor_tensor(out=fl(begc), in0=fl(beta), in1=fl(egc), op=ALU.mult),
                      reads=bkeys + ["egc"], writes=["begc"])
                kb.op("dve", lambda e: e.tensor_scalar(out=fl(nbeta), in0=fl(beta), scalar1=-1.0, scalar2=None, op0=ALU.mult),
                      reads=bkeys, writes=["nbeta"])
            with kb.phase():
                bA = kb.ps("bA", [128, 512], F32)
                bB = kb.ps("bB", [128, 512], F32)
                bD = kb.ps("bD", [128, 512], F32)
                bE = kb.ps("bE", [128, 512], F32)
                bF = kb.ps("bF", [128, 512], F32)
                bG = kb.ps("bG", [128, 1024], BF16)
                bH = kb.ps("bH", [128, 512], F32)
                bI = kb.ps("bI", [128, 512], F32)
                T2 = lambda nm, dt: kb.sb(nm, [128, 2, 128], dt)
                St = T2("St", F32)
                Sb = T2("Sb", BF16)
                kb.op("pool", lambda e: e.memset(St[:], 0.0), writes=["St"])
                kb.op("pool", lambda e: e.memset(Sb[:], 0.0), writes=["Sb"])
                dg = T2("dg", F32)
                Dm = T2("Dm", F32)
                Ds = T2("Ds", F32)
                Nf = T2("Nf", F32)
                Noff = T2("Noff", F32)
                M = [T2("M%d" % i, F32) for i in range(2)]
                MT = [T2("MT%d" % i, F32) for i in range(2)]
                X = [T2("X%d" % i, F32) for i in range(2)]
                Bi = T2("Bi", F32)
                Pm = T2("Pm", F32)
                PTm = T2("PTm", F32)
                P2T = T2("P2T", F32)
                Ym = T2("Ym", F32)
                Wm = T2("Wm", F32)
                Xf = T2("Xf", BF16)
                intra = T2("intra", BF16)
                intraT = [T2("intraT%d" % i, BF16) for i in range(2)]
                kbg = T2("kbg", BF16)
                kdec = [T2("kdec%d" % i, BF16) for i in range(2)]
                u = [T2("u%d" % i, F32) for i in range(2)]
                wT = [T2("wT%d" % i, BF16) for i in range(2)]
                vn = T2("vn", BF16)
                tq = T2("tq", F32)
                o = T2("o", F32)
                junk = T2("junkC", F32)
                st = kb.sb("stC", [128, 2], F32)
                on = T2("on", F32)
                ob = T2("obC", BF16)
                oTt = [kb.sb("oTtC%d" % i, [128, 2, 512], BF16) for i in range(2)]
                f2 = lambda t: t[:].rearrange("p h d -> p (h d)")

                def mm2(rec, bank, bkey, c0, lhs, rhs, reads):
                    for hh in range(2):
                        rec("pe", lambda e, hh=hh: e.matmul(bank[:, c0 + hh * 128:c0 + (hh + 1) * 128], lhsT=lhs(hh), rhs=rhs(hh),
                                                            start=True, stop=True), reads=reads, writes=[bkey])

                def gen_prep(n):
                    lst = []

                    def P(*a, **k):
                        lst.append(lambda: kb.op(*a, **k))
                    tok = slice(n * 128, (n + 1) * 128)
                    nb = n % 2
                    P("pe", lambda e: e.matmul(bA[:, 0:128], lhsT=kT[:, tok], rhs=kT[:, tok], start=True, stop=True),
                      reads=["kT"], writes=["bA"])
                    P("pe", lambda e: e.matmul(bA[:, 128:256], lhsT=qT[:, tok], rhs=kT[:, tok], start=True, stop=True),
                      reads=["qT", "kT"], writes=["bA"])
                    P("dve", lambda e: e.tensor_tensor(out=dg[:], in0=_b1(identf[:]), in1=_b2(gc[:, n, :]), op=ALU.mult),
                      reads=["identf", "gc"], writes=["dg"])
                    P("pe", lambda e: e.matmul(bB[:, 0:256], lhsT=ones_f[:], rhs=f2(dg), start=True, stop=True),
                      reads=["dg", "ones_f"], writes=["bB"])
                    P("dve", lambda e: e.tensor_tensor(out=Dm[:], in0=_b2(gc[:, n, :]), in1=_h2(bB[:, 0:256]), op=ALU.subtract),
                      reads=["bB", "gc"], writes=["Dm", "bB"])
                    P("dve", lambda e: e.tensor_scalar(out=f2(Dm), in0=f2(Dm), scalar1=0.0, scalar2=None, op0=ALU.min),
                      reads=["Dm"], writes=["Dm"])
                    P("act", lambda e: e.activation(out=f2(Dm), in_=f2(Dm), func=AF.Exp), reads=["Dm"], writes=["Dm"])
                    P("pool", lambda e: e.tensor_tensor(out=Ds[:], in0=Dm[:], in1=_b1(strm[:]), op=ALU.mult),
                      reads=["Dm", "strm"], writes=["Ds"])
                    P("pool", lambda e: e.tensor_tensor(out=Dm[:], in0=Dm[:], in1=_b1(lowm[:]), op=ALU.mult),
                      reads=["Dm", "lowm", "Ds"], writes=["Dm"])
                    P("pool", lambda e: e.tensor_tensor(out=Ds[:], in0=Ds[:], in1=_b2(nbeta[:, n, :]), op=ALU.mult),
                      reads=["Ds", "nbeta"], writes=["Ds"])
                    P("dve", lambda e: e.tensor_tensor(out=Nf[:], in0=Ds[:], in1=_b1(bA[:, 0:128]), op=ALU.mult),
                      reads=["bA", "Ds"], writes=["Nf", "bA"])
                    P("dve", lambda e: e.tensor_tensor(out=intra[:], in0=Dm[:], in1=_b1(bA[:, 128:256]), op=ALU.mult),
                      reads=["bA", "Dm"], writes=["intra", "bA"])
                    for hh in range(2):
                        P("pe", lambda e, hh=hh: e.transpose(bG[:, hh * 128:(hh + 1) * 128], intra[:, hh, :], identb[:]),
                          reads=["intra", "identb"], writes=["bG"])
                    P("act", lambda e: e.copy(out=f2(intraT[nb]), in_=bG[:, 0:256]), reads=["bG"], writes=[("intraT", nb), "bG"])
                    P("pool", lambda e: e.tensor_tensor(out=M[0][:], in0=Nf[:], in1=_b1(bdm[:]), op=ALU.mult),
                      reads=["Nf", "bdm"], writes=[("M", 0)])
                    P("pool", lambda e: e.tensor_tensor(out=Noff[:], in0=Nf[:], in1=_b1(offm[:]), op=ALU.mult),
                      reads=["Nf", "offm"], writes=["Noff"])
                    for hh in range(2):
                        P("pe", lambda e, hh=hh: e.transpose(bD[:, hh * 128:(hh + 1) * 128], M[0][:, hh, :], identf[:]),
                          reads=[("M", 0), "identf"], writes=["bD"])
                    P("act", lambda e: e.copy(out=f2(MT[0]), in_=bD[:, 0:256]), reads=["bD"], writes=[("MT", 0), "bD"])
                    P("dve", lambda e: e.tensor_tensor(out=X[0][:], in0=MT[0][:], in1=_b1(identf[:]), op=ALU.add),
                      reads=[("MT", 0), "identf"], writes=[("X", 0)])
                    xi = 0
                    mi = 0
                    for lev in range(1, 5):
                        mo = 1 - mi
                        mm2(P, bD, "bD", 0, lambda hh, mi=mi: MT[mi][:, hh, :], lambda hh, mi=mi: M[mi][:, hh, :],
                            [("M", mi), ("MT", mi)])
                        if lev < 4:
                            mm2(P, bE, "bE", 0, lambda hh, mi=mi: M[mi][:, hh, :], lambda hh, mi=mi: MT[mi][:, hh, :],
                                [("M", mi), ("MT", mi)])
                        P("act", lambda e, mo=mo: e.copy(out=f2(M[mo]), in_=bD[:, 0:256]), reads=["bD"], writes=[("M", mo), "bD"])
                        if lev < 4:
                            P("dve", lambda e, mo=mo: e.tensor_copy(out=f2(MT[mo]), in_=bE[:, 0:256]), reads=["bE"],
                              writes=[("MT", mo), "bE"])
                        mm2(P, bF, "bF", 0, lambda hh, mo=mo: M[mo][:, hh, :], lambda hh, xi=xi: X[xi][:, hh, :],
                            [("M", mo), ("X", xi)])
                        P("dve", lambda e, xi=xi: e.tensor_tensor(out=f2(X[1 - xi]), in0=bF[:, 0:256], in1=f2(X[xi]), op=ALU.add),
                          reads=["bF", ("X", xi)], writes=[("X", 1 - xi), "bF"])
                        xi = 1 - xi
                        mi = mo
                    BiT = X[xi]
                    bk = ("X", xi)
                    for hh in range(2):
                        P("pe", lambda e, hh=hh: e.transpose(bD[:, hh * 128:(hh + 1) * 128], BiT[:, hh, :], identf[:]),
                          reads=[bk, "identf"], writes=["bD"])
                    P("act", lambda e: e.copy(out=f2(Bi), in_=bD[:, 0:256]), reads=["bD"], writes=["Bi", "bD"])
                    mm2(P, bE, "bE", 0, lambda hh: Noff[:, hh, :], lambda hh: BiT[:, hh, :], ["Noff", bk])
                    mm2(P, bF, "bF", 0, lambda hh: BiT[:, hh, :], lambda hh: Noff[:, hh, :], ["Noff", bk])
                    P("dve", lambda e: e.tensor_copy(out=f2(Pm), in_=bE[:, 0:256]), reads=["bE"], writes=["Pm", "bE"])
                    P("act", lambda e: e.copy(out=f2(PTm), in_=bF[:, 0:256]), reads=["bF"], writes=["PTm", "bF"])
                    mm2(P, bD, "bD", 0, lambda hh: Pm[:, hh, :], lambda hh: PTm[:, hh, :], ["Pm", "PTm"])
                    P("act", lambda e: e.copy(out=f2(P2T), in_=bD[:, 0:256]), reads=["bD"], writes=["P2T", "bD"])
                    P("dve", lambda e: e.tensor_tensor(out=Ym[:], in0=Pm[:], in1=_b1(identf[:]), op=ALU.add),
                      reads=["Pm", "identf"], writes=["Ym"])
                    mm2(P, bE, "bE", 0, lambda hh: P2T[:, hh, :], lambda hh: Ym[:, hh, :], ["P2T", "Ym"])
                    P("dve", lambda e: e.tensor_tensor(out=f2(Wm), in0=bE[:, 0:256], in1=f2(Ym), op=ALU.add),
                      reads=["bE", "Ym"], writes=["Wm", "bE"])
                    mm2(P, bF, "bF", 0, lambda hh: Bi[:, hh, :], lambda hh: Wm[:, hh, :], ["Bi", "Wm"])
                    P("act", lambda e: e.copy(out=f2(Xf), in_=bF[:, 0:256]), reads=["bF"], writes=["Xf", "bF"])
                    P("dve", lambda e: e.tensor_tensor(out=kbg[:], in0=_b1(ktm[:, n, :]), in1=_b2(begc[:, n, :]), op=ALU.mult),
                      reads=["ktm", "begc"], writes=["kbg"])
                    P("pool", lambda e: e.tensor_tensor(out=kdec[nb][:], in0=_b1(ktm[:, n, :]), in1=_b2(ekd[:, n, :]), op=ALU.mult),
                      reads=["ktm", "ekd"], writes=[("kdec", nb)])
                    mm2(P, bA, "bA", 256, lambda hh: Xf[:, hh, :], lambda hh: vb[:, n, hh, :], ["Xf", "vb"])
                    mm2(P, bB, "bB", 256, lambda hh: kbg[:, hh, :], lambda hh: Xf[:, hh, :], ["Xf", "kbg"])
                    P("act", lambda e: e.copy(out=f2(u[nb]), in_=bA[:, 256:512]), reads=["bA"], writes=[("u", nb), "bA"])
                    P("dve", lambda e: e.tensor_copy(out=f2(wT[nb]), in_=bB[:, 256:512]), reads=["bB"], writes=[("wT", nb), "bB"])
                    return lst

                def gen_scan(n):
                    lst = []

                    def Sx(*a, **k):
                        lst.append(lambda: kb.op(*a, **k))
                    tok = slice(n * 128, (n + 1) * 128)
                    nb = n % 2
                    mm2(Sx, bH, "bH", 0, lambda hh: wT[nb][:, hh, :], lambda hh: Sb[:, hh, :], [("wT", nb), "Sb"])
                    Sx("dve", lambda e: e.tensor_tensor(out=f2(vn), in0=f2(u[nb]), in1=bH[:, 0:256], op=ALU.subtract),
                       reads=[("u", nb), "bH"], writes=["vn", "bH"])
                    mm2(Sx, bH, "bH", 256, lambda hh: qT[:, tok], lambda hh: Sb[:, hh, :], ["qT", "Sb"])
                    mm2(Sx, bI, "bI", 0, lambda hh: intraT[nb][:, hh, :], lambda hh: vn[:, hh, :], [("intraT", nb), "vn"])
                    Sx("dve", lambda e: e.tensor_tensor(out=tq[:], in0=_b2(egc[:, n, :]), in1=_h2(bH[:, 256:512]), op=ALU.mult),
                       reads=["bH", "egc"], writes=["tq", "bH"])
                    Sx("dve", lambda e: e.tensor_tensor(out=f2(o), in0=f2(tq), in1=bI[:, 0:256], op=ALU.add),
                       reads=["tq", "bI"], writes=["o", "bI"])
                    mm2(Sx, bI, "bI", 256, lambda hh: kdec[nb][:, hh, :], lambda hh: vn[:, hh, :], [("kdec", nb), "vn"])
                    Sx("pool", lambda e: e.tensor_tensor(out=St[:], in0=St[:], in1=_b2(egl[:, n, :]), op=ALU.mult),
                       reads=["St", "egl"], writes=["St"])
                    Sx("dve", lambda e: e.tensor_tensor(out=f2(St), in0=f2(St), in1=bI[:, 256:512], op=ALU.add),
                       reads=["St", "bI"], writes=["St", "bI"])
                    Sx("act", lambda e: e.copy(out=f2(Sb), in_=f2(St)), reads=["St"], writes=["Sb"])
                    for hh in range(2):
                        Sx("act", lambda e, hh=hh: e.activation(out=junk[:, hh, :], in_=o[:, hh, :], func=AF.Square,
                                                                scale=128 ** -0.5, accum_out=st[:, hh:hh + 1]),
                           reads=["o"], writes=[("junkC", hh), ("stC", hh)])
                    Sx("act", lambda e: e.activation(out=st[:], in_=st[:], func=AF.Ln, bias=EPS),
                       reads=[("stC", 0), ("stC", 1)], writes=["stC"])
                    Sx("act", lambda e: e.activation(out=st[:], in_=st[:], func=AF.Exp, scale=-0.5), reads=["stC"], writes=["stC"])
                    Sx("dve", lambda e: e.tensor_tensor(out=on[:], in0=o[:], in1=_b2(st[:]), op=ALU.mult),
                       reads=["o", "stC"], writes=["on", ("stC", 0), ("stC", 1)])
                    Sx("dve", lambda e: e.tensor_tensor(out=on[:], in0=on[:], in1=_b1(wn[:]), op=ALU.mult),
                       reads=["on", "wn"], writes=["on"])
                    Sx("pool", lambda e: e.tensor_tensor(out=ob[:], in0=on[:], in1=Zs[:, n, :, :], op=ALU.mult),
                       reads=["on", "Zs"], writes=["obC"])
                    s2 = n % 2
                    for hh in range(2):
                        Sx("pe", lambda e, hh=hh: e.transpose(bG[:, 512 + (s2 * 2 + hh) * 128:512 + (s2 * 2 + hh + 1) * 128],
                                                               ob[:, hh, :], identb[:]), reads=["obC", "identb"], writes=["bG"])
                    if s2 == 1:
                        qc = n // 4
                        ot = oTt[qc % 2]
                        half = (n % 4) // 2
                        Sx("act", lambda e: e.copy(
                            out=ot[:, :, half * 256:(half + 1) * 256].rearrange("p h (b d) -> p b h d", b=2),
                            in_=bG[:, 512:1024].rearrange("p (b h d) -> p b h d", b=2, h=2)),
                            reads=["bG"], writes=[("oTtC", qc % 2), "bG"])
                        if n % 4 == 3:
                            j, off = qc // 2, (qc % 2) * 512
                            lst.append(lambda: kb.dma(
                                "sp", oT_out[j, pr * 256:(pr + 1) * 256, off:off + 512].rearrange("(h p) t -> p h t", p=128),
                                ot[:], reads=[("oTtC", qc % 2)], is_out=True))
                    return lst

                for th in gen_prep(0):
                    th()
                for n in range(64):
                    a = gen_prep(n + 1) if n + 1 < 64 else []
                    b = gen_scan(n)
                    ia = ib = 0
                    while ia < len(a) or ib < len(b):
                        for _ in range(2):
                            if ia < len(a):
                                a[ia]()
                                ia += 1
                        if ib < len(b):
                            b[ib]()
                            ib += 1
    return kb.done()


_PROGS = {}


def _prog(key, builder):
    if key not in _PROGS:
        _PROGS[key] = builder()
    return _PROGS[key]


def _run(nc, in_maps):
    res = run_bass_kernel_spmd(nc, in_maps, core_ids=list(range(NCORES)))
    return res.results


def _pk(w):
    return np.ascontiguousarray(np.asarray(w, np.float32).reshape(16, 128).T)


def _lambda_init(layer):
    return 0.8 - 0.6 * math.exp(-0.3 * layer)


def _mixer_A(xg, layer, slot, inp):
    w_in = inp["a_w_in"][slot]
    lamv = np.stack([inp["a_lam_q1"][slot], inp["a_lam_k1"][slot], inp["a_lam_q2"][slot], inp["a_lam_k2"][slot]])
    li = _lambda_init(layer)
    cst = np.tile(np.array([[li, 1.0 - li]], np.float32), (128, 1))
    subw = np.ascontiguousarray(inp["a_sub_norm"][slot].reshape(1, 256))
    maps = []
    for hd in range(NCORES):
        cols = np.concatenate([np.arange(m * 1024 + hd * 128, m * 1024 + hd * 128 + 128) for m in range(2)] +
                              [2048 + np.arange(m * 1024 + hd * 128, m * 1024 + hd * 128 + 128) for m in range(2)] +
                              [4096 + np.arange(hd * 256, hd * 256 + 256)])
        maps.append({"xg": xg, "w_loc": np.ascontiguousarray(w_in[:, cols]), "lamv": lamv, "subw": subw, "cst": cst})
    return _a2a(_run(_prog("A", build_A), maps), "oT_out")


def _a2a(res, key):
    return [np.ascontiguousarray(np.concatenate([res[r][key][j] for r in range(NCORES)], axis=0))
            for j in range(NCORES)]


def _mixer_B(xg, slot, inp):
    w_in = inp["b_w_in"][slot]
    qkw = np.ascontiguousarray(np.stack([inp["b_q_norm"][slot], inp["b_k_norm"][slot]], axis=1).astype(np.float32))
    maps = []
    for c in range(NCORES):
        wl = []
        for hl in range(2):
            h = 2 * c + hl
            cols = np.concatenate([np.arange(h * 128, h * 128 + 128) + o for o in (0, 2048, 4096, 6144)] +
                                  [np.array([8192 + h, 8192 + h])])
            wl.append(w_in[:, cols])
        fb = np.ascontiguousarray(inp["b_forget_bias"][slot][2 * c:2 * c + 2].reshape(1, 2).astype(np.float32))
        maps.append({"xg": xg, "w_loc": np.ascontiguousarray(np.stack(wl)), "qkw": qkw, "fb": fb})
    return _a2a(_run(_prog("B", build_B), maps), "oT_out")


def _mixer_C(xg, slot, inp):
    w_in = inp["c_w_in"][slot]
    conv = inp["c_conv_w"][slot]
    onw = np.ascontiguousarray(inp["c_out_norm"][slot].reshape(1, 128).astype(np.float32))
    a_log = inp["c_a_log"][slot]
    dtb = inp["c_dt_bias"][slot]
    maps = []
    for c in range(NCORES):
        wl, cws, hps = [], [], []
        for pr in range(2):
            hq = 2 * c + pr
            hv0 = 4 * c + 2 * pr
            hv1 = hv0 + 1
            qc = np.arange(hq * 128, hq * 128 + 128)
            v0 = np.arange(hv0 * 128, hv0 * 128 + 128)
            v1 = np.arange(hv1 * 128, hv1 * 128 + 128)
            cols = np.concatenate([qc, 2048 + qc, 4096 + v0, 4096 + v1, 8192 + v0, 8192 + v1,
                                   np.array([12288 + hv0, 12288 + hv1, 12320 + hv0, 12320 + hv1])])
            wl.append(w_in[:, cols])
            cws.append(np.concatenate([conv[:, ch].T for ch in (qc, 2048 + qc, 4096 + v0, 4096 + v1)], axis=1))
            hps.append([a_log[hv0], a_log[hv1], dtb[hv0], dtb[hv1]])
        maps.append({"xg": xg, "w_loc": np.ascontiguousarray(np.stack(wl)),
                     "cw": np.ascontiguousarray(np.stack(cws)).astype(np.float32),
                     "hp": np.array(hps, np.float32), "onw": onw})
    return _a2a(_run(_prog("C", build_C2), maps), "oT_out")


def kernel(**inp):
    inp = {k: np.asarray(v) for k, v in inp.items()}
    x = inp["x"][0]
    hs = [np.ascontiguousarray(x[c * TL:(c + 1) * TL]) for c in range(NCORES)]
    nrm0 = np.concatenate([_pk(inp["mix_norm"][0]), _pk(inp["mix_norm"][0])], axis=1)
    res = _run(_prog(("T", 0, True, False), lambda: build_T(2048, True, False)),
               [{"h_in": hs[c], "nrm": nrm0} for c in range(NCORES)])
    xg = np.ascontiguousarray(np.stack([res[c]["xnT_out"] for c in range(NCORES)]))
    depth = inp["mix_norm"].shape[0]
    out = None
    for i in range(depth):
        slot = i // 3
        if i % 3 == 0:
            oTs = _mixer_A(xg, i, slot, inp)
            w_out = inp["a_w_out"][slot]
        elif i % 3 == 1:
            oTs = _mixer_B(xg, slot, inp)
            w_out = inp["b_w_out"][slot]
        else:
            oTs = _mixer_C(xg, slot, inp)
            w_out = inp["c_w_out"][slot]
        last = i == depth - 1
        kdim = w_out.shape[0]
        nxt = inp["final_norm"] if last else inp["mix_norm"][i + 1]
        nrm = np.concatenate([_pk(inp["ffn_norm"][i]), _pk(nxt)], axis=1)
        maps = []
        for c in range(NCORES):
            m = {"h_in": hs[c], "oT_in": oTs[c], "w_out": w_out, "nrm": nrm, "w_gate": inp["ffn_w_gate"][i],
                 "w_up": inp["ffn_w_up"][i], "w_down": inp["ffn_w_down"][i]}
            if last:
                m["fnw"] = np.ascontiguousarray(inp["final_norm"].reshape(1, D))
            maps.append(m)
        res = _run(_prog(("T", kdim, False, last), lambda: build_T(kdim, False, last)), maps)
        if last:
            out = np.concatenate([res[c]["y_out"] for c in range(NCORES)], axis=0)[None]
        else:
            hs = [res[c]["h_out"] for c in range(NCORES)]
            xg = np.ascontiguousarray(np.stack([res[c]["xnT_out"] for c in range(NCORES)]))
    return out.astype(np.float32)
```

```python
import contextlib
import math
import numpy as np
import ml_dtypes
import concourse.bass as bass
import concourse.mybir as mybir
from concourse.bass_utils import run_bass_kernel_spmd

F32 = mybir.dt.float32
BF16 = mybir.dt.bfloat16
AF = mybir.ActivationFunctionType
ALU = mybir.AluOpType
AX = mybir.AxisListType

NCORES = 8
D = 2048
S = 8192
TL = S // NCORES
DFF = 5632
EPS = 1e-6


class _Op:
    __slots__ = ("stream", "fn", "deps", "is_dma", "semkey", "ticket", "signal", "idx")

    def __init__(self, stream, fn, is_dma=False, semkey=None):
        self.stream = stream
        self.fn = fn
        self.deps = []
        self.is_dma = is_dma
        self.semkey = semkey
        self.ticket = None
        self.signal = is_dma
        self.idx = -1


class KB:
    STREAMS = ("pe", "act", "dve", "pool", "sp")

    def __init__(self):
        self.nc = bass.Bass("TRN2", target_bir_lowering=False)
        self.es = contextlib.ExitStack()
        self.ops = []
        self.last_w = {}
        self.readers = {}
        self.last_on = {s: None for s in self.STREAMS}
        self.dma_open = []
        self.out_dmas = []

    def dram(self, name, shape, dt, kind):
        return self.nc.dram_tensor(name, list(shape), dt, kind=kind).ap()

    def _uniq(self, name):
        self._nuniq = getattr(self, "_nuniq", 0) + 1
        return "%s_%d" % (name, self._nuniq)

    def sb(self, name, shape, dt):
        return self.es.enter_context(self.nc.sbuf_tensor(self._uniq(name), list(shape), dt))

    def ps(self, name, shape, dt=F32):
        return self.es.enter_context(self.nc.psum_tensor(self._uniq(name), list(shape), dt))

    def _add(self, op, reads, writes):
        deps = []
        for k in reads:
            w = self.last_w.get(k)
            if w is not None:
                deps.append((w, "raw"))
        for k in writes:
            w = self.last_w.get(k)
            if w is not None:
                deps.append((w, "waw"))
            for r in self.readers.get(k, ()):
                deps.append((r, "war"))
        seen = set()
        for d, kind in deps:
            if d is op or id(d) in seen:
                continue
            if (not op.is_dma) and (not d.is_dma) and d.stream == op.stream:
                if op.stream == "pe" or (kind != "raw" and op.stream != "pool"):
                    continue
            seen.add(id(d))
            op.deps.append(d)
            d.signal = True
        op.idx = len(self.ops)
        self.ops.append(op)
        self.last_on[op.stream] = op
        for k in reads:
            lst = self.readers.setdefault(k, [])
            lst[:] = [r for r in lst if not (r.stream == op.stream and not r.is_dma and not op.is_dma)]
            lst.append(op)
        for k in writes:
            self.last_w[k] = op
            self.readers[k] = []
        return op

    def op(self, stream, fn, reads=(), writes=()):
        return self._add(_Op(stream, fn), reads, writes)

    def dma(self, queue, out, in_, reads=(), writes=(), semkey=None, is_out=False):
        if semkey is None:
            semkey = writes[0] if writes else reads[0]
        o = _Op(queue, lambda e: e.dma_start(out=out, in_=in_), is_dma=True, semkey=("dma", semkey))
        self._add(o, reads, writes)
        self.dma_open.append(o)
        if is_out:
            self.out_dmas.append(o)
        return o

    def barrier(self):
        lasts = [o for o in self.last_on.values() if o is not None] + list(self.dma_open)
        self.dma_open = []
        for s in self.STREAMS:
            b = _Op(s, None)
            for d in lasts:
                if d.stream == s and not d.is_dma and s == "pe":
                    continue
                b.deps.append(d)
                d.signal = True
            b.idx = len(self.ops)
            self.ops.append(b)

    def finish(self):
        b = _Op("sp", None)
        for d in self.out_dmas:
            b.deps.append(d)
        b.idx = len(self.ops)
        self.ops.append(b)

    def _init_emit(self):
        if getattr(self, "_sems", None) is None:
            E = self.es.enter_context
            self._sems = {s: E(self.nc.semaphore("sem_" + s)) for s in self.STREAMS}
            self._cnt = {s: 0 for s in self.STREAMS}
            self._dsem = {}
            self._dcnt = {}
            self._waited = {s: {} for s in self.STREAMS}
            self._emitted = 0

    def flush(self):
        nc = self.nc
        self._init_emit()
        E = self.es.enter_context
        new_ops = self.ops[self._emitted:]
        self._emitted = len(self.ops)
        for o in new_ops:
            if o.is_dma:
                if o.semkey not in self._dsem:
                    self._dsem[o.semkey] = E(nc.semaphore("dsem%d" % len(self._dsem)))
                    self._dcnt[o.semkey] = 0
                self._dcnt[o.semkey] += 16
                o.ticket = (self._dsem[o.semkey], self._dcnt[o.semkey])
            elif o.signal and o.fn is not None:
                self._cnt[o.stream] += 1
                o.ticket = (self._sems[o.stream], self._cnt[o.stream])
        by_stream = {s: [o for o in new_ops if o.stream == s] for s in self.STREAMS}

        def run(stream, eng):
            waited = self._waited[stream]
            for o in by_stream[stream]:
                need = {}
                for d in o.deps:
                    if d.ticket is None:
                        continue
                    sem, val = d.ticket
                    if need.get(id(sem), (None, 0))[1] < val:
                        need[id(sem)] = (sem, val)
                for sid, (sem, val) in need.items():
                    if waited.get(sid, 0) < val:
                        eng.wait_ge(sem, val)
                        waited[sid] = val
                if o.fn is None:
                    continue
                ins = o.fn(eng)
                if o.ticket is not None:
                    ins.then_inc(o.ticket[0], 16 if o.is_dma else 1)

        with nc.Block() as block:
            @block.tensor
            def _(e):
                run("pe", e)

            @block.scalar
            def _(e):
                run("act", e)

            @block.vector
            def _(e):
                run("dve", e)

            @block.gpsimd
            def _(e):
                run("pool", e)

            @block.sync
            def _(e):
                run("sp", e)

    @contextlib.contextmanager
    def phase(self):
        outer = self.es
        inner = contextlib.ExitStack()
        self._init_emit()
        self.es = inner
        try:
            yield
            self.es = outer
            self.barrier()
            self.flush()
        finally:
            self.es = outer
            inner.close()

    def done(self):
        self.finish()
        self.flush()
        self.es.close()
        return self.nc


def _ident(kb, name, dt):
    t = kb.sb(name, [128, 128], dt)
    kb.op("pool", lambda e: e.memset(t[:], 0.0), writes=[name])
    kb.op("pool", lambda e: e.affine_select(out=t[:], in_=t[:], pattern=[[-1, 128]], compare_op=ALU.not_equal,
                                            fill=1.0, base=0, channel_multiplier=1), reads=[name], writes=[name])
    return t


def _rsqrt(kb, out, in_, kin, kout, eps=EPS, rec=None):
    rec = rec or kb.op
    rec("act", lambda e: e.activation(out=out, in_=in_, func=AF.Ln, bias=eps), reads=[kin], writes=[kout])
    rec("act", lambda e: e.activation(out=out, in_=out, func=AF.Exp, scale=-0.5), reads=[kout], writes=[kout])


def _rms_to_xT(kb, h, hkey, nrm_sb, woff, xnT, xkey, identb, ss, rstd, tag):
    junk = kb.sb("junk" + tag, [128, D], BF16)
    xn = [kb.sb("xn%s%d" % (tag, i), [128, D], BF16) for i in range(2)]
    pT = [kb.ps("pT%s%d" % (tag, i), [128, 1024], BF16) for i in range(2)]
    for t in range(8):
        kb.op("act", lambda e, t=t: e.activation(out=junk[:], in_=h[:, t, :], func=AF.Square,
                                                 scale=1.0 / math.sqrt(D), accum_out=ss[:, t:t + 1]),
              reads=[(hkey, t)], writes=["junk" + tag, ("ss" + tag, t)])
        _rsqrt(kb, rstd[:, t:t + 1], ss[:, t:t + 1], ("ss" + tag, t), ("rstd" + tag, t))
        xb = xn[t % 2]
        kb.op("dve", lambda e, t=t, xb=xb: e.tensor_scalar(out=xb[:], in0=h[:, t, :], scalar1=rstd[:, t:t + 1], scalar2=None,
                                                           op0=ALU.mult),
              reads=[(hkey, t), ("rstd" + tag, t)], writes=[("xn" + tag, t % 2)])
        for half in range(2):
            pp = pT[half]
            for j in range(8):
                kt = half * 8 + j
                kb.op("pe", lambda e, pp=pp, j=j, kt=kt, xb=xb: e.transpose(pp[:, j * 128:(j + 1) * 128],
                                                                            xb[:, kt * 128:(kt + 1) * 128], identb[:]),
                      reads=[("xn" + tag, t % 2), "identb"], writes=[("pT" + tag, half)])
            kb.op("dve", lambda e, pp=pp, half=half, t=t: e.tensor_tensor(
                out=xnT[:, half * 8:(half + 1) * 8, t * 128:(t + 1) * 128],
                in0=pp[:, 0:1024].rearrange("p (k d) -> p k d", k=8),
                in1=nrm_sb[:, woff + half * 8:woff + half * 8 + 8].unsqueeze(2).broadcast_to([128, 8, 128]), op=ALU.mult),
                reads=[("pT" + tag, half), "nrm_sb"],
                writes=[(xkey, half * 8 + j, t) for j in range(8)] + [("pT" + tag, half)])


def build_T(kdim, first, last):
    kb = KB()
    KT = kdim // 128
    if first:
        h_in = kb.dram("h_in", [TL, D], F32, "ExternalInput")
        nrm = kb.dram("nrm", [128, 32], F32, "ExternalInput")
    else:
        h_in = kb.dram("h_in", [TL, D], F32, "ExternalInput")
        oT_in = kb.dram("oT_in", [kdim, TL], BF16, "ExternalInput")
        w_out = kb.dram("w_out", [kdim, D], F32, "ExternalInput")
        nrm = kb.dram("nrm", [128, 32], F32, "ExternalInput")
        w_gate = kb.dram("w_gate", [D, DFF], F32, "ExternalInput")
        w_up = kb.dram("w_up", [D, DFF], F32, "ExternalInput")
        w_down = kb.dram("w_down", [DFF, D], F32, "ExternalInput")
    if last:
        fnw = kb.dram("fnw", [1, D], F32, "ExternalInput")
        y_out = kb.dram("y_out", [TL, D], F32, "ExternalOutput")
    else:
        xnT_out = kb.dram("xnT_out", [D, TL], BF16, "ExternalOutput")
        if not first:
            h_out = kb.dram("h_out", [TL, D], F32, "ExternalOutput")

    h = kb.sb("h", [128, 8, D], F32)
    nrm_sb = kb.sb("nrm_sb", [128, 32], F32)
    ss = kb.sb("ss", [128, 16], F32)
    rstd = kb.sb("rstd", [128, 16], F32)
    identb = _ident(kb, "identb", BF16)
    kb.dma("sp", nrm_sb[:], nrm, writes=["nrm_sb"])

    def load_h():
        for t in range(8):
            kb.dma("sp", h[:, t, :], h_in[t * 128:(t + 1) * 128, :], writes=[("h", t)])
    if first:
        load_h()

    if not first:
        with kb.phase():
            oT = kb.sb("oT", [128, KT, TL], BF16)
            wo = [kb.sb("wo%d" % i, [128, KT, 512], BF16) for i in range(2)]
            pso = [kb.ps("pso%d" % i, [128, 512], F32) for i in range(4)]
            oview = oT_in.rearrange("(k p) t -> p k t", p=128)
            kg = KT // 4
            for g in range(4):
                kb.dma("sp", oT[:, g * kg:(g + 1) * kg, :], oview[:, g * kg:(g + 1) * kg, :], writes=[("oT", g)])
            load_h()
            wview = w_out.rearrange("(k p) n -> p k n", p=128)
            n = 0
            for c in range(4):
                wb = wo[c % 2]
                kb.dma("pool", wb[:], wview[:, :, c * 512:(c + 1) * 512], writes=[("wo", c % 2)])
                for t in range(8):
                    pb = pso[n % 4]
                    for k in range(KT):
                        kb.op("pe", lambda e, pb=pb, wb=wb, k=k, t=t: e.matmul(
                            pb[:], lhsT=oT[:, k, t * 128:(t + 1) * 128], rhs=wb[:, k, :], start=(k == 0), stop=(k == KT - 1)),
                            reads=[("oT", k // kg), ("wo", c % 2)], writes=[("pso", n % 4)])
                    kb.op("dve", lambda e, pb=pb, c=c, t=t: e.tensor_tensor(
                        out=h[:, t, c * 512:(c + 1) * 512], in0=h[:, t, c * 512:(c + 1) * 512], in1=pb[:], op=ALU.add),
                        reads=[("pso", n % 4), ("h", t)], writes=[("h", t), ("pso", n % 4)])
                    n += 1

        with kb.phase():
            xnT = kb.sb("xnT", [128, 16, TL], BF16)
            with kb.phase():
                _rms_to_xT(kb, h, "h", nrm_sb, 0, xnT, "xnT", identb, ss, rstd, "a")
            NGU = 4
            wgu = [kb.sb("wgu%d" % i, [128, 16, 256], BF16) for i in range(NGU)]
            wd = [kb.sb("wd%d" % i, [128, 11, 512], BF16) for i in range(2)]
            actT = kb.sb("actT", [128, 11, TL], BF16)
            sgt = [kb.sb("sgt%d" % i, [128, 512], F32) for i in range(2)]
            psg = [kb.ps("psg%d" % i, [128, 512], F32) for i in range(2)]
            psu = [kb.ps("psu%d" % i, [128, 512], F32) for i in range(2)]
            psd = [kb.ps("psd%d" % i, [128, 512], F32) for i in range(3)]
            gview = w_gate.rearrange("(k p) n -> p k n", p=128)
            uview = w_up.rearrange("(k p) n -> p k n", p=128)
            dview = w_down.rearrange("(k p) n -> p k n", p=128)
            xkeys = [("xnT", kt, t) for kt in range(16) for t in range(8)]
            nf = 0
            nh = 0
            nd = 0
            ndw = 0
            for g in range(4):
                for fi in range(11):
                    f = g * 11 + fi
                    sl = nf % NGU
                    wt = wgu[sl]
                    kb.dma("pool", wt[:, :, 0:128], gview[:, :, f * 128:(f + 1) * 128], writes=[("wgu", sl)])
                    kb.dma("pool", wt[:, :, 128:256], uview[:, :, f * 128:(f + 1) * 128], writes=[("wgu", sl)])
                    nf += 1
                    for half in range(2):
                        pg = psg[nh % 2]
                        pu = psu[nh % 2]
                        st = sgt[nh % 2]
                        hk = [("xnT", kt, t) for kt in range(16) for t in range(half * 4, half * 4 + 4)]
                        for k in range(16):
                            kb.op("pe", lambda e, pg=pg, wt=wt, k=k, half=half: e.matmul(
                                pg[:], lhsT=wt[:, k, 0:128], rhs=xnT[:, k, half * 512:(half + 1) * 512],
                                start=(k == 0), stop=(k == 15)),
                                reads=[("wgu", sl)] + (hk if k == 0 else []), writes=[("psg", nh % 2)])
                        for k in range(16):
                            kb.op("pe", lambda e, pu=pu, wt=wt, k=k, half=half: e.matmul(
                                pu[:], lhsT=wt[:, k, 128:256], rhs=xnT[:, k, half * 512:(half + 1) * 512],
                                start=(k == 0), stop=(k == 15)),
                                reads=[("wgu", sl)], writes=[("psu", nh % 2)])
                        kb.op("act", lambda e, pg=pg, st=st: e.activation(out=st[:], in_=pg[:], func=AF.Silu),
                              reads=[("psg", nh % 2)], writes=[("sgt", nh % 2), ("psg", nh % 2)])
                        kb.op("dve", lambda e, pu=pu, st=st, fi=fi, half=half: e.tensor_tensor(
                            out=actT[:, fi, half * 512:(half + 1) * 512], in0=st[:], in1=pu[:], op=ALU.mult),
                            reads=[("sgt", nh % 2), ("psu", nh % 2)], writes=[("actT", fi, half), ("psu", nh % 2)])
                        nh += 1
                for c in range(4):
                    wdt = wd[ndw % 2]
                    kb.dma("pool", wdt[:], dview[:, g * 11:(g + 1) * 11, c * 512:(c + 1) * 512], writes=[("wd", ndw % 2)])
                    for t in range(8):
                        pd = psd[nd % 3]
                        for fi in range(11):
                            kb.op("pe", lambda e, pd=pd, wdt=wdt, fi=fi, t=t: e.matmul(
                                pd[:], lhsT=actT[:, fi, t * 128:(t + 1) * 128], rhs=wdt[:, fi, :],
                                start=(fi == 0), stop=(fi == 10)),
                                reads=[("wd", ndw % 2), ("actT", fi, t // 4)], writes=[("psd", nd % 3)])
                        kb.op("dve", lambda e, pd=pd, c=c, t=t: e.tensor_tensor(
                            out=h[:, t, c * 512:(c + 1) * 512], in0=h[:, t, c * 512:(c + 1) * 512], in1=pd[:], op=ALU.add),
                            reads=[("psd", nd % 3), ("h", t)], writes=[("h", t), ("psd", nd % 3)])
                        nd += 1
                    ndw += 1

    if last:
        with kb.phase():
            fw = kb.sb("fw", [128, D], F32)
            junk = kb.sb("junkf", [128, D], BF16)
            yt = [kb.sb("yt%d" % i, [128, D], F32) for i in range(2)]
            kb.dma("sp", fw[:], fnw.partition_broadcast(128), writes=["fw"])
            for t in range(8):
                kb.op("act", lambda e, t=t: e.activation(out=junk[:], in_=h[:, t, :], func=AF.Square,
                                                         scale=1.0 / math.sqrt(D), accum_out=ss[:, t:t + 1]),
                      reads=[("h", t)], writes=["junkf", ("ssf", t)])
                _rsqrt(kb, rstd[:, t:t + 1], ss[:, t:t + 1], ("ssf", t), ("rstdf", t))
                y = yt[t % 2]
                kb.op("dve", lambda e, t=t, y=y: e.scalar_tensor_tensor(out=y[:], in0=h[:, t, :], scalar=rstd[:, t:t + 1],
                                                                        in1=fw[:], op0=ALU.mult, op1=ALU.mult),
                      reads=[("h", t), ("rstdf", t), "fw"], writes=[("yt", t % 2)])
                kb.dma("sp", y_out[t * 128:(t + 1) * 128, :], y[:], reads=[("yt", t % 2)], is_out=True)
    else:
        with kb.phase():
            xnT2 = kb.sb("xnT2", [128, 16, TL], BF16)
            _rms_to_xT(kb, h, "h", nrm_sb, 16, xnT2, "xnT2", identb, ss, rstd, "b")
            xkeys = [("xnT2", kt, t) for kt in range(16) for t in range(8)]
            xo = xnT_out.rearrange("(k p) t -> p k t", p=128)
            for g in range(4):
                kb.dma("sp", xo[:, g * 4:(g + 1) * 4, :], xnT2[:, g * 4:(g + 1) * 4, :],
                       reads=[("xnT2", kt, t) for kt in range(g * 4, g * 4 + 4) for t in range(8)],
                       semkey=("xo", g), is_out=True)
            if not first:
                for t in range(8):
                    kb.dma("sp", h_out[t * 128:(t + 1) * 128, :], h[:, t, :], reads=[("h", t)], semkey=("ho", t), is_out=True)
    return kb.done()


class _Rec:
    def __init__(self, kb):
        self.kb = kb
        self.lst = []

    def op(self, *a, **k):
        self.lst.append(lambda: self.kb.op(*a, **k))

    def dma(self, *a, **k):
        self.lst.append(lambda: self.kb.dma(*a, **k))

    def add(self, fn):
        self.lst.append(fn)


def _interleave(lists):
    idx = [0] * len(lists)
    left = sum(len(l) for l in lists)
    while left:
        for i, l in enumerate(lists):
            if idx[i] < len(l):
                l[idx[i]]()
                idx[i] += 1
                left -= 1


def _noop(ps):
    return None


def _load_xc(kb, xg, xc, c, slot):
    r, off = c // 2, (c % 2) * 512
    src = xg[r].rearrange("(k p) t -> p k t", p=128)
    kb.dma("sp", xc[:], src[:, :, off:off + 512], writes=[("xc", slot)])


def _proj_fm(kb, ps, pskey, w_sb, col0, xc, xslot, evac):
    for k in range(16):
        kb.op("pe", lambda e, k=k: e.matmul(ps[:], lhsT=w_sb[:, k, col0:col0 + 128], rhs=xc[:, k, :],
                                            start=(k == 0), stop=(k == 15)),
              reads=["w_sb", ("xc", xslot)], writes=[pskey])
    evac(ps)


def _proj_tm(kb, ps, pskey, w_sb, col0, ncol, xc, xslot, s, evac):
    for k in range(16):
        kb.op("pe", lambda e, k=k: e.matmul(ps[:, 0:ncol], lhsT=xc[:, k, s * 128:(s + 1) * 128],
                                            rhs=w_sb[:, k, col0:col0 + ncol], start=(k == 0), stop=(k == 15)),
              reads=["w_sb", ("xc", xslot)], writes=[pskey])
    evac(ps)


def _diag_masks(kb, name):
    ms = []
    for r in range(4):
        m = kb.sb("%s%d" % (name, r), [128, 512], BF16)
        key = (name, r)
        kb.op("pool", lambda e, m=m: e.memset(m[:], 1.0), writes=[key])
        for half in range(2):
            base = -(128 * r + 64 * half)
            kb.op("pool", lambda e, m=m, half=half, base=base: e.affine_select(
                out=m[half * 64:(half + 1) * 64, :], in_=m[half * 64:(half + 1) * 64, :], pattern=[[1, 512]],
                compare_op=ALU.is_ge, fill=0.0, base=base, channel_multiplier=0), reads=[key], writes=[key])
        ms.append(m)
    return ms


def _causal_masks(kb, name):
    ms = []
    for r in range(4):
        m = kb.sb("%s%d" % (name, r), [128, 512], BF16)
        key = (name, r)
        kb.op("pool", lambda e, m=m: e.memset(m[:], 1.0), writes=[key])
        kb.op("pool", lambda e, m=m, r=r: e.affine_select(
            out=m[:], in_=m[:], pattern=[[1, 512]], compare_op=ALU.is_ge, fill=0.0, base=-128 * r,
            channel_multiplier=-1), reads=[key], writes=[key])
        ms.append(m)
    return ms


def _attn_pipe(kb, tag, maps, Vaug, vreads, dv, masks, mkey, scale, finish):
    NPS, NPT = 3, 4
    pS = [kb.ps("pS%s%d" % (tag, i), [128, 512], F32) for i in range(NPS)]
    psO = [kb.ps("pO%s%d" % (tag, i), [128, 512], F32) for i in range(4)]
    PT = [kb.sb("PT%s%d" % (tag, i), [128, 512], BF16) for i in range(NPT)]
    items = [(qc, m, k) for qc in range(16) for m in range(len(maps)) for k in range(qc * 4 + 4)]

    def rec_score(i):
        qc, m, kb_i = items[i]
        mp = maps[m]
        qT, kT = mp["qT"], mp["kT"]
        r = kb_i - qc * 4
        q0 = max(r, 0) * 128
        ps = pS[i % NPS]
        pt = PT[i % NPT]
        psk = ("pS" + tag, i % NPS)
        ptk = ("PT" + tag, i % NPT)
        aux = mp.get("aux")
        kb.op("pe", lambda e: e.matmul(ps[:, q0:512], lhsT=kT[:, kb_i * 128:(kb_i + 1) * 128],
                                       rhs=qT[:, qc * 512 + q0:(qc + 1) * 512], start=True, stop=(aux is None)),
              reads=[mp["qkey"], mp["kkey"]], writes=[psk])
        if aux is not None:
            ka, qa = aux
            kb.op("pe", lambda e: e.matmul(ps[:, q0:512], lhsT=ka[:, kb_i * 128:(kb_i + 1) * 128],
                                           rhs=qa[:, qc * 512 + q0:(qc + 1) * 512], start=False, stop=True),
                  reads=[("ka", r) for r in range(6)] + [("qa", r) for r in range(6)], writes=[psk])
        kb.op("act", lambda e: e.activation(out=pt[:, q0:512], in_=ps[:, q0:512], func=AF.Exp, scale=scale),
              reads=[psk], writes=[ptk, psk])
        if r >= 0:
            kb.op("pool", lambda e: e.tensor_tensor(out=pt[:, q0:512], in0=pt[:, q0:512], in1=masks[r][:, q0:512],
                                                    op=ALU.mult), reads=[ptk, (mkey, r)], writes=[ptk])

    def rec_pv(i):
        qc, m, kb_i = items[i]
        r = kb_i - qc * 4
        pt = PT[i % NPT]
        ptk = ("PT" + tag, i % NPT)
        for s in range(4):
            if r >= 0 and s < r:
                continue
            last = qc * 4 + s
            kb.op("pe", lambda e, s=s, last=last: e.matmul(
                psO[s][:, 0:dv + 1], lhsT=pt[:, s * 128:(s + 1) * 128], rhs=Vaug[:, kb_i, 0:dv + 1],
                start=(kb_i == 0), stop=(kb_i == last)), reads=[ptk] + list(vreads), writes=[("pO" + tag, s)])

    n = len(items)
    rec_score(0)
    for i in range(n):
        if i + 1 < n:
            rec_score(i + 1)
        rec_pv(i)
        qc, m, kb_i = items[i]
        if kb_i == qc * 4 + 3:
            finish(qc, m, psO)


def build_A():
    kb = KB()
    xg = kb.dram("xg", [8, D, TL], BF16, "ExternalInput")
    w_loc = kb.dram("w_loc", [D, 768], F32, "ExternalInput")
    lamv = kb.dram("lamv", [4, 128], F32, "ExternalInput")
    subw = kb.dram("subw", [1, 256], F32, "ExternalInput")
    cst = kb.dram("cst", [128, 2], F32, "ExternalInput")
    oT_out = kb.dram("oT_out", [8, 256, TL], BF16, "ExternalOutput")

    qT = [kb.sb("qT%d" % m, [128, S], BF16) for m in range(2)]
    kT = [kb.sb("kT%d" % m, [128, S], BF16) for m in range(2)]
    Vaug = kb.sb("Vaug", [128, 64, 264], BF16)
    identb = _ident(kb, "identb", BF16)
    cst_sb = kb.sb("cst_sb", [128, 2], F32)
    sw = kb.sb("sw", [128, 256], F32)
    lam4 = kb.sb("lam4", [128, 4, 128], F32)
    lsc = kb.sb("lsc", [128, 8], F32)
    kb.dma("sp", cst_sb[:], cst, writes=["cst_sb"])
    kb.dma("sp", sw[:], subw.partition_broadcast(128), writes=["sw"])
    for i in range(4):
        kb.dma("sp", lam4[:, i, :], lamv[i:i + 1, :].partition_broadcast(128), writes=["lam4"], semkey=("lam4", i))
    kb.op("pool", lambda e: e.memset(Vaug[:, :, 256:257], 1.0), writes=["Vones"])
    prod = kb.sb("lprod", [128, 2, 128], F32)
    kb.op("dve", lambda e: e.tensor_tensor(out=prod[:, 0, :], in0=lam4[:, 0, :], in1=lam4[:, 1, :], op=ALU.mult),
          reads=["lam4"], writes=["lprod"])
    kb.op("dve", lambda e: e.tensor_tensor(out=prod[:, 1, :], in0=lam4[:, 2, :], in1=lam4[:, 3, :], op=ALU.mult),
          reads=["lam4"], writes=["lprod"])
    kb.op("dve", lambda e: e.reduce_sum(out=lsc[:, 0:2], in_=prod[:], axis=AX.X), reads=["lprod"], writes=["lsc"])
    kb.op("act", lambda e: e.activation(out=lsc[:, 2:4], in_=lsc[:, 0:2], func=AF.Exp), reads=["lsc"], writes=["lsc"])
    kb.op("dve", lambda e: e.tensor_tensor(out=lsc[:, 4:5], in0=lsc[:, 3:4], in1=lsc[:, 2:3], op=ALU.subtract),
          reads=["lsc"], writes=["lsc"])
    kb.op("dve", lambda e: e.tensor_tensor(out=lsc[:, 4:5], in0=lsc[:, 4:5], in1=cst_sb[:, 0:1], op=ALU.subtract),
          reads=["lsc", "cst_sb"], writes=["lsc"])
    kb.op("dve", lambda e: e.tensor_scalar(out=sw[:], in0=sw[:], scalar1=cst_sb[:, 1:2], scalar2=None, op0=ALU.mult),
          reads=["sw", "cst_sb"], writes=["sw"])

    with kb.phase():
        w_sb = kb.sb("w_sb", [128, 16, 768], BF16)
        kb.dma("pool", w_sb[:], w_loc.rearrange("(k p) n -> p k n", p=128), writes=["w_sb"])
        xcs = [kb.sb("xc%d" % i, [128, 16, 512], BF16) for i in range(2)]
        pp = [kb.ps("pp%d" % i, [128, 512], F32) for i in range(4)]
        n = 0
        dests = [qT[0], qT[1], kT[0], kT[1]]
        dkeys = ["qT0", "qT1", "kT0", "kT1"]
        for c in range(16):
            xc = xcs[c % 2]
            _load_xc(kb, xg, xc, c, c % 2)
            for j in range(4):
                ps = pp[n % 4]
                eng = "act" if n % 2 == 0 else "dve"

                def evac(ps, j=j, c=c, eng=eng, n=n):
                    dst = dests[j][:, c * 512:(c + 1) * 512]
                    if eng == "act":
                        kb.op("act", lambda e: e.copy(out=dst, in_=ps[:]), reads=[("pp", n % 4)],
                              writes=[dkeys[j], ("pp", n % 4)])
                    else:
                        kb.op("dve", lambda e: e.tensor_copy(out=dst, in_=ps[:]), reads=[("pp", n % 4)],
                              writes=[dkeys[j], ("pp", n % 4)])
                _proj_fm(kb, ps, ("pp", n % 4), w_sb, j * 128, xc, c % 2, evac)
                n += 1
            for s in range(4):
                ps = pp[n % 4]
                eng = "act" if n % 2 == 0 else "dve"

                def evac(ps, s=s, c=c, eng=eng, n=n):
                    dst = Vaug[:, c * 4 + s, 0:256]
                    if eng == "act":
                        kb.op("act", lambda e: e.copy(out=dst, in_=ps[:, 0:256]), reads=[("pp", n % 4)],
                              writes=["Vaug", ("pp", n % 4)])
                    else:
                        kb.op("dve", lambda e: e.tensor_copy(out=dst, in_=ps[:, 0:256]), reads=[("pp", n % 4)],
                              writes=["Vaug", ("pp", n % 4)])
                _proj_tm(kb, ps, ("pp", n % 4), w_sb, 512, 256, xc, c % 2, s, evac)
                n += 1

    with kb.phase():
        masks = _diag_masks(kb, "dmask")
        acc = [kb.sb("acc%d" % s, [128, 256], F32) for s in range(4)]
        rc = kb.sb("rc", [128, 8], F32)
        st = kb.sb("st", [128, 8], F32)
        junk = kb.sb("junkA", [128, 256], F32)
        ob = [kb.sb("ob%d" % s, [128, 256], BF16) for s in range(2)]
        oTt = [kb.sb("oTt%d" % i, [128, 2, 512], BF16) for i in range(2)]
        pTr = kb.ps("pTr", [128, 1024], BF16)
        scale = 128 ** -0.5

        def finish(qc, m, psO):
            for s in range(4):
                pk = ("pOA", s)
                kb.op("dve", lambda e, s=s: e.reciprocal(out=rc[:, s:s + 1], in_=psO[s][:, 256:257]),
                      reads=[pk], writes=[("rc", s), pk])
                if m == 0:
                    kb.op("dve", lambda e, s=s: e.tensor_scalar(out=acc[s][:], in0=psO[s][:, 0:256],
                                                                 scalar1=rc[:, s:s + 1], scalar2=None, op0=ALU.mult),
                          reads=[pk, ("rc", s)], writes=[("acc", s), pk])
                else:
                    kb.op("dve", lambda e, s=s: e.tensor_tensor(out=rc[:, s:s + 1], in0=rc[:, s:s + 1], in1=lsc[:, 4:5],
                                                                 op=ALU.mult),
                          reads=[("rc", s), "lsc"], writes=[("rc", s)])
                    kb.op("dve", lambda e, s=s: e.scalar_tensor_tensor(out=acc[s][:], in0=psO[s][:, 0:256],
                                                                        scalar=rc[:, s:s + 1], in1=acc[s][:],
                                                                        op0=ALU.mult, op1=ALU.add),
                          reads=[pk, ("rc", s), ("acc", s)], writes=[("acc", s), pk])
            if m == 0:
                return
            ot = oTt[qc % 2]
            for s in range(4):
                kb.op("act", lambda e, s=s: e.activation(out=junk[:], in_=acc[s][:], func=AF.Square, scale=1.0 / 16.0,
                                                         accum_out=st[:, s:s + 1]),
                      reads=[("acc", s)], writes=["junkA", ("st", s)])
                _rsqrt(kb, st[:, s:s + 1], st[:, s:s + 1], ("st", s), ("st", s))
                o = ob[s % 2]
                kb.op("dve", lambda e, s=s, o=o: e.scalar_tensor_tensor(out=o[:], in0=acc[s][:], scalar=st[:, s:s + 1],
                                                                         in1=sw[:], op0=ALU.mult, op1=ALU.mult),
                      reads=[("acc", s), ("st", s), "sw"], writes=[("ob", s % 2)])
                for f in range(2):
                    kb.op("pe", lambda e, s=s, f=f, o=o: e.transpose(pTr[:, (s * 2 + f) * 128:(s * 2 + f + 1) * 128],
                                                                     o[:, f * 128:(f + 1) * 128], identb[:]),
                          reads=[("ob", s % 2), "identb"], writes=["pTr"])
                kb.op("dve", lambda e, s=s, ot=ot: e.tensor_copy(
                    out=ot[:, :, s * 128:(s + 1) * 128],
                    in_=pTr[:, s * 256:(s + 1) * 256].rearrange("p (f t) -> p f t", f=2)),
                    reads=["pTr"], writes=[("oTt", qc % 2), "pTr"])
            j, off = qc // 2, (qc % 2) * 512
            kb.dma("sp", oT_out[j].rearrange("(f p) t -> p f t", p=128)[:, :, off:off + 512], ot[:],
                   reads=[("oTt", qc % 2)], is_out=True)

        maps = [dict(qT=qT[m], kT=kT[m], qkey="qT%d" % m, kkey="kT%d" % m) for m in range(2)]
        _attn_pipe(kb, "A", maps, Vaug, ["Vaug", "Vones"], 256, masks, "dmask", scale, finish)
    return kb.done()


def build_B():
    kb = KB()
    xg = kb.dram("xg", [8, D, TL], BF16, "ExternalInput")
    w_loc = kb.dram("w_loc", [2, D, 514], F32, "ExternalInput")
    qkw = kb.dram("qkw", [128, 2], F32, "ExternalInput")
    fb = kb.dram("fb", [1, 2], F32, "ExternalInput")
    oT_out = kb.dram("oT_out", [8, 256, TL], BF16, "ExternalOutput")

    identb = _ident(kb, "identb", BF16)
    ones_f = kb.sb("ones_f", [128, 128], F32)
    kb.op("pool", lambda e: e.memset(ones_f[:], 1.0), writes=["ones_f"])
    qkw_sb = kb.sb("qkw_sb", [128, 2], F32)
    kb.dma("sp", qkw_sb[:], qkw, writes=["qkw_sb"])
    kb.op("dve", lambda e: e.tensor_scalar(out=qkw_sb[:, 0:1], in0=qkw_sb[:, 0:1], scalar1=128 ** -0.5, scalar2=None,
                                           op0=ALU.mult), reads=["qkw_sb"], writes=["qkw_sb"])
    nfb = kb.sb("nfb", [1, 2], F32)
    kb.dma("sp", nfb[:], fb, writes=["nfb"])
    kb.op("dve", lambda e: e.tensor_scalar(out=nfb[:], in0=nfb[:], scalar1=-1.0, scalar2=None, op0=ALU.mult),
          reads=["nfb"], writes=["nfb"])
    ones_r = kb.sb("ones_r", [1, 512], F32)
    kb.op("pool", lambda e: e.memset(ones_r[:], 1.0), writes=["ones_r"])

    for hl in range(2):
        with kb.phase():
            qT = kb.sb("qT", [128, S], BF16)
            kT = kb.sb("kT", [128, S], BF16)
            Vaug = kb.sb("Vaug", [128, 64, 136], BF16)
            G = kb.sb("G", [128, 64, 128], BF16)
            ka = kb.sb("ka", [128, S], BF16)
            qa = kb.sb("qa", [128, S], BF16)
            kb.op("pool", lambda e: e.memset(Vaug[:, :, 128:129], 1.0), writes=["Vones"])
            kb.op("pool", lambda e: e.memset(ka[:], 0.0), writes=[("ka", r) for r in range(6)])
            kb.op("pool", lambda e: e.memset(qa[:], 0.0), writes=[("qa", r) for r in range(6)])
            kb.op("pool", lambda e: e.memset(ka[0:6, :], 1.0), writes=[("ka", r) for r in range(6)])
            kb.op("pool", lambda e: e.memset(qa[0:6, :], 1.0), writes=[("qa", r) for r in range(6)])
            with kb.phase():
                w_sb = kb.sb("w_sb", [128, 16, 514], BF16)
                kb.dma("pool", w_sb[:], w_loc[hl].rearrange("(k p) n -> p k n", p=128), writes=["w_sb"])
                xcs = [kb.sb("xc%d" % i, [128, 16, 512], BF16) for i in range(2)]
                pp = [kb.ps("pp%d" % i, [128, 512], F32) for i in range(4)]
                pss = [kb.ps("pss%d" % i, [128, 512], F32) for i in range(2)]
                psf = kb.ps("psf", [128, 512], F32)
                raw = [kb.sb("raw%d" % i, [128, 512], F32) for i in range(2)]
                sq = [kb.sb("sq%d" % i, [128, 512], F32) for i in range(2)]
                rs = [kb.sb("rs%d" % i, [128, 512], F32) for i in range(2)]
                e_t = kb.sb("e_t", [1, 512], F32)
                l_t = kb.sb("l_t", [1, 512], F32)
                cum = [kb.sb("cum%d" % i, [1, 512], F32) for i in range(2)]
                r1 = kb.sb("r1", [1, 512], F32)
                hml = kb.sb("hml", [1, 3, 512], BF16)
                nhml = kb.sb("nhml", [1, 3, 512], BF16)

                def do_chunk(c):
                    xc = xcs[c % 2]
                    xs = c % 2
                    _load_xc(kb, xg, xc, c, xs)

                    def qk_task(j):
                        R = _Rec(kb)
                        ps, pk = pp[j], ("pp", j)
                        rw, sqq, rss, psj = raw[j], sq[j], rs[j], pss[j]
                        R.add(lambda: _proj_fm(kb, ps, pk, w_sb, j * 128, xc, xs, _noop))
                        R.op("act", lambda e: e.copy(out=rw[:], in_=ps[:]), reads=[pk], writes=[("raw", j), pk])
                        R.op("dve", lambda e: e.tensor_tensor(out=sqq[:], in0=rw[:], in1=rw[:], op=ALU.mult),
                             reads=[("raw", j)], writes=[("sq", j)])
                        R.op("pe", lambda e: e.matmul(psj[:], lhsT=ones_f[:], rhs=sqq[:], start=True, stop=True),
                             reads=[("sq", j), "ones_f"], writes=[("pss", j)])
                        R.op("act", lambda e: e.activation(out=rss[:], in_=psj[:], func=AF.Ln, scale=1.0 / 128.0, bias=EPS),
                             reads=[("pss", j)], writes=[("rs", j), ("pss", j)])
                        R.op("act", lambda e: e.activation(out=rss[:], in_=rss[:], func=AF.Exp, scale=-0.5),
                             reads=[("rs", j)], writes=[("rs", j)])
                        dst = (qT if j == 0 else kT)[:, c * 512:(c + 1) * 512]
                        R.op("dve", lambda e: e.scalar_tensor_tensor(out=dst, in0=rw[:], scalar=qkw_sb[:, j:j + 1],
                                                                     in1=rss[:], op0=ALU.mult, op1=ALU.mult),
                             reads=[("raw", j), ("rs", j), "qkw_sb"], writes=["qT" if j == 0 else "kT"])
                        return R.lst

                    def vg_task(s, bank):
                        R = _Rec(kb)
                        ps, pk = pp[bank], ("pp", bank)
                        R.add(lambda: _proj_tm(kb, ps, pk, w_sb, 256, 256, xc, xs, s, _noop))
                        R.op("dve", lambda e: e.tensor_copy(out=Vaug[:, c * 4 + s, 0:128], in_=ps[:, 0:128]),
                             reads=[pk], writes=["Vaug", pk])
                        R.op("act", lambda e: e.activation(out=G[:, c * 4 + s, :], in_=ps[:, 128:256], func=AF.Sigmoid),
                             reads=[pk], writes=["G", pk])
                        return R.lst

                    def f_task():
                        R = _Rec(kb)

                        def mm():
                            for k in range(16):
                                kb.op("pe", lambda e, k=k: e.matmul(psf[0:1, :], lhsT=w_sb[:, k, 512:513], rhs=xc[:, k, :],
                                                                    start=(k == 0), stop=(k == 15)),
                                      reads=["w_sb", ("xc", xs)], writes=["psf"])
                        R.add(mm)
                        R.op("act", lambda e: e.activation(out=e_t[:], in_=psf[0:1, :], func=AF.Exp, scale=-1.0,
                                                           bias=nfb[0:1, hl:hl + 1]),
                             reads=["psf", "nfb"], writes=["e_t", "psf"])
                        R.op("act", lambda e: e.activation(out=l_t[:], in_=e_t[:], func=AF.Ln, bias=1.0),
                             reads=["e_t"], writes=["l_t"])
                        cu = cum[c % 2]
                        prev = cum[(c + 1) % 2]
                        init = 0.0 if c == 0 else prev[:, 511:512]
                        R.op("dve", lambda e: e.tensor_tensor_scan(out=cu[:], data0=ones_r[:], data1=l_t[:], initial=init,
                                                                   op0=ALU.mult, op1=ALU.subtract),
                             reads=["l_t", "ones_r", ("cum", (c + 1) % 2)], writes=[("cum", c % 2)])
                        ck = ("cum", c % 2)
                        R.op("dve", lambda e: e.tensor_copy(out=hml[:, 0, :], in_=cu[:]), reads=[ck], writes=["hml"])
                        R.op("dve", lambda e: e.tensor_tensor(out=r1[:], in0=cu[:], in1=hml[:, 0, :], op=ALU.subtract),
                             reads=[ck, "hml"], writes=["r1"])
                        R.op("dve", lambda e: e.tensor_copy(out=hml[:, 1, :], in_=r1[:]), reads=["r1"], writes=["hml"])
                        R.op("dve", lambda e: e.tensor_tensor(out=r1[:], in0=r1[:], in1=hml[:, 1, :], op=ALU.subtract),
                             reads=["r1", "hml"], writes=["r1"])
                        R.op("dve", lambda e: e.tensor_copy(out=hml[:, 2, :], in_=r1[:]), reads=["r1"], writes=["hml"])
                        R.op("dve", lambda e: e.tensor_scalar(out=nhml[:], in0=hml[:], scalar1=-1.0, scalar2=None, op0=ALU.mult),
                             reads=["hml"], writes=["nhml"])
                        for a in range(3):
                            R.dma("sp", qa[a:a + 1, c * 512:(c + 1) * 512], hml[0:1, a, :], reads=["hml"], writes=[("qa", a)],
                                  semkey=("auxq", a))
                            R.dma("sp", ka[3 + a:4 + a, c * 512:(c + 1) * 512], nhml[0:1, a, :], reads=["nhml"],
                                  writes=[("ka", 3 + a)], semkey=("auxk", a))
                        return R.lst

                    _interleave([qk_task(0), qk_task(1), vg_task(0, 2), vg_task(1, 3)])
                    _interleave([vg_task(2, 0), vg_task(3, 1), f_task()])
                for c in range(16):
                    do_chunk(c)
            with kb.phase():
                masks = _causal_masks(kb, "cmask")
                rc = kb.sb("rc", [128, 4], F32)
                ob = [kb.sb("ob%d" % i, [128, 128], BF16) for i in range(2)]
                oTt = [kb.sb("oTt%d" % i, [128, 512], BF16) for i in range(2)]
                pTr = kb.ps("pTr", [128, 1024], BF16)
                tag = "B"

                def finish(qc, m, psO):
                    ot = oTt[qc % 2]
                    for s in range(4):
                        pk = ("pO" + tag, s)
                        kb.op("dve", lambda e, s=s: e.reciprocal(out=rc[:, s:s + 1], in_=psO[s][:, 128:129]),
                              reads=[pk], writes=[("rc", s), pk])
                        o = ob[s % 2]
                        kb.op("dve", lambda e, s=s, o=o: e.scalar_tensor_tensor(
                            out=o[:], in0=psO[s][:, 0:128], scalar=rc[:, s:s + 1], in1=G[:, qc * 4 + s, :],
                            op0=ALU.mult, op1=ALU.mult), reads=[pk, ("rc", s), "G"], writes=[("ob", s % 2), pk])
                        kb.op("pe", lambda e, s=s, o=o: e.transpose(pTr[:, s * 128:(s + 1) * 128], o[:], identb[:]),
                              reads=[("ob", s % 2), "identb"], writes=["pTr"])
                    kb.op("act", lambda e, ot=ot: e.copy(out=ot[:], in_=pTr[:, 0:512]), reads=["pTr"],
                          writes=[("oTt", qc % 2), "pTr"])
                    j, off = qc // 2, (qc % 2) * 512
                    kb.dma("sp", oT_out[j, hl * 128:(hl + 1) * 128, off:off + 512], ot[:], reads=[("oTt", qc % 2)],
                           is_out=True)
                _attn_pipe(kb, tag, [dict(qT=qT, kT=kT, qkey="qT", kkey="kT", aux=(ka, qa))], Vaug, ["Vaug", "Vones"],
                           128, masks, "cmask", 1.0, finish)
    return kb.done()


def _affine_mat(kb, name, dt, pattern, base, cm, op):
    t = kb.sb(name, [128, 128], dt)
    kb.op("pool", lambda e: e.memset(t[:], 1.0), writes=[name])
    kb.op("pool", lambda e: e.affine_select(out=t[:], in_=t[:], pattern=pattern, compare_op=op, fill=0.0, base=base,
                                            channel_multiplier=cm), reads=[name], writes=[name])
    return t


def _b1(ap):
    return ap.unsqueeze(1).broadcast_to([128, 2, 128])


def _b2(ap):
    return ap.unsqueeze(2).broadcast_to([128, 2, 128])


def _h2(ap):
    return ap.rearrange("p (h d) -> p h d", h=2)


def build_C2():
    kb = KB()
    xg = kb.dram("xg", [8, D, TL], BF16, "ExternalInput")
    w_loc = kb.dram("w_loc", [2, D, 772], F32, "ExternalInput")
    cw = kb.dram("cw", [2, 128, 16], F32, "ExternalInput")
    hp = kb.dram("hp", [2, 4], F32, "ExternalInput")
    onw = kb.dram("onw", [1, 128], F32, "ExternalInput")
    oT_out = kb.dram("oT_out", [8, 512, TL], BF16, "ExternalOutput")

    identb = _ident(kb, "identb", BF16)
    identf = _ident(kb, "identf", F32)
    ones_f = kb.sb("ones_f", [128, 128], F32)
    kb.op("pool", lambda e: e.memset(ones_f[:], 1.0), writes=["ones_f"])
    triu = _affine_mat(kb, "triu", F32, [[1, 128]], 0, -1, ALU.is_ge)
    sel = _affine_mat(kb, "sel", F32, [[0, 128]], -127, 1, ALU.is_equal)
    lowm = _affine_mat(kb, "lowm", F32, [[-1, 128]], 0, 1, ALU.is_ge)
    strm = _affine_mat(kb, "strm", F32, [[-1, 128]], -1, 1, ALU.is_ge)
    wn = kb.sb("wn", [128, 128], F32)
    kb.dma("sp", wn[:], onw.partition_broadcast(128), writes=["wn"])
    bdm = kb.sb("bdm", [128, 128], F32)
    offm = kb.sb("offm", [128, 128], F32)
    kb.op("pool", lambda e: e.memset(bdm[:], 0.0), writes=["bdm"])
    kb.op("pool", lambda e: e.memset(offm[:], 1.0), writes=["offm"])
    for b in range(4):
        kb.op("pool", lambda e, b=b: e.memset(bdm[32 * b:32 * b + 32, 32 * b:32 * b + 32], 1.0), writes=["bdm"])
        kb.op("pool", lambda e, b=b: e.memset(offm[32 * b:32 * b + 32, 32 * b:32 * b + 32], 0.0), writes=["offm"])

    for pr in range(2):
        with kb.phase():
            qT = kb.sb("qT", [128, S], BF16)
            kT = kb.sb("kT", [128, S], BF16)
            ktm = kb.sb("ktm", [128, 64, 128], BF16)
            vb = kb.sb("vb", [128, 64, 2, 128], BF16)
            Zs = kb.sb("Zs", [128, 64, 2, 128], BF16)
            beta = kb.sb("beta", [128, 64, 2], F32)
            g = kb.sb("g", [128, 64, 2], F32)
            cw_sb = kb.sb("cw_sb", [128, 16], F32)
            hp_sb = kb.sb("hp_sb", [128, 4], F32)
            kb.dma("sp", cw_sb[:], cw[pr], writes=["cw_sb"])
            kb.dma("sp", hp_sb[:], hp[pr:pr + 1, :].partition_broadcast(128), writes=["hp_sb"])
            kb.op("act", lambda e: e.activation(out=hp_sb[:, 0:2], in_=hp_sb[:, 0:2], func=AF.Exp), reads=["hp_sb"],
                  writes=["hp_sb"])
            kb.op("dve", lambda e: e.tensor_scalar(out=hp_sb[:, 0:2], in0=hp_sb[:, 0:2], scalar1=-1.0, scalar2=None,
                                                   op0=ALU.mult), reads=["hp_sb"], writes=["hp_sb"])
            with kb.phase():
                w_sb = kb.sb("w_sb", [128, 16, 772], BF16)
                kb.dma("pool", w_sb[:], w_loc[pr].rearrange("(k p) n -> p k n", p=128), writes=["w_sb"])
                xcs = [kb.sb("xc%d" % i, [128, 16, 512], BF16) for i in range(2)]
                pp = [kb.ps("pp%d" % i, [128, 512], F32) for i in range(4)]
                pss = [kb.ps("pss%d" % i, [128, 512], F32) for i in range(2)]
                ptr = kb.ps("ptr", [128, 1024], BF16)
                ptr2 = kb.ps("ptr2", [128, 1024], BF16)
                rawc = [kb.sb("rawc%d" % j, [128, 515], F32) for j in range(4)]
                acc = [kb.sb("cacc%d" % i, [128, 512], F32) for i in range(4)]
                sil = acc
                sq = [kb.sb("sq%d" % i, [128, 512], F32) for i in range(2)]
                rs = [kb.sb("rs%d" % i, [128, 512], F32) for i in range(2)]
                vbf = [kb.sb("vbf%d" % i, [128, 512], BF16) for i in range(2)]
                et = kb.sb("et", [128, 4, 2], F32)
                for j in range(4):
                    kb.op("pool", lambda e, j=j: e.memset(rawc[j][:], 0.0), writes=[("rawc", j)])
                def do_chunk(c):
                    xc = xcs[c % 2]
                    xs = c % 2
                    _load_xc(kb, xg, xc, c, xs)
                    tasks = []
                    for s in range(4):
                        R = _Rec(kb)
                        ps = pp[s]
                        pk = ("pp", s)
                        blk = c * 4 + s
                        R.add(lambda ps=ps, pk=pk, s=s: _proj_tm(kb, ps, pk, w_sb, 512, 260, xc, xs, s, _noop))
                        R.op("act", lambda e, ps=ps, blk=blk: e.activation(out=Zs[:, blk, :, :], in_=_h2(ps[:, 0:256]), func=AF.Silu),
                             reads=[pk], writes=["Zs", pk])
                        R.op("act", lambda e, ps=ps, blk=blk: e.activation(out=beta[:, blk, :], in_=ps[:, 256:258], func=AF.Sigmoid),
                             reads=[pk], writes=[("beta", blk), pk])
                        R.op("dve", lambda e, ps=ps, s=s: e.tensor_tensor(out=et[:, s, :], in0=ps[:, 258:260], in1=hp_sb[:, 2:4],
                                                                          op=ALU.add), reads=[pk, "hp_sb"], writes=[("et", s), pk])
                        R.op("act", lambda e, s=s: e.activation(out=et[:, s, :], in_=et[:, s, :], func=AF.Exp),
                             reads=[("et", s)], writes=[("et", s)])
                        R.op("act", lambda e, s=s: e.activation(out=et[:, s, :], in_=et[:, s, :], func=AF.Ln, bias=1.0),
                             reads=[("et", s)], writes=[("et", s)])
                        R.op("dve", lambda e, s=s, blk=blk: e.tensor_tensor(out=g[:, blk, :], in0=et[:, s, :], in1=hp_sb[:, 0:2],
                                                                            op=ALU.mult), reads=[("et", s), "hp_sb"], writes=["g"])
                        tasks.append(R.lst)
                    _interleave(tasks)
                    tasks = []
                    for j in range(4):
                        R = _Rec(kb)
                        ps = pp[j]
                        pk = ("pp", j)
                        rw, ac, sl = rawc[j], acc[j], sil[j]
                        rk = ("rawc", j)
                        R.add(lambda ps=ps, pk=pk, j=j: _proj_fm(kb, ps, pk, w_sb, j * 128, xc, xs, _noop))
                        if c > 0:
                            R.op("dve", lambda e, rw=rw: e.tensor_copy(out=rw[:, 0:3], in_=rw[:, 512:515]), reads=[rk], writes=[rk])
                        R.op("act", lambda e, rw=rw, ps=ps: e.copy(out=rw[:, 3:515], in_=ps[:]), reads=[pk], writes=[rk, pk])
                        R.op("dve", lambda e, rw=rw, ac=ac, j=j: e.tensor_scalar(out=ac[:], in0=rw[:, 0:512],
                                                                                  scalar1=cw_sb[:, j * 4:j * 4 + 1],
                                                                                  scalar2=None, op0=ALU.mult),
                             reads=[rk, "cw_sb"], writes=[("cacc", j)])
                        for tap in range(1, 4):
                            R.op("dve", lambda e, tap=tap, rw=rw, ac=ac, j=j: e.scalar_tensor_tensor(
                                out=ac[:], in0=rw[:, tap:tap + 512], scalar=cw_sb[:, j * 4 + tap:j * 4 + tap + 1],
                                in1=ac[:], op0=ALU.mult, op1=ALU.add), reads=[rk, "cw_sb", ("cacc", j)], writes=[("cacc", j)])
                        R.op("act", lambda e, ac=ac, sl=sl: e.activation(out=sl[:], in_=ac[:], func=AF.Silu), reads=[("cacc", j)],
                             writes=[("cacc", j)])
                        if j < 2:
                            sqq, rss, psj = sq[j], rs[j], pss[j]
                            R.op("dve", lambda e, sl=sl, sqq=sqq: e.tensor_tensor(out=sqq[:], in0=sl[:], in1=sl[:], op=ALU.mult),
                                 reads=[("cacc", j)], writes=[("sq", j)])
                            R.op("pe", lambda e, sqq=sqq, psj=psj: e.matmul(psj[:], lhsT=ones_f[:], rhs=sqq[:], start=True, stop=True),
                                 reads=[("sq", j), "ones_f"], writes=[("pss", j)])
                            R.op("act", lambda e, rss=rss, psj=psj: e.activation(out=rss[:], in_=psj[:], func=AF.Ln, bias=EPS),
                                 reads=[("pss", j)], writes=[("rs", j), ("pss", j)])
                            R.op("act", lambda e, rss=rss: e.activation(out=rss[:], in_=rss[:], func=AF.Exp, scale=-0.5),
                                 reads=[("rs", j)], writes=[("rs", j)])
                            dst = (qT if j == 0 else kT)[:, c * 512:(c + 1) * 512]
                            sc = 128 ** -0.5 if j == 0 else 1.0
                            dk = ("qTc", c) if j == 0 else ("kTc", c)
                            R.op("dve", lambda e, dst=dst, sl=sl, sc=sc, rss=rss: e.scalar_tensor_tensor(
                                out=dst, in0=sl[:], scalar=sc, in1=rss[:], op0=ALU.mult, op1=ALU.mult),
                                reads=[("cacc", j), ("rs", j)], writes=[dk, "qT" if j == 0 else "kT"])
                            if j == 1:
                                for s in range(4):
                                    R.op("pe", lambda e, s=s: e.transpose(
                                        ptr[:, s * 128:(s + 1) * 128], kT[:, c * 512 + s * 128:c * 512 + (s + 1) * 128],
                                        identb[:]), reads=[dk, "identb"], writes=["ptr"])
                                R.op("act", lambda e: e.copy(out=ktm[:, c * 4:(c + 1) * 4, :],
                                                             in_=ptr[:, 0:512].rearrange("p (s d) -> p s d", s=4)),
                                     reads=["ptr"], writes=["ktm", "ptr"])
                        else:
                            hh = j - 2
                            vf = vbf[hh]
                            pt_, ptk, pc0 = (ptr, "ptr", 512) if hh == 0 else (ptr2, "ptr2", 0)
                            R.op("dve", lambda e, vf=vf, sl=sl: e.tensor_copy(out=vf[:], in_=sl[:]), reads=[("cacc", j)],
                                 writes=[("vbf", hh)])
                            for s in range(4):
                                R.op("pe", lambda e, s=s, vf=vf, pt_=pt_, pc0=pc0: e.transpose(
                                    pt_[:, pc0 + s * 128:pc0 + (s + 1) * 128], vf[:, s * 128:(s + 1) * 128], identb[:]),
                                    reads=[("vbf", hh), "identb"], writes=[ptk])
                            for s in range(4):
                                blk = c * 4 + s
                                R.op("dve", lambda e, s=s, blk=blk, hh=hh, pt_=pt_, pc0=pc0: e.tensor_scalar(
                                    out=vb[:, blk, hh, :], in0=pt_[:, pc0 + s * 128:pc0 + (s + 1) * 128],
                                    scalar1=beta[:, blk, hh:hh + 1], scalar2=None, op0=ALU.mult),
                                    reads=[ptk, ("beta", blk)], writes=["vb", ptk])
                        tasks.append(R.lst)
                    _interleave(tasks)
                for c in range(16):
                    do_chunk(c)
            gc = kb.sb("gc", [128, 64, 2], F32)
            glb = kb.sb("glb", [128, 64, 2], F32)
            egc = kb.sb("egc", [128, 64, 2], F32)
            ekd = kb.sb("ekd", [128, 64, 2], F32)
            egl = kb.sb("egl", [128, 64, 2], F32)
            begc = kb.sb("begc", [128, 64, 2], F32)
            nbeta = kb.sb("nbeta", [128, 64, 2], F32)
            fl = lambda t: t[:].rearrange("p a b -> p (a b)")
            with kb.phase():
                pg = kb.ps("pg", [128, 512], F32)
                kb.op("pe", lambda e: e.matmul(pg[:, 0:128], lhsT=triu[:], rhs=fl(g), start=True, stop=True),
                      reads=["triu", "g"], writes=["pg"])
                kb.op("dve", lambda e: e.tensor_copy(out=fl(gc), in_=pg[:, 0:128]), reads=["pg"], writes=["gc", "pg"])
                kb.op("pe", lambda e: e.matmul(pg[:, 128:256], lhsT=sel[:], rhs=fl(gc), start=True, stop=True),
                      reads=["sel", "gc"], writes=["pg"])
                kb.op("dve", lambda e: e.tensor_copy(out=fl(glb), in_=pg[:, 128:256]), reads=["pg"], writes=["glb", "pg"])
                kb.op("act", lambda e: e.activation(out=fl(egc), in_=fl(gc), func=AF.Exp), reads=["gc"], writes=["egc"])
                kb.op("act", lambda e: e.activation(out=fl(egl), in_=fl(glb), func=AF.Exp), reads=["glb"], writes=["egl"])
                kb.op("dve", lambda e: e.tensor_tensor(out=fl(ekd), in0=fl(glb), in1=fl(gc), op=ALU.subtract),
                      reads=["glb", "gc"], writes=["ekd"])
                kb.op("act", lambda e: e.activation(out=fl(ekd), in_=fl(ekd), func=AF.Exp), reads=["ekd"], writes=["ekd"])
                bkeys = [("beta", b) for b in range(64)]
                kb.op("dve", lambda e: e.tensor_tensor(out=fl(begc), in0=fl(beta), in1=fl(egc), op=ALU.mult),
                      reads=bkeys + ["egc"], writes=["begc"])
                kb.op("dve", lambda e: e.tensor_scalar(out=fl(nbeta), in0=fl(beta), scalar1=-1.0, scalar2=None, op0=ALU.mult),
                      reads=bkeys, writes=["nbeta"])
            with kb.phase():
                bA = kb.ps("bA", [128, 512], F32)
                bB = kb.ps("bB", [128, 512], F32)
                bD = kb.ps("bD", [128, 512], F32)
                bE = kb.ps("bE", [128, 512], F32)
                bF = kb.ps("bF", [128, 512], F32)
                bG = kb.ps("bG", [128, 1024], BF16)
                bH = kb.ps("bH", [128, 512], F32)
                bI = kb.ps("bI", [128, 512], F32)
                T2 = lambda nm, dt: kb.sb(nm, [128, 2, 128], dt)
                St = T2("St", F32)
                Sb = T2("Sb", BF16)
                kb.op("pool", lambda e: e.memset(St[:], 0.0), writes=["St"])
                kb.op("pool", lambda e: e.memset(Sb[:], 0.0), writes=["Sb"])
                dg = T2("dg", F32)
                Dm = T2("Dm", F32)
                Ds = T2("Ds", F32)
                Nf = T2("Nf", F32)
                Noff = T2("Noff", F32)
                M = [T2("M%d" % i, F32) for i in range(2)]
                MT = [T2("MT%d" % i, F32) for i in range(2)]
                X = [T2("X%d" % i, F32) for i in range(2)]
                Bi = T2("Bi", F32)
                Pm = T2("Pm", F32)
                PTm = T2("PTm", F32)
                P2T = T2("P2T", F32)
                Ym = T2("Ym", F32)
                Wm = T2("Wm", F32)
                Xf = T2("Xf", BF16)
                intra = T2("intra", BF16)
                intraT = [T2("intraT%d" % i, BF16) for i in range(2)]
                kbg = T2("kbg", BF16)
                kdec = [T2("kdec%d" % i, BF16) for i in range(2)]
                u = [T2("u%d" % i, F32) for i in range(2)]
                wT = [T2("wT%d" % i, BF16) for i in range(2)]
                vn = T2("vn", BF16)
                tq = T2("tq", F32)
                o = T2("o", F32)
                junk = T2("junkC", F32)
                st = kb.sb("stC", [128, 2], F32)
                on = T2("on", F32)
                ob = T2("obC", BF16)
                oTt = [kb.sb("oTtC%d" % i, [128, 2, 512], BF16) for i in range(2)]
                f2 = lambda t: t[:].rearrange("p h d -> p (h d)")

                def mm2(rec, bank, bkey, c0, lhs, rhs, reads):
                    for hh in range(2):
                        rec("pe", lambda e, hh=hh: e.matmul(bank[:, c0 + hh * 128:c0 + (hh + 1) * 128], lhsT=lhs(hh), rhs=rhs(hh),
                                                            start=True, stop=True), reads=reads, writes=[bkey])

                def gen_prep(n):
                    lst = []

                    def P(*a, **k):
                        lst.append(lambda: kb.op(*a, **k))
                    tok = slice(n * 128, (n + 1) * 128)
                    nb = n % 2
                    P("pe", lambda e: e.matmul(bA[:, 0:128], lhsT=kT[:, tok], rhs=kT[:, tok], start=True, stop=True),
                      reads=["kT"], writes=["bA"])
                    P("pe", lambda e: e.matmul(bA[:, 128:256], lhsT=qT[:, tok], rhs=kT[:, tok], start=True, stop=True),
                      reads=["qT", "kT"], writes=["bA"])
                    P("dve", lambda e: e.tensor_tensor(out=dg[:], in0=_b1(identf[:]), in1=_b2(gc[:, n, :]), op=ALU.mult),
                      reads=["identf", "gc"], writes=["dg"])
                    P("pe", lambda e: e.matmul(bB[:, 0:256], lhsT=ones_f[:], rhs=f2(dg), start=True, stop=True),
                      reads=["dg", "ones_f"], writes=["bB"])
                    P("dve", lambda e: e.tensor_tensor(out=Dm[:], in0=_b2(gc[:, n, :]), in1=_h2(bB[:, 0:256]), op=ALU.subtract),
                      reads=["bB", "gc"], writes=["Dm", "bB"])
                    P("dve", lambda e: e.tensor_scalar(out=f2(Dm), in0=f2(Dm), scalar1=0.0, scalar2=None, op0=ALU.min),
                      reads=["Dm"], writes=["Dm"])
                    P("act", lambda e: e.activation(out=f2(Dm), in_=f2(Dm), func=AF.Exp), reads=["Dm"], writes=["Dm"])
                    P("pool", lambda e: e.tensor_tensor(out=Ds[:], in0=Dm[:], in1=_b1(strm[:]), op=ALU.mult),
                      reads=["Dm", "strm"], writes=["Ds"])
                    P("pool", lambda e: e.tensor_tensor(out=Dm[:], in0=Dm[:], in1=_b1(lowm[:]), op=ALU.mult),
                      reads=["Dm", "lowm", "Ds"], writes=["Dm"])
                    P("pool", lambda e: e.tensor_tensor(out=Ds[:], in0=Ds[:], in1=_b2(nbeta[:, n, :]), op=ALU.mult),
                      reads=["Ds", "nbeta"], writes=["Ds"])
                    P("dve", lambda e: e.tensor_tensor(out=Nf[:], in0=Ds[:], in1=_b1(bA[:, 0:128]), op=ALU.mult),
                      reads=["bA", "Ds"], writes=["Nf", "bA"])
                    P("dve", lambda e: e.tensor_tensor(out=intra[:], in0=Dm[:], in1=_b1(bA[:, 128:256]), op=ALU.mult),
                      reads=["bA", "Dm"], writes=["intra", "bA"])
                    for hh in range(2):
                        P("pe", lambda e, hh=hh: e.transpose(bG[:, hh * 128:(hh + 1) * 128], intra[:, hh, :], identb[:]),
                          reads=["intra", "identb"], writes=["bG"])
                    P("act", lambda e: e.copy(out=f2(intraT[nb]), in_=bG[:, 0:256]), reads=["bG"], writes=[("intraT", nb), "bG"])
                    P("pool", lambda e: e.tensor_tensor(out=M[0][:], in0=Nf[:], in1=_b1(bdm[:]), op=ALU.mult),
                      reads=["Nf", "bdm"], writes=[("M", 0)])
                    P("pool", lambda e: e.tensor_tensor(out=Noff[:], in0=Nf[:], in1=_b1(offm[:]), op=ALU.mult),
                      reads=["Nf", "offm"], writes=["Noff"])
                    for hh in range(2):
                        P("pe", lambda e, hh=hh: e.transpose(bD[:, hh * 128:(hh + 1) * 128], M[0][:, hh, :], identf[:]),
                          reads=[("M", 0), "identf"], writes=["bD"])
                    P("act", lambda e: e.copy(out=f2(MT[0]), in_=bD[:, 0:256]), reads=["bD"], writes=[("MT", 0), "bD"])
                    P("dve", lambda e: e.tensor_tensor(out=X[0][:], in0=MT[0][:], in1=_b1(identf[:]), op=ALU.add),
                      reads=[("MT", 0), "identf"], writes=[("X", 0)])
                    xi = 0
                    mi = 0
                    for lev in range(1, 5):
                        mo = 1 - mi
                        mm2(P, bD, "bD", 0, lambda hh, mi=mi: MT[mi][:, hh, :], lambda hh, mi=mi: M[mi][:, hh, :],
                            [("M", mi), ("MT", mi)])
                        if lev < 4:
                            mm2(P, bE, "bE", 0, lambda hh, mi=mi: M[mi][:, hh, :], lambda hh, mi=mi: MT[mi][:, hh, :],
                                [("M", mi), ("MT", mi)])
                        P("act", lambda e, mo=mo: e.copy(out=f2(M[mo]), in_=bD[:, 0:256]), reads=["bD"], writes=[("M", mo), "bD"])
                        if lev < 4:
                            P("dve", lambda e, mo=mo: e.tensor_copy(out=f2(MT[mo]), in_=bE[:, 0:256]), reads=["bE"],
                              writes=[("MT", mo), "bE"])
                        mm2(P, bF, "bF", 0, lambda hh, mo=mo: M[mo][:, hh, :], lambda hh, xi=xi: X[xi][:, hh, :],
                            [("M", mo), ("X", xi)])
                        P("dve", lambda e, xi=xi: e.tensor_tensor(out=f2(X[1 - xi]), in0=bF[:, 0:256], in1=f2(X[xi]), op=ALU.add),
                          reads=["bF", ("X", xi)], writes=[("X", 1 - xi), "bF"])
                        xi = 1 - xi
                        mi = mo
                    BiT = X[xi]
                    bk = ("X", xi)
                    for hh in range(2):
                        P("pe", lambda e, hh=hh: e.transpose(bD[:, hh * 128:(hh + 1) * 128], BiT[:, hh, :], identf[:]),
                          reads=[bk, "identf"], writes=["bD"])
                    P("act", lambda e: e.copy(out=f2(Bi), in_=bD[:, 0:256]), reads=["bD"], writes=["Bi", "bD"])
                    mm2(P, bE, "bE", 0, lambda hh: Noff[:, hh, :], lambda hh: BiT[:, hh, :], ["Noff", bk])
                    mm2(P, bF, "bF", 0, lambda hh: BiT[:, hh, :], lambda hh: Noff[:, hh, :], ["Noff", bk])
                    P("dve", lambda e: e.tensor_copy(out=f2(Pm), in_=bE[:, 0:256]), reads=["bE"], writes=["Pm", "bE"])
                    P("act", lambda e: e.copy(out=f2(PTm), in_=bF[:, 0:256]), reads=["bF"], writes=["PTm", "bF"])
                    mm2(P, bD, "bD", 0, lambda hh: Pm[:, hh, :], lambda hh: PTm[:, hh, :], ["Pm", "PTm"])
                    P("act", lambda e: e.copy(out=f2(P2T), in_=bD[:, 0:256]), reads=["bD"], writes=["P2T", "bD"])
                    P("dve", lambda e: e.tensor_tensor(out=Ym[:], in0=Pm[:], in1=_b1(identf[:]), op=ALU.add),
                      reads=["Pm", "identf"], writes=["Ym"])
                    mm2(P, bE, "bE", 0, lambda hh: P2T[:, hh, :], lambda hh: Ym[:, hh, :], ["P2T", "Ym"])
                    P("dve", lambda e: e.tensor_tensor(out=f2(Wm), in0=bE[:, 0:256], in1=f2(Ym), op=ALU.add),
                      reads=["bE", "Ym"], writes=["Wm", "bE"])
                    mm2(P, bF, "bF", 0, lambda hh: Bi[:, hh, :], lambda hh: Wm[:, hh, :], ["Bi", "Wm"])
                    P("act", lambda e: e.copy(out=f2(Xf), in_=bF[:, 0:256]), reads=["bF"], writes=["Xf", "bF"])
                    P("dve", lambda e: e.tensor_tensor(out=kbg[:], in0=_b1(ktm[:, n, :]), in1=_b2(begc[:, n, :]), op=ALU.mult),
                      reads=["ktm", "begc"], writes=["kbg"])
                    P("pool", lambda e: e.tensor_tensor(out=kdec[nb][:], in0=_b1(ktm[:, n, :]), in1=_b2(ekd[:, n, :]), op=ALU.mult),
                      reads=["ktm", "ekd"], writes=[("kdec", nb)])
                    mm2(P, bA, "bA", 256, lambda hh: Xf[:, hh, :], lambda hh: vb[:, n, hh, :], ["Xf", "vb"])
                    mm2(P, bB, "bB", 256, lambda hh: kbg[:, hh, :], lambda hh: Xf[:, hh, :], ["Xf", "kbg"])
                    P("act", lambda e: e.copy(out=f2(u[nb]), in_=bA[:, 256:512]), reads=["bA"], writes=[("u", nb), "bA"])
                    P("dve", lambda e: e.tensor_copy(out=f2(wT[nb]), in_=bB[:, 256:512]), reads=["bB"], writes=[("wT", nb), "bB"])
                    return lst

                def gen_scan(n):
                    lst = []

                    def Sx(*a, **k):
                        lst.append(lambda: kb.op(*a, **k))
                    tok = slice(n * 128, (n + 1) * 128)
                    nb = n % 2
                    mm2(Sx, bH, "bH", 0, lambda hh: wT[nb][:, hh, :], lambda hh: Sb[:, hh, :], [("wT", nb), "Sb"])
                    Sx("dve", lambda e: e.tensor_tensor(out=f2(vn), in0=f2(u[nb]), in1=bH[:, 0:256], op=ALU.subtract),
                       reads=[("u", nb), "bH"], writes=["vn", "bH"])
                    mm2(Sx, bH, "bH", 256, lambda hh: qT[:, tok], lambda hh: Sb[:, hh, :], ["qT", "Sb"])
                    mm2(Sx, bI, "bI", 0, lambda hh: intraT[nb][:, hh, :], lambda hh: vn[:, hh, :], [("intraT", nb), "vn"])
                    Sx("dve", lambda e: e.tensor_tensor(out=tq[:], in0=_b2(egc[:, n, :]), in1=_h2(bH[:, 256:512]), op=ALU.mult),
                       reads=["bH", "egc"], writes=["tq", "bH"])
                    Sx("dve", lambda e: e.tensor_tensor(out=f2(o), in0=f2(tq), in1=bI[:, 0:256], op=ALU.add),
                       reads=["tq", "bI"], writes=["o", "bI"])
                    mm2(Sx, bI, "bI", 256, lambda hh: kdec[nb][:, hh, :], lambda hh: vn[:, hh, :], [("kdec", nb), "vn"])
                    Sx("pool", lambda e: e.tensor_tensor(out=St[:], in0=St[:], in1=_b2(egl[:, n, :]), op=ALU.mult),
                       reads=["St", "egl"], writes=["St"])
                    Sx("dve", lambda e: e.tensor_tensor(out=f2(St), in0=f2(St), in1=bI[:, 256:512], op=ALU.add),
                       reads=["St", "bI"], writes=["St", "bI"])
                    Sx("act", lambda e: e.copy(out=f2(Sb), in_=f2(St)), reads=["St"], writes=["Sb"])
                    for hh in range(2):
                        Sx("act", lambda e, hh=hh: e.activation(out=junk[:, hh, :], in_=o[:, hh, :], func=AF.Square,
                                                                scale=128 ** -0.5, accum_out=st[:, hh:hh + 1]),
                           reads=["o"], writes=[("junkC", hh), ("stC", hh)])
                    Sx("act", lambda e: e.activation(out=st[:], in_=st[:], func=AF.Ln, bias=EPS),
                       reads=[("stC", 0), ("stC", 1)], writes=["stC"])
                    Sx("act", lambda e: e.activation(out=st[:], in_=st[:], func=AF.Exp, scale=-0.5), reads=["stC"], writes=["stC"])
                    Sx("dve", lambda e: e.tensor_tensor(out=on[:], in0=o[:], in1=_b2(st[:]), op=ALU.mult),
                       reads=["o", "stC"], writes=["on", ("stC", 0), ("stC", 1)])
                    Sx("dve", lambda e: e.tensor_tensor(out=on[:], in0=on[:], in1=_b1(wn[:]), op=ALU.mult),
                       reads=["on", "wn"], writes=["on"])
                    Sx("pool", lambda e: e.tensor_tensor(out=ob[:], in0=on[:], in1=Zs[:, n, :, :], op=ALU.mult),
                       reads=["on", "Zs"], writes=["obC"])
                    s2 = n % 2
                    for hh in range(2):
                        Sx("pe", lambda e, hh=hh: e.transpose(bG[:, 512 + (s2 * 2 + hh) * 128:512 + (s2 * 2 + hh + 1) * 128],
                                                               ob[:, hh, :], identb[:]), reads=["obC", "identb"], writes=["bG"])
                    if s2 == 1:
                        qc = n // 4
                        ot = oTt[qc % 2]
                        half = (n % 4) // 2
                        Sx("act", lambda e: e.copy(
                            out=ot[:, :, half * 256:(half + 1) * 256].rearrange("p h (b d) -> p b h d", b=2),
                            in_=bG[:, 512:1024].rearrange("p (b h d) -> p b h d", b=2, h=2)),
                            reads=["bG"], writes=[("oTtC", qc % 2), "bG"])
                        if n % 4 == 3:
                            j, off = qc // 2, (qc % 2) * 512
                            lst.append(lambda: kb.dma(
                                "sp", oT_out[j, pr * 256:(pr + 1) * 256, off:off + 512].rearrange("(h p) t -> p h t", p=128),
                                ot[:], reads=[("oTtC", qc % 2)], is_out=True))
                    return lst

                for th in gen_prep(0):
                    th()
                for n in range(64):
                    a = gen_prep(n + 1) if n + 1 < 64 else []
                    b = gen_scan(n)
                    ia = ib = 0
                    while ia < len(a) or ib < len(b):
                        for _ in range(2):
                            if ia < len(a):
                                a[ia]()
                                ia += 1
                        if ib < len(b):
                            b[ib]()
                            ib += 1
    return kb.done()


_PROGS = {}


def _prog(key, builder):
    if key not in _PROGS:
        _PROGS[key] = builder()
    return _PROGS[key]


def _run(nc, in_maps):
    res = run_bass_kernel_spmd(nc, in_maps, core_ids=list(range(NCORES)))
    return res.results


def _pk(w):
    return np.ascontiguousarray(np.asarray(w, np.float32).reshape(16, 128).T)


def _lambda_init(layer):
    return 0.8 - 0.6 * math.exp(-0.3 * layer)


def _mixer_A(xg, layer, slot, inp):
    w_in = inp["a_w_in"][slot]
    lamv = np.stack([inp["a_lam_q1"][slot], inp["a_lam_k1"][slot], inp["a_lam_q2"][slot], inp["a_lam_k2"][slot]])
    li = _lambda_init(layer)
    cst = np.tile(np.array([[li, 1.0 - li]], np.float32), (128, 1))
    subw = np.ascontiguousarray(inp["a_sub_norm"][slot].reshape(1, 256))
    maps = []
    for hd in range(NCORES):
        cols = np.concatenate([np.arange(m * 1024 + hd * 128, m * 1024 + hd * 128 + 128) for m in range(2)] +
                              [2048 + np.arange(m * 1024 + hd * 128, m * 1024 + hd * 128 + 128) for m in range(2)] +
                              [4096 + np.arange(hd * 256, hd * 256 + 256)])
        maps.append({"xg": xg, "w_loc": np.ascontiguousarray(w_in[:, cols]), "lamv": lamv, "subw": subw, "cst": cst})
    return _a2a(_run(_prog("A", build_A), maps), "oT_out")


def _a2a(res, key):
    return [np.ascontiguousarray(np.concatenate([res[r][key][j] for r in range(NCORES)], axis=0))
            for j in range(NCORES)]


def _mixer_B(xg, slot, inp):
    w_in = inp["b_w_in"][slot]
    qkw = np.ascontiguousarray(np.stack([inp["b_q_norm"][slot], inp["b_k_norm"][slot]], axis=1).astype(np.float32))
    maps = []
    for c in range(NCORES):
        wl = []
        for hl in range(2):
            h = 2 * c + hl
            cols = np.concatenate([np.arange(h * 128, h * 128 + 128) + o for o in (0, 2048, 4096, 6144)] +
                                  [np.array([8192 + h, 8192 + h])])
            wl.append(w_in[:, cols])
        fb = np.ascontiguousarray(inp["b_forget_bias"][slot][2 * c:2 * c + 2].reshape(1, 2).astype(np.float32))
        maps.append({"xg": xg, "w_loc": np.ascontiguousarray(np.stack(wl)), "qkw": qkw, "fb": fb})
    return _a2a(_run(_prog("B", build_B), maps), "oT_out")


def _mixer_C(xg, slot, inp):
    w_in = inp["c_w_in"][slot]
    conv = inp["c_conv_w"][slot]
    onw = np.ascontiguousarray(inp["c_out_norm"][slot].reshape(1, 128).astype(np.float32))
    a_log = inp["c_a_log"][slot]
    dtb = inp["c_dt_bias"][slot]
    maps = []
    for c in range(NCORES):
        wl, cws, hps = [], [], []
        for pr in range(2):
            hq = 2 * c + pr
            hv0 = 4 * c + 2 * pr
            hv1 = hv0 + 1
            qc = np.arange(hq * 128, hq * 128 + 128)
            v0 = np.arange(hv0 * 128, hv0 * 128 + 128)
            v1 = np.arange(hv1 * 128, hv1 * 128 + 128)
            cols = np.concatenate([qc, 2048 + qc, 4096 + v0, 4096 + v1, 8192 + v0, 8192 + v1,
                                   np.array([12288 + hv0, 12288 + hv1, 12320 + hv0, 12320 + hv1])])
            wl.append(w_in[:, cols])
            cws.append(np.concatenate([conv[:, ch].T for ch in (qc, 2048 + qc, 4096 + v0, 4096 + v1)], axis=1))
            hps.append([a_log[hv0], a_log[hv1], dtb[hv0], dtb[hv1]])
        maps.append({"xg": xg, "w_loc": np.ascontiguousarray(np.stack(wl)),
                     "cw": np.ascontiguousarray(np.stack(cws)).astype(np.float32),
                     "hp": np.array(hps, np.float32), "onw": onw})
    return _a2a(_run(_prog("C", build_C2), maps), "oT_out")


def kernel(**inp):
    inp = {k: np.asarray(v) for k, v in inp.items()}
    x = inp["x"][0]
    hs = [np.ascontiguousarray(x[c * TL:(c + 1) * TL]) for c in range(NCORES)]
    nrm0 = np.concatenate([_pk(inp["mix_norm"][0]), _pk(inp["mix_norm"][0])], axis=1)
    res = _run(_prog(("T", 0, True, False), lambda: build_T(2048, True, False)),
               [{"h_in": hs[c], "nrm": nrm0} for c in range(NCORES)])
    xg = np.ascontiguousarray(np.stack([res[c]["xnT_out"] for c in range(NCORES)]))
    depth = inp["mix_norm"].shape[0]
    out = None
    for i in range(depth):
        slot = i // 3
        if i % 3 == 0:
            oTs = _mixer_A(xg, i, slot, inp)
            w_out = inp["a_w_out"][slot]
        elif i % 3 == 1:
            oTs = _mixer_B(xg, slot, inp)
            w_out = inp["b_w_out"][slot]
        else:
            oTs = _mixer_C(xg, slot, inp)
            w_out = inp["c_w_out"][slot]
        last = i == depth - 1
        kdim = w_out.shape[0]
        nxt = inp["final_norm"] if last else inp["mix_norm"][i + 1]
        nrm = np.concatenate([_pk(inp["ffn_norm"][i]), _pk(nxt)], axis=1)
        maps = []
        for c in range(NCORES):
            m = {"h_in": hs[c], "oT_in": oTs[c], "w_out": w_out, "nrm": nrm, "w_gate": inp["ffn_w_gate"][i],
                 "w_up": inp["ffn_w_up"][i], "w_down": inp["ffn_w_down"][i]}
            if last:
                m["fnw"] = np.ascontiguousarray(inp["final_norm"].reshape(1, D))
            maps.append(m)
        res = _run(_prog(("T", kdim, False, last), lambda: build_T(kdim, False, last)), maps)
        if last:
            out = np.concatenate([res[c]["y_out"] for c in range(NCORES)], axis=0)[None]
        else:
            hs = [res[c]["h_out"] for c in range(NCORES)]
            xg = np.ascontiguousarray(np.stack([res[c]["xnT_out"] for c in range(NCORES)]))
    return out.astype(np.float32)
```
